# Optimizing a Trainium2 kernel written in Bass

```python
import jax, jax.numpy as jnp
from jax import lax
import numpy as np

D_MODEL = 1024
BATCH = 8
SEQ = 4096
DEPTH = 1

DN_HEADS = 8
DN_DK = 128
DN_DV = 128
DN_CONV = 4
DN_CHUNK = 64
DIL_GROUPS = ((128, 1), (512, 4), (2048, 16))
DIL_HEADS = 4
DIL_DH = 128
ATT_BLOCK = 128
NORM_EPS = 1e-6

N_DIL = len(DIL_GROUPS)
DN_QK_W = DN_HEADS * DN_DK
DN_V_W = DN_HEADS * DN_DV
DIL_W = DIL_HEADS * DIL_DH
PROJ_SIZES = (DN_QK_W, DN_QK_W, DN_V_W, DN_V_W, DN_HEADS, DN_HEADS,
              N_DIL * DIL_W, N_DIL * DIL_W, N_DIL * DIL_W, DIL_W, D_MODEL, D_MODEL)
PROJ_W = sum(PROJ_SIZES)

kernel_name = "hybrid_deltanet_dilated_alibi_block"


def _rmsnorm(x, w):
    xf = x.astype(jnp.float32)
    y = xf * lax.rsqrt(jnp.mean(xf * xf, axis=-1, keepdims=True) + NORM_EPS)
    return (y * w.astype(jnp.float32)).astype(x.dtype)


def _l2norm(x):
    return x * lax.rsqrt(jnp.sum(x * x, axis=-1, keepdims=True) + NORM_EPS)


def _split_cols(t, sizes):
    out, start = [], 0
    for s in sizes:
        out.append(t[..., start:start + s])
        start += s
    return out


def _causal_conv(u, w):
    K, C = w.shape
    return lax.conv_general_dilated(
        u, w[:, None, :].astype(u.dtype), window_strides=(1,), padding=[(K - 1, 0)],
        dimension_numbers=('NWC', 'WIO', 'NWC'), feature_group_count=C)


def _alibi_slopes(n):
    return 2.0 ** (-8.0 * jnp.arange(1, n + 1, dtype=jnp.float32) / n)


def _gated_delta_rule(q, k, v, beta, g):
    Bn, Sn, H, dk = q.shape
    dv = v.shape[-1]
    C = DN_CHUNK
    N = Sn // C

    def chunk(t):
        t = t.reshape((Bn, N, C, H) + t.shape[3:])
        return jnp.moveaxis(t, 3, 1)

    q = chunk(q) * (dk ** -0.5)
    k, v, beta, g = chunk(k), chunk(v), chunk(beta), chunk(g)
    gc = jnp.cumsum(g, axis=-1)
    causal = jnp.tril(jnp.ones((C, C), dtype=bool))
    strict = jnp.tril(jnp.ones((C, C), dtype=bool), -1)
    gamma = jnp.exp(jnp.where(causal, gc[..., :, None] - gc[..., None, :], -jnp.inf))

    kb = k * beta[..., None]
    a = jnp.einsum('bhnid,bhnjd->bhnij', kb, k) * gamma
    m = jnp.where(strict, a, 0.0) + jnp.eye(C, dtype=a.dtype)
    rhs = jnp.concatenate([v * beta[..., None], kb * jnp.exp(gc)[..., None]], axis=-1)
    sol = lax.linalg.triangular_solve(m, rhs, left_side=True, lower=True, unit_diagonal=True)
    u, w = sol[..., :dv], sol[..., dv:]

    aqk = jnp.einsum('bhnid,bhnjd->bhnij', q, k) * gamma
    qd = q * jnp.exp(gc)[..., None]
    kd = k * jnp.exp(gc[..., -1:] - gc)[..., None]
    dlast = jnp.exp(gc[..., -1])

    def step(state, xs):
        u_n, w_n, aqk_n, qd_n, kd_n, dl_n = xs
        v_new = u_n - jnp.einsum('bhck,bhkv->bhcv', w_n, state)
        o = jnp.einsum('bhck,bhkv->bhcv', qd_n, state) + jnp.einsum('bhij,bhjv->bhiv', aqk_n, v_new)
        state = state * dl_n[..., None, None] + jnp.einsum('bhck,bhcv->bhkv', kd_n, v_new)
        return state, o

    xs = (jnp.moveaxis(u, 2, 0), jnp.moveaxis(w, 2, 0), jnp.moveaxis(aqk, 2, 0),
          jnp.moveaxis(qd, 2, 0), jnp.moveaxis(kd, 2, 0), jnp.moveaxis(dlast, 2, 0))
    s0 = jnp.zeros((Bn, H, dk, dv), jnp.float32)
    _, o = lax.scan(step, s0, xs)
    o = jnp.moveaxis(o, 0, 2)
    return jnp.moveaxis(o, 1, 3).reshape(Bn, Sn, H, dv)


def _dilated_group(q, k, v, window, dilation, slopes):
    Bn, Sn, H, dh = q.shape
    L = Sn // dilation
    span = window // dilation
    nb = -(-L // ATT_BLOCK)
    n_prev = -(-span // ATT_BLOCK)
    Lp = nb * ATT_BLOCK
    KW = (n_prev + 1) * ATT_BLOCK

    def sub(t):
        return jnp.swapaxes(t.reshape(Bn, L, dilation, H, dh), 1, 2)

    qb = jnp.pad(sub(q), ((0, 0), (0, 0), (0, Lp - L), (0, 0), (0, 0)))
    qb = qb.reshape(Bn, dilation, nb, ATT_BLOCK, H, dh)

    def windows(t):
        t = jnp.pad(sub(t), ((0, 0), (0, 0), (n_prev * ATT_BLOCK, Lp - L), (0, 0), (0, 0)))
        t = t.reshape(Bn, dilation, nb + n_prev, ATT_BLOCK, H, dh)
        return jnp.concatenate([t[:, :, j:j + nb] for j in range(n_prev + 1)], axis=3)

    kw, vw = windows(k), windows(v)
    lq = (jnp.arange(nb)[:, None, None] * ATT_BLOCK + jnp.arange(ATT_BLOCK)[None, :, None])
    dist = n_prev * ATT_BLOCK + jnp.arange(ATT_BLOCK)[:, None] - jnp.arange(KW)[None, :]
    valid = (dist >= 0) & (dist <= span) & (lq - dist >= 0)
    alibi = slopes[:, None, None] * (dist * dilation).astype(jnp.float32)[None]

    s = jnp.einsum('bdnqhe,bdnkhe->bdnhqk', qb, kw).astype(jnp.float32) * (dh ** -0.5) - alibi
    s = jnp.where(valid[None, None, :, None], s, -jnp.inf)
    mx = jnp.max(s, axis=-1)
    p = jnp.exp(s - mx[..., None])
    den = jnp.sum(p, axis=-1)
    num = jnp.einsum('bdnhqk,bdnkhe->bdnqhe', p, vw.astype(jnp.float32))

    def back(t):
        t = t.reshape((Bn, dilation, Lp) + t.shape[4:])[:, :, :L]
        return jnp.swapaxes(t, 1, 2).reshape((Bn, Sn) + t.shape[3:])

    return back(num), back(jnp.swapaxes(den, 3, 4)), back(jnp.swapaxes(mx, 3, 4))


def _dilated_attention(q, k, v):
    Bn, Sn, _ = q.shape
    q = q.reshape(Bn, Sn, N_DIL, DIL_HEADS, DIL_DH)
    k = k.reshape(Bn, Sn, N_DIL, DIL_HEADS, DIL_DH)
    v = v.reshape(Bn, Sn, N_DIL, DIL_HEADS, DIL_DH)
    slopes = _alibi_slopes(N_DIL * DIL_HEADS).reshape(N_DIL, DIL_HEADS)
    parts = [_dilated_group(q[:, :, i], k[:, :, i], v[:, :, i], win, dil, slopes[i])
             for i, (win, dil) in enumerate(DIL_GROUPS)]
    m_all = parts[0][2]
    for _, _, mx in parts[1:]:
        m_all = jnp.maximum(m_all, mx)
    num = 0.0
    den = 0.0
    for nm, dn, mx in parts:
        sc = jnp.exp(mx - m_all)
        num = num + nm * sc[..., None]
        den = den + dn * sc
    return (num / den[..., None]).reshape(Bn, Sn, DIL_W)


def setup_inputs(seed: int = 0) -> dict:
    key = jax.random.key(seed)
    ks = jax.random.split(key, 12)
    f32 = jnp.float32
    x = jax.random.normal(ks[0], (BATCH, SEQ, D_MODEL), f32)
    norm_w = 1.0 + 0.01 * jax.random.normal(ks[1], (DEPTH, D_MODEL), f32)
    w_in = jax.random.normal(ks[2], (DEPTH, D_MODEL, PROJ_W), f32) * D_MODEL ** -0.5
    conv_w = jax.random.normal(ks[3], (DEPTH, DN_CONV, 2 * DN_QK_W + DN_V_W), f32) * DN_CONV ** -0.5
    a_log = jnp.log(jax.random.uniform(ks[4], (DEPTH, DN_HEADS), f32, 1.0, 16.0))
    dt = jnp.exp(jax.random.uniform(ks[5], (DEPTH, DN_HEADS), f32, np.log(1e-3), np.log(1e-1)))
    dt_bias = dt + jnp.log(-jnp.expm1(-dt))
    dn_norm_w = 1.0 + 0.01 * jax.random.normal(ks[6], (DEPTH, DN_DV), f32)
    w_o_dn = jax.random.normal(ks[7], (DEPTH, DN_V_W, D_MODEL), f32) * DN_V_W ** -0.5
    w_o_dil = jax.random.normal(ks[8], (DEPTH, DIL_W, D_MODEL), f32) * DIL_W ** -0.5
    w_out = jax.random.normal(ks[9], (DEPTH, D_MODEL, D_MODEL), f32) * D_MODEL ** -0.5
    final_norm_w = 1.0 + 0.01 * jax.random.normal(ks[10], (D_MODEL,), f32)
    return {"x": x, "norm_w": norm_w, "w_in": w_in, "conv_w": conv_w, "a_log": a_log,
            "dt_bias": dt_bias, "dn_norm_w": dn_norm_w, "w_o_dn": w_o_dn, "w_o_dil": w_o_dil,
            "w_out": w_out, "final_norm_w": final_norm_w}


def reference(x, norm_w, w_in, conv_w, a_log, dt_bias, dn_norm_w, w_o_dn, w_o_dil, w_out, final_norm_w):
    Bn, Sn, _ = x.shape
    f32 = jnp.float32
    for l in range(DEPTH):
        h = _rmsnorm(x, norm_w[l])
        proj = h @ w_in[l]
        (q_a, k_a, v_a, z_a, b_a, a_a, q_b, k_b, v_b, z_b, g_a, g_b) = _split_cols(proj, PROJ_SIZES)

        qkv = jax.nn.silu(_causal_conv(jnp.concatenate([q_a, k_a, v_a], axis=-1), conv_w[l]))
        q_a, k_a, v_a = _split_cols(qkv, (DN_QK_W, DN_QK_W, DN_V_W))
        qh = _l2norm(q_a.reshape(Bn, Sn, DN_HEADS, DN_DK).astype(f32))
        kh = _l2norm(k_a.reshape(Bn, Sn, DN_HEADS, DN_DK).astype(f32))
        vh = v_a.reshape(Bn, Sn, DN_HEADS, DN_DV).astype(f32)
        beta = jax.nn.sigmoid(b_a.astype(f32))
        g = -jnp.exp(a_log[l].astype(f32)) * jax.nn.softplus(a_a.astype(f32) + dt_bias[l].astype(f32))
        o_a = _gated_delta_rule(qh, kh, vh, beta, g)
        o_a = _rmsnorm(o_a, dn_norm_w[l]) * jax.nn.silu(z_a.reshape(Bn, Sn, DN_HEADS, DN_DV).astype(f32))
        y_a = o_a.reshape(Bn, Sn, DN_V_W).astype(x.dtype) @ w_o_dn[l]

        o_b = _dilated_attention(q_b, k_b, v_b) * jax.nn.silu(z_b.astype(f32))
        y_b = o_b.astype(x.dtype) @ w_o_dil[l]

        merged = jax.nn.sigmoid(g_a) * y_a + jax.nn.sigmoid(g_b) * y_b
        x = x + merged @ w_out[l]
    return _rmsnorm(x, final_norm_w)
```

```python
import contextlib
import numpy as np
import concourse.bass as bass
import concourse.mybir as mybir
from concourse.bass_utils import run_bass_kernel_spmd

ACT = mybir.ActivationFunctionType
ALU = mybir.AluOpType
F32 = mybir.dt.float32
BF16 = mybir.dt.bfloat16

T = 4096
D = 1024
NEG = -30000.0
EPS = 1e-6
O_QA, O_KA, O_VA, O_ZA, O_BA, O_QB, O_KB, O_VB, O_ZB, O_GA, O_GB = (
    0, 1024, 2048, 3072, 4096, 4112, 5648, 7184, 8720, 9232, 10256)
DIL = (1, 4, 16)


class _Op:
    __slots__ = ("eng", "fn", "deps", "signal", "ticket", "is_dma", "dsem", "dval")

    def __init__(self, eng, fn, is_dma=False):
        self.eng = eng
        self.fn = fn
        self.deps = []
        self.signal = False
        self.ticket = None
        self.is_dma = is_dma
        self.dsem = None
        self.dval = None


class _Res:
    __slots__ = ("w", "r", "rd")

    def __init__(self):
        self.w = None
        self.r = {}
        self.rd = []


class Prog:
    ENGS = ("pe", "act", "dve", "pool", "sp")

    def __init__(self, nc, n_dma_sems=32):
        self.nc = nc
        self.streams = {e: [] for e in self.ENGS}
        self.res = {}
        self.n_dma_sems = n_dma_sems
        self.dma_cnt = [0] * n_dma_sems
        self.dma_last = [None] * n_dma_sems
        self.dma_rr = 0
        self.fence_ops = []

    def fence(self):
        ops = []
        for e in self.ENGS:
            for o in reversed(self.streams[e]):
                if not o.is_dma:
                    ops.append(o)
                    break
        for d in self.dma_last:
            if d is not None:
                ops.append(d)
        self.fence_ops = ops

    def _r(self, k):
        r = self.res.get(k)
        if r is None:
            r = self.res[k] = _Res()
        return r

    def op(self, eng, fn, reads=(), writes=(), is_dma=False):
        o = _Op(eng, fn, is_dma)
        deps = [(d, "raw") for d in self.fence_ops]
        for k in reads:
            r = self._r(k)
            if r.w is not None:
                deps.append((r.w, "raw"))
        for k in writes:
            r = self._r(k)
            if r.w is not None:
                deps.append((r.w, "waw"))
            for rd in r.r.values():
                deps.append((rd, "war"))
            for rd in r.rd:
                deps.append((rd, "war"))
        if is_dma:
            i = self.dma_rr
            self.dma_rr = (i + 1) % self.n_dma_sems
            o.dsem = i
            self.dma_cnt[i] += 1
            o.dval = 16 * self.dma_cnt[i]
            if self.dma_last[i] is not None:
                deps.append((self.dma_last[i], "raw"))
            self.dma_last[i] = o
        seen = set()
        for d, kind in deps:
            if d is o or id(d) in seen:
                continue
            if not d.is_dma and d.eng == eng and not is_dma:
                if eng == "pe":
                    continue
                if kind != "raw":
                    continue
            seen.add(id(d))
            d.signal = True
            o.deps.append(d)
        for k in reads:
            r = self._r(k)
            if is_dma:
                r.rd.append(o)
            else:
                r.r[eng] = o
        for k in writes:
            r = self._r(k)
            r.w = o
            r.r = {}
            r.rd = []
        self.streams[eng].append(o)
        return o

    def dma(self, fn, reads=(), writes=(), q="sp"):
        return self.op(q, fn, reads, writes, is_dma=True)

    def emit(self, final_wait_ops=()):
        nc = self.nc
        for e in self.ENGS:
            c = 0
            for o in self.streams[e]:
                if o.is_dma:
                    continue
                if o.signal:
                    c += 1
                    o.ticket = c
        with contextlib.ExitStack() as es:
            esem = {e: es.enter_context(nc.semaphore("s_" + e)) for e in self.ENGS}
            dsem = [es.enter_context(nc.semaphore("d_%d" % i)) for i in range(self.n_dma_sems)]
            block = es.enter_context(nc.Block())

            def run(e, engobj):
                waited = {}

                def wait_for(d):
                    if d.is_dma:
                        key, sem, val = ("d", d.dsem), dsem[d.dsem], d.dval
                    else:
                        key, sem, val = ("e", d.eng), esem[d.eng], d.ticket
                    if waited.get(key, 0) >= val:
                        return
                    waited[key] = val
                    engobj.wait_ge(sem, val)

                for o in self.streams[e]:
                    for d in o.deps:
                        wait_for(d)
                    ins = o.fn(engobj)
                    if o.is_dma:
                        ins.then_inc(dsem[o.dsem], 16)
                    elif o.signal:
                        ins.then_inc(esem[e], 1)
                if e == "sp":
                    for d in final_wait_ops:
                        wait_for(d)

            @block.tensor
            def _(eng):
                run("pe", eng)

            @block.scalar
            def _(eng):
                run("act", eng)

            @block.vector
            def _(eng):
                run("dve", eng)

            @block.gpsimd
            def _(eng):
                run("pool", eng)

            @block.sync
            def _(eng):
                run("sp", eng)


def build_nc(dbg=None, stop_after=None):
    dbg = dbg or {}
    nc = bass.Bass("TRN2", target_bir_lowering=False)
    dt = nc.dram_tensor
    x_d = dt("x", [T, D], F32, kind="ExternalInput").ap()
    w_in = dt("w_in", [D, 11280], F32, kind="ExternalInput").ap()
    w_odn = dt("w_o_dn", [1024, 1024], F32, kind="ExternalInput").ap()
    w_odil = dt("w_o_dil", [512, 1024], F32, kind="ExternalInput").ap()
    w_out = dt("w_out", [1024, 1024], F32, kind="ExternalInput").ap()
    normw_d = dt("norm_w", [1, D], F32, kind="ExternalInput").ap()
    fnw_d = dt("final_norm_w", [1, D], F32, kind="ExternalInput").ap()
    cw_d = dt("conv_w_l", [128, 96], F32, kind="ExternalInput").ap()
    alog_d = dt("a_log", [1, 8], F32, kind="ExternalInput").ap()
    dtb_d = dt("dt_bias", [1, 8], F32, kind="ExternalInput").ap()
    dnw_d = dt("dn_norm_w_l", [128, 1], F32, kind="ExternalInput").ap()
    cst_d = dt("consts", [128, 128 + 12 * 256 + 64 * 4], F32, kind="ExternalInput").ap()
    out_d = dt("out", [T, D], F32, kind="ExternalOutput").ap()
    ob_scr = dt("ob_scr", [4, 128, T], BF16).ap()
    oa_scr = dt("oa_scr", [8, 128, T], BF16).ap()
    sg_scr = dt("sg_scr", [16, 128, T], BF16).ap()
    gc_scr = dt("gc_scr", [8, T], F32).ap()
    gcl_scr = dt("gcl_scr", [8, T], F32).ap()
    dbg_out = {}
    for name, (shape, dtype) in dbg.items():
        dbg_out[name] = dt("dbg_" + name, list(shape), dtype, kind="ExternalOutput").ap()

    P = Prog(nc)
    ARENA = 212000
    arena = nc.alloc_sbuf_tensor("arena", [128, ARENA // 2], BF16)
    cursor = [0]

    def alloc(free_shape, dtype, parts=128):
        n = 1
        for s in free_shape:
            n *= s
        esz = 4 if dtype == F32 else 2
        nbytes = (n * esz + 63) // 64 * 64
        off = cursor[0]
        cursor[0] += nbytes
        assert cursor[0] <= ARENA, ("SBUF overflow", cursor[0])
        ap = arena[0:parts, off // 2: off // 2 + n * esz // 2]
        if dtype == F32:
            ap = ap.bitcast(F32)
        if len(free_shape) == 2:
            ap = ap.rearrange("p (a b) -> p a b", b=free_shape[1])
        elif len(free_shape) == 3:
            ap = ap.rearrange("p (a b c) -> p a b c", b=free_shape[1], c=free_shape[2])
        return ap

    ps = [nc.alloc_psum_tensor("ps%d" % i, [128, 512], F32) for i in range(8)]
    psb = [p[:].bitcast(BF16) for p in ps]
    bank_rr = {}

    def bank(group):
        lst = {"proj": (0, 1), "misc": (2,), "pre": (3, 4), "q1": (5,), "q4": (6,), "po": (7,),
               "all": tuple(range(8)), "s": (3, 4), "a": (5, 7), "b": (6, 2)}[group]
        i = bank_rr.get(group, 0)
        bank_rr[group] = i + 1
        return lst[i % len(lst)]

    def dump(name, src_ap, reads):
        if name in dbg_out:
            P.dma(lambda q, a=src_ap, o=dbg_out[name]: q.dma_start(out=o, in_=a), reads=reads, writes=[("dbg", name)])

    hT = alloc([8, T], BF16)
    cst = alloc([128 + 12 * 256 + 256], F32)
    identf = cst[:, 0:128]
    alibi = cst[:, 128:128 + 3072].rearrange("p (g w) -> p g w", w=256)
    MK = 128 + 3072
    mask_incl = cst[0:64, MK:MK + 64]
    mask_strict = cst[0:64, MK + 64:MK + 128]
    triu_f = cst[0:64, MK + 128:MK + 192]
    ones64f = cst[0:64, MK + 192:MK + 256]
    ident_bf = alloc([128], BF16)
    ones_bf = alloc([128], BF16)
    ones_f = alloc([128], F32)
    cw = alloc([96], F32)
    dnw = alloc([1], F32)
    WB_N = 4
    wb = [alloc([8, 128], BF16) for _ in range(WB_N)]
    wb_rr = [0]
    beta_t = alloc([64, 8], F32, 64)
    gc_t = alloc([64, 8], F32, 64)
    negeg_t = alloc([64, 8], F32, 64)
    wdec_t = alloc([64, 8], F32, 64)
    dl_t = alloc([64, 8], F32)
    persist_end = cursor[0]

    P.dma(lambda q: q.dma_start(out=cst, in_=cst_d), writes=["cst"])
    P.dma(lambda q: q.dma_start(out=cw, in_=cw_d), writes=["cw"])
    P.dma(lambda q: q.dma_start(out=dnw, in_=dnw_d), writes=["dnw"])
    P.op("dve", lambda e: e.tensor_copy(out=ident_bf, in_=identf), reads=["cst"], writes=["ident_bf"])
    P.op("pool", lambda e: e.memset(ones_bf, 1.0), writes=["ones_bf"])
    P.op("pool", lambda e: e.memset(ones_f, 1.0), writes=["ones_f"])

    def load_w(src, c0, ncols=128, kchunks=8, dst=None, key=None):
        if dst is None:
            i = wb_rr[0] % WB_N
            wb_rr[0] += 1
            dst, key = wb[i], ("wb", i)
        P.dma(lambda q, d=dst, s=src, c0=c0, n=ncols, kc=kchunks: q.dma_start(
            out=d[:, 0:kc, 0:n], in_=s[:, c0:c0 + n].rearrange("(k p) c -> p k c", p=128)),
            writes=[key], q="pool")
        return dst, key

    def hkeys(t0, t1):
        return [("hT", i) for i in range(t0 // 128, (t1 + 127) // 128)]

    def proj_tile(wt, wkey, ncols, tt8, grp="proj"):
        b = bank(grp)
        for k in range(8):
            P.op("pe", lambda e, b=b, k=k, wt=wt, n=ncols, tt8=tt8: e.matmul(
                ps[b][0:n, :], lhsT=wt[:, k, 0:n], rhs=hT[:, k, tt8 * 512:(tt8 + 1) * 512],
                start=(k == 0), stop=(k == 7)),
                reads=[wkey] + hkeys(tt8 * 512, tt8 * 512 + 512), writes=[("ps", b)])
        return b

    ph = cursor[0]
    normw_b = alloc([D], F32)
    xs = [alloc([D], F32) for _ in range(2)]
    junk = alloc([D], BF16)
    xb = [alloc([D], BF16) for _ in range(2)]
    ss0 = [alloc([1], F32) for _ in range(2)]
    rs0 = [alloc([1], F32) for _ in range(2)]
    P.dma(lambda q: q.dma_start(out=normw_b, in_=normw_d.partition_broadcast(128)), writes=["normw_b"])
    for tt in range(32):
        i = tt % 2
        P.dma(lambda q, i=i, tt=tt: q.dma_start(out=xs[i], in_=x_d[tt * 128:(tt + 1) * 128, :]), writes=[("xs", i)])
        P.op("pool", lambda e, i=i: e.memset(ss0[i], 0.0), writes=[("ss0", i)])
        P.op("act", lambda e, i=i: e.activation(out=junk, in_=xs[i], func=ACT.Square, accum_out=ss0[i]),
             reads=[("xs", i), ("ss0", i)], writes=["junk", ("ss0", i)])
        P.op("act", lambda e, i=i: e.activation(out=ss0[i], in_=ss0[i], func=ACT.Sqrt, scale=1.0 / D, bias=EPS),
             reads=[("ss0", i)], writes=[("ss0", i)])
        P.op("dve", lambda e, i=i: e.reciprocal(out=rs0[i], in_=ss0[i]), reads=[("ss0", i)], writes=[("rs0", i)])
        P.op("dve", lambda e, i=i: e.scalar_tensor_tensor(out=xb[i], in0=xs[i], scalar=rs0[i], in1=normw_b,
                                                          op0=ALU.mult, op1=ALU.mult),
             reads=[("xs", i), ("rs0", i), "normw_b"], writes=[("xb", i)])
        b = bank("all")
        for k in range(8):
            P.op("pe", lambda e, b=b, k=k, i=i: e.transpose(out=psb[b][:, k * 128:(k + 1) * 128],
                                                           in_=xb[i][:, k * 128:(k + 1) * 128], identity=ident_bf),
                 reads=[("xb", i), "ident_bf"], writes=[("ps", b)])
        eng = "act" if tt % 2 == 0 else "dve"
        if eng == "act":
            fn = lambda e, b=b, tt=tt: e.activation(out=hT[:, :, tt * 128:(tt + 1) * 128],
                                                    in_=psb[b].rearrange("p (k t) -> p k t", t=128), func=ACT.Copy)
        else:
            fn = lambda e, b=b, tt=tt: e.tensor_copy(out=hT[:, :, tt * 128:(tt + 1) * 128],
                                                     in_=psb[b].rearrange("p (k t) -> p k t", t=128))
        P.op(eng, fn, writes=[("hT", tt), ("ps", b)])
    dump("hT", hT, hkeys(0, T))
    cursor[0] = ph
    P.fence()

    def phase_a0():
        ph = cursor[0]
        w16, w16k = load_w(w_in, O_BA, 16)
        alog_b = alloc([8], F32, 64)
        dtb_b = alloc([8], F32, 64)
        P.dma(lambda q: q.dma_start(out=alog_b, in_=alog_d.partition_broadcast(64)), writes=["alog_b"])
        P.dma(lambda q: q.dma_start(out=dtb_b, in_=dtb_d.partition_broadcast(64)), writes=["dtb_b"])
        Gsb = alloc([64, 16], F32, 64)
        names = ["xa", "ax", "ee", "ll", "sp", "g", "lb", "gcl", "tmp"]
        A = {n: alloc([64, 8], F32, 64) for n in names}
        glast = alloc([64, 8], F32)
        tb = alloc([8, 64], F32, 64)
        for half in range(2):
            b = bank("all")
            for n in range(32 * half, 32 * half + 32):
                for k in range(8):
                    P.op("pe", lambda e, b=b, n=n, k=k: e.matmul(
                        ps[b][0:64, (n % 32) * 16:(n % 32) * 16 + 16], lhsT=hT[:, k, n * 64:(n + 1) * 64],
                        rhs=w16[:, k, 0:16], start=(k == 0), stop=(k == 7)),
                        reads=[w16k] + hkeys(n * 64, n * 64 + 64), writes=[("ps", b)])
            P.op("act", lambda e, b=b, half=half: e.activation(
                out=Gsb[:, 32 * half:32 * half + 32, :], in_=ps[b][0:64, :].rearrange("p (n c) -> p n c", c=16),
                func=ACT.Copy), writes=["Gsb", ("ps", b)])
        bb = Gsb[:, :, 0:8]
        aa = Gsb[:, :, 8:16]
        bc = lambda v: v.unsqueeze(1).to_broadcast([64, 64, 8])
        P.op("act", lambda e: e.activation(out=beta_t, in_=bb, func=ACT.Sigmoid), reads=["Gsb"], writes=["beta_t"])
        P.op("act", lambda e: e.activation(out=A["lb"], in_=beta_t, func=ACT.Ln), reads=["beta_t"], writes=["lb"])
        P.op("dve", lambda e: e.tensor_tensor(out=A["xa"], in0=aa, in1=bc(dtb_b), op=ALU.add),
             reads=["Gsb", "dtb_b"], writes=["xa"])
        P.op("act", lambda e: e.activation(out=A["ax"], in_=A["xa"], func=ACT.Abs), reads=["xa"], writes=["ax"])
        P.op("act", lambda e: e.activation(out=A["ee"], in_=A["ax"], func=ACT.Exp, scale=-1.0), reads=["ax"], writes=["ee"])
        P.op("act", lambda e: e.activation(out=A["ll"], in_=A["ee"], func=ACT.Ln, bias=1.0), reads=["ee"], writes=["ll"])
        P.op("dve", lambda e: e.scalar_tensor_tensor(out=A["sp"], in0=A["xa"], scalar=0.0, in1=A["ll"],
                                                     op0=ALU.max, op1=ALU.add), reads=["xa", "ll"], writes=["sp"])
        P.op("act", lambda e: e.activation(out=alog_b, in_=alog_b, func=ACT.Exp), reads=["alog_b"], writes=["alog_b"])
        P.op("dve", lambda e: e.scalar_tensor_tensor(out=A["g"], in0=A["sp"], scalar=-1.0, in1=bc(alog_b),
                                                     op0=ALU.mult, op1=ALU.mult), reads=["sp", "alog_b"], writes=["g"])
        gflat = A["g"].rearrange("p n h -> p (n h)")
        b1 = bank("all")
        P.op("pe", lambda e: e.matmul(ps[b1][0:64, :], lhsT=triu_f, rhs=gflat, start=True, stop=True),
             reads=["g", "cst"], writes=[("ps", b1)])
        b2 = bank("all")
        P.op("pe", lambda e: e.matmul(ps[b2][:, :], lhsT=ones_f[0:64, :], rhs=gflat, start=True, stop=True),
             reads=["g", "ones_f"], writes=[("ps", b2)])
        fl = lambda v: v.rearrange("p n h -> p (n h)")
        P.op("act", lambda e: e.activation(out=fl(gc_t), in_=ps[b1][0:64, :], func=ACT.Copy), writes=["gc_t", ("ps", b1)])
        P.op("dve", lambda e: e.tensor_copy(out=fl(glast), in_=ps[b2][:, :]), writes=["glast", ("ps", b2)])
        P.op("act", lambda e: e.activation(out=negeg_t, in_=gc_t, func=ACT.Exp), reads=["gc_t"], writes=["negeg_t"])
        P.op("dve", lambda e: e.tensor_scalar(out=negeg_t, in0=negeg_t, scalar1=-1.0, scalar2=None, op0=ALU.mult),
             reads=["negeg_t"], writes=["negeg_t"])
        P.op("dve", lambda e: e.tensor_tensor(out=A["tmp"], in0=glast[0:64], in1=gc_t, op=ALU.subtract),
             reads=["glast", "gc_t"], writes=["tmp"])
        P.op("act", lambda e: e.activation(out=wdec_t, in_=A["tmp"], func=ACT.Exp), reads=["tmp"], writes=["wdec_t"])
        P.op("act", lambda e: e.activation(out=dl_t, in_=glast, func=ACT.Exp), reads=["glast"], writes=["dl_t"])
        P.op("dve", lambda e: e.tensor_tensor(out=A["gcl"], in0=gc_t, in1=A["lb"], op=ALU.add),
             reads=["gc_t", "lb"], writes=["gcl"])
        for nm, src, scr in (("gc", gc_t, gc_scr), ("gcl", A["gcl"], gcl_scr)):
            b = bank("all")
            for h in range(8):
                P.op("pe", lambda e, b=b, h=h, src=src: e.transpose(out=ps[b][0:64, h * 64:(h + 1) * 64],
                                                                   in_=src[:, :, h], identity=identf[0:64, 0:64]),
                     reads=["gc_t" if nm == "gc" else "gcl", "cst"], writes=[("ps", b)])
            P.op("dve", lambda e, b=b: e.tensor_copy(out=tb, in_=ps[b][0:64, :].rearrange("p (h c) -> p h c", c=64)),
                 writes=["tb", ("ps", b)])
            P.dma(lambda q, scr=scr: q.dma_start(out=scr.rearrange("h (n c) -> n h c", c=64), in_=tb),
                  reads=["tb"], writes=[nm + "_scr"])
        dump("gc_t", gc_t, ["gc_t"])
        dump("beta_t", beta_t, ["beta_t"])
        dump("g_t", A["g"], ["g"])
        cursor[0] = ph

    phase_a0()
    P.fence()

    def phase_b():
        ph = cursor[0]
        qT = alloc([T], BF16)
        kT = alloc([T], BF16)
        v_sb = alloc([32, 128], BF16)
        acc_n = alloc([T], F32)
        acc_d = alloc([T], F32)
        NS = 3
        s_sb = [alloc([256], F32) for _ in range(NS)]
        p_sb = [alloc([256], BF16) for _ in range(4)]
        zs = [alloc([512], F32) for _ in range(2)]
        rc = [alloc([512], F32) for _ in range(2)]
        ob = [alloc([512], BF16) for _ in range(2)]
        scale = 128 ** -0.5
        for h in range(4):
            for gi in range(3):
                d = DIL[gi]
                L = T // d
                nb = L // 128
                M = 512 // d
                gidx = gi * 4 + h
                wq, wqk = load_w(w_in, O_QB + gi * 512 + h * 128)
                wk, wkk = load_w(w_in, O_KB + gi * 512 + h * 128)
                wv, wvk = load_w(w_in, O_VB + gi * 512 + h * 128)
                q3 = qT.rearrange("p (r m) -> p r m", r=d)
                k3 = kT.rearrange("p (r m) -> p r m", r=d)
                for tt8 in range(8):
                    b = proj_tile(wq, wqk, 128, tt8)
                    P.op("act", lambda e, b=b, tt8=tt8, q3=q3, M=M, d=d: e.activation(
                        out=q3[:, :, tt8 * M:(tt8 + 1) * M].rearrange("p r m -> p m r"),
                        in_=ps[b][:, :].rearrange("p (m r) -> p m r", r=d), func=ACT.Copy, scale=scale),
                        writes=["qT", ("ps", b)])
                    b = proj_tile(wk, wkk, 128, tt8)
                    P.op("dve", lambda e, b=b, tt8=tt8, k3=k3, M=M, d=d: e.tensor_copy(
                        out=k3[:, :, tt8 * M:(tt8 + 1) * M].rearrange("p r m -> p m r"),
                        in_=ps[b][:, :].rearrange("p (m r) -> p m r", r=d)),
                        writes=["kT", ("ps", b)])
                for t4 in range(8):
                    b = bank("proj")
                    for s in range(4):
                        tid = t4 * 4 + s
                        r, j = tid // nb, tid % nb
                        t0 = 128 * j * d + r
                        for k in range(8):
                            P.op("pe", lambda e, b=b, s=s, k=k, t0=t0, d=d, wv=wv: e.matmul(
                                ps[b][:, s * 128:(s + 1) * 128], lhsT=hT[:, k, t0:t0 + 127 * d + 1:d], rhs=wv[:, k, :],
                                start=(k == 0), stop=(k == 7)),
                                reads=[wvk] + hkeys(128 * j * d, 128 * (j + 1) * d), writes=[("ps", b)])
                    P.op("dve", lambda e, b=b, t4=t4: e.tensor_copy(
                        out=v_sb[:, t4 * 4:(t4 + 1) * 4, :], in_=ps[b][:, :].rearrange("p (s c) -> p s c", c=128)),
                        writes=["v_sb", ("ps", b)])
                cnt = 0
                for r in range(d):
                    prev = None
                    bn = bd = None
                    for j in range(nb):
                        W = 256 if j + 1 < nb else 128
                        c0 = r * L + j * 128
                        bs = bank("s")
                        P.op("pe", lambda e, bs=bs, c0=c0, W=W: e.matmul(
                            ps[bs][:, 0:W], lhsT=kT[:, c0:c0 + 128], rhs=qT[:, c0:c0 + W], start=True, stop=True),
                            reads=["qT", "kT"], writes=[("ps", bs)])
                        si = cnt % NS
                        pi = cnt % 4
                        cnt += 1
                        P.op("dve", lambda e, bs=bs, si=si, W=W, gidx=gidx: e.tensor_tensor(
                            out=s_sb[si][:, 0:W], in0=ps[bs][:, 0:W], in1=alibi[:, gidx, 0:W], op=ALU.add),
                            reads=["cst"], writes=[("s_sb", si), ("ps", bs)])
                        P.op("act", lambda e, si=si, pi=pi, W=W: e.activation(
                            out=p_sb[pi][:, 0:W], in_=s_sb[si][:, 0:W], func=ACT.Exp),
                            reads=[("s_sb", si)], writes=[("p_sb", pi)])
                        if j % 4 == 0:
                            bn, bd = bank("a"), bank("b")
                        col = (j % 4) * 128
                        tid = r * nb + j
                        for (bk, lhs_prev, lhs_cur, lk) in ((bn, None, None, "v_sb"), (bd, ones_bf, ones_bf, "ones_bf")):
                            first = True
                            if j > 0:
                                lp = v_sb[:, tid - 1, :] if lhs_prev is None else lhs_prev
                                P.op("pe", lambda e, bk=bk, col=col, lp=lp, pp=prev: e.matmul(
                                    ps[bk][:, col:col + 128], lhsT=lp, rhs=p_sb[pp][:, 128:256], start=True, stop=False),
                                    reads=[lk, ("p_sb", prev)], writes=[("ps", bk)])
                                first = False
                            lc = v_sb[:, tid, :] if lhs_cur is None else lhs_cur
                            P.op("pe", lambda e, bk=bk, col=col, lc=lc, pi=pi, first=first: e.matmul(
                                ps[bk][:, col:col + 128], lhsT=lc, rhs=p_sb[pi][:, 0:128], start=first, stop=True),
                                reads=[lk, ("p_sb", pi)], writes=[("ps", bk)])
                        prev = pi
                        if j % 4 == 3 or j == nb - 1:
                            n0 = (j // 4) * 4
                            nq = (j - n0 + 1) * 128
                            sl = slice(128 * n0 * d + r, 128 * n0 * d + r + (nq - 1) * d + 1, d)
                            for (bk, acc, key, eng) in ((bn, acc_n, "acc_n", "act"), (bd, acc_d, "acc_d", "dve")):
                                if gi == 0:
                                    if eng == "act":
                                        P.op("act", lambda e, bk=bk, acc=acc, sl=sl, nq=nq: e.activation(
                                            out=acc[:, sl], in_=ps[bk][:, 0:nq], func=ACT.Copy),
                                            writes=[key, ("ps", bk)])
                                    else:
                                        P.op("dve", lambda e, bk=bk, acc=acc, sl=sl, nq=nq: e.tensor_copy(
                                            out=acc[:, sl], in_=ps[bk][:, 0:nq]), writes=[key, ("ps", bk)])
                                else:
                                    P.op("dve", lambda e, bk=bk, acc=acc, sl=sl, nq=nq: e.tensor_tensor(
                                        out=acc[:, sl], in0=ps[bk][:, 0:nq], in1=acc[:, sl], op=ALU.add),
                                        reads=[key], writes=[key, ("ps", bk)])
            wz, wzk = load_w(w_in, O_ZB + h * 128)
            for tt8 in range(8):
                i = tt8 % 2
                sl = slice(tt8 * 512, (tt8 + 1) * 512)
                b = proj_tile(wz, wzk, 128, tt8)
                P.op("act", lambda e, b=b, i=i: e.activation(out=zs[i], in_=ps[b][:, :], func=ACT.Silu),
                     writes=[("zs", i), ("ps", b)])
                P.op("dve", lambda e, i=i, sl=sl: e.reciprocal(out=rc[i], in_=acc_d[:, sl]), reads=["acc_d"], writes=[("rc", i)])
                P.op("dve", lambda e, i=i, sl=sl: e.tensor_tensor(out=rc[i], in0=rc[i], in1=acc_n[:, sl], op=ALU.mult),
                     reads=["acc_n", ("rc", i)], writes=[("rc", i)])
                P.op("dve", lambda e, i=i: e.tensor_tensor(out=ob[i], in0=rc[i], in1=zs[i], op=ALU.mult),
                     reads=[("rc", i), ("zs", i)], writes=[("ob", i)])
                P.dma(lambda q, i=i, h=h, sl=sl: q.dma_start(out=ob_scr[h, :, sl], in_=ob[i]),
                      reads=[("ob", i)], writes=[("ob_scr", h, tt8)])
        cursor[0] = ph

    if stop_after != "a0":
        phase_b()
        P.fence()
        dump("ob_scr", ob_scr, [])

    def phase_a():
        ph = cursor[0]
        NB2 = 2
        upre = {t: [alloc([515], F32) for _ in range(2)] for t in "qkv"}
        for t in "qkv":
            P.op("pool", lambda e, t=t: e.memset(upre[t][1][:, 512:515], 0.0), writes=[("upre", t, 1)])
        cacc = {t: alloc([512], F32) for t in "qkv"}
        th = {t: alloc([512], F32) for t in "qkvz"}
        yq = alloc([512], F32)
        yk = alloc([512], F32)
        sq = {t: alloc([512], BF16) for t in "qk"}
        rin = {t: alloc([512], F32) for t in "qk"}
        vTt = alloc([512], BF16)
        khT = [alloc([512], BF16) for _ in range(NB2)]
        qhT = [alloc([512], BF16) for _ in range(NB2)]
        qgT = [alloc([512], BF16) for _ in range(NB2)]
        zsT = [alloc([512], BF16) for _ in range(NB2)]
        Ktok = [alloc([8, 128], BF16, 64) for _ in range(NB2)]
        Vtok = [alloc([8, 128], BF16, 64) for _ in range(NB2)]
        AqkT = [alloc([8, 64], BF16, 64) for _ in range(NB2)]
        TT = [alloc([8, 64], BF16, 64) for _ in range(NB2)]
        gcB = [alloc([512], F32) for _ in range(2)]
        gclB = [alloc([512], F32, 64) for _ in range(2)]
        egB = alloc([512], F32)
        E1 = alloc([8, 64], F32, 64)
        E2 = alloc([8, 64], F32, 64)
        GT = alloc([8, 64], F32, 64)
        GTb = alloc([8, 64], F32, 64)
        Pk = [alloc([8, 64], BF16, 64) for _ in range(2)]
        PkT = [alloc([8, 64], BF16, 64) for _ in range(2)]
        Xb = [alloc([8, 64], BF16, 64) for _ in range(2)]
        S_f = alloc([128], F32)
        S_b = alloc([128], BF16)
        Rp = [alloc([128], BF16, 64) for _ in range(2)]
        vnew = [alloc([128], BF16, 64) for _ in range(2)]
        vnd = [alloc([128], BF16, 64) for _ in range(2)]
        oraw = alloc([512], F32)
        osq = alloc([512], BF16)
        orst = alloc([512], F32)
        oa = [alloc([512], BF16) for _ in range(2)]
        wsets = [[alloc([8, 128], BF16) for _ in range(4)] for _ in range(2)]

        def load_head_w(h):
            ws = wsets[h % 2]
            for ti, off in enumerate((O_QA, O_KA, O_VA, O_ZA)):
                load_w(w_in, off + h * 128, dst=ws[ti], key=("wh", h % 2, ti))

        load_head_w(0)
        step = [0]
        for h in range(8):
            if h + 1 < 8:
                load_head_w(h + 1)
            ws = wsets[h % 2]
            for tt8 in range(8):
                st = step[0]
                step[0] += 1
                bi = st % NB2
                ui = st % 2
                t0 = tt8 * 512
                P.dma(lambda q, ui=ui, h=h, t0=t0: q.dma_start(
                    out=gcB[ui], in_=gc_scr[h, t0:t0 + 512].partition_broadcast(128)),
                    reads=["gc_scr"], writes=[("gcB", ui)])
                P.dma(lambda q, ui=ui, h=h, t0=t0: q.dma_start(
                    out=gclB[ui], in_=gcl_scr[h, t0:t0 + 512].partition_broadcast(64)),
                    reads=["gcl_scr"], writes=[("gclB", ui)])
                pb = {}
                for ti, t in enumerate("qkvz"):
                    pb[t] = proj_tile(ws[ti], ("wh", h % 2, ti), 128, tt8)
                    if t == "z":
                        P.op("act", lambda e, b=pb[t]: e.activation(out=th["z"], in_=ps[b][:, :], func=ACT.Tanh, scale=0.5),
                             writes=[("th", "z"), ("ps", pb[t])])
                        P.op("dve", lambda e, b=pb[t], bi=bi: e.scalar_tensor_tensor(
                            out=zsT[bi], in0=th["z"], scalar=1.0, in1=ps[b][:, :], op0=ALU.add, op1=ALU.mult),
                            reads=[("th", "z")], writes=[("zsT", bi), ("ps", pb[t])])
                    else:
                        u = upre[t][ui]
                        up = upre[t][1 - ui]
                        if tt8 == 0:
                            P.op("pool", lambda e, u=u: e.memset(u[:, 0:3], 0.0), writes=[("upre", t, ui)])
                        else:
                            P.op("pool", lambda e, u=u, up=up: e.tensor_copy(out=u[:, 0:3], in_=up[:, 512:515]),
                                 reads=[("upre", t, 1 - ui)], writes=[("upre", t, ui)])
                        P.op("act", lambda e, b=pb[t], u=u: e.activation(out=u[:, 3:515], in_=ps[b][:, :], func=ACT.Copy),
                             writes=[("upre", t, ui), ("ps", pb[t])])
                for ti, t in enumerate("qkv"):
                    u = upre[t][ui]
                    g = ti * 8 + h
                    ce = "dve"
                    P.op(ce, lambda e, u=u, t=t, g=g: e.tensor_scalar(
                        out=cacc[t], in0=u[:, 0:512], scalar1=cw[:, g * 4:g * 4 + 1], scalar2=None, op0=ALU.mult),
                        reads=[("upre", t, ui), "cw"], writes=[("cacc", t)])
                    for kk in range(1, 4):
                        P.op(ce, lambda e, u=u, t=t, g=g, kk=kk: e.scalar_tensor_tensor(
                            out=cacc[t], in0=u[:, kk:kk + 512], scalar=cw[:, g * 4 + kk:g * 4 + kk + 1], in1=cacc[t],
                            op0=ALU.mult, op1=ALU.add),
                            reads=[("upre", t, ui), "cw", ("cacc", t)], writes=[("cacc", t)])
                    P.op("act", lambda e, t=t: e.activation(out=th[t], in_=cacc[t], func=ACT.Tanh, scale=0.5),
                         reads=[("cacc", t)], writes=[("th", t)])
                    dst = {"q": yq, "k": yk, "v": vTt}[t]
                    P.op("dve", lambda e, t=t, dst=dst: e.scalar_tensor_tensor(
                        out=dst, in0=th[t], scalar=1.0, in1=cacc[t], op0=ALU.add, op1=ALU.mult),
                        reads=[("th", t), ("cacc", t)], writes=[("y", t)])
                for t, y in (("q", yq), ("k", yk)):
                    P.op("act", lambda e, t=t, y=y: e.activation(out=sq[t], in_=y, func=ACT.Square),
                         reads=[("y", t)], writes=[("sq", t)])
                    b = bank("misc")
                    P.op("pe", lambda e, b=b, t=t: e.matmul(ps[b][:, :], lhsT=ones_bf, rhs=sq[t], start=True, stop=True),
                         reads=[("sq", t), "ones_bf"], writes=[("ps", b)])
                    P.op("act", lambda e, b=b, t=t: e.activation(out=rin[t], in_=ps[b][:, :], func=ACT.Sqrt, scale=0.25, bias=EPS),
                         writes=[("rin", t), ("ps", b)])
                    P.op("dve", lambda e, t=t: e.reciprocal(out=rin[t], in_=rin[t]), reads=[("rin", t)], writes=[("rin", t)])
                P.op("dve", lambda e, bi=bi: e.scalar_tensor_tensor(out=khT[bi], in0=yk, scalar=0.5, in1=rin["k"],
                                                                    op0=ALU.mult, op1=ALU.mult),
                     reads=[("y", "k"), ("rin", "k")], writes=[("khT", bi)])
                P.op("dve", lambda e, bi=bi: e.scalar_tensor_tensor(out=qhT[bi], in0=yq, scalar=0.5 * 128 ** -0.5, in1=rin["q"],
                                                                    op0=ALU.mult, op1=ALU.mult),
                     reads=[("y", "q"), ("rin", "q")], writes=[("qhT", bi)])
                P.op("act", lambda e, ui=ui: e.activation(out=egB, in_=gcB[ui], func=ACT.Exp), reads=[("gcB", ui)], writes=["egB"])
                P.op("dve", lambda e, bi=bi: e.tensor_tensor(out=qgT[bi], in0=qhT[bi], in1=egB, op=ALU.mult),
                     reads=[("qhT", bi), "egB"], writes=[("qgT", bi)])
                for (src, skey, dst, dkey, sc) in ((khT[bi], ("khT", bi), Ktok[bi], ("Ktok", bi), 1.0),
                                                   (vTt, ("y", "v"), Vtok[bi], ("Vtok", bi), 0.5)):
                    b = bank("misc")
                    for c in range(8):
                        P.op("pe", lambda e, b=b, c=c, src=src: e.transpose(
                            out=psb[b][0:64, c * 128:(c + 1) * 128], in_=src[:, c * 64:(c + 1) * 64], identity=ident_bf),
                            reads=[skey, "ident_bf"], writes=[("ps", b)])
                    P.op("act", lambda e, b=b, dst=dst, sc=sc: e.activation(
                        out=dst, in_=psb[b][0:64, :].rearrange("p (c k) -> p c k", k=128), func=ACT.Copy, scale=sc),
                        writes=[dkey, ("ps", b)])
                n0 = tt8 * 8
                gcJ = gc_t[:, n0:n0 + 8, h].unsqueeze(2).to_broadcast([64, 8, 64])
                bJ = beta_t[:, n0:n0 + 8, h].unsqueeze(2).to_broadcast([64, 8, 64])
                m8 = lambda m: m.unsqueeze(1).to_broadcast([64, 8, 64])
                v3 = lambda a: a.rearrange("p (c i) -> p c i", i=64)
                bkk, bqk = bank("pre"), bank("pre")
                for c in range(8):
                    cs = slice(c * 64, (c + 1) * 64)
                    P.op("pe", lambda e, cs=cs, bi=bi, b=bkk: e.matmul(ps[b][0:64, cs], lhsT=khT[bi][:, cs], rhs=khT[bi][:, cs],
                                                                        start=True, stop=True),
                         reads=[("khT", bi)], writes=[("ps", bkk)])
                for c in range(8):
                    cs = slice(c * 64, (c + 1) * 64)
                    P.op("pe", lambda e, cs=cs, bi=bi, b=bqk: e.matmul(ps[b][0:64, cs], lhsT=khT[bi][:, cs], rhs=qhT[bi][:, cs],
                                                                        start=True, stop=True),
                         reads=[("khT", bi), ("qhT", bi)], writes=[("ps", bqk)])
                P.op("pool", lambda e, ui=ui, gcJ=gcJ: e.tensor_tensor(out=E1, in0=v3(gcB[ui][0:64, :]), in1=gcJ, op=ALU.subtract),
                     reads=[("gcB", ui), "gc_t"], writes=["E1"])
                P.op("pool", lambda e: e.tensor_tensor(out=E1, in0=E1, in1=m8(mask_incl), op=ALU.add),
                     reads=["E1", "cst"], writes=["E1"])
                P.op("act", lambda e: e.activation(out=GT, in_=E1, func=ACT.Exp), reads=["E1"], writes=["GT"])
                P.op("pool", lambda e, ui=ui, gcJ=gcJ: e.tensor_tensor(out=E2, in0=v3(gclB[ui]), in1=gcJ, op=ALU.subtract),
                     reads=[("gclB", ui), "gc_t"], writes=["E2"])
                P.op("pool", lambda e: e.tensor_tensor(out=E2, in0=E2, in1=m8(mask_strict), op=ALU.add),
                     reads=["E2", "cst"], writes=["E2"])
                P.op("act", lambda e: e.activation(out=GTb, in_=E2, func=ACT.Exp), reads=["E2"], writes=["GTb"])
                P.op("dve", lambda e, bi=bi, b=bqk: e.tensor_tensor(out=AqkT[bi], in0=v3(ps[b][0:64, :]), in1=GT, op=ALU.mult),
                     reads=["GT"], writes=[("AqkT", bi), ("ps", bqk)])
                P.op("dve", lambda e, b=bkk: e.scalar_tensor_tensor(out=Pk[0], in0=v3(ps[b][0:64, :]), scalar=-1.0, in1=GTb,
                                                                    op0=ALU.mult, op1=ALU.mult),
                     reads=["GTb"], writes=[("Pk", 0), ("ps", bkk)])
                fl = lambda a: a.rearrange("p c i -> p (c i)")
                b = bank("pre")
                for c in range(8):
                    P.op("pe", lambda e, b=b, c=c: e.transpose(out=psb[b][0:64, c * 64:(c + 1) * 64], in_=Pk[0][:, c, :],
                                                               identity=ident_bf[0:64, 0:64]),
                         reads=[("Pk", 0), "ident_bf"], writes=[("ps", b)])
                P.op("act", lambda e, b=b: e.activation(out=fl(PkT[0]), in_=psb[b][0:64, 0:512], func=ACT.Copy),
                     writes=[("PkT", 0), ("ps", b)])
                P.op("pool", lambda e: e.tensor_tensor(out=Xb[0], in0=Pk[0], in1=m8(identf[0:64, 0:64]), op=ALU.add),
                     reads=[("Pk", 0), "cst"], writes=[("Xb", 0)])
                cur = 0
                for lvl in range(5):
                    nxt = 1 - cur
                    if lvl < 4:
                        ba = bank("pre")
                        for c in range(8):
                            cs = slice(c * 64, (c + 1) * 64)
                            P.op("pe", lambda e, b=ba, c=c, cs=cs, cur=cur: e.matmul(
                                ps[b][0:64, cs], lhsT=PkT[cur][:, c, :], rhs=Pk[cur][:, c, :], start=True, stop=True),
                                reads=[("Pk", cur), ("PkT", cur)], writes=[("ps", ba)])
                    bt = bank("pre")
                    for c in range(8):
                        cs = slice(c * 64, (c + 1) * 64)
                        P.op("pe", lambda e, b=bt, c=c, cs=cs, cur=cur: e.matmul(
                            ps[b][0:64, cs], lhsT=Pk[cur][:, c, :], rhs=PkT[cur][:, c, :], start=True, stop=True),
                            reads=[("Pk", cur), ("PkT", cur)], writes=[("ps", bt)])
                    if lvl < 4:
                        P.op("act", lambda e, b=ba, nxt=nxt: e.activation(out=fl(Pk[nxt]), in_=ps[b][0:64, :], func=ACT.Copy),
                             writes=[("Pk", nxt), ("ps", ba)])
                    P.op("dve", lambda e, b=bt, nxt=nxt: e.tensor_copy(out=fl(PkT[nxt]), in_=ps[b][0:64, :]),
                         writes=[("PkT", nxt), ("ps", bt)])
                    bx = bank("pre")
                    for c in range(8):
                        cs = slice(c * 64, (c + 1) * 64)
                        P.op("pe", lambda e, b=bx, c=c, cs=cs, cur=cur, nxt=nxt: e.matmul(
                            ps[b][0:64, cs], lhsT=PkT[nxt][:, c, :], rhs=Xb[cur][:, c, :], start=True, stop=True),
                            reads=[("PkT", nxt), ("Xb", cur)], writes=[("ps", bx)])
                    P.op("dve", lambda e, b=bx, cur=cur, nxt=nxt: e.tensor_tensor(
                        out=fl(Xb[nxt]), in0=ps[b][0:64, :], in1=fl(Xb[cur]), op=ALU.add),
                        reads=[("Xb", cur)], writes=[("Xb", nxt), ("ps", bx)])
                    cur = nxt
                P.op("pool", lambda e, bi=bi, cur=cur, bJ=bJ: e.tensor_tensor(out=TT[bi], in0=Xb[cur], in1=bJ, op=ALU.mult),
                     reads=[("Xb", cur), "beta_t"], writes=[("TT", bi)])
                bo = bank("po")
                for c in range(8):
                    n = n0 + c
                    cs = slice(c * 64, (c + 1) * 64)
                    ri = n % 2
                    first = (n == 0)
                    if not first:
                        b1 = bank("q1")
                        P.op("pe", lambda e, b=b1, cs=cs, bi=bi: e.matmul(ps[b][0:64, 0:128], lhsT=khT[bi][:, cs], rhs=S_b,
                                                                           start=True, stop=True),
                             reads=[("khT", bi), "S_b"], writes=[("ps", b1)])
                        P.op("dve", lambda e, b=b1, ri=ri, bi=bi, c=c, n=n, h=h: e.scalar_tensor_tensor(
                            out=Rp[ri], in0=ps[b][0:64, 0:128], scalar=negeg_t[:, n, h:h + 1], in1=Vtok[bi][:, c, :],
                            op0=ALU.mult, op1=ALU.add),
                            reads=["negeg_t", ("Vtok", bi)], writes=[("Rp", ri), ("ps", b1)])
                        rsrc, rkey = Rp[ri], ("Rp", ri)
                    else:
                        rsrc, rkey = Vtok[bi][:, c, :], ("Vtok", bi)
                    b2 = bank("q1")
                    P.op("pe", lambda e, b=b2, bi=bi, c=c, rsrc=rsrc: e.matmul(ps[b][0:64, 0:128], lhsT=TT[bi][:, c, :], rhs=rsrc,
                                                                                start=True, stop=True),
                         reads=[("TT", bi), rkey], writes=[("ps", b2)])
                    P.op("act", lambda e, b=b2, ri=ri: e.activation(out=vnew[ri], in_=ps[b][0:64, 0:128], func=ACT.Copy),
                         writes=[("vnew", ri), ("ps", b2)])
                    P.op("act", lambda e, b=b2, ri=ri, n=n, h=h: e.activation(out=vnd[ri], in_=ps[b][0:64, 0:128], func=ACT.Copy,
                                                                               scale=wdec_t[:, n, h:h + 1]),
                         reads=["wdec_t"], writes=[("vnd", ri), ("ps", b2)])
                    if not first:
                        P.op("pe", lambda e, b=bo, cs=cs, bi=bi: e.matmul(ps[b][:, cs], lhsT=S_b, rhs=qgT[bi][:, cs],
                                                                           start=True, stop=False),
                             reads=["S_b", ("qgT", bi)], writes=[("ps", bo)])
                    P.op("pe", lambda e, b=bo, cs=cs, bi=bi, c=c, ri=ri, first=first: e.matmul(
                        ps[b][:, cs], lhsT=vnew[ri], rhs=AqkT[bi][:, c, :], start=first, stop=True),
                        reads=[("vnew", ri), ("AqkT", bi)], writes=[("ps", bo)])
                    b4 = bank("q4")
                    P.op("pe", lambda e, b=b4, bi=bi, c=c, ri=ri: e.matmul(ps[b][:, 0:128], lhsT=Ktok[bi][:, c, :], rhs=vnd[ri],
                                                                            start=True, stop=True),
                         reads=[("Ktok", bi), ("vnd", ri)], writes=[("ps", b4)])
                    if first:
                        P.op("dve", lambda e, b=b4: e.tensor_copy(out=S_f, in_=ps[b][:, 0:128]), writes=["S_f", ("ps", b4)])
                    else:
                        P.op("dve", lambda e, b=b4, n=n, h=h: e.scalar_tensor_tensor(
                            out=S_f, in0=S_f, scalar=dl_t[:, n, h:h + 1], in1=ps[b][:, 0:128], op0=ALU.mult, op1=ALU.add),
                            reads=["S_f", "dl_t"], writes=["S_f", ("ps", b4)])
                    P.op("act", lambda e: e.activation(out=S_b, in_=S_f, func=ACT.Copy), reads=["S_f"], writes=["S_b"])
                oi = st % 2
                P.op("act", lambda e, b=bo: e.activation(out=oraw, in_=ps[b][:, :], func=ACT.Copy), writes=["oraw", ("ps", bo)])
                if h == 0 and tt8 == 0:
                    dump("khT0", khT[bi], [("khT", bi)])
                    dump("qhT0", qhT[bi], [("qhT", bi)])
                    dump("qgT0", qgT[bi], [("qgT", bi)])
                    dump("Ktok0", Ktok[bi], [("Ktok", bi)])
                    dump("Vtok0", Vtok[bi], [("Vtok", bi)])
                    dump("AqkT0", AqkT[bi], [("AqkT", bi)])
                    dump("TT0", TT[bi], [("TT", bi)])
                    dump("oraw0", oraw, ["oraw"])
                    dump("GT0", GT, ["GT"])
                    dump("GTb0", GTb, ["GTb"])
                P.op("act", lambda e: e.activation(out=osq, in_=oraw, func=ACT.Square), reads=["oraw"], writes=["osq"])
                b = bank("misc")
                P.op("pe", lambda e, b=b: e.matmul(ps[b][:, :], lhsT=ones_bf, rhs=osq, start=True, stop=True),
                     reads=["osq", "ones_bf"], writes=[("ps", b)])
                P.op("act", lambda e, b=b: e.activation(out=orst, in_=ps[b][:, :], func=ACT.Sqrt, scale=1.0 / 128, bias=EPS),
                     writes=["orst", ("ps", b)])
                P.op("dve", lambda e: e.reciprocal(out=orst, in_=orst), reads=["orst"], writes=["orst"])
                P.op("dve", lambda e: e.scalar_tensor_tensor(out=oraw, in0=oraw, scalar=dnw[:, 0:1], in1=orst,
                                                             op0=ALU.mult, op1=ALU.mult),
                     reads=["oraw", "orst", "dnw"], writes=["oraw"])
                P.op("dve", lambda e, oi=oi, bi=bi: e.scalar_tensor_tensor(out=oa[oi], in0=oraw, scalar=0.5, in1=zsT[bi],
                                                                           op0=ALU.mult, op1=ALU.mult),
                     reads=["oraw", ("zsT", bi)], writes=[("oa", oi)])
                P.dma(lambda q, oi=oi, h=h, t0=t0: q.dma_start(out=oa_scr[h, :, t0:t0 + 512], in_=oa[oi]),
                      reads=[("oa", oi)], writes=[("oa_scr", h, tt8)])
        cursor[0] = ph

    if stop_after not in ("a0", "b"):
        phase_a()
        P.fence()
        dump("oa_scr", oa_scr, [])

    def phase_c0():
        ph = cursor[0]
        sg = [alloc([512], BF16) for _ in range(4)]
        cnt = 0
        for c16 in range(16):
            off = (O_GA + c16 * 128) if c16 < 8 else (O_GB + (c16 - 8) * 128)
            wt, wk = load_w(w_in, off)
            for tt8 in range(8):
                b = proj_tile(wt, wk, 128, tt8, grp="all")
                i = cnt % 4
                cnt += 1
                P.op("act", lambda e, b=b, i=i: e.activation(out=sg[i], in_=ps[b][:, :], func=ACT.Sigmoid),
                     writes=[("sg", i), ("ps", b)])
                P.dma(lambda q, i=i, c16=c16, tt8=tt8: q.dma_start(out=sg_scr[c16, :, tt8 * 512:(tt8 + 1) * 512], in_=sg[i]),
                      reads=[("sg", i)], writes=[("sg_scr", tt8)])
        cursor[0] = ph

    def phase_c1():
        ph = cursor[0]
        wodn = alloc([8, 1024], BF16)
        wodil = alloc([4, 1024], BF16)
        wout = alloc([8, 1024], BF16)
        fnw_b = alloc([D], F32)
        for (dst, src, kc, key) in ((wodn, w_odn, 8, "wodn"), (wodil, w_odil, 4, "wodil"), (wout, w_out, 8, "wout")):
            for half in range(2):
                P.dma(lambda q, dst=dst, src=src, kc=kc, half=half: q.dma_start(
                    out=dst[:, 0:kc, half * 512:(half + 1) * 512],
                    in_=src[:, half * 512:(half + 1) * 512].rearrange("(k p) c -> p k c", p=128)),
                    writes=[(key, half)], q="pool")
        P.dma(lambda q: q.dma_start(out=fnw_b, in_=fnw_d.partition_broadcast(128)), writes=["fnw_b"])
        oat = [alloc([8, 512], BF16) for _ in range(2)]
        obt = [alloc([4, 512], BF16) for _ in range(2)]
        sgt = [alloc([16, 512], BF16) for _ in range(2)]
        mT = alloc([8, 512], BF16)
        m1 = [alloc([512], F32) for _ in range(2)]
        m2 = [alloc([512], F32) for _ in range(2)]
        xr = [alloc([D], F32) for _ in range(2)]
        xo = [alloc([D], F32) for _ in range(2)]
        yo = [alloc([D], F32) for _ in range(2)]
        ss = [alloc([1], F32) for _ in range(2)]
        rs = [alloc([1], F32) for _ in range(2)]
        junk2 = alloc([D], BF16)
        cnt = 0
        for tt8 in range(8):
            i = tt8 % 2
            sl = slice(tt8 * 512, (tt8 + 1) * 512)
            P.dma(lambda q, i=i, sl=sl: q.dma_start(out=oat[i], in_=oa_scr[:, :, sl].rearrange("h p t -> p h t")),
                  reads=[("oa_scr", hh, tt8) for hh in range(8)], writes=[("oat", i)])
            P.dma(lambda q, i=i, sl=sl: q.dma_start(out=obt[i], in_=ob_scr[:, :, sl].rearrange("h p t -> p h t")),
                  reads=[("ob_scr", hh, tt8) for hh in range(4)], writes=[("obt", i)])
            P.dma(lambda q, i=i, sl=sl: q.dma_start(out=sgt[i], in_=sg_scr[:, :, sl].rearrange("h p t -> p h t")),
                  reads=[("sg_scr", tt8)], writes=[("sgt", i)])
            for c in range(8):
                cs = slice(c * 128, (c + 1) * 128)
                ba = bank("all")
                for k in range(8):
                    P.op("pe", lambda e, b=ba, k=k, cs=cs, i=i: e.matmul(ps[b][:, :], lhsT=wodn[:, k, cs], rhs=oat[i][:, k, :],
                                                                         start=(k == 0), stop=(k == 7)),
                         reads=[("wodn", c // 4), ("oat", i)], writes=[("ps", ba)])
                bb = bank("all")
                for k in range(4):
                    P.op("pe", lambda e, b=bb, k=k, cs=cs, i=i: e.matmul(ps[b][:, :], lhsT=wodil[:, k, cs], rhs=obt[i][:, k, :],
                                                                         start=(k == 0), stop=(k == 3)),
                         reads=[("wodil", c // 4), ("obt", i)], writes=[("ps", bb)])
                mi = cnt % 2
                cnt += 1
                P.op("dve", lambda e, b=ba, mi=mi, i=i, c=c: e.tensor_tensor(out=m1[mi], in0=ps[b][:, :], in1=sgt[i][:, c, :], op=ALU.mult),
                     reads=[("sgt", i)], writes=[("m1", mi), ("ps", ba)])
                P.op("dve", lambda e, b=bb, mi=mi, i=i, c=c: e.tensor_tensor(out=m2[mi], in0=ps[b][:, :], in1=sgt[i][:, 8 + c, :], op=ALU.mult),
                     reads=[("sgt", i)], writes=[("m2", mi), ("ps", bb)])
                P.op("pool", lambda e, mi=mi, c=c: e.tensor_tensor(out=mT[:, c, :], in0=m1[mi], in1=m2[mi], op=ALU.add),
                     reads=[("m1", mi), ("m2", mi)], writes=[("mT", c)])
            for sub in range(4):
                tok0 = tt8 * 512 + sub * 128
                xi = sub % 2
                P.dma(lambda q, xi=xi, tok0=tok0: q.dma_start(out=xr[xi], in_=x_d[tok0:tok0 + 128, :]), writes=[("xr", xi)])
                for half in range(2):
                    b = bank("all")
                    hs = slice(half * 512, (half + 1) * 512)
                    for c in range(8):
                        P.op("pe", lambda e, b=b, c=c, sub=sub, hs=hs: e.matmul(
                            ps[b][:, :], lhsT=mT[:, c, sub * 128:(sub + 1) * 128], rhs=wout[:, c, hs], start=(c == 0), stop=(c == 7)),
                            reads=[("mT", c), ("wout", half)], writes=[("ps", b)])
                    P.op("dve", lambda e, b=b, xi=xi, hs=hs: e.tensor_tensor(out=xo[xi][:, hs], in0=ps[b][:, :], in1=xr[xi][:, hs], op=ALU.add),
                         reads=[("xr", xi)], writes=[("xo", xi, half), ("ps", b)])
                P.op("pool", lambda e, xi=xi: e.memset(ss[xi], 0.0), writes=[("ss", xi)])
                P.op("act", lambda e, xi=xi: e.activation(out=junk2, in_=xo[xi], func=ACT.Square, accum_out=ss[xi]),
                     reads=[("xo", xi, 0), ("xo", xi, 1), ("ss", xi)], writes=["junk2", ("ss", xi)])
                P.op("act", lambda e, xi=xi: e.activation(out=ss[xi], in_=ss[xi], func=ACT.Sqrt, scale=1.0 / D, bias=EPS),
                     reads=[("ss", xi)], writes=[("ss", xi)])
                P.op("dve", lambda e, xi=xi: e.reciprocal(out=rs[xi], in_=ss[xi]), reads=[("ss", xi)], writes=[("rs", xi)])
                P.op("dve", lambda e, xi=xi: e.scalar_tensor_tensor(out=yo[xi], in0=xo[xi], scalar=rs[xi], in1=fnw_b,
                                                                    op0=ALU.mult, op1=ALU.mult),
                     reads=[("xo", xi, 0), ("xo", xi, 1), ("rs", xi), "fnw_b"], writes=[("yo", xi)])
                P.dma(lambda q, xi=xi, tok0=tok0: q.dma_start(out=out_d[tok0:tok0 + 128, :], in_=yo[xi]),
                      reads=[("yo", xi)], writes=[("out", tok0)])
        cursor[0] = ph

    if stop_after is None:
        phase_c0()
        P.fence()
        dump("sg_scr", sg_scr, [])
        cursor[0] = 0
        phase_c1()

    P.emit(final_wait_ops=[o for o in P.dma_last if o is not None])
    return nc


def _consts():
    c = np.zeros((128, 128 + 12 * 256 + 256), np.float32)
    c[:, 0:128] = np.eye(128, dtype=np.float32)
    slopes = (2.0 ** (-8.0 * np.arange(1, 13, dtype=np.float32) / 12)).reshape(3, 4)
    jk = np.arange(128)[:, None]
    iq = np.arange(128)[None, :]
    for gi in range(3):
        for h in range(4):
            s = slopes[gi, h] * DIL[gi]
            d0 = (iq - jk).astype(np.float32)
            b0 = np.where(iq >= jk, -s * d0, NEG)
            d1 = (128 + iq - jk).astype(np.float32)
            b1 = np.where(iq <= jk, -s * d1, NEG)
            g = gi * 4 + h
            c[:, 128 + g * 256:128 + g * 256 + 128] = b0
            c[:, 128 + g * 256 + 128:128 + g * 256 + 256] = b1
    MK = 128 + 3072
    j = np.arange(64)[:, None]
    i = np.arange(64)[None, :]
    c[0:64, MK:MK + 64] = np.where(i >= j, 0.0, NEG)
    c[0:64, MK + 64:MK + 128] = np.where(i > j, 0.0, NEG)
    c[0:64, MK + 128:MK + 192] = (j <= i).astype(np.float32)
    c[0:64, MK + 192:MK + 256] = 1.0
    return c


_NC_CACHE = {}


def _host_inputs(x, norm_w, w_in, conv_w, a_log, dt_bias, dn_norm_w, w_o_dn, w_o_dil, w_out, final_norm_w):
    f = lambda a: np.ascontiguousarray(np.asarray(a, dtype=np.float32))
    cw = f(conv_w)[0].reshape(4, 24, 128).transpose(2, 1, 0).reshape(128, 96)
    shared = {
        "w_in": f(w_in)[0], "w_o_dn": f(w_o_dn)[0], "w_o_dil": f(w_o_dil)[0], "w_out": f(w_out)[0],
        "norm_w": f(norm_w).reshape(1, D), "final_norm_w": f(final_norm_w).reshape(1, D),
        "conv_w_l": np.ascontiguousarray(cw), "a_log": f(a_log).reshape(1, 8), "dt_bias": f(dt_bias).reshape(1, 8),
        "dn_norm_w_l": f(dn_norm_w).reshape(128, 1), "consts": _consts(),
    }
    xs = f(x)
    return [dict(shared, x=xs[b]) for b in range(xs.shape[0])]


def kernel(x, norm_w, w_in, conv_w, a_log, dt_bias, dn_norm_w, w_o_dn, w_o_dil, w_out, final_norm_w):
    in_maps = _host_inputs(x, norm_w, w_in, conv_w, a_log, dt_bias, dn_norm_w, w_o_dn, w_o_dil, w_out, final_norm_w)
    if "nc" not in _NC_CACHE:
        _NC_CACHE["nc"] = build_nc()
    res = run_bass_kernel_spmd(_NC_CACHE["nc"], in_maps, core_ids=list(range(len(in_maps))))
    return np.stack([np.asarray(r["out"], dtype=np.float32).reshape(T, D) for r in res.results], axis=0)
```

```python
import contextlib
import numpy as np
import concourse.bass as bass
import concourse.mybir as mybir
from concourse.bass_utils import run_bass_kernel_spmd

ACT = mybir.ActivationFunctionType
ALU = mybir.AluOpType
F32 = mybir.dt.float32
BF16 = mybir.dt.bfloat16

T = 4096
D = 1024
NEG = -30000.0
PIPELINE_A = True
EPS = 1e-6
O_QA, O_KA, O_VA, O_ZA, O_BA, O_QB, O_KB, O_VB, O_ZB, O_GA, O_GB = (
    0, 1024, 2048, 3072, 4096, 4112, 5648, 7184, 8720, 9232, 10256)
DIL = (1, 4, 16)


class _Op:
    __slots__ = ("eng", "fn", "deps", "signal", "ticket", "is_dma", "dsem", "dval")

    def __init__(self, eng, fn, is_dma=False):
        self.eng = eng
        self.fn = fn
        self.deps = []
        self.signal = False
        self.ticket = None
        self.is_dma = is_dma
        self.dsem = None
        self.dval = None


class _Res:
    __slots__ = ("w", "r", "rd")

    def __init__(self):
        self.w = None
        self.r = {}
        self.rd = []


class Prog:
    ENGS = ("pe", "act", "dve", "pool", "sp")

    def __init__(self, nc, n_dma_sems=32):
        self.nc = nc
        self.streams = {e: [] for e in self.ENGS}
        self.res = {}
        self.n_dma_sems = n_dma_sems
        self.dma_cnt = [0] * n_dma_sems
        self.dma_last = [None] * n_dma_sems
        self.dma_rr = 0
        self.fence_ops = []

    def fence(self):
        ops = []
        for e in self.ENGS:
            for o in reversed(self.streams[e]):
                if not o.is_dma:
                    ops.append(o)
                    break
        for d in self.dma_last:
            if d is not None:
                ops.append(d)
        self.fence_ops = ops

    def _r(self, k):
        r = self.res.get(k)
        if r is None:
            r = self.res[k] = _Res()
        return r

    def op(self, eng, fn, reads=(), writes=(), is_dma=False):
        o = _Op(eng, fn, is_dma)
        deps = [(d, "raw") for d in self.fence_ops]
        for k in reads:
            r = self._r(k)
            if r.w is not None:
                deps.append((r.w, "raw"))
        for k in writes:
            r = self._r(k)
            if r.w is not None:
                deps.append((r.w, "waw"))
            for rd in r.r.values():
                deps.append((rd, "war"))
            for rd in r.rd:
                deps.append((rd, "war"))
        if is_dma:
            i = self.dma_rr
            self.dma_rr = (i + 1) % self.n_dma_sems
            o.dsem = i
            self.dma_cnt[i] += 1
            o.dval = 16 * self.dma_cnt[i]
            if self.dma_last[i] is not None:
                deps.append((self.dma_last[i], "raw"))
            self.dma_last[i] = o
        seen = set()
        for d, kind in deps:
            if d is o or id(d) in seen:
                continue
            if not d.is_dma and d.eng == eng and not is_dma:
                if eng == "pe":
                    continue
                if kind != "raw":
                    continue
            seen.add(id(d))
            d.signal = True
            o.deps.append(d)
        for k in reads:
            r = self._r(k)
            if is_dma:
                r.rd.append(o)
            else:
                r.r[eng] = o
        for k in writes:
            r = self._r(k)
            r.w = o
            r.r = {}
            r.rd = []
        self.streams[eng].append(o)
        return o

    def dma(self, fn, reads=(), writes=(), q="sp"):
        return self.op(q, fn, reads, writes, is_dma=True)

    def emit(self, final_wait_ops=()):
        nc = self.nc
        for e in self.ENGS:
            c = 0
            for o in self.streams[e]:
                if o.is_dma:
                    continue
                if o.signal:
                    c += 1
                    o.ticket = c
        with contextlib.ExitStack() as es:
            esem = {e: es.enter_context(nc.semaphore("s_" + e)) for e in self.ENGS}
            dsem = [es.enter_context(nc.semaphore("d_%d" % i)) for i in range(self.n_dma_sems)]
            block = es.enter_context(nc.Block())

            def run(e, engobj):
                waited = {}

                def wait_for(d):
                    if d.is_dma:
                        key, sem, val = ("d", d.dsem), dsem[d.dsem], d.dval
                    else:
                        key, sem, val = ("e", d.eng), esem[d.eng], d.ticket
                    if waited.get(key, 0) >= val:
                        return
                    waited[key] = val
                    engobj.wait_ge(sem, val)

                for o in self.streams[e]:
                    for d in o.deps:
                        wait_for(d)
                    ins = o.fn(engobj)
                    if o.is_dma:
                        ins.then_inc(dsem[o.dsem], 16)
                    elif o.signal:
                        ins.then_inc(esem[e], 1)
                if e == "sp":
                    for d in final_wait_ops:
                        wait_for(d)

            @block.tensor
            def _(eng):
                run("pe", eng)

            @block.scalar
            def _(eng):
                run("act", eng)

            @block.vector
            def _(eng):
                run("dve", eng)

            @block.gpsimd
            def _(eng):
                run("pool", eng)

            @block.sync
            def _(eng):
                run("sp", eng)


def build_nc(dbg=None, stop_after=None):
    dbg = dbg or {}
    nc = bass.Bass("TRN2", target_bir_lowering=False)
    dt = nc.dram_tensor
    x_d = dt("x", [T, D], F32, kind="ExternalInput").ap()
    w_in = dt("w_in", [D, 11280], F32, kind="ExternalInput").ap()
    w_odn = dt("w_o_dn", [1024, 1024], F32, kind="ExternalInput").ap()
    w_odil = dt("w_o_dil", [512, 1024], F32, kind="ExternalInput").ap()
    w_out = dt("w_out", [1024, 1024], F32, kind="ExternalInput").ap()
    normw_d = dt("norm_w", [1, D], F32, kind="ExternalInput").ap()
    fnw_d = dt("final_norm_w", [1, D], F32, kind="ExternalInput").ap()
    cw_d = dt("conv_w_l", [128, 96], F32, kind="ExternalInput").ap()
    alog_d = dt("a_log", [1, 8], F32, kind="ExternalInput").ap()
    dtb_d = dt("dt_bias", [1, 8], F32, kind="ExternalInput").ap()
    dnw_d = dt("dn_norm_w_l", [128, 1], F32, kind="ExternalInput").ap()
    cst_d = dt("consts", [128, 128 + 12 * 256 + 64 * 4], F32, kind="ExternalInput").ap()
    out_d = dt("out", [T, D], F32, kind="ExternalOutput").ap()
    ob_scr = dt("ob_scr", [4, 128, T], BF16).ap()
    oa_scr = dt("oa_scr", [8, 128, T], BF16).ap()
    sg_scr = dt("sg_scr", [16, 128, T], BF16).ap()
    gc_scr = dt("gc_scr", [8, T], F32).ap()
    gcl_scr = dt("gcl_scr", [8, T], F32).ap()
    dbg_out = {}
    for name, (shape, dtype) in dbg.items():
        dbg_out[name] = dt("dbg_" + name, list(shape), dtype, kind="ExternalOutput").ap()

    P = Prog(nc)
    ARENA = 212000
    arena = nc.alloc_sbuf_tensor("arena", [128, ARENA // 2], BF16)
    cursor = [0]

    def alloc(free_shape, dtype, parts=128):
        n = 1
        for s in free_shape:
            n *= s
        esz = 4 if dtype == F32 else 2
        nbytes = (n * esz + 63) // 64 * 64
        off = cursor[0]
        cursor[0] += nbytes
        assert cursor[0] <= ARENA, ("SBUF overflow", cursor[0])
        ap = arena[0:parts, off // 2: off // 2 + n * esz // 2]
        if dtype == F32:
            ap = ap.bitcast(F32)
        if len(free_shape) == 2:
            ap = ap.rearrange("p (a b) -> p a b", b=free_shape[1])
        elif len(free_shape) == 3:
            ap = ap.rearrange("p (a b c) -> p a b c", b=free_shape[1], c=free_shape[2])
        return ap

    ps = [nc.alloc_psum_tensor("ps%d" % i, [128, 512], F32) for i in range(8)]
    psb = [p[:].bitcast(BF16) for p in ps]
    bank_rr = {}

    def bank(group):
        lst = {"proj": (0, 1), "misc": (2,), "pc": (0, 1, 2), "pre": (3, 4), "q1": (5,), "q4": (6,), "po": (7,),
               "all": tuple(range(8)), "s": (3, 4), "a": (5, 7), "b": (6, 2)}[group]
        i = bank_rr.get(group, 0)
        bank_rr[group] = i + 1
        return lst[i % len(lst)]

    def dump(name, src_ap, reads):
        if name in dbg_out:
            P.dma(lambda q, a=src_ap, o=dbg_out[name]: q.dma_start(out=o, in_=a), reads=reads, writes=[("dbg", name)])

    hT = alloc([8, T], BF16)
    cst = alloc([128 + 12 * 256 + 256], F32)
    identf = cst[:, 0:128]
    alibi = cst[:, 128:128 + 3072].rearrange("p (g w) -> p g w", w=256)
    MK = 128 + 3072
    mask_incl = cst[0:64, MK:MK + 64]
    mask_strict = cst[0:64, MK + 64:MK + 128]
    triu_f = cst[0:64, MK + 128:MK + 192]
    ones64f = cst[0:64, MK + 192:MK + 256]
    ident_bf = alloc([128], BF16)
    ones_bf = alloc([128], BF16)
    ones_f = alloc([128], F32)
    cw = alloc([96], F32)
    dnw = alloc([1], F32)
    WB_N = 4
    wb = [alloc([8, 128], BF16) for _ in range(WB_N)]
    wb_rr = [0]
    beta_t = alloc([64, 8], F32, 64)
    gc_t = alloc([64, 8], F32, 64)
    negeg_t = alloc([64, 8], F32, 64)
    wdec_t = alloc([64, 8], F32, 64)
    dl_t = alloc([64, 8], F32)
    persist_end = cursor[0]

    P.dma(lambda q: q.dma_start(out=cst, in_=cst_d), writes=["cst"])
    P.dma(lambda q: q.dma_start(out=cw, in_=cw_d), writes=["cw"])
    P.dma(lambda q: q.dma_start(out=dnw, in_=dnw_d), writes=["dnw"])
    P.op("dve", lambda e: e.tensor_copy(out=ident_bf, in_=identf), reads=["cst"], writes=["ident_bf"])
    P.op("pool", lambda e: e.memset(ones_bf, 1.0), writes=["ones_bf"])
    P.op("pool", lambda e: e.memset(ones_f, 1.0), writes=["ones_f"])

    def load_w(src, c0, ncols=128, kchunks=8, dst=None, key=None):
        if dst is None:
            i = wb_rr[0] % WB_N
            wb_rr[0] += 1
            dst, key = wb[i], ("wb", i)
        P.dma(lambda q, d=dst, s=src, c0=c0, n=ncols, kc=kchunks: q.dma_start(
            out=d[:, 0:kc, 0:n], in_=s[:, c0:c0 + n].rearrange("(k p) c -> p k c", p=128)),
            writes=[key], q="pool")
        return dst, key

    def hkeys(t0, t1):
        return [("hT", i) for i in range(t0 // 128, (t1 + 127) // 128)]

    def proj_tile(wt, wkey, ncols, tt8, grp="proj"):
        b = bank(grp)
        for k in range(8):
            P.op("pe", lambda e, b=b, k=k, wt=wt, n=ncols, tt8=tt8: e.matmul(
                ps[b][0:n, :], lhsT=wt[:, k, 0:n], rhs=hT[:, k, tt8 * 512:(tt8 + 1) * 512],
                start=(k == 0), stop=(k == 7)),
                reads=[wkey] + hkeys(tt8 * 512, tt8 * 512 + 512), writes=[("ps", b)])
        return b

    ph = cursor[0]
    normw_b = alloc([D], F32)
    xs = [alloc([D], F32) for _ in range(2)]
    junk = alloc([D], BF16)
    xb = [alloc([D], BF16) for _ in range(2)]
    ss0 = [alloc([1], F32) for _ in range(2)]
    rs0 = [alloc([1], F32) for _ in range(2)]
    P.dma(lambda q: q.dma_start(out=normw_b, in_=normw_d.partition_broadcast(128)), writes=["normw_b"])
    for tt in range(32):
        i = tt % 2
        P.dma(lambda q, i=i, tt=tt: q.dma_start(out=xs[i], in_=x_d[tt * 128:(tt + 1) * 128, :]), writes=[("xs", i)])
        P.op("pool", lambda e, i=i: e.memset(ss0[i], 0.0), writes=[("ss0", i)])
        P.op("act", lambda e, i=i: e.activation(out=junk, in_=xs[i], func=ACT.Square, accum_out=ss0[i]),
             reads=[("xs", i), ("ss0", i)], writes=["junk", ("ss0", i)])
        P.op("act", lambda e, i=i: e.activation(out=ss0[i], in_=ss0[i], func=ACT.Sqrt, scale=1.0 / D, bias=EPS),
             reads=[("ss0", i)], writes=[("ss0", i)])
        P.op("dve", lambda e, i=i: e.reciprocal(out=rs0[i], in_=ss0[i]), reads=[("ss0", i)], writes=[("rs0", i)])
        P.op("dve", lambda e, i=i: e.scalar_tensor_tensor(out=xb[i], in0=xs[i], scalar=rs0[i], in1=normw_b,
                                                          op0=ALU.mult, op1=ALU.mult),
             reads=[("xs", i), ("rs0", i), "normw_b"], writes=[("xb", i)])
        b = bank("all")
        for k in range(8):
            P.op("pe", lambda e, b=b, k=k, i=i: e.transpose(out=psb[b][:, k * 128:(k + 1) * 128],
                                                           in_=xb[i][:, k * 128:(k + 1) * 128], identity=ident_bf),
                 reads=[("xb", i), "ident_bf"], writes=[("ps", b)])
        eng = "act" if tt % 2 == 0 else "dve"
        if eng == "act":
            fn = lambda e, b=b, tt=tt: e.activation(out=hT[:, :, tt * 128:(tt + 1) * 128],
                                                    in_=psb[b].rearrange("p (k t) -> p k t", t=128), func=ACT.Copy)
        else:
            fn = lambda e, b=b, tt=tt: e.tensor_copy(out=hT[:, :, tt * 128:(tt + 1) * 128],
                                                     in_=psb[b].rearrange("p (k t) -> p k t", t=128))
        P.op(eng, fn, writes=[("hT", tt), ("ps", b)])
    dump("hT", hT, hkeys(0, T))
    cursor[0] = ph
    P.fence()

    def phase_a0():
        ph = cursor[0]
        w16, w16k = load_w(w_in, O_BA, 16)
        alog_b = alloc([8], F32, 64)
        dtb_b = alloc([8], F32, 64)
        P.dma(lambda q: q.dma_start(out=alog_b, in_=alog_d.partition_broadcast(64)), writes=["alog_b"])
        P.dma(lambda q: q.dma_start(out=dtb_b, in_=dtb_d.partition_broadcast(64)), writes=["dtb_b"])
        Gsb = alloc([64, 16], F32, 64)
        names = ["xa", "ax", "ee", "ll", "sp", "g", "lb", "gcl", "tmp"]
        A = {n: alloc([64, 8], F32, 64) for n in names}
        glast = alloc([64, 8], F32)
        tb = alloc([8, 64], F32, 64)
        for half in range(2):
            b = bank("all")
            for n in range(32 * half, 32 * half + 32):
                for k in range(8):
                    P.op("pe", lambda e, b=b, n=n, k=k: e.matmul(
                        ps[b][0:64, (n % 32) * 16:(n % 32) * 16 + 16], lhsT=hT[:, k, n * 64:(n + 1) * 64],
                        rhs=w16[:, k, 0:16], start=(k == 0), stop=(k == 7)),
                        reads=[w16k] + hkeys(n * 64, n * 64 + 64), writes=[("ps", b)])
            P.op("act", lambda e, b=b, half=half: e.activation(
                out=Gsb[:, 32 * half:32 * half + 32, :], in_=ps[b][0:64, :].rearrange("p (n c) -> p n c", c=16),
                func=ACT.Copy), writes=["Gsb", ("ps", b)])
        bb = Gsb[:, :, 0:8]
        aa = Gsb[:, :, 8:16]
        bc = lambda v: v.unsqueeze(1).to_broadcast([64, 64, 8])
        P.op("act", lambda e: e.activation(out=beta_t, in_=bb, func=ACT.Sigmoid), reads=["Gsb"], writes=["beta_t"])
        P.op("act", lambda e: e.activation(out=A["lb"], in_=beta_t, func=ACT.Ln), reads=["beta_t"], writes=["lb"])
        P.op("dve", lambda e: e.tensor_tensor(out=A["xa"], in0=aa, in1=bc(dtb_b), op=ALU.add),
             reads=["Gsb", "dtb_b"], writes=["xa"])
        P.op("act", lambda e: e.activation(out=A["ax"], in_=A["xa"], func=ACT.Abs), reads=["xa"], writes=["ax"])
        P.op("act", lambda e: e.activation(out=A["ee"], in_=A["ax"], func=ACT.Exp, scale=-1.0), reads=["ax"], writes=["ee"])
        P.op("act", lambda e: e.activation(out=A["ll"], in_=A["ee"], func=ACT.Ln, bias=1.0), reads=["ee"], writes=["ll"])
        P.op("dve", lambda e: e.scalar_tensor_tensor(out=A["sp"], in0=A["xa"], scalar=0.0, in1=A["ll"],
                                                     op0=ALU.max, op1=ALU.add), reads=["xa", "ll"], writes=["sp"])
        P.op("act", lambda e: e.activation(out=alog_b, in_=alog_b, func=ACT.Exp), reads=["alog_b"], writes=["alog_b"])
        P.op("dve", lambda e: e.scalar_tensor_tensor(out=A["g"], in0=A["sp"], scalar=-1.0, in1=bc(alog_b),
                                                     op0=ALU.mult, op1=ALU.mult), reads=["sp", "alog_b"], writes=["g"])
        gflat = A["g"].rearrange("p n h -> p (n h)")
        b1 = bank("all")
        P.op("pe", lambda e: e.matmul(ps[b1][0:64, :], lhsT=triu_f, rhs=gflat, start=True, stop=True),
             reads=["g", "cst"], writes=[("ps", b1)])
        b2 = bank("all")
        P.op("pe", lambda e: e.matmul(ps[b2][:, :], lhsT=ones_f[0:64, :], rhs=gflat, start=True, stop=True),
             reads=["g", "ones_f"], writes=[("ps", b2)])
        fl = lambda v: v.rearrange("p n h -> p (n h)")
        P.op("act", lambda e: e.activation(out=fl(gc_t), in_=ps[b1][0:64, :], func=ACT.Copy), writes=["gc_t", ("ps", b1)])
        P.op("dve", lambda e: e.tensor_copy(out=fl(glast), in_=ps[b2][:, :]), writes=["glast", ("ps", b2)])
        P.op("act", lambda e: e.activation(out=negeg_t, in_=gc_t, func=ACT.Exp), reads=["gc_t"], writes=["negeg_t"])
        P.op("dve", lambda e: e.tensor_scalar(out=negeg_t, in0=negeg_t, scalar1=-1.0, scalar2=None, op0=ALU.mult),
             reads=["negeg_t"], writes=["negeg_t"])
        P.op("dve", lambda e: e.tensor_tensor(out=A["tmp"], in0=glast[0:64], in1=gc_t, op=ALU.subtract),
             reads=["glast", "gc_t"], writes=["tmp"])
        P.op("act", lambda e: e.activation(out=wdec_t, in_=A["tmp"], func=ACT.Exp), reads=["tmp"], writes=["wdec_t"])
        P.op("act", lambda e: e.activation(out=dl_t, in_=glast, func=ACT.Exp), reads=["glast"], writes=["dl_t"])
        P.op("dve", lambda e: e.tensor_tensor(out=A["gcl"], in0=gc_t, in1=A["lb"], op=ALU.add),
             reads=["gc_t", "lb"], writes=["gcl"])
        for nm, src, scr in (("gc", gc_t, gc_scr), ("gcl", A["gcl"], gcl_scr)):
            b = bank("all")
            for h in range(8):
                P.op("pe", lambda e, b=b, h=h, src=src: e.transpose(out=ps[b][0:64, h * 64:(h + 1) * 64],
                                                                   in_=src[:, :, h], identity=identf[0:64, 0:64]),
                     reads=["gc_t" if nm == "gc" else "gcl", "cst"], writes=[("ps", b)])
            P.op("dve", lambda e, b=b: e.tensor_copy(out=tb, in_=ps[b][0:64, :].rearrange("p (h c) -> p h c", c=64)),
                 writes=["tb", ("ps", b)])
            P.dma(lambda q, scr=scr: q.dma_start(out=scr.rearrange("h (n c) -> n h c", c=64), in_=tb),
                  reads=["tb"], writes=[nm + "_scr"])
        dump("gc_t", gc_t, ["gc_t"])
        dump("beta_t", beta_t, ["beta_t"])
        dump("g_t", A["g"], ["g"])
        cursor[0] = ph

    phase_a0()
    P.fence()

    def phase_b():
        ph = cursor[0]
        qT = alloc([T], BF16)
        kT = alloc([T], BF16)
        v_sb = alloc([32, 128], BF16)
        acc_n = alloc([T], F32)
        acc_d = alloc([T], F32)
        NS = 3
        s_sb = [alloc([256], F32) for _ in range(NS)]
        p_sb = [alloc([256], BF16) for _ in range(4)]
        zs = [alloc([512], F32) for _ in range(2)]
        rc = [alloc([512], F32) for _ in range(2)]
        ob = [alloc([512], BF16) for _ in range(2)]
        scale = 128 ** -0.5
        for h in range(4):
            for gi in range(3):
                d = DIL[gi]
                L = T // d
                nb = L // 128
                M = 512 // d
                gidx = gi * 4 + h
                wq, wqk = load_w(w_in, O_QB + gi * 512 + h * 128)
                wk, wkk = load_w(w_in, O_KB + gi * 512 + h * 128)
                wv, wvk = load_w(w_in, O_VB + gi * 512 + h * 128)
                q3 = qT.rearrange("p (r m) -> p r m", r=d)
                k3 = kT.rearrange("p (r m) -> p r m", r=d)
                for tt8 in range(8):
                    b = proj_tile(wq, wqk, 128, tt8)
                    P.op("act", lambda e, b=b, tt8=tt8, q3=q3, M=M, d=d: e.activation(
                        out=q3[:, :, tt8 * M:(tt8 + 1) * M].rearrange("p r m -> p m r"),
                        in_=ps[b][:, :].rearrange("p (m r) -> p m r", r=d), func=ACT.Copy, scale=scale),
                        writes=["qT", ("ps", b)])
                    b = proj_tile(wk, wkk, 128, tt8)
                    P.op("dve", lambda e, b=b, tt8=tt8, k3=k3, M=M, d=d: e.tensor_copy(
                        out=k3[:, :, tt8 * M:(tt8 + 1) * M].rearrange("p r m -> p m r"),
                        in_=ps[b][:, :].rearrange("p (m r) -> p m r", r=d)),
                        writes=["kT", ("ps", b)])
                for t4 in range(8):
                    b = bank("proj")
                    for s in range(4):
                        tid = t4 * 4 + s
                        r, j = tid // nb, tid % nb
                        t0 = 128 * j * d + r
                        for k in range(8):
                            P.op("pe", lambda e, b=b, s=s, k=k, t0=t0, d=d, wv=wv: e.matmul(
                                ps[b][:, s * 128:(s + 1) * 128], lhsT=hT[:, k, t0:t0 + 127 * d + 1:d], rhs=wv[:, k, :],
                                start=(k == 0), stop=(k == 7)),
                                reads=[wvk] + hkeys(128 * j * d, 128 * (j + 1) * d), writes=[("ps", b)])
                    P.op("dve", lambda e, b=b, t4=t4: e.tensor_copy(
                        out=v_sb[:, t4 * 4:(t4 + 1) * 4, :], in_=ps[b][:, :].rearrange("p (s c) -> p s c", c=128)),
                        writes=["v_sb", ("ps", b)])
                cnt = 0
                for r in range(d):
                    prev = None
                    bn = bd = None
                    for j in range(nb):
                        W = 256 if j + 1 < nb else 128
                        c0 = r * L + j * 128
                        bs = bank("s")
                        P.op("pe", lambda e, bs=bs, c0=c0, W=W: e.matmul(
                            ps[bs][:, 0:W], lhsT=kT[:, c0:c0 + 128], rhs=qT[:, c0:c0 + W], start=True, stop=True),
                            reads=["qT", "kT"], writes=[("ps", bs)])
                        si = cnt % NS
                        pi = cnt % 4
                        cnt += 1
                        P.op("dve", lambda e, bs=bs, si=si, W=W, gidx=gidx: e.tensor_tensor(
                            out=s_sb[si][:, 0:W], in0=ps[bs][:, 0:W], in1=alibi[:, gidx, 0:W], op=ALU.add),
                            reads=["cst"], writes=[("s_sb", si), ("ps", bs)])
                        P.op("act", lambda e, si=si, pi=pi, W=W: e.activation(
                            out=p_sb[pi][:, 0:W], in_=s_sb[si][:, 0:W], func=ACT.Exp),
                            reads=[("s_sb", si)], writes=[("p_sb", pi)])
                        if j % 4 == 0:
                            bn, bd = bank("a"), bank("b")
                        col = (j % 4) * 128
                        tid = r * nb + j
                        for (bk, lhs_prev, lhs_cur, lk) in ((bn, None, None, "v_sb"), (bd, ones_bf, ones_bf, "ones_bf")):
                            first = True
                            if j > 0:
                                lp = v_sb[:, tid - 1, :] if lhs_prev is None else lhs_prev
                                P.op("pe", lambda e, bk=bk, col=col, lp=lp, pp=prev: e.matmul(
                                    ps[bk][:, col:col + 128], lhsT=lp, rhs=p_sb[pp][:, 128:256], start=True, stop=False),
                                    reads=[lk, ("p_sb", prev)], writes=[("ps", bk)])
                                first = False
                            lc = v_sb[:, tid, :] if lhs_cur is None else lhs_cur
                            P.op("pe", lambda e, bk=bk, col=col, lc=lc, pi=pi, first=first: e.matmul(
                                ps[bk][:, col:col + 128], lhsT=lc, rhs=p_sb[pi][:, 0:128], start=first, stop=True),
                                reads=[lk, ("p_sb", pi)], writes=[("ps", bk)])
                        prev = pi
                        if j % 4 == 3 or j == nb - 1:
                            n0 = (j // 4) * 4
                            nq = (j - n0 + 1) * 128
                            sl = slice(128 * n0 * d + r, 128 * n0 * d + r + (nq - 1) * d + 1, d)
                            for (bk, acc, key, eng) in ((bn, acc_n, "acc_n", "act"), (bd, acc_d, "acc_d", "dve")):
                                if gi == 0:
                                    if eng == "act":
                                        P.op("act", lambda e, bk=bk, acc=acc, sl=sl, nq=nq: e.activation(
                                            out=acc[:, sl], in_=ps[bk][:, 0:nq], func=ACT.Copy),
                                            writes=[key, ("ps", bk)])
                                    else:
                                        P.op("dve", lambda e, bk=bk, acc=acc, sl=sl, nq=nq: e.tensor_copy(
                                            out=acc[:, sl], in_=ps[bk][:, 0:nq]), writes=[key, ("ps", bk)])
                                else:
                                    P.op("dve", lambda e, bk=bk, acc=acc, sl=sl, nq=nq: e.tensor_tensor(
                                        out=acc[:, sl], in0=ps[bk][:, 0:nq], in1=acc[:, sl], op=ALU.add),
                                        reads=[key], writes=[key, ("ps", bk)])
            wz, wzk = load_w(w_in, O_ZB + h * 128)
            for tt8 in range(8):
                i = tt8 % 2
                sl = slice(tt8 * 512, (tt8 + 1) * 512)
                b = proj_tile(wz, wzk, 128, tt8)
                P.op("act", lambda e, b=b, i=i: e.activation(out=zs[i], in_=ps[b][:, :], func=ACT.Silu),
                     writes=[("zs", i), ("ps", b)])
                P.op("dve", lambda e, i=i, sl=sl: e.reciprocal(out=rc[i], in_=acc_d[:, sl]), reads=["acc_d"], writes=[("rc", i)])
                P.op("dve", lambda e, i=i, sl=sl: e.tensor_tensor(out=rc[i], in0=rc[i], in1=acc_n[:, sl], op=ALU.mult),
                     reads=["acc_n", ("rc", i)], writes=[("rc", i)])
                P.op("dve", lambda e, i=i: e.tensor_tensor(out=ob[i], in0=rc[i], in1=zs[i], op=ALU.mult),
                     reads=[("rc", i), ("zs", i)], writes=[("ob", i)])
                P.dma(lambda q, i=i, h=h, sl=sl: q.dma_start(out=ob_scr[h, :, sl], in_=ob[i]),
                      reads=[("ob", i)], writes=[("ob_scr", h, tt8)])
        cursor[0] = ph

    if stop_after != "a0":
        phase_b()
        P.fence()
        dump("ob_scr", ob_scr, [])

    def phase_a():
        ph = cursor[0]
        NB2 = 2
        upre = {t: [alloc([516], BF16) for _ in range(2)] for t in "qkv"}
        for t in "qkv":
            P.op("pool", lambda e, t=t: e.memset(upre[t][1][:, 512:515], 0.0), writes=[("upre", t, 1)])
        dg = alloc([12, 128], BF16)
        th = {t: alloc([512], F32) for t in "qkvz"}
        yq = alloc([512], F32)
        yk = alloc([512], F32)
        sq = {t: alloc([512], BF16) for t in "qk"}
        rin = {t: alloc([512], F32) for t in "qk"}
        vTt = alloc([512], BF16)
        khT = [alloc([512], BF16) for _ in range(NB2)]
        qhT = [alloc([512], BF16) for _ in range(NB2)]
        qgT = [alloc([512], BF16) for _ in range(NB2)]
        zsT = [alloc([512], BF16) for _ in range(NB2)]
        Ktok = [alloc([8, 128], BF16, 64) for _ in range(NB2)]
        Vtok = [alloc([8, 128], BF16, 64) for _ in range(NB2)]
        AqkT = [alloc([8, 64], BF16, 64) for _ in range(NB2)]
        TT = [alloc([8, 64], BF16, 64) for _ in range(NB2)]
        gcB = [alloc([512], F32) for _ in range(2)]
        gclB = [alloc([512], F32, 64) for _ in range(2)]
        egB = alloc([512], F32)
        E1 = alloc([8, 64], F32, 64)
        E2 = alloc([8, 64], F32, 64)
        GT = alloc([8, 64], F32, 64)
        GTb = alloc([8, 64], F32, 64)
        Pk = [alloc([8, 64], BF16, 64) for _ in range(2)]
        PkT = [alloc([8, 64], BF16, 64) for _ in range(2)]
        Xb = [alloc([8, 64], BF16, 64) for _ in range(2)]
        S_f = alloc([128], F32)
        S_b = alloc([128], BF16)
        Rp = [alloc([128], BF16, 64) for _ in range(2)]
        vnew = [alloc([128], BF16, 64) for _ in range(2)]
        vnd = [alloc([128], BF16, 64) for _ in range(2)]
        oraw = alloc([512], F32)
        osq = alloc([512], BF16)
        orst = alloc([512], F32)
        oa = [alloc([512], BF16) for _ in range(2)]
        wsets = [[alloc([8, 128], BF16) for _ in range(4)] for _ in range(2)]

        def load_head_w(h):
            ws = wsets[h % 2]
            for ti, off in enumerate((O_QA, O_KA, O_VA, O_ZA)):
                load_w(w_in, off + h * 128, dst=ws[ti], key=("wh", h % 2, ti))

        m8 = lambda m: m.unsqueeze(1).to_broadcast([64, 8, 64])
        v3 = lambda a: a.rearrange("p (c i) -> p c i", i=64)
        fl = lambda a: a.rearrange("p c i -> p (c i)")

        def rsqrt_from_psum(b, dst, key, scale):
            P.op("act", lambda e: e.activation(out=dst, in_=ps[b][:, :], func=ACT.Ln, scale=scale, bias=EPS),
                 writes=[key, ("ps", b)])
            P.op("act", lambda e: e.activation(out=dst, in_=dst, func=ACT.Exp, scale=-0.5), reads=[key], writes=[key])

        def stage12_tasks(st):
            h, tt8 = st // 8, st % 8
            bi = st % NB2
            ui = st % 2
            t0 = tt8 * 512
            ws = wsets[h % 2]
            tasks = []
            A = tasks.append

            def t_pre():
                if tt8 == 0:
                    if h + 1 < 8:
                        load_head_w(h + 1)
                    for ti in range(3):
                        for kk in range(4):
                            g = ti * 8 + h
                            P.op("dve", lambda e, ti=ti, kk=kk, g=g: e.tensor_scalar(
                                out=dg[:, ti * 4 + kk, :], in0=identf, scalar1=cw[:, g * 4 + kk:g * 4 + kk + 1], scalar2=None,
                                op0=ALU.mult), reads=["cst", "cw"], writes=["dg"])
                P.dma(lambda q: q.dma_start(out=gcB[ui], in_=gc_scr[h, t0:t0 + 512].partition_broadcast(128)),
                      reads=["gc_scr"], writes=[("gcB", ui)])
                P.dma(lambda q: q.dma_start(out=gclB[ui], in_=gcl_scr[h, t0:t0 + 512].partition_broadcast(64)),
                      reads=["gcl_scr"], writes=[("gclB", ui)])
            A(t_pre)

            def t_projz():
                b = proj_tile(ws[3], ("wh", h % 2, 3), 128, tt8, grp="pc")
                P.op("act", lambda e: e.activation(out=th["z"], in_=ps[b][:, :], func=ACT.Tanh, scale=0.5),
                     writes=[("th", "z"), ("ps", b)])
                P.op("dve", lambda e: e.scalar_tensor_tensor(out=zsT[bi], in0=th["z"], scalar=1.0, in1=ps[b][:, :],
                                                             op0=ALU.add, op1=ALU.mult),
                     reads=[("th", "z")], writes=[("zsT", bi), ("ps", b)])
            A(t_projz)

            for ti, t in enumerate("qkv"):
                def t_proj(ti=ti, t=t):
                    u = upre[t][ui]
                    up = upre[t][1 - ui]
                    b = proj_tile(ws[ti], ("wh", h % 2, ti), 128, tt8, grp="pc")
                    if tt8 == 0:
                        P.op("pool", lambda e: e.memset(u[:, 0:3], 0.0), writes=[("upre", t, ui)])
                    else:
                        P.op("pool", lambda e: e.tensor_copy(out=u[:, 0:3], in_=up[:, 512:515]),
                             reads=[("upre", t, 1 - ui)], writes=[("upre", t, ui)])
                    P.op("act", lambda e: e.activation(out=u[:, 3:515], in_=ps[b][:, :], func=ACT.Copy),
                         writes=[("upre", t, ui), ("ps", b)])
                A(t_proj)

            for ti, t in enumerate("qkv"):
                def t_conv(ti=ti, t=t):
                    u = upre[t][ui]
                    b = bank("pc")
                    for kk in range(4):
                        P.op("pe", lambda e, kk=kk: e.matmul(ps[b][:, :], lhsT=dg[:, ti * 4 + kk, :], rhs=u[:, kk:kk + 512],
                                                             start=(kk == 0), stop=(kk == 3)),
                             reads=["dg", ("upre", t, ui)], writes=[("ps", b)])
                    P.op("act", lambda e: e.activation(out=th[t], in_=ps[b][:, :], func=ACT.Tanh, scale=0.5),
                         writes=[("th", t), ("ps", b)])
                    dst = {"q": yq, "k": yk, "v": vTt}[t]
                    P.op("dve", lambda e: e.scalar_tensor_tensor(out=dst, in0=th[t], scalar=1.0, in1=ps[b][:, :],
                                                                 op0=ALU.add, op1=ALU.mult),
                         reads=[("th", t)], writes=[("y", t), ("ps", b)])
                A(t_conv)

            for t, y in (("q", yq), ("k", yk)):
                def t_norm(t=t, y=y):
                    P.op("act", lambda e: e.activation(out=sq[t], in_=y, func=ACT.Square), reads=[("y", t)], writes=[("sq", t)])
                    b = bank("pc")
                    P.op("pe", lambda e: e.matmul(ps[b][:, :], lhsT=ones_bf, rhs=sq[t], start=True, stop=True),
                         reads=[("sq", t), "ones_bf"], writes=[("ps", b)])
                    rsqrt_from_psum(b, rin[t], ("rin", t), 0.25)
                A(t_norm)

            def t_hat():
                P.op("dve", lambda e: e.scalar_tensor_tensor(out=khT[bi], in0=yk, scalar=0.5, in1=rin["k"],
                                                             op0=ALU.mult, op1=ALU.mult),
                     reads=[("y", "k"), ("rin", "k")], writes=[("khT", bi)])
                P.op("dve", lambda e: e.scalar_tensor_tensor(out=qhT[bi], in0=yq, scalar=0.5 * 128 ** -0.5, in1=rin["q"],
                                                             op0=ALU.mult, op1=ALU.mult),
                     reads=[("y", "q"), ("rin", "q")], writes=[("qhT", bi)])
                P.op("act", lambda e: e.activation(out=egB, in_=gcB[ui], func=ACT.Exp), reads=[("gcB", ui)], writes=["egB"])
                P.op("pool", lambda e: e.tensor_tensor(out=qgT[bi], in0=qhT[bi], in1=egB, op=ALU.mult),
                     reads=[("qhT", bi), "egB"], writes=[("qgT", bi)])
            A(t_hat)

            for (src, skey, dst, dkey, sc) in ((khT[bi], ("khT", bi), Ktok[bi], ("Ktok", bi), 1.0),
                                               (vTt, ("y", "v"), Vtok[bi], ("Vtok", bi), 0.5)):
                def t_tok(src=src, skey=skey, dst=dst, dkey=dkey, sc=sc):
                    b = bank("pc")
                    for c in range(8):
                        P.op("pe", lambda e, c=c: e.transpose(out=psb[b][0:64, c * 128:(c + 1) * 128],
                                                              in_=src[:, c * 64:(c + 1) * 64], identity=ident_bf),
                             reads=[skey, "ident_bf"], writes=[("ps", b)])
                    P.op("act", lambda e: e.activation(out=dst, in_=psb[b][0:64, :].rearrange("p (c k) -> p c k", k=128),
                                                       func=ACT.Copy, scale=sc), writes=[dkey, ("ps", b)])
                A(t_tok)

            n0 = tt8 * 8
            gcJ = gc_t[:, n0:n0 + 8, h].unsqueeze(2).to_broadcast([64, 8, 64])
            bJ = beta_t[:, n0:n0 + 8, h].unsqueeze(2).to_broadcast([64, 8, 64])
            hold = {}

            def t_gates():
                P.op("pool", lambda e: e.tensor_tensor(out=E1, in0=v3(gcB[ui][0:64, :]), in1=gcJ, op=ALU.subtract),
                     reads=[("gcB", ui), "gc_t"], writes=["E1"])
                P.op("pool", lambda e: e.tensor_tensor(out=E1, in0=E1, in1=m8(mask_incl), op=ALU.add),
                     reads=["E1", "cst"], writes=["E1"])
                P.op("act", lambda e: e.activation(out=GT, in_=E1, func=ACT.Exp), reads=["E1"], writes=["GT"])
                P.op("pool", lambda e: e.tensor_tensor(out=E2, in0=v3(gclB[ui]), in1=gcJ, op=ALU.subtract),
                     reads=[("gclB", ui), "gc_t"], writes=["E2"])
                P.op("pool", lambda e: e.tensor_tensor(out=E2, in0=E2, in1=m8(mask_strict), op=ALU.add),
                     reads=["E2", "cst"], writes=["E2"])
                P.op("act", lambda e: e.activation(out=GTb, in_=E2, func=ACT.Exp), reads=["E2"], writes=["GTb"])
            A(t_gates)

            def t_kkqk():
                bkk, bqk = bank("pre"), bank("pre")
                for c in range(8):
                    cs = slice(c * 64, (c + 1) * 64)
                    P.op("pe", lambda e, cs=cs: e.matmul(ps[bkk][0:64, cs], lhsT=khT[bi][:, cs], rhs=khT[bi][:, cs],
                                                         start=True, stop=True), reads=[("khT", bi)], writes=[("ps", bkk)])
                for c in range(8):
                    cs = slice(c * 64, (c + 1) * 64)
                    P.op("pe", lambda e, cs=cs: e.matmul(ps[bqk][0:64, cs], lhsT=khT[bi][:, cs], rhs=qhT[bi][:, cs],
                                                         start=True, stop=True), reads=[("khT", bi), ("qhT", bi)], writes=[("ps", bqk)])
                P.op("dve", lambda e: e.tensor_tensor(out=AqkT[bi], in0=v3(ps[bqk][0:64, :]), in1=GT, op=ALU.mult),
                     reads=["GT"], writes=[("AqkT", bi), ("ps", bqk)])
                P.op("dve", lambda e: e.scalar_tensor_tensor(out=Pk[0], in0=v3(ps[bkk][0:64, :]), scalar=-1.0, in1=GTb,
                                                             op0=ALU.mult, op1=ALU.mult),
                     reads=["GTb"], writes=[("Pk", 0), ("ps", bkk)])
            A(t_kkqk)

            def t_pt():
                b = bank("pre")
                for c in range(8):
                    P.op("pe", lambda e, c=c: e.transpose(out=psb[b][0:64, c * 64:(c + 1) * 64], in_=Pk[0][:, c, :],
                                                          identity=ident_bf[0:64, 0:64]),
                         reads=[("Pk", 0), "ident_bf"], writes=[("ps", b)])
                P.op("act", lambda e: e.activation(out=fl(PkT[0]), in_=psb[b][0:64, 0:512], func=ACT.Copy),
                     writes=[("PkT", 0), ("ps", b)])
                P.op("pool", lambda e: e.tensor_tensor(out=Xb[0], in0=Pk[0], in1=m8(identf[0:64, 0:64]), op=ALU.add),
                     reads=[("Pk", 0), "cst"], writes=[("Xb", 0)])
            A(t_pt)

            for lvl in range(5):
                cur = lvl % 2
                nxt = 1 - cur

                def t_sq(lvl=lvl, cur=cur, nxt=nxt):
                    if lvl < 4:
                        ba = bank("pre")
                        for c in range(8):
                            cs = slice(c * 64, (c + 1) * 64)
                            P.op("pe", lambda e, c=c, cs=cs: e.matmul(ps[ba][0:64, cs], lhsT=PkT[cur][:, c, :], rhs=Pk[cur][:, c, :],
                                                                      start=True, stop=True),
                                 reads=[("Pk", cur), ("PkT", cur)], writes=[("ps", ba)])
                    bt = bank("pre")
                    for c in range(8):
                        cs = slice(c * 64, (c + 1) * 64)
                        P.op("pe", lambda e, c=c, cs=cs: e.matmul(ps[bt][0:64, cs], lhsT=Pk[cur][:, c, :], rhs=PkT[cur][:, c, :],
                                                                  start=True, stop=True),
                             reads=[("Pk", cur), ("PkT", cur)], writes=[("ps", bt)])
                    if lvl < 4:
                        P.op("act", lambda e: e.activation(out=fl(Pk[nxt]), in_=ps[ba][0:64, :], func=ACT.Copy),
                             writes=[("Pk", nxt), ("ps", ba)])
                    P.op("dve", lambda e: e.tensor_copy(out=fl(PkT[nxt]), in_=ps[bt][0:64, :]),
                         writes=[("PkT", nxt), ("ps", bt)])
                A(t_sq)

                def t_x(lvl=lvl, cur=cur, nxt=nxt):
                    bx = bank("pre")
                    for c in range(8):
                        cs = slice(c * 64, (c + 1) * 64)
                        P.op("pe", lambda e, c=c, cs=cs: e.matmul(ps[bx][0:64, cs], lhsT=PkT[nxt][:, c, :], rhs=Xb[cur][:, c, :],
                                                                  start=True, stop=True),
                             reads=[("PkT", nxt), ("Xb", cur)], writes=[("ps", bx)])
                    P.op("dve", lambda e: e.tensor_tensor(out=fl(Xb[nxt]), in0=ps[bx][0:64, :], in1=fl(Xb[cur]), op=ALU.add),
                         reads=[("Xb", cur)], writes=[("Xb", nxt), ("ps", bx)])
                    if lvl == 4:
                        P.op("pool", lambda e: e.tensor_tensor(out=TT[bi], in0=Xb[nxt], in1=bJ, op=ALU.mult),
                             reads=[("Xb", nxt), "beta_t"], writes=[("TT", bi)])
                A(t_x)
            return tasks

        def stage3_tasks(st):
            h, tt8 = st // 8, st % 8
            bi = st % NB2
            t0 = tt8 * 512
            n0 = tt8 * 8
            tasks = []
            A = tasks.append
            hold = {}

            def t_begin():
                hold["bo"] = bank("po")
            A(t_begin)
            for c in range(8):
                n = n0 + c
                cs = slice(c * 64, (c + 1) * 64)
                ri = n % 2
                first = (n == 0)

                def t_a(c=c, n=n, cs=cs, ri=ri, first=first):
                    if not first:
                        b1 = bank("q1")
                        P.op("pe", lambda e: e.matmul(ps[b1][0:64, 0:128], lhsT=khT[bi][:, cs], rhs=S_b, start=True, stop=True),
                             reads=[("khT", bi), "S_b"], writes=[("ps", b1)])
                        P.op("dve", lambda e: e.scalar_tensor_tensor(
                            out=Rp[ri], in0=ps[b1][0:64, 0:128], scalar=negeg_t[:, n, h:h + 1], in1=Vtok[bi][:, c, :],
                            op0=ALU.mult, op1=ALU.add),
                            reads=["negeg_t", ("Vtok", bi)], writes=[("Rp", ri), ("ps", b1)])
                A(t_a)

                def t_b(c=c, n=n, cs=cs, ri=ri, first=first):
                    if not first:
                        rsrc, rkey = Rp[ri], ("Rp", ri)
                    else:
                        rsrc, rkey = Vtok[bi][:, c, :], ("Vtok", bi)
                    b2 = bank("q1")
                    P.op("pe", lambda e: e.matmul(ps[b2][0:64, 0:128], lhsT=TT[bi][:, c, :], rhs=rsrc, start=True, stop=True),
                         reads=[("TT", bi), rkey], writes=[("ps", b2)])
                    P.op("act", lambda e: e.activation(out=vnew[ri], in_=ps[b2][0:64, 0:128], func=ACT.Copy),
                         writes=[("vnew", ri), ("ps", b2)])
                    P.op("act", lambda e: e.activation(out=vnd[ri], in_=ps[b2][0:64, 0:128], func=ACT.Copy,
                                                       scale=wdec_t[:, n, h:h + 1]),
                         reads=["wdec_t"], writes=[("vnd", ri), ("ps", b2)])
                A(t_b)

                def t_c(c=c, n=n, cs=cs, ri=ri, first=first):
                    bo = hold["bo"]
                    if not first:
                        P.op("pe", lambda e: e.matmul(ps[bo][:, cs], lhsT=S_b, rhs=qgT[bi][:, cs], start=True, stop=False),
                             reads=["S_b", ("qgT", bi)], writes=[("ps", bo)])
                    P.op("pe", lambda e: e.matmul(ps[bo][:, cs], lhsT=vnew[ri], rhs=AqkT[bi][:, c, :], start=first, stop=True),
                         reads=[("vnew", ri), ("AqkT", bi)], writes=[("ps", bo)])
                    b4 = bank("q4")
                    P.op("pe", lambda e: e.matmul(ps[b4][:, 0:128], lhsT=Ktok[bi][:, c, :], rhs=vnd[ri], start=True, stop=True),
                         reads=[("Ktok", bi), ("vnd", ri)], writes=[("ps", b4)])
                    if first:
                        P.op("dve", lambda e: e.tensor_copy(out=S_f, in_=ps[b4][:, 0:128]), writes=["S_f", ("ps", b4)])
                    else:
                        P.op("dve", lambda e: e.scalar_tensor_tensor(
                            out=S_f, in0=S_f, scalar=dl_t[:, n, h:h + 1], in1=ps[b4][:, 0:128], op0=ALU.mult, op1=ALU.add),
                            reads=["S_f", "dl_t"], writes=["S_f", ("ps", b4)])
                    P.op("act", lambda e: e.activation(out=S_b, in_=S_f, func=ACT.Copy), reads=["S_f"], writes=["S_b"])
                A(t_c)

            def t_epi():
                bo = hold["bo"]
                oi = st % 2
                P.op("act", lambda e: e.activation(out=oraw, in_=ps[bo][:, :], func=ACT.Copy), writes=["oraw", ("ps", bo)])
                if st == 0:
                    dump("khT0", khT[bi], [("khT", bi)])
                    dump("oraw0", oraw, ["oraw"])
                P.op("act", lambda e: e.activation(out=osq, in_=oraw, func=ACT.Square), reads=["oraw"], writes=["osq"])
                b = bank("pc")
                P.op("pe", lambda e: e.matmul(ps[b][:, :], lhsT=ones_bf, rhs=osq, start=True, stop=True),
                     reads=["osq", "ones_bf"], writes=[("ps", b)])
                rsqrt_from_psum(b, orst, "orst", 1.0 / 128)
                P.op("dve", lambda e: e.scalar_tensor_tensor(out=oraw, in0=oraw, scalar=dnw[:, 0:1], in1=orst,
                                                             op0=ALU.mult, op1=ALU.mult),
                     reads=["oraw", "orst", "dnw"], writes=["oraw"])
                P.op("dve", lambda e: e.scalar_tensor_tensor(out=oa[oi], in0=oraw, scalar=0.5, in1=zsT[bi],
                                                             op0=ALU.mult, op1=ALU.mult),
                     reads=["oraw", ("zsT", bi)], writes=[("oa", oi)])
                P.dma(lambda q: q.dma_start(out=oa_scr[h, :, t0:t0 + 512], in_=oa[oi]),
                      reads=[("oa", oi)], writes=[("oa_scr", h, tt8)])
            A(t_epi)
            return tasks

        load_head_w(0)
        NSTEP = 64
        for t in stage12_tasks(0):
            t()
        for st in range(NSTEP):
            t3 = stage3_tasks(st)
            t12 = stage12_tasks(st + 1) if st + 1 < NSTEP else []
            if not PIPELINE_A:
                for t in t3:
                    t()
                for t in t12:
                    t()
                continue
            n3, n12 = len(t3), len(t12)
            j = 0
            for i, t in enumerate(t3):
                t()
                tgt = (n12 * (i + 1) + n3 - 1) // n3
                while j < min(tgt, n12):
                    t12[j]()
                    j += 1
            while j < n12:
                t12[j]()
                j += 1
        cursor[0] = ph

    if stop_after not in ("a0", "b"):
        phase_a()
        P.fence()
        dump("oa_scr", oa_scr, [])

    def phase_c0():
        ph = cursor[0]
        sg = [alloc([512], BF16) for _ in range(4)]
        cnt = 0
        for c16 in range(16):
            off = (O_GA + c16 * 128) if c16 < 8 else (O_GB + (c16 - 8) * 128)
            wt, wk = load_w(w_in, off)
            for tt8 in range(8):
                b = proj_tile(wt, wk, 128, tt8, grp="all")
                i = cnt % 4
                cnt += 1
                P.op("act", lambda e, b=b, i=i: e.activation(out=sg[i], in_=ps[b][:, :], func=ACT.Sigmoid),
                     writes=[("sg", i), ("ps", b)])
                P.dma(lambda q, i=i, c16=c16, tt8=tt8: q.dma_start(out=sg_scr[c16, :, tt8 * 512:(tt8 + 1) * 512], in_=sg[i]),
                      reads=[("sg", i)], writes=[("sg_scr", tt8)])
        cursor[0] = ph

    def phase_c1():
        ph = cursor[0]
        wodn = alloc([8, 1024], BF16)
        wodil = alloc([4, 1024], BF16)
        wout = alloc([8, 1024], BF16)
        fnw_b = alloc([D], F32)
        for (dst, src, kc, key) in ((wodn, w_odn, 8, "wodn"), (wodil, w_odil, 4, "wodil"), (wout, w_out, 8, "wout")):
            for half in range(2):
                P.dma(lambda q, dst=dst, src=src, kc=kc, half=half: q.dma_start(
                    out=dst[:, 0:kc, half * 512:(half + 1) * 512],
                    in_=src[:, half * 512:(half + 1) * 512].rearrange("(k p) c -> p k c", p=128)),
                    writes=[(key, half)], q="pool")
        P.dma(lambda q: q.dma_start(out=fnw_b, in_=fnw_d.partition_broadcast(128)), writes=["fnw_b"])
        oat = [alloc([8, 512], BF16) for _ in range(2)]
        obt = [alloc([4, 512], BF16) for _ in range(2)]
        sgt = [alloc([16, 512], BF16) for _ in range(2)]
        mT = alloc([8, 512], BF16)
        m1 = [alloc([512], F32) for _ in range(2)]
        m2 = [alloc([512], F32) for _ in range(2)]
        xr = [alloc([D], F32) for _ in range(2)]
        xo = [alloc([D], F32) for _ in range(2)]
        yo = [alloc([D], F32) for _ in range(2)]
        ss = [alloc([1], F32) for _ in range(2)]
        rs = [alloc([1], F32) for _ in range(2)]
        junk2 = alloc([D], BF16)
        cnt = 0
        for tt8 in range(8):
            i = tt8 % 2
            sl = slice(tt8 * 512, (tt8 + 1) * 512)
            P.dma(lambda q, i=i, sl=sl: q.dma_start(out=oat[i], in_=oa_scr[:, :, sl].rearrange("h p t -> p h t")),
                  reads=[("oa_scr", hh, tt8) for hh in range(8)], writes=[("oat", i)])
            P.dma(lambda q, i=i, sl=sl: q.dma_start(out=obt[i], in_=ob_scr[:, :, sl].rearrange("h p t -> p h t")),
                  reads=[("ob_scr", hh, tt8) for hh in range(4)], writes=[("obt", i)])
            P.dma(lambda q, i=i, sl=sl: q.dma_start(out=sgt[i], in_=sg_scr[:, :, sl].rearrange("h p t -> p h t")),
                  reads=[("sg_scr", tt8)], writes=[("sgt", i)])
            for c in range(8):
                cs = slice(c * 128, (c + 1) * 128)
                ba = bank("all")
                for k in range(8):
                    P.op("pe", lambda e, b=ba, k=k, cs=cs, i=i: e.matmul(ps[b][:, :], lhsT=wodn[:, k, cs], rhs=oat[i][:, k, :],
                                                                         start=(k == 0), stop=(k == 7)),
                         reads=[("wodn", c // 4), ("oat", i)], writes=[("ps", ba)])
                bb = bank("all")
                for k in range(4):
                    P.op("pe", lambda e, b=bb, k=k, cs=cs, i=i: e.matmul(ps[b][:, :], lhsT=wodil[:, k, cs], rhs=obt[i][:, k, :],
                                                                         start=(k == 0), stop=(k == 3)),
                         reads=[("wodil", c // 4), ("obt", i)], writes=[("ps", bb)])
                mi = cnt % 2
                cnt += 1
                P.op("dve", lambda e, b=ba, mi=mi, i=i, c=c: e.tensor_tensor(out=m1[mi], in0=ps[b][:, :], in1=sgt[i][:, c, :], op=ALU.mult),
                     reads=[("sgt", i)], writes=[("m1", mi), ("ps", ba)])
                P.op("dve", lambda e, b=bb, mi=mi, i=i, c=c: e.tensor_tensor(out=m2[mi], in0=ps[b][:, :], in1=sgt[i][:, 8 + c, :], op=ALU.mult),
                     reads=[("sgt", i)], writes=[("m2", mi), ("ps", bb)])
                P.op("pool", lambda e, mi=mi, c=c: e.tensor_tensor(out=mT[:, c, :], in0=m1[mi], in1=m2[mi], op=ALU.add),
                     reads=[("m1", mi), ("m2", mi)], writes=[("mT", c)])
            for sub in range(4):
                tok0 = tt8 * 512 + sub * 128
                xi = sub % 2
                P.dma(lambda q, xi=xi, tok0=tok0: q.dma_start(out=xr[xi], in_=x_d[tok0:tok0 + 128, :]), writes=[("xr", xi)])
                for half in range(2):
                    b = bank("all")
                    hs = slice(half * 512, (half + 1) * 512)
                    for c in range(8):
                        P.op("pe", lambda e, b=b, c=c, sub=sub, hs=hs: e.matmul(
                            ps[b][:, :], lhsT=mT[:, c, sub * 128:(sub + 1) * 128], rhs=wout[:, c, hs], start=(c == 0), stop=(c == 7)),
                            reads=[("mT", c), ("wout", half)], writes=[("ps", b)])
                    P.op("dve", lambda e, b=b, xi=xi, hs=hs: e.tensor_tensor(out=xo[xi][:, hs], in0=ps[b][:, :], in1=xr[xi][:, hs], op=ALU.add),
                         reads=[("xr", xi)], writes=[("xo", xi, half), ("ps", b)])
                P.op("pool", lambda e, xi=xi: e.memset(ss[xi], 0.0), writes=[("ss", xi)])
                P.op("act", lambda e, xi=xi: e.activation(out=junk2, in_=xo[xi], func=ACT.Square, accum_out=ss[xi]),
                     reads=[("xo", xi, 0), ("xo", xi, 1), ("ss", xi)], writes=["junk2", ("ss", xi)])
                P.op("act", lambda e, xi=xi: e.activation(out=ss[xi], in_=ss[xi], func=ACT.Sqrt, scale=1.0 / D, bias=EPS),
                     reads=[("ss", xi)], writes=[("ss", xi)])
                P.op("dve", lambda e, xi=xi: e.reciprocal(out=rs[xi], in_=ss[xi]), reads=[("ss", xi)], writes=[("rs", xi)])
                P.op("dve", lambda e, xi=xi: e.scalar_tensor_tensor(out=yo[xi], in0=xo[xi], scalar=rs[xi], in1=fnw_b,
                                                                    op0=ALU.mult, op1=ALU.mult),
                     reads=[("xo", xi, 0), ("xo", xi, 1), ("rs", xi), "fnw_b"], writes=[("yo", xi)])
                P.dma(lambda q, xi=xi, tok0=tok0: q.dma_start(out=out_d[tok0:tok0 + 128, :], in_=yo[xi]),
                      reads=[("yo", xi)], writes=[("out", tok0)])
        cursor[0] = ph

    if stop_after is None:
        phase_c0()
        P.fence()
        dump("sg_scr", sg_scr, [])
        cursor[0] = 0
        phase_c1()

    P.emit(final_wait_ops=[o for o in P.dma_last if o is not None])
    return nc


def _consts():
    c = np.zeros((128, 128 + 12 * 256 + 256), np.float32)
    c[:, 0:128] = np.eye(128, dtype=np.float32)
    slopes = (2.0 ** (-8.0 * np.arange(1, 13, dtype=np.float32) / 12)).reshape(3, 4)
    jk = np.arange(128)[:, None]
    iq = np.arange(128)[None, :]
    for gi in range(3):
        for h in range(4):
            s = slopes[gi, h] * DIL[gi]
            d0 = (iq - jk).astype(np.float32)
            b0 = np.where(iq >= jk, -s * d0, NEG)
            d1 = (128 + iq - jk).astype(np.float32)
            b1 = np.where(iq <= jk, -s * d1, NEG)
            g = gi * 4 + h
            c[:, 128 + g * 256:128 + g * 256 + 128] = b0
            c[:, 128 + g * 256 + 128:128 + g * 256 + 256] = b1
    MK = 128 + 3072
    j = np.arange(64)[:, None]
    i = np.arange(64)[None, :]
    c[0:64, MK:MK + 64] = np.where(i >= j, 0.0, NEG)
    c[0:64, MK + 64:MK + 128] = np.where(i > j, 0.0, NEG)
    c[0:64, MK + 128:MK + 192] = (j <= i).astype(np.float32)
    c[0:64, MK + 192:MK + 256] = 1.0
    return c


_NC_CACHE = {}


def _host_inputs(x, norm_w, w_in, conv_w, a_log, dt_bias, dn_norm_w, w_o_dn, w_o_dil, w_out, final_norm_w):
    f = lambda a: np.ascontiguousarray(np.asarray(a, dtype=np.float32))
    cw = f(conv_w)[0].reshape(4, 24, 128).transpose(2, 1, 0).reshape(128, 96)
    shared = {
        "w_in": f(w_in)[0], "w_o_dn": f(w_o_dn)[0], "w_o_dil": f(w_o_dil)[0], "w_out": f(w_out)[0],
        "norm_w": f(norm_w).reshape(1, D), "final_norm_w": f(final_norm_w).reshape(1, D),
        "conv_w_l": np.ascontiguousarray(cw), "a_log": f(a_log).reshape(1, 8), "dt_bias": f(dt_bias).reshape(1, 8),
        "dn_norm_w_l": f(dn_norm_w).reshape(128, 1), "consts": _consts(),
    }
    xs = f(x)
    return [dict(shared, x=xs[b]) for b in range(xs.shape[0])]


def kernel(x, norm_w, w_in, conv_w, a_log, dt_bias, dn_norm_w, w_o_dn, w_o_dil, w_out, final_norm_w):
    in_maps = _host_inputs(x, norm_w, w_in, conv_w, a_log, dt_bias, dn_norm_w, w_o_dn, w_o_dil, w_out, final_norm_w)
    if "nc" not in _NC_CACHE:
        _NC_CACHE["nc"] = build_nc()
    res = run_bass_kernel_spmd(_NC_CACHE["nc"], in_maps, core_ids=list(range(len(in_maps))))
    return np.stack([np.asarray(r["out"], dtype=np.float32).reshape(T, D) for r in res.results], axis=0)
```

```python
import contextlib
import numpy as np
import concourse.bass as bass
import concourse.mybir as mybir
from concourse.bass_utils import run_bass_kernel_spmd

ACT = mybir.ActivationFunctionType
ALU = mybir.AluOpType
F32 = mybir.dt.float32
BF16 = mybir.dt.bfloat16

T = 4096
D = 1024
NEG = -30000.0
PIPELINE_A = True
EPS = 1e-6
O_QA, O_KA, O_VA, O_ZA, O_BA, O_QB, O_KB, O_VB, O_ZB, O_GA, O_GB = (
    0, 1024, 2048, 3072, 4096, 4112, 5648, 7184, 8720, 9232, 10256)
DIL = (1, 4, 16)


class _Op:
    __slots__ = ("eng", "fn", "deps", "signal", "ticket", "is_dma", "dsem", "dval")

    def __init__(self, eng, fn, is_dma=False):
        self.eng = eng
        self.fn = fn
        self.deps = []
        self.signal = False
        self.ticket = None
        self.is_dma = is_dma
        self.dsem = None
        self.dval = None


class _Res:
    __slots__ = ("w", "r", "rd")

    def __init__(self):
        self.w = None
        self.r = {}
        self.rd = []


class Prog:
    ENGS = ("pe", "act", "dve", "pool", "sp")

    def __init__(self, nc, n_dma_sems=32):
        self.nc = nc
        self.streams = {e: [] for e in self.ENGS}
        self.res = {}
        self.n_dma_sems = n_dma_sems
        self.dma_cnt = [0] * n_dma_sems
        self.dma_last = [None] * n_dma_sems
        self.dma_rr = 0
        self.fence_ops = []

    def fence(self):
        ops = []
        for e in self.ENGS:
            for o in reversed(self.streams[e]):
                if not o.is_dma:
                    ops.append(o)
                    break
        for d in self.dma_last:
            if d is not None:
                ops.append(d)
        self.fence_ops = ops

    def _r(self, k):
        r = self.res.get(k)
        if r is None:
            r = self.res[k] = _Res()
        return r

    def op(self, eng, fn, reads=(), writes=(), is_dma=False):
        o = _Op(eng, fn, is_dma)
        deps = [(d, "raw") for d in self.fence_ops]
        for k in reads:
            r = self._r(k)
            if r.w is not None:
                deps.append((r.w, "raw"))
        for k in writes:
            r = self._r(k)
            if r.w is not None:
                deps.append((r.w, "waw"))
            for rd in r.r.values():
                deps.append((rd, "war"))
            for rd in r.rd:
                deps.append((rd, "war"))
        if is_dma:
            i = self.dma_rr
            self.dma_rr = (i + 1) % self.n_dma_sems
            o.dsem = i
            self.dma_cnt[i] += 1
            o.dval = 16 * self.dma_cnt[i]
            if self.dma_last[i] is not None:
                deps.append((self.dma_last[i], "raw"))
            self.dma_last[i] = o
        seen = set()
        for d, kind in deps:
            if d is o or id(d) in seen:
                continue
            if not d.is_dma and d.eng == eng and not is_dma:
                if eng == "pe":
                    continue
                if kind != "raw":
                    continue
            seen.add(id(d))
            d.signal = True
            o.deps.append(d)
        for k in reads:
            r = self._r(k)
            if is_dma:
                r.rd.append(o)
            else:
                r.r[eng] = o
        for k in writes:
            r = self._r(k)
            r.w = o
            r.r = {}
            r.rd = []
        self.streams[eng].append(o)
        return o

    def dma(self, fn, reads=(), writes=(), q="sp"):
        return self.op(q, fn, reads, writes, is_dma=True)

    def emit(self, final_wait_ops=()):
        nc = self.nc
        for e in self.ENGS:
            c = 0
            for o in self.streams[e]:
                if o.is_dma:
                    continue
                if o.signal:
                    c += 1
                    o.ticket = c
        with contextlib.ExitStack() as es:
            esem = {e: es.enter_context(nc.semaphore("s_" + e)) for e in self.ENGS}
            dsem = [es.enter_context(nc.semaphore("d_%d" % i)) for i in range(self.n_dma_sems)]
            block = es.enter_context(nc.Block())

            def run(e, engobj):
                waited = {}

                def wait_for(d):
                    if d.is_dma:
                        key, sem, val = ("d", d.dsem), dsem[d.dsem], d.dval
                    else:
                        key, sem, val = ("e", d.eng), esem[d.eng], d.ticket
                    if waited.get(key, 0) >= val:
                        return
                    waited[key] = val
                    engobj.wait_ge(sem, val)

                for o in self.streams[e]:
                    for d in o.deps:
                        wait_for(d)
                    ins = o.fn(engobj)
                    if o.is_dma:
                        ins.then_inc(dsem[o.dsem], 16)
                    elif o.signal:
                        ins.then_inc(esem[e], 1)
                if e == "sp":
                    for d in final_wait_ops:
                        wait_for(d)

            @block.tensor
            def _(eng):
                run("pe", eng)

            @block.scalar
            def _(eng):
                run("act", eng)

            @block.vector
            def _(eng):
                run("dve", eng)

            @block.gpsimd
            def _(eng):
                run("pool", eng)

            @block.sync
            def _(eng):
                run("sp", eng)


def build_nc(dbg=None, stop_after=None):
    dbg = dbg or {}
    nc = bass.Bass("TRN2", target_bir_lowering=False)
    dt = nc.dram_tensor
    x_d = dt("x", [T, D], F32, kind="ExternalInput").ap()
    w_in = dt("w_in", [D, 11280], F32, kind="ExternalInput").ap()
    w_odn = dt("w_o_dn", [1024, 1024], F32, kind="ExternalInput").ap()
    w_odil = dt("w_o_dil", [512, 1024], F32, kind="ExternalInput").ap()
    w_out = dt("w_out", [1024, 1024], F32, kind="ExternalInput").ap()
    normw_d = dt("norm_w", [1, D], F32, kind="ExternalInput").ap()
    fnw_d = dt("final_norm_w", [1, D], F32, kind="ExternalInput").ap()
    cw_d = dt("conv_w_l", [128, 96], F32, kind="ExternalInput").ap()
    alog_d = dt("a_log", [1, 8], F32, kind="ExternalInput").ap()
    dtb_d = dt("dt_bias", [1, 8], F32, kind="ExternalInput").ap()
    dnw_d = dt("dn_norm_w_l", [128, 1], F32, kind="ExternalInput").ap()
    cst_d = dt("consts", [128, 128 + 12 * 256 + 64 * 4], F32, kind="ExternalInput").ap()
    out_d = dt("out", [T, D], F32, kind="ExternalOutput").ap()
    ob_scr = dt("ob_scr", [4, 128, T], BF16).ap()
    oa_scr = dt("oa_scr", [8, 128, T], BF16).ap()
    sg_scr = dt("sg_scr", [16, 128, T], BF16).ap()
    gc_scr = dt("gc_scr", [8, T], F32).ap()
    gcl_scr = dt("gcl_scr", [8, T], F32).ap()
    dbg_out = {}
    for name, (shape, dtype) in dbg.items():
        dbg_out[name] = dt("dbg_" + name, list(shape), dtype, kind="ExternalOutput").ap()

    P = Prog(nc)
    ARENA = 212000
    arena = nc.alloc_sbuf_tensor("arena", [128, ARENA // 2], BF16)
    cursor = [0]

    def alloc(free_shape, dtype, parts=128):
        n = 1
        for s in free_shape:
            n *= s
        esz = 4 if dtype == F32 else 2
        nbytes = (n * esz + 63) // 64 * 64
        off = cursor[0]
        cursor[0] += nbytes
        assert cursor[0] <= ARENA, ("SBUF overflow", cursor[0])
        ap = arena[0:parts, off // 2: off // 2 + n * esz // 2]
        if dtype == F32:
            ap = ap.bitcast(F32)
        if len(free_shape) == 2:
            ap = ap.rearrange("p (a b) -> p a b", b=free_shape[1])
        elif len(free_shape) == 3:
            ap = ap.rearrange("p (a b c) -> p a b c", b=free_shape[1], c=free_shape[2])
        return ap

    ps = [nc.alloc_psum_tensor("ps%d" % i, [128, 512], F32) for i in range(8)]
    psb = [p[:].bitcast(BF16) for p in ps]
    bank_rr = {}

    def bank(group):
        lst = {"proj": (0, 1), "misc": (2,), "pc": (0, 1, 2), "pre": (3, 4), "q1": (5,), "q4": (6,), "po": (7,),
               "all": tuple(range(8)), "s": (3, 4), "a": (5, 7), "b": (6, 2)}[group]
        i = bank_rr.get(group, 0)
        bank_rr[group] = i + 1
        return lst[i % len(lst)]

    def dump(name, src_ap, reads):
        if name in dbg_out:
            P.dma(lambda q, a=src_ap, o=dbg_out[name]: q.dma_start(out=o, in_=a), reads=reads, writes=[("dbg", name)])

    hT = alloc([8, T], BF16)
    cst = alloc([128 + 12 * 256 + 256], F32)
    identf = cst[:, 0:128]
    alibi = cst[:, 128:128 + 3072].rearrange("p (g w) -> p g w", w=256)
    MK = 128 + 3072
    mask_incl = cst[0:64, MK:MK + 64]
    mask_strict = cst[0:64, MK + 64:MK + 128]
    triu_f = cst[0:64, MK + 128:MK + 192]
    ones64f = cst[0:64, MK + 192:MK + 256]
    ident_bf = alloc([128], BF16)
    ones_bf = alloc([128], BF16)
    ones_f = alloc([128], F32)
    cw = alloc([96], F32)
    dnw = alloc([1], F32)
    WB_N = 4
    wb = [alloc([8, 128], BF16) for _ in range(WB_N)]
    wb_rr = [0]
    beta_t = alloc([64, 8], F32, 64)
    gc_t = alloc([64, 8], F32, 64)
    negeg_t = alloc([64, 8], F32, 64)
    wdec_t = alloc([64, 8], F32, 64)
    dl_t = alloc([64, 8], F32)
    persist_end = cursor[0]

    P.dma(lambda q: q.dma_start(out=cst, in_=cst_d), writes=["cst"])
    P.dma(lambda q: q.dma_start(out=cw, in_=cw_d), writes=["cw"])
    P.dma(lambda q: q.dma_start(out=dnw, in_=dnw_d), writes=["dnw"])
    P.op("dve", lambda e: e.tensor_copy(out=ident_bf, in_=identf), reads=["cst"], writes=["ident_bf"])
    P.op("pool", lambda e: e.memset(ones_bf, 1.0), writes=["ones_bf"])
    P.op("pool", lambda e: e.memset(ones_f, 1.0), writes=["ones_f"])

    def load_w(src, c0, ncols=128, kchunks=8, dst=None, key=None):
        if dst is None:
            i = wb_rr[0] % WB_N
            wb_rr[0] += 1
            dst, key = wb[i], ("wb", i)
        P.dma(lambda q, d=dst, s=src, c0=c0, n=ncols, kc=kchunks: q.dma_start(
            out=d[:, 0:kc, 0:n], in_=s[:, c0:c0 + n].rearrange("(k p) c -> p k c", p=128)),
            writes=[key], q="pool")
        return dst, key

    def hkeys(t0, t1):
        return [("hT", i) for i in range(t0 // 128, (t1 + 127) // 128)]

    def proj_tile(wt, wkey, ncols, tt8, grp="proj"):
        b = bank(grp)
        for k in range(8):
            P.op("pe", lambda e, b=b, k=k, wt=wt, n=ncols, tt8=tt8: e.matmul(
                ps[b][0:n, :], lhsT=wt[:, k, 0:n], rhs=hT[:, k, tt8 * 512:(tt8 + 1) * 512],
                start=(k == 0), stop=(k == 7)),
                reads=[wkey] + hkeys(tt8 * 512, tt8 * 512 + 512), writes=[("ps", b)])
        return b

    ph = cursor[0]
    normw_b = alloc([D], F32)
    xs = [alloc([D], F32) for _ in range(2)]
    junk = alloc([D], BF16)
    xb = [alloc([D], BF16) for _ in range(2)]
    ss0 = [alloc([1], F32) for _ in range(2)]
    rs0 = [alloc([1], F32) for _ in range(2)]
    P.dma(lambda q: q.dma_start(out=normw_b, in_=normw_d.partition_broadcast(128)), writes=["normw_b"])
    for tt in range(32):
        i = tt % 2
        P.dma(lambda q, i=i, tt=tt: q.dma_start(out=xs[i], in_=x_d[tt * 128:(tt + 1) * 128, :]), writes=[("xs", i)])
        P.op("pool", lambda e, i=i: e.memset(ss0[i], 0.0), writes=[("ss0", i)])
        P.op("act", lambda e, i=i: e.activation(out=junk, in_=xs[i], func=ACT.Square, accum_out=ss0[i]),
             reads=[("xs", i), ("ss0", i)], writes=["junk", ("ss0", i)])
        P.op("act", lambda e, i=i: e.activation(out=ss0[i], in_=ss0[i], func=ACT.Sqrt, scale=1.0 / D, bias=EPS),
             reads=[("ss0", i)], writes=[("ss0", i)])
        P.op("dve", lambda e, i=i: e.reciprocal(out=rs0[i], in_=ss0[i]), reads=[("ss0", i)], writes=[("rs0", i)])
        P.op("dve", lambda e, i=i: e.scalar_tensor_tensor(out=xb[i], in0=xs[i], scalar=rs0[i], in1=normw_b,
                                                          op0=ALU.mult, op1=ALU.mult),
             reads=[("xs", i), ("rs0", i), "normw_b"], writes=[("xb", i)])
        b = bank("all")
        for k in range(8):
            P.op("pe", lambda e, b=b, k=k, i=i: e.transpose(out=psb[b][:, k * 128:(k + 1) * 128],
                                                           in_=xb[i][:, k * 128:(k + 1) * 128], identity=ident_bf),
                 reads=[("xb", i), "ident_bf"], writes=[("ps", b)])
        eng = "act" if tt % 2 == 0 else "dve"
        if eng == "act":
            fn = lambda e, b=b, tt=tt: e.activation(out=hT[:, :, tt * 128:(tt + 1) * 128],
                                                    in_=psb[b].rearrange("p (k t) -> p k t", t=128), func=ACT.Copy)
        else:
            fn = lambda e, b=b, tt=tt: e.tensor_copy(out=hT[:, :, tt * 128:(tt + 1) * 128],
                                                     in_=psb[b].rearrange("p (k t) -> p k t", t=128))
        P.op(eng, fn, writes=[("hT", tt), ("ps", b)])
    dump("hT", hT, hkeys(0, T))
    cursor[0] = ph
    P.fence()

    def phase_a0():
        ph = cursor[0]
        w16, w16k = load_w(w_in, O_BA, 16)
        alog_b = alloc([8], F32, 64)
        dtb_b = alloc([8], F32, 64)
        P.dma(lambda q: q.dma_start(out=alog_b, in_=alog_d.partition_broadcast(64)), writes=["alog_b"])
        P.dma(lambda q: q.dma_start(out=dtb_b, in_=dtb_d.partition_broadcast(64)), writes=["dtb_b"])
        Gsb = alloc([64, 16], F32, 64)
        names = ["xa", "ax", "ee", "ll", "sp", "g", "lb", "gcl", "tmp"]
        A = {n: alloc([64, 8], F32, 64) for n in names}
        glast = alloc([64, 8], F32)
        tb = alloc([8, 64], F32, 64)
        for half in range(2):
            b = bank("all")
            for n in range(32 * half, 32 * half + 32):
                for k in range(8):
                    P.op("pe", lambda e, b=b, n=n, k=k: e.matmul(
                        ps[b][0:64, (n % 32) * 16:(n % 32) * 16 + 16], lhsT=hT[:, k, n * 64:(n + 1) * 64],
                        rhs=w16[:, k, 0:16], start=(k == 0), stop=(k == 7)),
                        reads=[w16k] + hkeys(n * 64, n * 64 + 64), writes=[("ps", b)])
            P.op("act", lambda e, b=b, half=half: e.activation(
                out=Gsb[:, 32 * half:32 * half + 32, :], in_=ps[b][0:64, :].rearrange("p (n c) -> p n c", c=16),
                func=ACT.Copy), writes=["Gsb", ("ps", b)])
        bb = Gsb[:, :, 0:8]
        aa = Gsb[:, :, 8:16]
        bc = lambda v: v.unsqueeze(1).to_broadcast([64, 64, 8])
        P.op("act", lambda e: e.activation(out=beta_t, in_=bb, func=ACT.Sigmoid), reads=["Gsb"], writes=["beta_t"])
        P.op("act", lambda e: e.activation(out=A["lb"], in_=beta_t, func=ACT.Ln), reads=["beta_t"], writes=["lb"])
        P.op("dve", lambda e: e.tensor_tensor(out=A["xa"], in0=aa, in1=bc(dtb_b), op=ALU.add),
             reads=["Gsb", "dtb_b"], writes=["xa"])
        P.op("act", lambda e: e.activation(out=A["ax"], in_=A["xa"], func=ACT.Abs), reads=["xa"], writes=["ax"])
        P.op("act", lambda e: e.activation(out=A["ee"], in_=A["ax"], func=ACT.Exp, scale=-1.0), reads=["ax"], writes=["ee"])
        P.op("act", lambda e: e.activation(out=A["ll"], in_=A["ee"], func=ACT.Ln, bias=1.0), reads=["ee"], writes=["ll"])
        P.op("dve", lambda e: e.scalar_tensor_tensor(out=A["sp"], in0=A["xa"], scalar=0.0, in1=A["ll"],
                                                     op0=ALU.max, op1=ALU.add), reads=["xa", "ll"], writes=["sp"])
        P.op("act", lambda e: e.activation(out=alog_b, in_=alog_b, func=ACT.Exp), reads=["alog_b"], writes=["alog_b"])
        P.op("dve", lambda e: e.scalar_tensor_tensor(out=A["g"], in0=A["sp"], scalar=-1.0, in1=bc(alog_b),
                                                     op0=ALU.mult, op1=ALU.mult), reads=["sp", "alog_b"], writes=["g"])
        gflat = A["g"].rearrange("p n h -> p (n h)")
        b1 = bank("all")
        P.op("pe", lambda e: e.matmul(ps[b1][0:64, :], lhsT=triu_f, rhs=gflat, start=True, stop=True),
             reads=["g", "cst"], writes=[("ps", b1)])
        b2 = bank("all")
        P.op("pe", lambda e: e.matmul(ps[b2][:, :], lhsT=ones_f[0:64, :], rhs=gflat, start=True, stop=True),
             reads=["g", "ones_f"], writes=[("ps", b2)])
        fl = lambda v: v.rearrange("p n h -> p (n h)")
        P.op("act", lambda e: e.activation(out=fl(gc_t), in_=ps[b1][0:64, :], func=ACT.Copy), writes=["gc_t", ("ps", b1)])
        P.op("dve", lambda e: e.tensor_copy(out=fl(glast), in_=ps[b2][:, :]), writes=["glast", ("ps", b2)])
        P.op("act", lambda e: e.activation(out=negeg_t, in_=gc_t, func=ACT.Exp), reads=["gc_t"], writes=["negeg_t"])
        P.op("dve", lambda e: e.tensor_scalar(out=negeg_t, in0=negeg_t, scalar1=-1.0, scalar2=None, op0=ALU.mult),
             reads=["negeg_t"], writes=["negeg_t"])
        P.op("dve", lambda e: e.tensor_tensor(out=A["tmp"], in0=glast[0:64], in1=gc_t, op=ALU.subtract),
             reads=["glast", "gc_t"], writes=["tmp"])
        P.op("act", lambda e: e.activation(out=wdec_t, in_=A["tmp"], func=ACT.Exp), reads=["tmp"], writes=["wdec_t"])
        P.op("act", lambda e: e.activation(out=dl_t, in_=glast, func=ACT.Exp), reads=["glast"], writes=["dl_t"])
        P.op("dve", lambda e: e.tensor_tensor(out=A["gcl"], in0=gc_t, in1=A["lb"], op=ALU.add),
             reads=["gc_t", "lb"], writes=["gcl"])
        for nm, src, scr in (("gc", gc_t, gc_scr), ("gcl", A["gcl"], gcl_scr)):
            b = bank("all")
            for h in range(8):
                P.op("pe", lambda e, b=b, h=h, src=src: e.transpose(out=ps[b][0:64, h * 64:(h + 1) * 64],
                                                                   in_=src[:, :, h], identity=identf[0:64, 0:64]),
                     reads=["gc_t" if nm == "gc" else "gcl", "cst"], writes=[("ps", b)])
            P.op("dve", lambda e, b=b: e.tensor_copy(out=tb, in_=ps[b][0:64, :].rearrange("p (h c) -> p h c", c=64)),
                 writes=["tb", ("ps", b)])
            P.dma(lambda q, scr=scr: q.dma_start(out=scr.rearrange("h (n c) -> n h c", c=64), in_=tb),
                  reads=["tb"], writes=[nm + "_scr"])
        dump("gc_t", gc_t, ["gc_t"])
        dump("beta_t", beta_t, ["beta_t"])
        dump("g_t", A["g"], ["g"])
        cursor[0] = ph

    phase_a0()
    P.fence()

    def interleave(ta, tb):
        na, nb_ = len(ta), len(tb)
        j = 0
        for i, t in enumerate(ta):
            t()
            tgt = (nb_ * (i + 1) + na - 1) // max(na, 1)
            while j < min(tgt, nb_):
                tb[j]()
                j += 1
        while j < nb_:
            tb[j]()
            j += 1

    def phase_b():
        ph = cursor[0]
        qTs = [alloc([T], BF16) for _ in range(2)]
        kTs = [alloc([T], BF16) for _ in range(2)]
        vsbs = [alloc([32, 128], BF16) for _ in range(2)]
        vT = alloc([T], BF16)
        acc_n = alloc([T], F32)
        acc_d = alloc([T], F32)
        NS = 3
        s_sb = [alloc([256], F32) for _ in range(NS)]
        p_sb = [alloc([256], BF16) for _ in range(4)]
        zs = [alloc([512], F32) for _ in range(2)]
        rc = [alloc([512], F32) for _ in range(2)]
        ob = [alloc([512], BF16) for _ in range(2)]
        scale = 128 ** -0.5
        jobs = [(h, gi) for h in range(4) for gi in range(3)]
        cnt = [0]

        def proj_tasks(jn):
            h, gi = jobs[jn]
            bi = jn % 2
            qT, kT, v_sb = qTs[bi], kTs[bi], vsbs[bi]
            d = DIL[gi]
            L = T // d
            nb = L // 128
            M = 512 // d
            tasks = []
            hold = {}

            def t_w():
                hold["q"] = load_w(w_in, O_QB + gi * 512 + h * 128)
                hold["k"] = load_w(w_in, O_KB + gi * 512 + h * 128)
                hold["v"] = load_w(w_in, O_VB + gi * 512 + h * 128)
            tasks.append(t_w)
            q3 = qT.rearrange("p (r m) -> p r m", r=d)
            k3 = kT.rearrange("p (r m) -> p r m", r=d)
            for tt8 in range(8):
                def t_q(tt8=tt8):
                    wq, wqk = hold["q"]
                    b = proj_tile(wq, wqk, 128, tt8)
                    P.op("act", lambda e: e.activation(out=qT[:, tt8 * 512:(tt8 + 1) * 512], in_=ps[b][:, :],
                                                       func=ACT.Copy, scale=scale),
                         writes=[("qT", bi), ("ps", b)])
                tasks.append(t_q)

                def t_k(tt8=tt8):
                    wk, wkk = hold["k"]
                    b = proj_tile(wk, wkk, 128, tt8)
                    P.op("dve", lambda e: e.tensor_copy(out=kT[:, tt8 * 512:(tt8 + 1) * 512], in_=ps[b][:, :]),
                         writes=[("kT", bi), ("ps", b)])
                tasks.append(t_k)

                def t_v(tt8=tt8):
                    wv, wvk = hold["v"]
                    b = proj_tile(wv, wvk, 128, tt8)
                    P.op("act", lambda e: e.activation(out=vT[:, tt8 * 512:(tt8 + 1) * 512], in_=ps[b][:, :], func=ACT.Copy),
                         writes=[("vT", tt8), ("ps", b)])
                tasks.append(t_v)
            for t8 in range(4):
                def t_vt(t8=t8):
                    b = bank("proj")
                    for s in range(8):
                        tid = t8 * 8 + s
                        r, j = tid // nb, tid % nb
                        t0 = 128 * j * d + r
                        P.op("pe", lambda e, s=s, t0=t0: e.transpose(
                            out=psb[b][:, s * 128:(s + 1) * 128], in_=vT[:, t0:t0 + 127 * d + 1:d], identity=ident_bf),
                            reads=[("vT", i) for i in range((128 * j * d) // 512, (128 * (j + 1) * d + 511) // 512)] + ["ident_bf"],
                            writes=[("ps", b)])
                    P.op("dve", lambda e: e.tensor_copy(
                        out=v_sb[:, t8 * 8:(t8 + 1) * 8, :], in_=psb[b][:, :].rearrange("p (s c) -> p s c", c=128)),
                        writes=[("v_sb", bi), ("ps", b)])
                tasks.append(t_vt)
            return tasks

        def core_tasks(jn):
            h, gi = jobs[jn]
            bi = jn % 2
            qT, kT, v_sb = qTs[bi], kTs[bi], vsbs[bi]
            d = DIL[gi]
            L = T // d
            nb = L // 128
            gidx = gi * 4 + h
            tasks = []
            st = {"bn": None, "bd": None}
            pis = {}

            def tok(r, j0, nblk):
                a = (128 * j0) * d + r
                return slice(a, a + (128 * nblk - 1) * d + 1, d)

            def t_qk(r, j):
                W = 2 if j + 1 < nb else 1
                bs = bank("s")
                P.op("pe", lambda e: e.matmul(ps[bs][:, 0:128 * W], lhsT=kT[:, tok(r, j, 1)], rhs=qT[:, tok(r, j, W)],
                                              start=True, stop=True),
                     reads=[("qT", bi), ("kT", bi)], writes=[("ps", bs)])
                si = cnt[0] % NS
                pi = cnt[0] % 4
                cnt[0] += 1
                pis[(r, j)] = pi
                P.op("dve", lambda e: e.tensor_tensor(out=s_sb[si][:, 0:128 * W], in0=ps[bs][:, 0:128 * W],
                                                      in1=alibi[:, gidx, 0:128 * W], op=ALU.add),
                     reads=["cst"], writes=[("s_sb", si), ("ps", bs)])
                P.op("act", lambda e: e.activation(out=p_sb[pi][:, 0:128 * W], in_=s_sb[si][:, 0:128 * W], func=ACT.Exp),
                     reads=[("s_sb", si)], writes=[("p_sb", pi)])

            def t_pv(r, j):
                if j % 4 == 0:
                    st["bn"], st["bd"] = bank("a"), bank("b")
                bn, bd = st["bn"], st["bd"]
                pi = pis[(r, j)]
                prev = pis[(r, j - 1)] if j > 0 else None
                col = (j % 4) * 128
                tid = r * nb + j
                for (bk, is_den, lk) in ((bn, False, ("v_sb", bi)), (bd, True, "ones_bf")):
                    first = True
                    if j > 0:
                        lp = ones_bf if is_den else v_sb[:, tid - 1, :]
                        P.op("pe", lambda e, bk=bk, lp=lp: e.matmul(
                            ps[bk][:, col:col + 128], lhsT=lp, rhs=p_sb[prev][:, 128:256], start=True, stop=False),
                            reads=[lk, ("p_sb", prev)], writes=[("ps", bk)])
                        first = False
                    lc = ones_bf if is_den else v_sb[:, tid, :]
                    P.op("pe", lambda e, bk=bk, lc=lc, first=first: e.matmul(
                        ps[bk][:, col:col + 128], lhsT=lc, rhs=p_sb[pi][:, 0:128], start=first, stop=True),
                        reads=[lk, ("p_sb", pi)], writes=[("ps", bk)])
                if j % 4 == 3 or j == nb - 1:
                    n0 = (j // 4) * 4
                    nq = (j - n0 + 1) * 128
                    sl = slice(128 * n0 * d + r, 128 * n0 * d + r + (nq - 1) * d + 1, d)
                    for (bk, acc, key, eng) in ((bn, acc_n, "acc_n", "act"), (bd, acc_d, "acc_d", "dve")):
                        if gi == 0:
                            if eng == "act":
                                P.op("act", lambda e, bk=bk, acc=acc: e.activation(
                                    out=acc[:, sl], in_=ps[bk][:, 0:nq], func=ACT.Copy), writes=[key, ("ps", bk)])
                            else:
                                P.op("dve", lambda e, bk=bk, acc=acc: e.tensor_copy(
                                    out=acc[:, sl], in_=ps[bk][:, 0:nq]), writes=[key, ("ps", bk)])
                        else:
                            P.op("dve", lambda e, bk=bk, acc=acc: e.tensor_tensor(
                                out=acc[:, sl], in0=ps[bk][:, 0:nq], in1=acc[:, sl], op=ALU.add),
                                reads=[key], writes=[key, ("ps", bk)])

            seq = [(r, j) for r in range(d) for j in range(nb)]
            SK = 2
            for idx in range(len(seq) + SK):
                def t_blk(idx=idx):
                    if idx < len(seq):
                        t_qk(*seq[idx])
                    if idx >= SK:
                        t_pv(*seq[idx - SK])
                tasks.append(t_blk)
            if gi == 2:
                hold = {}

                def t_wz():
                    hold["z"] = load_w(w_in, O_ZB + h * 128)
                tasks.append(t_wz)
                for tt8 in range(8):
                    def t_fin(tt8=tt8):
                        wz, wzk = hold["z"]
                        i = tt8 % 2
                        sl = slice(tt8 * 512, (tt8 + 1) * 512)
                        b = proj_tile(wz, wzk, 128, tt8)
                        P.op("act", lambda e: e.activation(out=zs[i], in_=ps[b][:, :], func=ACT.Tanh, scale=0.5),
                             writes=[("zs", i), ("ps", b)])
                        P.op("dve", lambda e: e.scalar_tensor_tensor(out=zs[i], in0=zs[i], scalar=1.0, in1=ps[b][:, :],
                                                                     op0=ALU.add, op1=ALU.mult),
                             reads=[("zs", i)], writes=[("zs", i), ("ps", b)])
                        P.op("act", lambda e: e.activation(out=rc[i], in_=acc_d[:, sl], func=ACT.Ln), reads=["acc_d"], writes=[("rc", i)])
                        P.op("act", lambda e: e.activation(out=rc[i], in_=rc[i], func=ACT.Exp, scale=-1.0), reads=[("rc", i)], writes=[("rc", i)])
                        P.op("dve", lambda e: e.tensor_tensor(out=rc[i], in0=rc[i], in1=acc_n[:, sl], op=ALU.mult),
                             reads=["acc_n", ("rc", i)], writes=[("rc", i)])
                        P.op("dve", lambda e: e.scalar_tensor_tensor(out=ob[i], in0=rc[i], scalar=0.5, in1=zs[i],
                                                                     op0=ALU.mult, op1=ALU.mult),
                             reads=[("rc", i), ("zs", i)], writes=[("ob", i)])
                        P.dma(lambda q: q.dma_start(out=ob_scr[h, :, sl], in_=ob[i]),
                              reads=[("ob", i)], writes=[("ob_scr", h, tt8)])
                    tasks.append(t_fin)
            return tasks

        for t in proj_tasks(0):
            t()
        for jn in range(len(jobs)):
            interleave(core_tasks(jn), proj_tasks(jn + 1) if jn + 1 < len(jobs) else [])
        cursor[0] = ph

    if stop_after != "a0":
        phase_b()
        P.fence()
        dump("ob_scr", ob_scr, [])

    def phase_a():
        ph = cursor[0]
        NB2 = 2
        upre = {t: [alloc([516], BF16) for _ in range(2)] for t in "qkv"}
        for t in "qkv":
            P.op("pool", lambda e, t=t: e.memset(upre[t][1][:, 512:515], 0.0), writes=[("upre", t, 1)])
        dg = alloc([12, 128], BF16)
        th = {t: alloc([512], F32) for t in "qkvz"}
        yq = alloc([512], F32)
        yk = alloc([512], F32)
        sq = {t: alloc([512], BF16) for t in "qk"}
        rin = {t: alloc([512], F32) for t in "qk"}
        vTt = alloc([512], BF16)
        khT = [alloc([512], BF16) for _ in range(NB2)]
        qhT = [alloc([512], BF16) for _ in range(NB2)]
        qgT = [alloc([512], BF16) for _ in range(NB2)]
        zsT = [alloc([512], BF16) for _ in range(NB2)]
        Ktok = [alloc([8, 128], BF16, 64) for _ in range(NB2)]
        Vtok = [alloc([8, 128], BF16, 64) for _ in range(NB2)]
        AqkT = [alloc([8, 64], BF16, 64) for _ in range(NB2)]
        TT = [alloc([8, 64], BF16, 64) for _ in range(NB2)]
        gcB = [alloc([512], F32) for _ in range(2)]
        gclB = [alloc([512], F32, 64) for _ in range(2)]
        egB = alloc([512], F32)
        E1 = alloc([8, 64], F32, 64)
        E2 = alloc([8, 64], F32, 64)
        GT = alloc([8, 64], F32, 64)
        GTb = alloc([8, 64], F32, 64)
        Pk = [alloc([8, 64], BF16, 64) for _ in range(2)]
        PkT = [alloc([8, 64], BF16, 64) for _ in range(2)]
        Xb = [alloc([8, 64], BF16, 64) for _ in range(2)]
        S_f = alloc([128], F32)
        S_b = alloc([128], BF16)
        Rp = [alloc([128], BF16, 64) for _ in range(2)]
        vnew = [alloc([128], BF16, 64) for _ in range(2)]
        vnd = [alloc([128], BF16, 64) for _ in range(2)]
        oraw = alloc([512], F32)
        osq = alloc([512], BF16)
        orst = alloc([512], F32)
        oa = [alloc([512], BF16) for _ in range(2)]
        wsets = [[alloc([8, 128], BF16) for _ in range(4)] for _ in range(2)]

        def load_head_w(h):
            ws = wsets[h % 2]
            for ti, off in enumerate((O_QA, O_KA, O_VA, O_ZA)):
                load_w(w_in, off + h * 128, dst=ws[ti], key=("wh", h % 2, ti))

        m8 = lambda m: m.unsqueeze(1).to_broadcast([64, 8, 64])
        v3 = lambda a: a.rearrange("p (c i) -> p c i", i=64)
        fl = lambda a: a.rearrange("p c i -> p (c i)")

        def rsqrt_from_psum(b, dst, key, scale):
            P.op("act", lambda e: e.activation(out=dst, in_=ps[b][:, :], func=ACT.Ln, scale=scale, bias=EPS),
                 writes=[key, ("ps", b)])
            P.op("act", lambda e: e.activation(out=dst, in_=dst, func=ACT.Exp, scale=-0.5), reads=[key], writes=[key])

        def stage12_tasks(st):
            h, tt8 = st // 8, st % 8
            bi = st % NB2
            ui = st % 2
            t0 = tt8 * 512
            ws = wsets[h % 2]
            tasks = []
            A = tasks.append

            def t_pre():
                if tt8 == 0:
                    if h + 1 < 8:
                        load_head_w(h + 1)
                    for ti in range(3):
                        for kk in range(4):
                            g = ti * 8 + h
                            P.op("dve", lambda e, ti=ti, kk=kk, g=g: e.tensor_scalar(
                                out=dg[:, ti * 4 + kk, :], in0=identf, scalar1=cw[:, g * 4 + kk:g * 4 + kk + 1], scalar2=None,
                                op0=ALU.mult), reads=["cst", "cw"], writes=["dg"])
                P.dma(lambda q: q.dma_start(out=gcB[ui], in_=gc_scr[h, t0:t0 + 512].partition_broadcast(128)),
                      reads=["gc_scr"], writes=[("gcB", ui)])
                P.dma(lambda q: q.dma_start(out=gclB[ui], in_=gcl_scr[h, t0:t0 + 512].partition_broadcast(64)),
                      reads=["gcl_scr"], writes=[("gclB", ui)])
            A(t_pre)

            def t_projz():
                b = proj_tile(ws[3], ("wh", h % 2, 3), 128, tt8, grp="pc")
                P.op("act", lambda e: e.activation(out=th["z"], in_=ps[b][:, :], func=ACT.Tanh, scale=0.5),
                     writes=[("th", "z"), ("ps", b)])
                P.op("dve", lambda e: e.scalar_tensor_tensor(out=zsT[bi], in0=th["z"], scalar=1.0, in1=ps[b][:, :],
                                                             op0=ALU.add, op1=ALU.mult),
                     reads=[("th", "z")], writes=[("zsT", bi), ("ps", b)])
            A(t_projz)

            for ti, t in enumerate("qkv"):
                def t_proj(ti=ti, t=t):
                    u = upre[t][ui]
                    up = upre[t][1 - ui]
                    b = proj_tile(ws[ti], ("wh", h % 2, ti), 128, tt8, grp="pc")
                    if tt8 == 0:
                        P.op("pool", lambda e: e.memset(u[:, 0:3], 0.0), writes=[("upre", t, ui)])
                    else:
                        P.op("pool", lambda e: e.tensor_copy(out=u[:, 0:3], in_=up[:, 512:515]),
                             reads=[("upre", t, 1 - ui)], writes=[("upre", t, ui)])
                    P.op("act", lambda e: e.activation(out=u[:, 3:515], in_=ps[b][:, :], func=ACT.Copy),
                         writes=[("upre", t, ui), ("ps", b)])
                A(t_proj)

            for ti, t in enumerate("qkv"):
                def t_conv(ti=ti, t=t):
                    u = upre[t][ui]
                    b = bank("pc")
                    for kk in range(4):
                        P.op("pe", lambda e, kk=kk: e.matmul(ps[b][:, :], lhsT=dg[:, ti * 4 + kk, :], rhs=u[:, kk:kk + 512],
                                                             start=(kk == 0), stop=(kk == 3)),
                             reads=["dg", ("upre", t, ui)], writes=[("ps", b)])
                    P.op("act", lambda e: e.activation(out=th[t], in_=ps[b][:, :], func=ACT.Tanh, scale=0.5),
                         writes=[("th", t), ("ps", b)])
                    dst = {"q": yq, "k": yk, "v": vTt}[t]
                    P.op("dve", lambda e: e.scalar_tensor_tensor(out=dst, in0=th[t], scalar=1.0, in1=ps[b][:, :],
                                                                 op0=ALU.add, op1=ALU.mult),
                         reads=[("th", t)], writes=[("y", t), ("ps", b)])
                A(t_conv)

            for t, y in (("q", yq), ("k", yk)):
                def t_norm(t=t, y=y):
                    P.op("act", lambda e: e.activation(out=sq[t], in_=y, func=ACT.Square), reads=[("y", t)], writes=[("sq", t)])
                    b = bank("pc")
                    P.op("pe", lambda e: e.matmul(ps[b][:, :], lhsT=ones_bf, rhs=sq[t], start=True, stop=True),
                         reads=[("sq", t), "ones_bf"], writes=[("ps", b)])
                    rsqrt_from_psum(b, rin[t], ("rin", t), 0.25)
                A(t_norm)

            def t_hat():
                P.op("dve", lambda e: e.scalar_tensor_tensor(out=khT[bi], in0=yk, scalar=0.5, in1=rin["k"],
                                                             op0=ALU.mult, op1=ALU.mult),
                     reads=[("y", "k"), ("rin", "k")], writes=[("khT", bi)])
                P.op("dve", lambda e: e.scalar_tensor_tensor(out=qhT[bi], in0=yq, scalar=0.5 * 128 ** -0.5, in1=rin["q"],
                                                             op0=ALU.mult, op1=ALU.mult),
                     reads=[("y", "q"), ("rin", "q")], writes=[("qhT", bi)])
                P.op("act", lambda e: e.activation(out=egB, in_=gcB[ui], func=ACT.Exp), reads=[("gcB", ui)], writes=["egB"])
                P.op("pool", lambda e: e.tensor_tensor(out=qgT[bi], in0=qhT[bi], in1=egB, op=ALU.mult),
                     reads=[("qhT", bi), "egB"], writes=[("qgT", bi)])
            A(t_hat)

            for (src, skey, dst, dkey, sc) in ((khT[bi], ("khT", bi), Ktok[bi], ("Ktok", bi), 1.0),
                                               (vTt, ("y", "v"), Vtok[bi], ("Vtok", bi), 0.5)):
                def t_tok(src=src, skey=skey, dst=dst, dkey=dkey, sc=sc):
                    b = bank("pc")
                    for c in range(8):
                        P.op("pe", lambda e, c=c: e.transpose(out=psb[b][0:64, c * 128:(c + 1) * 128],
                                                              in_=src[:, c * 64:(c + 1) * 64], identity=ident_bf),
                             reads=[skey, "ident_bf"], writes=[("ps", b)])
                    P.op("act", lambda e: e.activation(out=dst, in_=psb[b][0:64, :].rearrange("p (c k) -> p c k", k=128),
                                                       func=ACT.Copy, scale=sc), writes=[dkey, ("ps", b)])
                A(t_tok)

            n0 = tt8 * 8
            gcJ = gc_t[:, n0:n0 + 8, h].unsqueeze(2).to_broadcast([64, 8, 64])
            bJ = beta_t[:, n0:n0 + 8, h].unsqueeze(2).to_broadcast([64, 8, 64])
            hold = {}

            def t_gates():
                P.op("pool", lambda e: e.tensor_tensor(out=E1, in0=v3(gcB[ui][0:64, :]), in1=gcJ, op=ALU.subtract),
                     reads=[("gcB", ui), "gc_t"], writes=["E1"])
                P.op("pool", lambda e: e.tensor_tensor(out=E1, in0=E1, in1=m8(mask_incl), op=ALU.add),
                     reads=["E1", "cst"], writes=["E1"])
                P.op("act", lambda e: e.activation(out=GT, in_=E1, func=ACT.Exp), reads=["E1"], writes=["GT"])
                P.op("pool", lambda e: e.tensor_tensor(out=E2, in0=v3(gclB[ui]), in1=gcJ, op=ALU.subtract),
                     reads=[("gclB", ui), "gc_t"], writes=["E2"])
                P.op("pool", lambda e: e.tensor_tensor(out=E2, in0=E2, in1=m8(mask_strict), op=ALU.add),
                     reads=["E2", "cst"], writes=["E2"])
                P.op("act", lambda e: e.activation(out=GTb, in_=E2, func=ACT.Exp), reads=["E2"], writes=["GTb"])
            A(t_gates)

            def t_kkqk():
                bkk, bqk = bank("pre"), bank("pre")
                for c in range(8):
                    cs = slice(c * 64, (c + 1) * 64)
                    P.op("pe", lambda e, cs=cs: e.matmul(ps[bkk][0:64, cs], lhsT=khT[bi][:, cs], rhs=khT[bi][:, cs],
                                                         start=True, stop=True), reads=[("khT", bi)], writes=[("ps", bkk)])
                for c in range(8):
                    cs = slice(c * 64, (c + 1) * 64)
                    P.op("pe", lambda e, cs=cs: e.matmul(ps[bqk][0:64, cs], lhsT=khT[bi][:, cs], rhs=qhT[bi][:, cs],
                                                         start=True, stop=True), reads=[("khT", bi), ("qhT", bi)], writes=[("ps", bqk)])
                P.op("dve", lambda e: e.tensor_tensor(out=AqkT[bi], in0=v3(ps[bqk][0:64, :]), in1=GT, op=ALU.mult),
                     reads=["GT"], writes=[("AqkT", bi), ("ps", bqk)])
                P.op("dve", lambda e: e.scalar_tensor_tensor(out=Pk[0], in0=v3(ps[bkk][0:64, :]), scalar=-1.0, in1=GTb,
                                                             op0=ALU.mult, op1=ALU.mult),
                     reads=["GTb"], writes=[("Pk", 0), ("ps", bkk)])
            A(t_kkqk)

            def t_pt():
                b = bank("pre")
                for c in range(8):
                    P.op("pe", lambda e, c=c: e.transpose(out=psb[b][0:64, c * 64:(c + 1) * 64], in_=Pk[0][:, c, :],
                                                          identity=ident_bf[0:64, 0:64]),
                         reads=[("Pk", 0), "ident_bf"], writes=[("ps", b)])
                P.op("act", lambda e: e.activation(out=fl(PkT[0]), in_=psb[b][0:64, 0:512], func=ACT.Copy),
                     writes=[("PkT", 0), ("ps", b)])
                P.op("pool", lambda e: e.tensor_tensor(out=Xb[0], in0=Pk[0], in1=m8(identf[0:64, 0:64]), op=ALU.add),
                     reads=[("Pk", 0), "cst"], writes=[("Xb", 0)])
            A(t_pt)

            for lvl in range(5):
                cur = lvl % 2
                nxt = 1 - cur

                def t_sq(lvl=lvl, cur=cur, nxt=nxt):
                    if lvl < 4:
                        ba = bank("pre")
                        for c in range(8):
                            cs = slice(c * 64, (c + 1) * 64)
                            P.op("pe", lambda e, c=c, cs=cs: e.matmul(ps[ba][0:64, cs], lhsT=PkT[cur][:, c, :], rhs=Pk[cur][:, c, :],
                                                                      start=True, stop=True),
                                 reads=[("Pk", cur), ("PkT", cur)], writes=[("ps", ba)])
                    bt = bank("pre")
                    for c in range(8):
                        cs = slice(c * 64, (c + 1) * 64)
                        P.op("pe", lambda e, c=c, cs=cs: e.matmul(ps[bt][0:64, cs], lhsT=Pk[cur][:, c, :], rhs=PkT[cur][:, c, :],
                                                                  start=True, stop=True),
                             reads=[("Pk", cur), ("PkT", cur)], writes=[("ps", bt)])
                    if lvl < 4:
                        P.op("act", lambda e: e.activation(out=fl(Pk[nxt]), in_=ps[ba][0:64, :], func=ACT.Copy),
                             writes=[("Pk", nxt), ("ps", ba)])
                    P.op("dve", lambda e: e.tensor_copy(out=fl(PkT[nxt]), in_=ps[bt][0:64, :]),
                         writes=[("PkT", nxt), ("ps", bt)])
                A(t_sq)

                def t_x(lvl=lvl, cur=cur, nxt=nxt):
                    bx = bank("pre")
                    for c in range(8):
                        cs = slice(c * 64, (c + 1) * 64)
                        P.op("pe", lambda e, c=c, cs=cs: e.matmul(ps[bx][0:64, cs], lhsT=PkT[nxt][:, c, :], rhs=Xb[cur][:, c, :],
                                                                  start=True, stop=True),
                             reads=[("PkT", nxt), ("Xb", cur)], writes=[("ps", bx)])
                    P.op("dve", lambda e: e.tensor_tensor(out=fl(Xb[nxt]), in0=ps[bx][0:64, :], in1=fl(Xb[cur]), op=ALU.add),
                         reads=[("Xb", cur)], writes=[("Xb", nxt), ("ps", bx)])
                    if lvl == 4:
                        P.op("pool", lambda e: e.tensor_tensor(out=TT[bi], in0=Xb[nxt], in1=bJ, op=ALU.mult),
                             reads=[("Xb", nxt), "beta_t"], writes=[("TT", bi)])
                A(t_x)
            return tasks

        def stage3_tasks(st):
            h, tt8 = st // 8, st % 8
            bi = st % NB2
            t0 = tt8 * 512
            n0 = tt8 * 8
            tasks = []
            A = tasks.append
            hold = {}

            def t_begin():
                hold["bo"] = bank("po")
            A(t_begin)
            for c in range(8):
                n = n0 + c
                cs = slice(c * 64, (c + 1) * 64)
                ri = n % 2
                first = (n == 0)

                def t_a(c=c, n=n, cs=cs, ri=ri, first=first):
                    if not first:
                        b1 = bank("q1")
                        P.op("pe", lambda e: e.matmul(ps[b1][0:64, 0:128], lhsT=khT[bi][:, cs], rhs=S_b, start=True, stop=True),
                             reads=[("khT", bi), "S_b"], writes=[("ps", b1)])
                        P.op("dve", lambda e: e.scalar_tensor_tensor(
                            out=Rp[ri], in0=ps[b1][0:64, 0:128], scalar=negeg_t[:, n, h:h + 1], in1=Vtok[bi][:, c, :],
                            op0=ALU.mult, op1=ALU.add),
                            reads=["negeg_t", ("Vtok", bi)], writes=[("Rp", ri), ("ps", b1)])
                A(t_a)

                def t_b(c=c, n=n, cs=cs, ri=ri, first=first):
                    if not first:
                        rsrc, rkey = Rp[ri], ("Rp", ri)
                    else:
                        rsrc, rkey = Vtok[bi][:, c, :], ("Vtok", bi)
                    b2 = bank("q1")
                    P.op("pe", lambda e: e.matmul(ps[b2][0:64, 0:128], lhsT=TT[bi][:, c, :], rhs=rsrc, start=True, stop=True),
                         reads=[("TT", bi), rkey], writes=[("ps", b2)])
                    P.op("act", lambda e: e.activation(out=vnew[ri], in_=ps[b2][0:64, 0:128], func=ACT.Copy),
                         writes=[("vnew", ri), ("ps", b2)])
                    P.op("act", lambda e: e.activation(out=vnd[ri], in_=ps[b2][0:64, 0:128], func=ACT.Copy,
                                                       scale=wdec_t[:, n, h:h + 1]),
                         reads=["wdec_t"], writes=[("vnd", ri), ("ps", b2)])
                A(t_b)

                def t_c(c=c, n=n, cs=cs, ri=ri, first=first):
                    bo = hold["bo"]
                    if not first:
                        P.op("pe", lambda e: e.matmul(ps[bo][:, cs], lhsT=S_b, rhs=qgT[bi][:, cs], start=True, stop=False),
                             reads=["S_b", ("qgT", bi)], writes=[("ps", bo)])
                    P.op("pe", lambda e: e.matmul(ps[bo][:, cs], lhsT=vnew[ri], rhs=AqkT[bi][:, c, :], start=first, stop=True),
                         reads=[("vnew", ri), ("AqkT", bi)], writes=[("ps", bo)])
                    b4 = bank("q4")
                    P.op("pe", lambda e: e.matmul(ps[b4][:, 0:128], lhsT=Ktok[bi][:, c, :], rhs=vnd[ri], start=True, stop=True),
                         reads=[("Ktok", bi), ("vnd", ri)], writes=[("ps", b4)])
                    if first:
                        P.op("dve", lambda e: e.tensor_copy(out=S_f, in_=ps[b4][:, 0:128]), writes=["S_f", ("ps", b4)])
                    else:
                        P.op("dve", lambda e: e.scalar_tensor_tensor(
                            out=S_f, in0=S_f, scalar=dl_t[:, n, h:h + 1], in1=ps[b4][:, 0:128], op0=ALU.mult, op1=ALU.add),
                            reads=["S_f", "dl_t"], writes=["S_f", ("ps", b4)])
                    P.op("act", lambda e: e.activation(out=S_b, in_=S_f, func=ACT.Copy), reads=["S_f"], writes=["S_b"])
                A(t_c)

            def t_epi():
                bo = hold["bo"]
                oi = st % 2
                P.op("act", lambda e: e.activation(out=oraw, in_=ps[bo][:, :], func=ACT.Copy), writes=["oraw", ("ps", bo)])
                if st == 0:
                    dump("khT0", khT[bi], [("khT", bi)])
                    dump("oraw0", oraw, ["oraw"])
                P.op("act", lambda e: e.activation(out=osq, in_=oraw, func=ACT.Square), reads=["oraw"], writes=["osq"])
                b = bank("pc")
                P.op("pe", lambda e: e.matmul(ps[b][:, :], lhsT=ones_bf, rhs=osq, start=True, stop=True),
                     reads=["osq", "ones_bf"], writes=[("ps", b)])
                rsqrt_from_psum(b, orst, "orst", 1.0 / 128)
                P.op("dve", lambda e: e.scalar_tensor_tensor(out=oraw, in0=oraw, scalar=dnw[:, 0:1], in1=orst,
                                                             op0=ALU.mult, op1=ALU.mult),
                     reads=["oraw", "orst", "dnw"], writes=["oraw"])
                P.op("dve", lambda e: e.scalar_tensor_tensor(out=oa[oi], in0=oraw, scalar=0.5, in1=zsT[bi],
                                                             op0=ALU.mult, op1=ALU.mult),
                     reads=["oraw", ("zsT", bi)], writes=[("oa", oi)])
                P.dma(lambda q: q.dma_start(out=oa_scr[h, :, t0:t0 + 512], in_=oa[oi]),
                      reads=[("oa", oi)], writes=[("oa_scr", h, tt8)])
            A(t_epi)
            return tasks

        load_head_w(0)
        NSTEP = 64
        for t in stage12_tasks(0):
            t()
        for st in range(NSTEP):
            t3 = stage3_tasks(st)
            t12 = stage12_tasks(st + 1) if st + 1 < NSTEP else []
            if not PIPELINE_A:
                for t in t3:
                    t()
                for t in t12:
                    t()
                continue
            n3, n12 = len(t3), len(t12)
            j = 0
            for i, t in enumerate(t3):
                t()
                tgt = (n12 * (i + 1) + n3 - 1) // n3
                while j < min(tgt, n12):
                    t12[j]()
                    j += 1
            while j < n12:
                t12[j]()
                j += 1
        cursor[0] = ph

    if stop_after not in ("a0", "b"):
        phase_a()
        P.fence()
        dump("oa_scr", oa_scr, [])

    def phase_c0():
        ph = cursor[0]
        sg = [alloc([512], BF16) for _ in range(4)]
        cnt = 0
        for c16 in range(16):
            off = (O_GA + c16 * 128) if c16 < 8 else (O_GB + (c16 - 8) * 128)
            wt, wk = load_w(w_in, off)
            for tt8 in range(8):
                b = proj_tile(wt, wk, 128, tt8, grp="all")
                i = cnt % 4
                cnt += 1
                P.op("act", lambda e, b=b, i=i: e.activation(out=sg[i], in_=ps[b][:, :], func=ACT.Sigmoid),
                     writes=[("sg", i), ("ps", b)])
                P.dma(lambda q, i=i, c16=c16, tt8=tt8: q.dma_start(out=sg_scr[c16, :, tt8 * 512:(tt8 + 1) * 512], in_=sg[i]),
                      reads=[("sg", i)], writes=[("sg_scr", tt8)])
        cursor[0] = ph

    def phase_c1():
        ph = cursor[0]
        wodn = alloc([8, 1024], BF16)
        wodil = alloc([4, 1024], BF16)
        wout = alloc([8, 1024], BF16)
        fnw_b = alloc([D], F32)
        for (dst, src, kc, key) in ((wodn, w_odn, 8, "wodn"), (wodil, w_odil, 4, "wodil"), (wout, w_out, 8, "wout")):
            for half in range(2):
                P.dma(lambda q, dst=dst, src=src, kc=kc, half=half: q.dma_start(
                    out=dst[:, 0:kc, half * 512:(half + 1) * 512],
                    in_=src[:, half * 512:(half + 1) * 512].rearrange("(k p) c -> p k c", p=128)),
                    writes=[(key, half)], q="pool")
        P.dma(lambda q: q.dma_start(out=fnw_b, in_=fnw_d.partition_broadcast(128)), writes=["fnw_b"])
        oat = [alloc([8, 512], BF16) for _ in range(2)]
        obt = [alloc([4, 512], BF16) for _ in range(2)]
        sgt = [alloc([16, 512], BF16) for _ in range(2)]
        mT = alloc([8, 512], BF16)
        m1 = [alloc([512], F32) for _ in range(2)]
        m2 = [alloc([512], F32) for _ in range(2)]
        xr = [alloc([D], F32) for _ in range(2)]
        xo = [alloc([D], F32) for _ in range(2)]
        yo = [alloc([D], F32) for _ in range(2)]
        ss = [alloc([1], F32) for _ in range(2)]
        rs = [alloc([1], F32) for _ in range(2)]
        junk2 = alloc([D], BF16)
        cnt = 0
        for tt8 in range(8):
            i = tt8 % 2
            sl = slice(tt8 * 512, (tt8 + 1) * 512)
            P.dma(lambda q, i=i, sl=sl: q.dma_start(out=oat[i], in_=oa_scr[:, :, sl].rearrange("h p t -> p h t")),
                  reads=[("oa_scr", hh, tt8) for hh in range(8)], writes=[("oat", i)])
            P.dma(lambda q, i=i, sl=sl: q.dma_start(out=obt[i], in_=ob_scr[:, :, sl].rearrange("h p t -> p h t")),
                  reads=[("ob_scr", hh, tt8) for hh in range(4)], writes=[("obt", i)])
            P.dma(lambda q, i=i, sl=sl: q.dma_start(out=sgt[i], in_=sg_scr[:, :, sl].rearrange("h p t -> p h t")),
                  reads=[("sg_scr", tt8)], writes=[("sgt", i)])
            for c in range(8):
                cs = slice(c * 128, (c + 1) * 128)
                ba = bank("all")
                for k in range(8):
                    P.op("pe", lambda e, b=ba, k=k, cs=cs, i=i: e.matmul(ps[b][:, :], lhsT=wodn[:, k, cs], rhs=oat[i][:, k, :],
                                                                         start=(k == 0), stop=(k == 7)),
                         reads=[("wodn", c // 4), ("oat", i)], writes=[("ps", ba)])
                bb = bank("all")
                for k in range(4):
                    P.op("pe", lambda e, b=bb, k=k, cs=cs, i=i: e.matmul(ps[b][:, :], lhsT=wodil[:, k, cs], rhs=obt[i][:, k, :],
                                                                         start=(k == 0), stop=(k == 3)),
                         reads=[("wodil", c // 4), ("obt", i)], writes=[("ps", bb)])
                mi = cnt % 2
                cnt += 1
                P.op("dve", lambda e, b=ba, mi=mi, i=i, c=c: e.tensor_tensor(out=m1[mi], in0=ps[b][:, :], in1=sgt[i][:, c, :], op=ALU.mult),
                     reads=[("sgt", i)], writes=[("m1", mi), ("ps", ba)])
                P.op("dve", lambda e, b=bb, mi=mi, i=i, c=c: e.tensor_tensor(out=m2[mi], in0=ps[b][:, :], in1=sgt[i][:, 8 + c, :], op=ALU.mult),
                     reads=[("sgt", i)], writes=[("m2", mi), ("ps", bb)])
                P.op("pool", lambda e, mi=mi, c=c: e.tensor_tensor(out=mT[:, c, :], in0=m1[mi], in1=m2[mi], op=ALU.add),
                     reads=[("m1", mi), ("m2", mi)], writes=[("mT", c)])
            for sub in range(4):
                tok0 = tt8 * 512 + sub * 128
                xi = sub % 2
                P.dma(lambda q, xi=xi, tok0=tok0: q.dma_start(out=xr[xi], in_=x_d[tok0:tok0 + 128, :]), writes=[("xr", xi)])
                for half in range(2):
                    b = bank("all")
                    hs = slice(half * 512, (half + 1) * 512)
                    for c in range(8):
                        P.op("pe", lambda e, b=b, c=c, sub=sub, hs=hs: e.matmul(
                            ps[b][:, :], lhsT=mT[:, c, sub * 128:(sub + 1) * 128], rhs=wout[:, c, hs], start=(c == 0), stop=(c == 7)),
                            reads=[("mT", c), ("wout", half)], writes=[("ps", b)])
                    P.op("dve", lambda e, b=b, xi=xi, hs=hs: e.tensor_tensor(out=xo[xi][:, hs], in0=ps[b][:, :], in1=xr[xi][:, hs], op=ALU.add),
                         reads=[("xr", xi)], writes=[("xo", xi, half), ("ps", b)])
                P.op("pool", lambda e, xi=xi: e.memset(ss[xi], 0.0), writes=[("ss", xi)])
                P.op("act", lambda e, xi=xi: e.activation(out=junk2, in_=xo[xi], func=ACT.Square, accum_out=ss[xi]),
                     reads=[("xo", xi, 0), ("xo", xi, 1), ("ss", xi)], writes=["junk2", ("ss", xi)])
                P.op("act", lambda e, xi=xi: e.activation(out=ss[xi], in_=ss[xi], func=ACT.Sqrt, scale=1.0 / D, bias=EPS),
                     reads=[("ss", xi)], writes=[("ss", xi)])
                P.op("dve", lambda e, xi=xi: e.reciprocal(out=rs[xi], in_=ss[xi]), reads=[("ss", xi)], writes=[("rs", xi)])
                P.op("dve", lambda e, xi=xi: e.scalar_tensor_tensor(out=yo[xi], in0=xo[xi], scalar=rs[xi], in1=fnw_b,
                                                                    op0=ALU.mult, op1=ALU.mult),
                     reads=[("xo", xi, 0), ("xo", xi, 1), ("rs", xi), "fnw_b"], writes=[("yo", xi)])
                P.dma(lambda q, xi=xi, tok0=tok0: q.dma_start(out=out_d[tok0:tok0 + 128, :], in_=yo[xi]),
                      reads=[("yo", xi)], writes=[("out", tok0)])
        cursor[0] = ph

    if stop_after is None:
        phase_c0()
        P.fence()
        dump("sg_scr", sg_scr, [])
        cursor[0] = 0
        phase_c1()

    P.emit(final_wait_ops=[o for o in P.dma_last if o is not None])
    return nc


def _consts():
    c = np.zeros((128, 128 + 12 * 256 + 256), np.float32)
    c[:, 0:128] = np.eye(128, dtype=np.float32)
    slopes = (2.0 ** (-8.0 * np.arange(1, 13, dtype=np.float32) / 12)).reshape(3, 4)
    jk = np.arange(128)[:, None]
    iq = np.arange(128)[None, :]
    for gi in range(3):
        for h in range(4):
            s = slopes[gi, h] * DIL[gi]
            d0 = (iq - jk).astype(np.float32)
            b0 = np.where(iq >= jk, -s * d0, NEG)
            d1 = (128 + iq - jk).astype(np.float32)
            b1 = np.where(iq <= jk, -s * d1, NEG)
            g = gi * 4 + h
            c[:, 128 + g * 256:128 + g * 256 + 128] = b0
            c[:, 128 + g * 256 + 128:128 + g * 256 + 256] = b1
    MK = 128 + 3072
    j = np.arange(64)[:, None]
    i = np.arange(64)[None, :]
    c[0:64, MK:MK + 64] = np.where(i >= j, 0.0, NEG)
    c[0:64, MK + 64:MK + 128] = np.where(i > j, 0.0, NEG)
    c[0:64, MK + 128:MK + 192] = (j <= i).astype(np.float32)
    c[0:64, MK + 192:MK + 256] = 1.0
    return c


_NC_CACHE = {}


def _host_inputs(x, norm_w, w_in, conv_w, a_log, dt_bias, dn_norm_w, w_o_dn, w_o_dil, w_out, final_norm_w):
    f = lambda a: np.ascontiguousarray(np.asarray(a, dtype=np.float32))
    cw = f(conv_w)[0].reshape(4, 24, 128).transpose(2, 1, 0).reshape(128, 96)
    shared = {
        "w_in": f(w_in)[0], "w_o_dn": f(w_o_dn)[0], "w_o_dil": f(w_o_dil)[0], "w_out": f(w_out)[0],
        "norm_w": f(norm_w).reshape(1, D), "final_norm_w": f(final_norm_w).reshape(1, D),
        "conv_w_l": np.ascontiguousarray(cw), "a_log": f(a_log).reshape(1, 8), "dt_bias": f(dt_bias).reshape(1, 8),
        "dn_norm_w_l": f(dn_norm_w).reshape(128, 1), "consts": _consts(),
    }
    xs = f(x)
    return [dict(shared, x=xs[b]) for b in range(xs.shape[0])]


def kernel(x, norm_w, w_in, conv_w, a_log, dt_bias, dn_norm_w, w_o_dn, w_o_dil, w_out, final_norm_w):
    in_maps = _host_inputs(x, norm_w, w_in, conv_w, a_log, dt_bias, dn_norm_w, w_o_dn, w_o_dil, w_out, final_norm_w)
    if "nc" not in _NC_CACHE:
        _NC_CACHE["nc"] = build_nc()
    res = run_bass_kernel_spmd(_NC_CACHE["nc"], in_maps, core_ids=list(range(len(in_maps))))
    return np.stack([np.asarray(r["out"], dtype=np.float32).reshape(T, D) for r in res.results], axis=0)
```

```python
import contextlib
import numpy as np
import concourse.bass as bass
import concourse.mybir as mybir
from concourse.bass_utils import run_bass_kernel_spmd

ACT = mybir.ActivationFunctionType
ALU = mybir.AluOpType
F32 = mybir.dt.float32
BF16 = mybir.dt.bfloat16

T = 4096
D = 1024
NEG = -30000.0
PIPELINE_A = True
RESCHEDULE = True
SCHED_IDENTITY = False
EPS = 1e-6
O_QA, O_KA, O_VA, O_ZA, O_BA, O_QB, O_KB, O_VB, O_ZB, O_GA, O_GB = (
    0, 1024, 2048, 3072, 4096, 4112, 5648, 7184, 8720, 9232, 10256)
DIL = (1, 4, 16)


class _Op:
    __slots__ = ("eng", "fn", "deps", "signal", "ticket", "is_dma", "dsem", "dval", "odeps", "cost", "idx", "seg", "war", "pos")

    def __init__(self, eng, fn, is_dma=False):
        self.eng = eng
        self.fn = fn
        self.odeps = []
        self.cost = 0.0
        self.idx = 0
        self.seg = 0
        self.war = []
        self.pos = 0
        self.deps = []
        self.signal = False
        self.ticket = None
        self.is_dma = is_dma
        self.dsem = None
        self.dval = None


class _Res:
    __slots__ = ("w", "r", "rd")

    def __init__(self):
        self.w = None
        self.r = []
        self.rd = []


class Prog:
    ENGS = ("pe", "act", "dve", "pool", "sp")

    def __init__(self, nc, n_dma_sems=32):
        self.nc = nc
        self.streams = {e: [] for e in self.ENGS}
        self.res = {}
        self.n_dma_sems = n_dma_sems
        self.dma_cnt = [0] * n_dma_sems
        self.dma_last = [None] * n_dma_sems
        self.dma_rr = 0
        self.fence_ops = []
        self.fence_dma = []
        self.seg = 0
        self.all_ops = []
        self.fd_default = {}

    @contextlib.contextmanager
    def fds(self, **kw):
        old = dict(self.fd_default)
        self.fd_default.update(kw)
        try:
            yield
        finally:
            self.fd_default = old

    def fence(self):
        self.fence_dma.append([d for d in self.dma_last if d is not None])
        self.seg += 1

    def _r(self, k):
        r = self.res.get(k)
        if r is None:
            r = self.res[k] = _Res()
        return r

    def op(self, eng, fn, reads=(), writes=(), is_dma=False, fd=None):
        o = _Op(eng, fn, is_dma)
        fd = fd or self.fd_default.get(eng)
        if is_dma:
            o.cost = 0.15
        elif eng == "pe":
            o.cost = max(64, fd or 64) / 2400.0 + 0.004
        elif eng == "act":
            o.cost = (224 + (fd or 512)) / 1200.0
        elif eng == "dve":
            o.cost = (110 + (fd or 512)) / 960.0
        else:
            o.cost = 0.2 + (fd or 512) / 1000.0
        o.idx = len(self.all_ops)
        o.seg = self.seg
        self.all_ops.append(o)
        deps = []
        for k in reads:
            r = self._r(k)
            if r.w is not None:
                deps.append((r.w, "raw"))
        for k in writes:
            r = self._r(k)
            if r.w is not None:
                deps.append((r.w, "waw"))
            for rd in r.r:
                if rd is not o:
                    o.war.append(rd)
            for rd in r.rd:
                if rd is not o:
                    o.war.append(rd)
        if is_dma:
            i = self.dma_rr
            self.dma_rr = (i + 1) % self.n_dma_sems
            o.dsem = i
            self.dma_cnt[i] += 1
            o.dval = 16 * self.dma_cnt[i]
            if self.dma_last[i] is not None:
                deps.append((self.dma_last[i], "raw"))
            self.dma_last[i] = o
        seen = set()
        oseen = set()
        for d, kind in deps:
            if d is not o and id(d) not in oseen:
                oseen.add(id(d))
                o.odeps.append(d)
        for d in o.war:
            if id(d) not in oseen:
                oseen.add(id(d))
                o.odeps.append(d)
        for d, kind in deps:
            if d is o or id(d) in seen:
                continue
            if not d.is_dma and d.eng == eng and not is_dma:
                if eng == "pe":
                    continue
                if kind != "raw":
                    continue
            seen.add(id(d))
            d.signal = True
            o.deps.append(d)
        for k in reads:
            r = self._r(k)
            if is_dma:
                r.rd.append(o)
            else:
                r.r.append(o)
        for k in writes:
            r = self._r(k)
            r.w = o
            r.r = []
            r.rd = []
        self.streams[eng].append(o)
        return o

    def dma(self, fn, reads=(), writes=(), q="sp"):
        return self.op(q, fn, reads, writes, is_dma=True)

    def reschedule(self, dma_latency=3.0):
        import heapq
        ops = self.all_ops
        n = len(ops)
        succ = [[] for _ in range(n)]
        indeg = [0] * n
        for o in ops:
            for d in o.odeps:
                if d.seg == o.seg:
                    succ[d.idx].append(o.idx)
                    indeg[o.idx] += 1
        prio = [0.0] * n
        for i in range(n - 1, -1, -1):
            o = ops[i]
            m = 0.0
            for j in succ[i]:
                if prio[j] > m:
                    m = prio[j]
            prio[i] = m + (dma_latency if o.is_dma else o.cost)
        if SCHED_IDENTITY:
            prio = [float(n - i) for i in range(n)]
        new_streams = {e: [] for e in self.ENGS}
        now = 0.0
        free_at = {e: 0.0 for e in self.ENGS}
        nseg = self.seg + 1
        byseg = [[] for _ in range(nseg)]
        for o in ops:
            byseg[o.seg].append(o.idx)
        for sg in range(nseg):
            idxs = byseg[sg]
            ready = {e: [] for e in self.ENGS}
            for i in idxs:
                if indeg[i] == 0:
                    heapq.heappush(ready[ops[i].eng], (-prio[i], i))
            events = []
            done = 0
            tot = len(idxs)
            while done < tot:
                started = False
                for e in self.ENGS:
                    if free_at[e] <= now and ready[e]:
                        _, i = heapq.heappop(ready[e])
                        o = ops[i]
                        new_streams[e].append(o)
                        free_at[e] = now + o.cost
                        heapq.heappush(events, (now + (dma_latency if o.is_dma else o.cost), i))
                        started = True
                if started:
                    continue
                cand = []
                if events:
                    cand.append(events[0][0])
                for e in self.ENGS:
                    if ready[e] and free_at[e] > now:
                        cand.append(free_at[e])
                now = min(cand)
                while events and events[0][0] <= now:
                    _, i = heapq.heappop(events)
                    done += 1
                    for j in succ[i]:
                        indeg[j] -= 1
                        if indeg[j] == 0:
                            heapq.heappush(ready[ops[j].eng], (-prio[j], j))
        for e in self.ENGS:
            assert len(new_streams[e]) == len(self.streams[e])
        self.streams = new_streams
        return now

    def apply_fences(self):
        last = {}
        pos = {e: 0 for e in self.ENGS}
        for sg in range(1, self.seg + 1):
            for e in self.ENGS:
                st = self.streams[e]
                while pos[e] < len(st) and st[pos[e]].seg < sg:
                    if not st[pos[e]].is_dma:
                        last[e] = st[pos[e]]
                    pos[e] += 1
            for e in self.ENGS:
                st = self.streams[e]
                if pos[e] < len(st) and st[pos[e]].seg == sg:
                    o = st[pos[e]]
                    extra = [d for d in last.values()] + list(self.fence_dma[sg - 1])
                    have = set(id(d) for d in o.deps)
                    for d in extra:
                        if d is o or id(d) in have:
                            continue
                        if not d.is_dma and d.eng == e and e == "pe":
                            continue
                        d.signal = True
                        o.deps.append(d)

    def emit(self, final_wait_ops=()):
        nc = self.nc
        for e in self.ENGS:
            for p_, o in enumerate(self.streams[e]):
                o.pos = p_
        for o in self.all_ops:
            if not o.war:
                continue
            best = {}
            have = set(id(d) for d in o.deps)
            for d in o.war:
                if d.is_dma:
                    if id(d) not in have:
                        have.add(id(d))
                        d.signal = True
                        o.deps.append(d)
                    continue
                b = best.get(d.eng)
                if b is None or d.pos > b.pos:
                    best[d.eng] = d
            for e, d in best.items():
                if e == o.eng and not o.is_dma:
                    continue
                if id(d) in have:
                    continue
                d.signal = True
                o.deps.append(d)
        self.apply_fences()
        for e in self.ENGS:
            c = 0
            for o in self.streams[e]:
                if o.is_dma:
                    continue
                if o.signal:
                    c += 1
                    o.ticket = c
        with contextlib.ExitStack() as es:
            esem = {e: es.enter_context(nc.semaphore("s_" + e)) for e in self.ENGS}
            dsem = [es.enter_context(nc.semaphore("d_%d" % i)) for i in range(self.n_dma_sems)]
            block = es.enter_context(nc.Block())

            def run(e, engobj):
                waited = {}

                def wait_for(d):
                    if d.is_dma:
                        key, sem, val = ("d", d.dsem), dsem[d.dsem], d.dval
                    else:
                        key, sem, val = ("e", d.eng), esem[d.eng], d.ticket
                    if waited.get(key, 0) >= val:
                        return
                    waited[key] = val
                    engobj.wait_ge(sem, val)

                for o in self.streams[e]:
                    for d in o.deps:
                        wait_for(d)
                    ins = o.fn(engobj)
                    if o.is_dma:
                        ins.then_inc(dsem[o.dsem], 16)
                    elif o.signal:
                        ins.then_inc(esem[e], 1)
                if e == "sp":
                    for d in final_wait_ops:
                        wait_for(d)

            @block.tensor
            def _(eng):
                run("pe", eng)

            @block.scalar
            def _(eng):
                run("act", eng)

            @block.vector
            def _(eng):
                run("dve", eng)

            @block.gpsimd
            def _(eng):
                run("pool", eng)

            @block.sync
            def _(eng):
                run("sp", eng)


def build_nc(dbg=None, stop_after=None):
    dbg = dbg or {}
    nc = bass.Bass("TRN2", target_bir_lowering=False)
    dt = nc.dram_tensor
    x_d = dt("x", [T, D], F32, kind="ExternalInput").ap()
    w_in = dt("w_in", [D, 11280], F32, kind="ExternalInput").ap()
    w_odn = dt("w_o_dn", [1024, 1024], F32, kind="ExternalInput").ap()
    w_odil = dt("w_o_dil", [512, 1024], F32, kind="ExternalInput").ap()
    w_out = dt("w_out", [1024, 1024], F32, kind="ExternalInput").ap()
    normw_d = dt("norm_w", [1, D], F32, kind="ExternalInput").ap()
    fnw_d = dt("final_norm_w", [1, D], F32, kind="ExternalInput").ap()
    cw_d = dt("conv_w_l", [128, 96], F32, kind="ExternalInput").ap()
    alog_d = dt("a_log", [1, 8], F32, kind="ExternalInput").ap()
    dtb_d = dt("dt_bias", [1, 8], F32, kind="ExternalInput").ap()
    dnw_d = dt("dn_norm_w_l", [128, 1], F32, kind="ExternalInput").ap()
    cst_d = dt("consts", [128, 128 + 12 * 256 + 64 * 4], F32, kind="ExternalInput").ap()
    out_d = dt("out", [T, D], F32, kind="ExternalOutput").ap()
    ob_scr = dt("ob_scr", [4, 128, T], BF16).ap()
    oa_scr = dt("oa_scr", [8, 128, T], BF16).ap()
    sg_scr = dt("sg_scr", [16, 128, T], BF16).ap()
    gc_scr = dt("gc_scr", [8, T], F32).ap()
    gcl_scr = dt("gcl_scr", [8, T], F32).ap()
    dbg_out = {}
    for name, (shape, dtype) in dbg.items():
        dbg_out[name] = dt("dbg_" + name, list(shape), dtype, kind="ExternalOutput").ap()

    P = Prog(nc)
    ARENA = 212000
    arena = nc.alloc_sbuf_tensor("arena", [128, ARENA // 2], BF16)
    cursor = [0]

    def alloc(free_shape, dtype, parts=128):
        n = 1
        for s in free_shape:
            n *= s
        esz = 4 if dtype == F32 else 2
        nbytes = (n * esz + 63) // 64 * 64
        off = cursor[0]
        cursor[0] += nbytes
        assert cursor[0] <= ARENA, ("SBUF overflow", cursor[0])
        ap = arena[0:parts, off // 2: off // 2 + n * esz // 2]
        if dtype == F32:
            ap = ap.bitcast(F32)
        if len(free_shape) == 2:
            ap = ap.rearrange("p (a b) -> p a b", b=free_shape[1])
        elif len(free_shape) == 3:
            ap = ap.rearrange("p (a b c) -> p a b c", b=free_shape[1], c=free_shape[2])
        return ap

    ps = [nc.alloc_psum_tensor("ps%d" % i, [128, 512], F32) for i in range(8)]
    psb = [p[:].bitcast(BF16) for p in ps]
    bank_rr = {}

    def bank(group):
        lst = {"proj": (0, 1), "misc": (2,), "pc": (0, 1, 2), "pre": (3, 4), "q1": (5,), "q4": (6,), "po": (7,),
               "all": tuple(range(8)), "s": (3, 4), "a": (5, 7), "b": (6, 2)}[group]
        i = bank_rr.get(group, 0)
        bank_rr[group] = i + 1
        return lst[i % len(lst)]

    def dump(name, src_ap, reads):
        if name in dbg_out:
            P.dma(lambda q, a=src_ap, o=dbg_out[name]: q.dma_start(out=o, in_=a), reads=reads, writes=[("dbg", name)])

    hT = alloc([8, T], BF16)
    cst = alloc([128 + 12 * 256 + 256], F32)
    identf = cst[:, 0:128]
    alibi = cst[:, 128:128 + 3072].rearrange("p (g w) -> p g w", w=256)
    MK = 128 + 3072
    mask_incl = cst[0:64, MK:MK + 64]
    mask_strict = cst[0:64, MK + 64:MK + 128]
    triu_f = cst[0:64, MK + 128:MK + 192]
    ones64f = cst[0:64, MK + 192:MK + 256]
    ident_bf = alloc([128], BF16)
    ones_bf = alloc([128], BF16)
    ones_f = alloc([128], F32)
    cw = alloc([96], F32)
    dnw = alloc([1], F32)
    WB_N = 4
    wb = [alloc([8, 128], BF16) for _ in range(WB_N)]
    wb_rr = [0]
    beta_t = alloc([64, 8], F32, 64)
    gc_t = alloc([64, 8], F32, 64)
    negeg_t = alloc([64, 8], F32, 64)
    wdec_t = alloc([64, 8], F32, 64)
    dl_t = alloc([64, 8], F32)
    persist_end = cursor[0]

    P.dma(lambda q: q.dma_start(out=cst, in_=cst_d), writes=["cst"])
    P.dma(lambda q: q.dma_start(out=cw, in_=cw_d), writes=["cw"])
    P.dma(lambda q: q.dma_start(out=dnw, in_=dnw_d), writes=["dnw"])
    P.op("dve", lambda e: e.tensor_copy(out=ident_bf, in_=identf), reads=["cst"], writes=["ident_bf"])
    P.op("pool", lambda e: e.memset(ones_bf, 1.0), writes=["ones_bf"])
    P.op("pool", lambda e: e.memset(ones_f, 1.0), writes=["ones_f"])

    def load_w(src, c0, ncols=128, kchunks=8, dst=None, key=None):
        if dst is None:
            i = wb_rr[0] % WB_N
            wb_rr[0] += 1
            dst, key = wb[i], ("wb", i)
        P.dma(lambda q, d=dst, s=src, c0=c0, n=ncols, kc=kchunks: q.dma_start(
            out=d[:, 0:kc, 0:n], in_=s[:, c0:c0 + n].rearrange("(k p) c -> p k c", p=128)),
            writes=[key], q="pool")
        return dst, key

    def hkeys(t0, t1):
        return [("hT", i) for i in range(t0 // 128, (t1 + 127) // 128)]

    def proj_tile(wt, wkey, ncols, tt8, grp="proj"):
        b = bank(grp)
        for k in range(8):
            P.op("pe", lambda e, b=b, k=k, wt=wt, n=ncols, tt8=tt8: e.matmul(
                ps[b][0:n, :], lhsT=wt[:, k, 0:n], rhs=hT[:, k, tt8 * 512:(tt8 + 1) * 512],
                start=(k == 0), stop=(k == 7)),
                reads=[wkey] + hkeys(tt8 * 512, tt8 * 512 + 512), writes=[("ps", b)], fd=512)
        return b

    ph = cursor[0]
    normw_b = alloc([D], F32)
    xs = [alloc([D], F32) for _ in range(2)]
    junk = alloc([D], BF16)
    xb = [alloc([D], BF16) for _ in range(2)]
    ss0 = [alloc([1], F32) for _ in range(2)]
    rs0 = [alloc([1], F32) for _ in range(2)]
    P.dma(lambda q: q.dma_start(out=normw_b, in_=normw_d.partition_broadcast(128)), writes=["normw_b"])
    for tt in range(32):
        i = tt % 2
        P.dma(lambda q, i=i, tt=tt: q.dma_start(out=xs[i], in_=x_d[tt * 128:(tt + 1) * 128, :]), writes=[("xs", i)])
        P.op("pool", lambda e, i=i: e.memset(ss0[i], 0.0), writes=[("ss0", i)])
        P.op("act", lambda e, i=i: e.activation(out=junk, in_=xs[i], func=ACT.Square, accum_out=ss0[i]),
             reads=[("xs", i), ("ss0", i)], writes=["junk", ("ss0", i)])
        P.op("act", lambda e, i=i: e.activation(out=ss0[i], in_=ss0[i], func=ACT.Sqrt, scale=1.0 / D, bias=EPS),
             reads=[("ss0", i)], writes=[("ss0", i)])
        P.op("dve", lambda e, i=i: e.reciprocal(out=rs0[i], in_=ss0[i]), reads=[("ss0", i)], writes=[("rs0", i)])
        P.op("dve", lambda e, i=i: e.scalar_tensor_tensor(out=xb[i], in0=xs[i], scalar=rs0[i], in1=normw_b,
                                                          op0=ALU.mult, op1=ALU.mult),
             reads=[("xs", i), ("rs0", i), "normw_b"], writes=[("xb", i)])
        b = bank("all")
        for k in range(8):
            P.op("pe", lambda e, b=b, k=k, i=i: e.transpose(out=psb[b][:, k * 128:(k + 1) * 128],
                                                           in_=xb[i][:, k * 128:(k + 1) * 128], identity=ident_bf),
                 reads=[("xb", i), "ident_bf"], writes=[("ps", b)])
        eng = "act" if tt % 2 == 0 else "dve"
        if eng == "act":
            fn = lambda e, b=b, tt=tt: e.activation(out=hT[:, :, tt * 128:(tt + 1) * 128],
                                                    in_=psb[b].rearrange("p (k t) -> p k t", t=128), func=ACT.Copy)
        else:
            fn = lambda e, b=b, tt=tt: e.tensor_copy(out=hT[:, :, tt * 128:(tt + 1) * 128],
                                                     in_=psb[b].rearrange("p (k t) -> p k t", t=128))
        P.op(eng, fn, writes=[("hT", tt), ("ps", b)])
    dump("hT", hT, hkeys(0, T))
    cursor[0] = ph
    P.fence()

    def phase_a0():
        ph = cursor[0]
        w16, w16k = load_w(w_in, O_BA, 16)
        alog_b = alloc([8], F32, 64)
        dtb_b = alloc([8], F32, 64)
        P.dma(lambda q: q.dma_start(out=alog_b, in_=alog_d.partition_broadcast(64)), writes=["alog_b"])
        P.dma(lambda q: q.dma_start(out=dtb_b, in_=dtb_d.partition_broadcast(64)), writes=["dtb_b"])
        Gsb = alloc([64, 16], F32, 64)
        names = ["xa", "ax", "ee", "ll", "sp", "g", "lb", "gcl", "tmp"]
        A = {n: alloc([64, 8], F32, 64) for n in names}
        glast = alloc([64, 8], F32)
        tb = alloc([8, 64], F32, 64)
        for half in range(2):
            b = bank("all")
            for n in range(32 * half, 32 * half + 32):
                for k in range(8):
                    P.op("pe", lambda e, b=b, n=n, k=k: e.matmul(
                        ps[b][0:64, (n % 32) * 16:(n % 32) * 16 + 16], lhsT=hT[:, k, n * 64:(n + 1) * 64],
                        rhs=w16[:, k, 0:16], start=(k == 0), stop=(k == 7)),
                        reads=[w16k] + hkeys(n * 64, n * 64 + 64), writes=[("ps", b)])
            P.op("act", lambda e, b=b, half=half: e.activation(
                out=Gsb[:, 32 * half:32 * half + 32, :], in_=ps[b][0:64, :].rearrange("p (n c) -> p n c", c=16),
                func=ACT.Copy), writes=["Gsb", ("ps", b)])
        bb = Gsb[:, :, 0:8]
        aa = Gsb[:, :, 8:16]
        bc = lambda v: v.unsqueeze(1).to_broadcast([64, 64, 8])
        P.op("act", lambda e: e.activation(out=beta_t, in_=bb, func=ACT.Sigmoid), reads=["Gsb"], writes=["beta_t"])
        P.op("act", lambda e: e.activation(out=A["lb"], in_=beta_t, func=ACT.Ln), reads=["beta_t"], writes=["lb"])
        P.op("dve", lambda e: e.tensor_tensor(out=A["xa"], in0=aa, in1=bc(dtb_b), op=ALU.add),
             reads=["Gsb", "dtb_b"], writes=["xa"])
        P.op("act", lambda e: e.activation(out=A["ax"], in_=A["xa"], func=ACT.Abs), reads=["xa"], writes=["ax"])
        P.op("act", lambda e: e.activation(out=A["ee"], in_=A["ax"], func=ACT.Exp, scale=-1.0), reads=["ax"], writes=["ee"])
        P.op("act", lambda e: e.activation(out=A["ll"], in_=A["ee"], func=ACT.Ln, bias=1.0), reads=["ee"], writes=["ll"])
        P.op("dve", lambda e: e.scalar_tensor_tensor(out=A["sp"], in0=A["xa"], scalar=0.0, in1=A["ll"],
                                                     op0=ALU.max, op1=ALU.add), reads=["xa", "ll"], writes=["sp"])
        P.op("act", lambda e: e.activation(out=alog_b, in_=alog_b, func=ACT.Exp), reads=["alog_b"], writes=["alog_b"])
        P.op("dve", lambda e: e.scalar_tensor_tensor(out=A["g"], in0=A["sp"], scalar=-1.0, in1=bc(alog_b),
                                                     op0=ALU.mult, op1=ALU.mult), reads=["sp", "alog_b"], writes=["g"])
        gflat = A["g"].rearrange("p n h -> p (n h)")
        b1 = bank("all")
        P.op("pe", lambda e: e.matmul(ps[b1][0:64, :], lhsT=triu_f, rhs=gflat, start=True, stop=True),
             reads=["g", "cst"], writes=[("ps", b1)])
        b2 = bank("all")
        P.op("pe", lambda e: e.matmul(ps[b2][:, :], lhsT=ones_f[0:64, :], rhs=gflat, start=True, stop=True),
             reads=["g", "ones_f"], writes=[("ps", b2)])
        fl = lambda v: v.rearrange("p n h -> p (n h)")
        P.op("act", lambda e: e.activation(out=fl(gc_t), in_=ps[b1][0:64, :], func=ACT.Copy), writes=["gc_t", ("ps", b1)])
        P.op("dve", lambda e: e.tensor_copy(out=fl(glast), in_=ps[b2][:, :]), writes=["glast", ("ps", b2)])
        P.op("act", lambda e: e.activation(out=negeg_t, in_=gc_t, func=ACT.Exp), reads=["gc_t"], writes=["negeg_t"])
        P.op("dve", lambda e: e.tensor_scalar(out=negeg_t, in0=negeg_t, scalar1=-1.0, scalar2=None, op0=ALU.mult),
             reads=["negeg_t"], writes=["negeg_t"])
        P.op("dve", lambda e: e.tensor_tensor(out=A["tmp"], in0=glast[0:64], in1=gc_t, op=ALU.subtract),
             reads=["glast", "gc_t"], writes=["tmp"])
        P.op("act", lambda e: e.activation(out=wdec_t, in_=A["tmp"], func=ACT.Exp), reads=["tmp"], writes=["wdec_t"])
        P.op("act", lambda e: e.activation(out=dl_t, in_=glast, func=ACT.Exp), reads=["glast"], writes=["dl_t"])
        P.op("dve", lambda e: e.tensor_tensor(out=A["gcl"], in0=gc_t, in1=A["lb"], op=ALU.add),
             reads=["gc_t", "lb"], writes=["gcl"])
        for nm, src, scr in (("gc", gc_t, gc_scr), ("gcl", A["gcl"], gcl_scr)):
            b = bank("all")
            for h in range(8):
                P.op("pe", lambda e, b=b, h=h, src=src: e.transpose(out=ps[b][0:64, h * 64:(h + 1) * 64],
                                                                   in_=src[:, :, h], identity=identf[0:64, 0:64]),
                     reads=["gc_t" if nm == "gc" else "gcl", "cst"], writes=[("ps", b)])
            P.op("dve", lambda e, b=b: e.tensor_copy(out=tb, in_=ps[b][0:64, :].rearrange("p (h c) -> p h c", c=64)),
                 writes=["tb", ("ps", b)])
            P.dma(lambda q, scr=scr: q.dma_start(out=scr.rearrange("h (n c) -> n h c", c=64), in_=tb),
                  reads=["tb"], writes=[nm + "_scr"])
        dump("gc_t", gc_t, ["gc_t"])
        dump("beta_t", beta_t, ["beta_t"])
        dump("g_t", A["g"], ["g"])
        cursor[0] = ph

    phase_a0()
    P.fence()

    def interleave(ta, tb):
        na, nb_ = len(ta), len(tb)
        j = 0
        for i, t in enumerate(ta):
            t()
            tgt = (nb_ * (i + 1) + na - 1) // max(na, 1)
            while j < min(tgt, nb_):
                tb[j]()
                j += 1
        while j < nb_:
            tb[j]()
            j += 1

    def phase_b():
        ph = cursor[0]
        qTs = [alloc([T], BF16) for _ in range(2)]
        kTs = [alloc([T], BF16) for _ in range(2)]
        vsbs = [alloc([32, 128], BF16) for _ in range(2)]
        vT = alloc([T], BF16)
        acc_n = alloc([T], F32)
        acc_d = alloc([T], F32)
        NS = 3
        s_sb = [alloc([256], F32) for _ in range(NS)]
        p_sb = [alloc([256], BF16) for _ in range(4)]
        zs = [alloc([512], F32) for _ in range(2)]
        rc = [alloc([512], F32) for _ in range(2)]
        ob = [alloc([512], BF16) for _ in range(2)]
        scale = 128 ** -0.5
        jobs = [(h, gi) for h in range(4) for gi in range(3)]
        cnt = [0]

        def proj_tasks(jn):
            h, gi = jobs[jn]
            bi = jn % 2
            qT, kT, v_sb = qTs[bi], kTs[bi], vsbs[bi]
            d = DIL[gi]
            L = T // d
            nb = L // 128
            M = 512 // d
            tasks = []
            hold = {}

            def t_w():
                hold["q"] = load_w(w_in, O_QB + gi * 512 + h * 128)
                hold["k"] = load_w(w_in, O_KB + gi * 512 + h * 128)
                hold["v"] = load_w(w_in, O_VB + gi * 512 + h * 128)
            tasks.append(t_w)
            q3 = qT.rearrange("p (r m) -> p r m", r=d)
            k3 = kT.rearrange("p (r m) -> p r m", r=d)
            for tt8 in range(8):
                def t_q(tt8=tt8):
                    wq, wqk = hold["q"]
                    b = proj_tile(wq, wqk, 128, tt8)
                    P.op("act", lambda e: e.activation(out=qT[:, tt8 * 512:(tt8 + 1) * 512], in_=ps[b][:, :],
                                                       func=ACT.Copy, scale=scale),
                         writes=[("qT", bi), ("ps", b)])
                tasks.append(t_q)

                def t_k(tt8=tt8):
                    wk, wkk = hold["k"]
                    b = proj_tile(wk, wkk, 128, tt8)
                    P.op("dve", lambda e: e.tensor_copy(out=kT[:, tt8 * 512:(tt8 + 1) * 512], in_=ps[b][:, :]),
                         writes=[("kT", bi), ("ps", b)])
                tasks.append(t_k)

                def t_v(tt8=tt8):
                    wv, wvk = hold["v"]
                    b = proj_tile(wv, wvk, 128, tt8)
                    P.op("act", lambda e: e.activation(out=vT[:, tt8 * 512:(tt8 + 1) * 512], in_=ps[b][:, :], func=ACT.Copy),
                         writes=[("vT", tt8), ("ps", b)])
                tasks.append(t_v)
            for t8 in range(4):
                def t_vt(t8=t8):
                    b = bank("proj")
                    for s in range(8):
                        tid = t8 * 8 + s
                        r, j = tid // nb, tid % nb
                        t0 = 128 * j * d + r
                        P.op("pe", lambda e, s=s, t0=t0: e.transpose(
                            out=psb[b][:, s * 128:(s + 1) * 128], in_=vT[:, t0:t0 + 127 * d + 1:d], identity=ident_bf),
                            reads=[("vT", i) for i in range((128 * j * d) // 512, (128 * (j + 1) * d + 511) // 512)] + ["ident_bf"],
                            writes=[("ps", b)])
                    P.op("dve", lambda e: e.tensor_copy(
                        out=v_sb[:, t8 * 8:(t8 + 1) * 8, :], in_=psb[b][:, :].rearrange("p (s c) -> p s c", c=128)),
                        writes=[("v_sb", bi), ("ps", b)])
                tasks.append(t_vt)
            return tasks

        def core_tasks(jn):
            h, gi = jobs[jn]
            bi = jn % 2
            qT, kT, v_sb = qTs[bi], kTs[bi], vsbs[bi]
            d = DIL[gi]
            L = T // d
            nb = L // 128
            gidx = gi * 4 + h
            tasks = []
            st = {"bn": None, "bd": None}
            pis = {}

            def tok(r, j0, nblk):
                a = (128 * j0) * d + r
                return slice(a, a + (128 * nblk - 1) * d + 1, d)

            def t_qk(r, j):
              with P.fds(pe=256, act=256, dve=256):
                W = 2 if j + 1 < nb else 1
                bs = bank("s")
                P.op("pe", lambda e: e.matmul(ps[bs][:, 0:128 * W], lhsT=kT[:, tok(r, j, 1)], rhs=qT[:, tok(r, j, W)],
                                              start=True, stop=True),
                     reads=[("qT", bi), ("kT", bi)], writes=[("ps", bs)])
                si = cnt[0] % NS
                pi = cnt[0] % 4
                cnt[0] += 1
                pis[(r, j)] = pi
                P.op("dve", lambda e: e.tensor_tensor(out=s_sb[si][:, 0:128 * W], in0=ps[bs][:, 0:128 * W],
                                                      in1=alibi[:, gidx, 0:128 * W], op=ALU.add),
                     reads=["cst"], writes=[("s_sb", si), ("ps", bs)])
                P.op("act", lambda e: e.activation(out=p_sb[pi][:, 0:128 * W], in_=s_sb[si][:, 0:128 * W], func=ACT.Exp),
                     reads=[("s_sb", si)], writes=[("p_sb", pi)])

            def t_pv(r, j):
              with P.fds(pe=128):
                if j % 4 == 0:
                    st["bn"], st["bd"] = bank("a"), bank("b")
                bn, bd = st["bn"], st["bd"]
                pi = pis[(r, j)]
                prev = pis[(r, j - 1)] if j > 0 else None
                col = (j % 4) * 128
                tid = r * nb + j
                for (bk, is_den, lk) in ((bn, False, ("v_sb", bi)), (bd, True, "ones_bf")):
                    first = True
                    if j > 0:
                        lp = ones_bf if is_den else v_sb[:, tid - 1, :]
                        P.op("pe", lambda e, bk=bk, lp=lp: e.matmul(
                            ps[bk][:, col:col + 128], lhsT=lp, rhs=p_sb[prev][:, 128:256], start=True, stop=False),
                            reads=[lk, ("p_sb", prev)], writes=[("ps", bk)])
                        first = False
                    lc = ones_bf if is_den else v_sb[:, tid, :]
                    P.op("pe", lambda e, bk=bk, lc=lc, first=first: e.matmul(
                        ps[bk][:, col:col + 128], lhsT=lc, rhs=p_sb[pi][:, 0:128], start=first, stop=True),
                        reads=[lk, ("p_sb", pi)], writes=[("ps", bk)])
                if j % 4 == 3 or j == nb - 1:
                    n0 = (j // 4) * 4
                    nq = (j - n0 + 1) * 128
                    sl = slice(128 * n0 * d + r, 128 * n0 * d + r + (nq - 1) * d + 1, d)
                    for (bk, acc, key, eng) in ((bn, acc_n, "acc_n", "act"), (bd, acc_d, "acc_d", "dve")):
                        if gi == 0:
                            if eng == "act":
                                P.op("act", lambda e, bk=bk, acc=acc: e.activation(
                                    out=acc[:, sl], in_=ps[bk][:, 0:nq], func=ACT.Copy), writes=[key, ("ps", bk)])
                            else:
                                P.op("dve", lambda e, bk=bk, acc=acc: e.tensor_copy(
                                    out=acc[:, sl], in_=ps[bk][:, 0:nq]), writes=[key, ("ps", bk)])
                        else:
                            P.op("dve", lambda e, bk=bk, acc=acc: e.tensor_tensor(
                                out=acc[:, sl], in0=ps[bk][:, 0:nq], in1=acc[:, sl], op=ALU.add),
                                reads=[key], writes=[key, ("ps", bk)])

            seq = [(r, j) for r in range(d) for j in range(nb)]
            SK = 2
            for idx in range(len(seq) + SK):
                def t_blk(idx=idx):
                    if idx < len(seq):
                        t_qk(*seq[idx])
                    if idx >= SK:
                        t_pv(*seq[idx - SK])
                tasks.append(t_blk)
            if gi == 2:
                hold = {}

                def t_wz():
                    hold["z"] = load_w(w_in, O_ZB + h * 128)
                tasks.append(t_wz)
                for tt8 in range(8):
                    def t_fin(tt8=tt8):
                        wz, wzk = hold["z"]
                        i = tt8 % 2
                        sl = slice(tt8 * 512, (tt8 + 1) * 512)
                        b = proj_tile(wz, wzk, 128, tt8)
                        P.op("act", lambda e: e.activation(out=zs[i], in_=ps[b][:, :], func=ACT.Tanh, scale=0.5),
                             writes=[("zs", i), ("ps", b)])
                        P.op("dve", lambda e: e.scalar_tensor_tensor(out=zs[i], in0=zs[i], scalar=1.0, in1=ps[b][:, :],
                                                                     op0=ALU.add, op1=ALU.mult),
                             reads=[("zs", i)], writes=[("zs", i), ("ps", b)])
                        P.op("act", lambda e: e.activation(out=rc[i], in_=acc_d[:, sl], func=ACT.Ln), reads=["acc_d"], writes=[("rc", i)])
                        P.op("act", lambda e: e.activation(out=rc[i], in_=rc[i], func=ACT.Exp, scale=-1.0), reads=[("rc", i)], writes=[("rc", i)])
                        P.op("dve", lambda e: e.tensor_tensor(out=rc[i], in0=rc[i], in1=acc_n[:, sl], op=ALU.mult),
                             reads=["acc_n", ("rc", i)], writes=[("rc", i)])
                        P.op("dve", lambda e: e.scalar_tensor_tensor(out=ob[i], in0=rc[i], scalar=0.5, in1=zs[i],
                                                                     op0=ALU.mult, op1=ALU.mult),
                             reads=[("rc", i), ("zs", i)], writes=[("ob", i)])
                        P.dma(lambda q: q.dma_start(out=ob_scr[h, :, sl], in_=ob[i]),
                              reads=[("ob", i)], writes=[("ob_scr", h, tt8)])
                    tasks.append(t_fin)
            return tasks

        for t in proj_tasks(0):
            t()
        for jn in range(len(jobs)):
            interleave(core_tasks(jn), proj_tasks(jn + 1) if jn + 1 < len(jobs) else [])
        cursor[0] = ph

    if stop_after != "a0":
        phase_b()
        P.fence()
        dump("ob_scr", ob_scr, [])

    def phase_a():
        ph = cursor[0]
        NB2 = 2
        upre = {t: [alloc([516], BF16) for _ in range(2)] for t in "qkv"}
        for t in "qkv":
            P.op("pool", lambda e, t=t: e.memset(upre[t][1][:, 512:515], 0.0), writes=[("upre", t, 1)])
        dg = alloc([12, 128], BF16)
        th = {t: alloc([512], F32) for t in "qkvz"}
        yq = alloc([512], F32)
        yk = alloc([512], F32)
        sq = {t: alloc([512], BF16) for t in "qk"}
        rin = {t: alloc([512], F32) for t in "qk"}
        vTt = alloc([512], BF16)
        khT = [alloc([512], BF16) for _ in range(NB2)]
        qhT = [alloc([512], BF16) for _ in range(NB2)]
        qgT = [alloc([512], BF16) for _ in range(3)]
        zsT = [alloc([512], BF16) for _ in range(3)]
        Ktok = [alloc([8, 128], BF16, 64) for _ in range(NB2)]
        Vtok = [alloc([8, 128], BF16, 64) for _ in range(NB2)]
        AqkT = [alloc([8, 64], BF16, 64) for _ in range(NB2)]
        TT = alloc([8, 64], BF16, 64)
        gcB = [alloc([512], F32) for _ in range(2)]
        gclB = [alloc([512], F32, 64) for _ in range(2)]
        egB = alloc([512], F32)
        E1 = alloc([8, 64], F32, 64)
        E2 = alloc([8, 64], F32, 64)
        GT = alloc([8, 64], F32, 64)
        GTb = alloc([8, 64], F32, 64)
        Pk = [alloc([8, 64], BF16, 64) for _ in range(2)]
        PkT = [alloc([8, 64], BF16, 64) for _ in range(2)]
        Xb = [alloc([8, 64], BF16, 64) for _ in range(2)]
        S_f = alloc([128], F32)
        S_b = alloc([128], BF16)
        vnew = [alloc([128], BF16, 64) for _ in range(2)]
        WnT = [alloc([8, 64], BF16) for _ in range(NB2)]
        Ubf = [alloc([8, 128], BF16, 64) for _ in range(NB2)]
        Kd = [alloc([8, 128], BF16, 64) for _ in range(NB2)]
        Kgn = alloc([8, 128], BF16, 64)
        oraw = alloc([512], F32)
        osq = alloc([512], BF16)
        orst = alloc([512], F32)
        oa = [alloc([512], BF16) for _ in range(2)]
        wsets = [[alloc([8, 128], BF16) for _ in range(4)] for _ in range(2)]

        def load_head_w(h):
            ws = wsets[h % 2]
            for ti, off in enumerate((O_QA, O_KA, O_VA, O_ZA)):
                load_w(w_in, off + h * 128, dst=ws[ti], key=("wh", h % 2, ti))

        m8 = lambda m: m.unsqueeze(1).to_broadcast([64, 8, 64])
        v3 = lambda a: a.rearrange("p (c i) -> p c i", i=64)
        fl = lambda a: a.rearrange("p c i -> p (c i)")

        def rsqrt_from_psum(b, dst, key, scale):
            P.op("act", lambda e: e.activation(out=dst, in_=ps[b][:, :], func=ACT.Ln, scale=scale, bias=EPS),
                 writes=[key, ("ps", b)])
            P.op("act", lambda e: e.activation(out=dst, in_=dst, func=ACT.Exp, scale=-0.5), reads=[key], writes=[key])

        def stage12_tasks(st):
            h, tt8 = st // 8, st % 8
            bi = st % NB2
            b3 = st % 3
            ui = st % 2
            t0 = tt8 * 512
            ws = wsets[h % 2]
            tasks = []
            A = tasks.append

            def t_pre():
                if tt8 == 0:
                    if h + 1 < 8:
                        load_head_w(h + 1)
                    for ti in range(3):
                        for kk in range(4):
                            g = ti * 8 + h
                            P.op("dve", lambda e, ti=ti, kk=kk, g=g: e.tensor_scalar(
                                out=dg[:, ti * 4 + kk, :], in0=identf, scalar1=cw[:, g * 4 + kk:g * 4 + kk + 1], scalar2=None,
                                op0=ALU.mult), reads=["cst", "cw"], writes=["dg"])
                P.dma(lambda q: q.dma_start(out=gcB[ui], in_=gc_scr[h, t0:t0 + 512].partition_broadcast(128)),
                      reads=["gc_scr"], writes=[("gcB", ui)])
                P.dma(lambda q: q.dma_start(out=gclB[ui], in_=gcl_scr[h, t0:t0 + 512].partition_broadcast(64)),
                      reads=["gcl_scr"], writes=[("gclB", ui)])
            A(t_pre)

            def proj_split(wt, wkey, hb):
                def mk(k0):
                    def f():
                        if k0 == 0:
                            hb["b"] = bank("pc")
                        b = hb["b"]
                        for k in range(k0, k0 + 2):
                            P.op("pe", lambda e, k=k: e.matmul(ps[b][:, :], lhsT=wt[:, k, :], rhs=hT[:, k, t0:t0 + 512],
                                                               start=(k == 0), stop=(k == 7)),
                                 reads=[wkey] + hkeys(t0, t0 + 512), writes=[("ps", b)], fd=512)
                    return f
                for k0 in (0, 2, 4):
                    A(mk(k0))
                return mk(6)

            hbz = {}
            last_z = proj_split(ws[3], ("wh", h % 2, 3), hbz)

            def t_projz():
                last_z()
                b = hbz["b"]
                P.op("act", lambda e: e.activation(out=th["z"], in_=ps[b][:, :], func=ACT.Tanh, scale=0.5),
                     writes=[("th", "z"), ("ps", b)])
                P.op("dve", lambda e: e.scalar_tensor_tensor(out=zsT[b3], in0=th["z"], scalar=1.0, in1=ps[b][:, :],
                                                             op0=ALU.add, op1=ALU.mult),
                     reads=[("th", "z")], writes=[("zsT", b3), ("ps", b)])
            A(t_projz)

            for ti, t in enumerate("qkv"):
                hbp = {}
                last_p = proj_split(ws[ti], ("wh", h % 2, ti), hbp)

                def t_proj(ti=ti, t=t, hbp=hbp, last_p=last_p):
                    u = upre[t][ui]
                    up = upre[t][1 - ui]
                    last_p()
                    b = hbp["b"]
                    if tt8 == 0:
                        P.op("pool", lambda e: e.memset(u[:, 0:3], 0.0), writes=[("upre", t, ui)])
                    else:
                        P.op("pool", lambda e: e.tensor_copy(out=u[:, 0:3], in_=up[:, 512:515]),
                             reads=[("upre", t, 1 - ui)], writes=[("upre", t, ui)])
                    P.op("act", lambda e: e.activation(out=u[:, 3:515], in_=ps[b][:, :], func=ACT.Copy),
                         writes=[("upre", t, ui), ("ps", b)])
                A(t_proj)

            for ti, t in enumerate("qkv"):
                hbc = {}

                def t_conv0(ti=ti, t=t, hbc=hbc):
                    u = upre[t][ui]
                    hbc["b"] = b = bank("pc")
                    for kk in range(2):
                        P.op("pe", lambda e, kk=kk: e.matmul(ps[b][:, :], lhsT=dg[:, ti * 4 + kk, :], rhs=u[:, kk:kk + 512],
                                                             start=(kk == 0), stop=False),
                             reads=["dg", ("upre", t, ui)], writes=[("ps", b)], fd=512)
                A(t_conv0)

                def t_conv(ti=ti, t=t, hbc=hbc):
                    u = upre[t][ui]
                    b = hbc["b"]
                    for kk in range(2, 4):
                        P.op("pe", lambda e, kk=kk: e.matmul(ps[b][:, :], lhsT=dg[:, ti * 4 + kk, :], rhs=u[:, kk:kk + 512],
                                                             start=False, stop=(kk == 3)),
                             reads=["dg", ("upre", t, ui)], writes=[("ps", b)], fd=512)
                    P.op("act", lambda e: e.activation(out=th[t], in_=ps[b][:, :], func=ACT.Tanh, scale=0.5),
                         writes=[("th", t), ("ps", b)])
                    dst = {"q": yq, "k": yk, "v": vTt}[t]
                    P.op("dve", lambda e: e.scalar_tensor_tensor(out=dst, in0=th[t], scalar=1.0, in1=ps[b][:, :],
                                                                 op0=ALU.add, op1=ALU.mult),
                         reads=[("th", t)], writes=[("y", t), ("ps", b)])
                A(t_conv)

            for t, y in (("q", yq), ("k", yk)):
                def t_norm(t=t, y=y):
                    P.op("act", lambda e: e.activation(out=sq[t], in_=y, func=ACT.Square), reads=[("y", t)], writes=[("sq", t)])
                    b = bank("pc")
                    P.op("pe", lambda e: e.matmul(ps[b][:, :], lhsT=ones_bf, rhs=sq[t], start=True, stop=True),
                         reads=[("sq", t), "ones_bf"], writes=[("ps", b)], fd=512)
                    rsqrt_from_psum(b, rin[t], ("rin", t), 0.25)
                A(t_norm)

            def t_hat():
                P.op("dve", lambda e: e.scalar_tensor_tensor(out=khT[bi], in0=yk, scalar=0.5, in1=rin["k"],
                                                             op0=ALU.mult, op1=ALU.mult),
                     reads=[("y", "k"), ("rin", "k")], writes=[("khT", bi)])
                P.op("dve", lambda e: e.scalar_tensor_tensor(out=qhT[bi], in0=yq, scalar=0.5 * 128 ** -0.5, in1=rin["q"],
                                                             op0=ALU.mult, op1=ALU.mult),
                     reads=[("y", "q"), ("rin", "q")], writes=[("qhT", bi)])
                P.op("act", lambda e: e.activation(out=egB, in_=gcB[ui], func=ACT.Exp), reads=[("gcB", ui)], writes=["egB"])
                P.op("pool", lambda e: e.tensor_tensor(out=qgT[b3], in0=qhT[bi], in1=egB, op=ALU.mult),
                     reads=[("qhT", bi), "egB"], writes=[("qgT", b3)])
            A(t_hat)

            for (src, skey, dst, dkey, sc) in ((khT[bi], ("khT", bi), Ktok[bi], ("Ktok", bi), 1.0),
                                               (vTt, ("y", "v"), Vtok[bi], ("Vtok", bi), 0.5)):
                def t_tok(src=src, skey=skey, dst=dst, dkey=dkey, sc=sc):
                    b = bank("pc")
                    for c in range(8):
                        P.op("pe", lambda e, c=c: e.transpose(out=psb[b][0:64, c * 128:(c + 1) * 128],
                                                              in_=src[:, c * 64:(c + 1) * 64], identity=ident_bf),
                             reads=[skey, "ident_bf"], writes=[("ps", b)])
                    P.op("act", lambda e: e.activation(out=dst, in_=psb[b][0:64, :].rearrange("p (c k) -> p c k", k=128),
                                                       func=ACT.Copy, scale=sc), writes=[dkey, ("ps", b)])
                A(t_tok)

            n0 = tt8 * 8
            gcJ = gc_t[:, n0:n0 + 8, h].unsqueeze(2).to_broadcast([64, 8, 64])
            bJ = beta_t[:, n0:n0 + 8, h].unsqueeze(2).to_broadcast([64, 8, 64])
            hold = {}

            split = [len(tasks)]

            def t_gates():
                P.op("pool", lambda e: e.tensor_tensor(out=E1, in0=v3(gcB[ui][0:64, :]), in1=gcJ, op=ALU.subtract),
                     reads=[("gcB", ui), "gc_t"], writes=["E1"])
                P.op("pool", lambda e: e.tensor_tensor(out=E1, in0=E1, in1=m8(mask_incl), op=ALU.add),
                     reads=["E1", "cst"], writes=["E1"])
                P.op("act", lambda e: e.activation(out=GT, in_=E1, func=ACT.Exp), reads=["E1"], writes=["GT"])
                P.op("pool", lambda e: e.tensor_tensor(out=E2, in0=v3(gclB[ui]), in1=gcJ, op=ALU.subtract),
                     reads=[("gclB", ui), "gc_t"], writes=["E2"])
                P.op("pool", lambda e: e.tensor_tensor(out=E2, in0=E2, in1=m8(mask_strict), op=ALU.add),
                     reads=["E2", "cst"], writes=["E2"])
                P.op("act", lambda e: e.activation(out=GTb, in_=E2, func=ACT.Exp), reads=["E2"], writes=["GTb"])
            A(t_gates)

            def t_kkqk():
                bkk, bqk = bank("pre"), bank("pre")
                for c in range(8):
                    cs = slice(c * 64, (c + 1) * 64)
                    P.op("pe", lambda e, cs=cs: e.matmul(ps[bkk][0:64, cs], lhsT=khT[bi][:, cs], rhs=khT[bi][:, cs],
                                                         start=True, stop=True), reads=[("khT", bi)], writes=[("ps", bkk)])
                for c in range(8):
                    cs = slice(c * 64, (c + 1) * 64)
                    P.op("pe", lambda e, cs=cs: e.matmul(ps[bqk][0:64, cs], lhsT=khT[bi][:, cs], rhs=qhT[bi][:, cs],
                                                         start=True, stop=True), reads=[("khT", bi), ("qhT", bi)], writes=[("ps", bqk)])
                P.op("dve", lambda e: e.tensor_tensor(out=AqkT[bi], in0=v3(ps[bqk][0:64, :]), in1=GT, op=ALU.mult),
                     reads=["GT"], writes=[("AqkT", bi), ("ps", bqk)])
                P.op("dve", lambda e: e.scalar_tensor_tensor(out=Pk[0], in0=v3(ps[bkk][0:64, :]), scalar=-1.0, in1=GTb,
                                                             op0=ALU.mult, op1=ALU.mult),
                     reads=["GTb"], writes=[("Pk", 0), ("ps", bkk)])
            A(t_kkqk)

            def t_pt():
                b = bank("pre")
                for c in range(8):
                    P.op("pe", lambda e, c=c: e.transpose(out=psb[b][0:64, c * 64:(c + 1) * 64], in_=Pk[0][:, c, :],
                                                          identity=ident_bf[0:64, 0:64]),
                         reads=[("Pk", 0), "ident_bf"], writes=[("ps", b)])
                P.op("act", lambda e: e.activation(out=fl(PkT[0]), in_=psb[b][0:64, 0:512], func=ACT.Copy),
                     writes=[("PkT", 0), ("ps", b)])
                P.op("pool", lambda e: e.tensor_tensor(out=Xb[0], in0=Pk[0], in1=m8(identf[0:64, 0:64]), op=ALU.add),
                     reads=[("Pk", 0), "cst"], writes=[("Xb", 0)])
            A(t_pt)

            for lvl in range(5):
                cur = lvl % 2
                nxt = 1 - cur

                def t_sq(lvl=lvl, cur=cur, nxt=nxt):
                    if lvl < 4:
                        ba = bank("pre")
                        for c in range(8):
                            cs = slice(c * 64, (c + 1) * 64)
                            P.op("pe", lambda e, c=c, cs=cs: e.matmul(ps[ba][0:64, cs], lhsT=PkT[cur][:, c, :], rhs=Pk[cur][:, c, :],
                                                                      start=True, stop=True),
                                 reads=[("Pk", cur), ("PkT", cur)], writes=[("ps", ba)])
                    bt = bank("pre")
                    for c in range(8):
                        cs = slice(c * 64, (c + 1) * 64)
                        P.op("pe", lambda e, c=c, cs=cs: e.matmul(ps[bt][0:64, cs], lhsT=Pk[cur][:, c, :], rhs=PkT[cur][:, c, :],
                                                                  start=True, stop=True),
                             reads=[("Pk", cur), ("PkT", cur)], writes=[("ps", bt)])
                    if lvl < 4:
                        P.op("act", lambda e: e.activation(out=fl(Pk[nxt]), in_=ps[ba][0:64, :], func=ACT.Copy),
                             writes=[("Pk", nxt), ("ps", ba)])
                    P.op("dve", lambda e: e.tensor_copy(out=fl(PkT[nxt]), in_=ps[bt][0:64, :]),
                         writes=[("PkT", nxt), ("ps", bt)])
                A(t_sq)

                def t_x(lvl=lvl, cur=cur, nxt=nxt):
                    bx = bank("pre")
                    for c in range(8):
                        cs = slice(c * 64, (c + 1) * 64)
                        P.op("pe", lambda e, c=c, cs=cs: e.matmul(ps[bx][0:64, cs], lhsT=PkT[nxt][:, c, :], rhs=Xb[cur][:, c, :],
                                                                  start=True, stop=True),
                             reads=[("PkT", nxt), ("Xb", cur)], writes=[("ps", bx)])
                    P.op("dve", lambda e: e.tensor_tensor(out=fl(Xb[nxt]), in0=ps[bx][0:64, :], in1=fl(Xb[cur]), op=ALU.add),
                         reads=[("Xb", cur)], writes=[("Xb", nxt), ("ps", bx)])
                    if lvl == 4:
                        P.op("pool", lambda e: e.tensor_tensor(out=TT, in0=Xb[nxt], in1=bJ, op=ALU.mult),
                             reads=[("Xb", nxt), "beta_t"], writes=["TT"])
                A(t_x)

            ngJ = negeg_t[:, n0:n0 + 8, h].unsqueeze(2).to_broadcast([64, 8, 128])
            wdJ = wdec_t[:, n0:n0 + 8, h].unsqueeze(2).to_broadcast([64, 8, 128])

            def t_kg():
                P.op("pool", lambda e: e.tensor_tensor(out=Kgn, in0=Ktok[bi], in1=ngJ, op=ALU.mult),
                     reads=[("Ktok", bi), "negeg_t"], writes=["Kgn"])
                P.op("pool", lambda e: e.tensor_tensor(out=Kd[bi], in0=Ktok[bi], in1=wdJ, op=ALU.mult),
                     reads=[("Ktok", bi), "wdec_t"], writes=[("Kd", bi)])
            tasks.insert(len(tasks) - 6, t_kg)

            def t_w():
                b = bank("pre")
                for c in range(8):
                    P.op("pe", lambda e, c=c: e.matmul(ps[b][:, c * 64:(c + 1) * 64], lhsT=Kgn[:, c, :], rhs=TT[:, c, :],
                                                       start=True, stop=True),
                         reads=["Kgn", "TT"], writes=[("ps", b)])
                P.op("act", lambda e: e.activation(out=fl(WnT[bi]), in_=ps[b][:, :], func=ACT.Copy),
                     writes=[("WnT", bi), ("ps", b)])
            A(t_w)

            for half in range(2):
                def t_u(half=half):
                    b = bank("pre")
                    for c4 in range(4):
                        c = half * 4 + c4
                        P.op("pe", lambda e, c=c, c4=c4: e.matmul(ps[b][0:64, c4 * 128:(c4 + 1) * 128], lhsT=TT[:, c, :],
                                                                  rhs=Vtok[bi][:, c, :], start=True, stop=True),
                             reads=["TT", ("Vtok", bi)], writes=[("ps", b)])
                    P.op("dve", lambda e: e.tensor_copy(out=Ubf[bi][:, half * 4:half * 4 + 4, :],
                                                        in_=ps[b][0:64, :].rearrange("p (c k) -> p c k", k=128)),
                         writes=[("Ubf", bi), ("ps", b)])
                A(t_u)
            return tasks[:split[0]], tasks[split[0]:]

        def stage3_tasks(st):
            h, tt8 = st // 8, st % 8
            bi = st % NB2
            b3 = st % 3
            t0 = tt8 * 512
            n0 = tt8 * 8
            tasks = []
            A = tasks.append
            hold = {}

            def t_begin():
                hold["bo"] = bank("po")
            A(t_begin)
            for c in range(8):
                n = n0 + c
                cs = slice(c * 64, (c + 1) * 64)
                ri = n % 2
                first = (n == 0)

                def t_a(c=c, n=n, cs=cs, ri=ri, first=first):
                  with P.fds(pe=128, act=128, dve=128):
                    b1 = bank("q1")
                    P.op("pe", lambda e: e.matmul(ps[b1][0:64, 0:128], lhsT=ident_bf[0:64, 0:64], rhs=Ubf[bi][:, c, :],
                                                  start=True, stop=first),
                         reads=["ident_bf", ("Ubf", bi)], writes=[("ps", b1)])
                    if not first:
                        P.op("pe", lambda e: e.matmul(ps[b1][0:64, 0:128], lhsT=WnT[bi][:, c, :], rhs=S_b, start=False, stop=True),
                             reads=[("WnT", bi), "S_b"], writes=[("ps", b1)])
                    P.op("act", lambda e: e.activation(out=vnew[ri], in_=ps[b1][0:64, 0:128], func=ACT.Copy),
                         writes=[("vnew", ri), ("ps", b1)])
                A(t_a)

                def t_c(c=c, n=n, cs=cs, ri=ri, first=first):
                  with P.fds(pe=128, act=128, dve=128):
                    bo = hold["bo"]
                    b4 = bank("q4")
                    P.op("pe", lambda e: e.matmul(ps[b4][:, 0:128], lhsT=Kd[bi][:, c, :], rhs=vnew[ri], start=True, stop=True),
                         reads=[("Kd", bi), ("vnew", ri)], writes=[("ps", b4)])
                    if not first:
                        P.op("pe", lambda e: e.matmul(ps[bo][:, cs], lhsT=S_b, rhs=qgT[b3][:, cs], start=True, stop=False),
                             reads=["S_b", ("qgT", b3)], writes=[("ps", bo)])
                    P.op("pe", lambda e: e.matmul(ps[bo][:, cs], lhsT=vnew[ri], rhs=AqkT[bi][:, c, :], start=first, stop=True),
                         reads=[("vnew", ri), ("AqkT", bi)], writes=[("ps", bo)])
                    if first:
                        P.op("dve", lambda e: e.tensor_copy(out=S_b, in_=ps[b4][:, 0:128]), writes=["S_b", ("ps", b4)])
                        P.op("dve", lambda e: e.tensor_copy(out=S_f, in_=ps[b4][:, 0:128]), writes=["S_f", ("ps", b4)])
                    else:
                        P.op("dve", lambda e: e.scalar_tensor_tensor(
                            out=S_b, in0=S_f, scalar=dl_t[:, n, h:h + 1], in1=ps[b4][:, 0:128], op0=ALU.mult, op1=ALU.add),
                            reads=["S_f", "dl_t"], writes=["S_b", ("ps", b4)])
                        P.op("dve", lambda e: e.scalar_tensor_tensor(
                            out=S_f, in0=S_f, scalar=dl_t[:, n, h:h + 1], in1=ps[b4][:, 0:128], op0=ALU.mult, op1=ALU.add),
                            reads=["S_f", "dl_t"], writes=["S_f", ("ps", b4)])
                A(t_c)

            def t_epi():
                bo = hold["bo"]
                oi = st % 2
                P.op("act", lambda e: e.activation(out=oraw, in_=ps[bo][:, :], func=ACT.Copy), writes=["oraw", ("ps", bo)])
                if st == 0:
                    dump("khT0", khT[bi], [("khT", bi)])
                    dump("oraw0", oraw, ["oraw"])
                P.op("act", lambda e: e.activation(out=osq, in_=oraw, func=ACT.Square), reads=["oraw"], writes=["osq"])
                b = bank("pc")
                P.op("pe", lambda e: e.matmul(ps[b][:, :], lhsT=ones_bf, rhs=osq, start=True, stop=True),
                     reads=["osq", "ones_bf"], writes=[("ps", b)], fd=512)
                rsqrt_from_psum(b, orst, "orst", 1.0 / 128)
                P.op("dve", lambda e: e.scalar_tensor_tensor(out=oraw, in0=oraw, scalar=dnw[:, 0:1], in1=orst,
                                                             op0=ALU.mult, op1=ALU.mult),
                     reads=["oraw", "orst", "dnw"], writes=["oraw"])
                P.op("dve", lambda e: e.scalar_tensor_tensor(out=oa[oi], in0=oraw, scalar=0.5, in1=zsT[b3],
                                                             op0=ALU.mult, op1=ALU.mult),
                     reads=["oraw", ("zsT", b3)], writes=[("oa", oi)])
                P.dma(lambda q: q.dma_start(out=oa_scr[h, :, t0:t0 + 512], in_=oa[oi]),
                      reads=[("oa", oi)], writes=[("oa_scr", h, tt8)])
            A(t_epi)
            return tasks

        load_head_w(0)
        NSTEP = 64

        def merge(ta, tb):
            out = []
            na, nb_ = len(ta), len(tb)
            jj = 0
            for ii, t in enumerate(ta):
                out.append(t)
                tgt = (nb_ * (ii + 1) + na - 1) // max(na, 1)
                while jj < min(tgt, nb_):
                    out.append(tb[jj])
                    jj += 1
            out.extend(tb[jj:])
            return out

        s1 = {}
        s2 = {}
        for st in range(NSTEP):
            s1[st], s2[st] = None, None

        def get12(st):
            if st >= NSTEP:
                return [], []
            return stage12_tasks(st)

        a1, a2 = get12(0)
        for t in a1:
            t()
        b1_, b2_ = get12(1)
        for t in merge(a2, b1_):
            t()
        pend2 = b2_
        for st in range(NSTEP):
            t3 = stage3_tasks(st)
            n1, n2 = get12(st + 2)
            filler = merge(pend2, n1) if len(pend2) >= len(n1) else merge(n1, pend2)
            pend2 = n2
            for t in merge(t3, filler):
                t()
        cursor[0] = ph

    if stop_after not in ("a0", "b"):
        phase_a()
        P.fence()
        dump("oa_scr", oa_scr, [])

    def phase_c0():
        ph = cursor[0]
        sg = [alloc([512], BF16) for _ in range(4)]
        cnt = 0
        for c16 in range(16):
            off = (O_GA + c16 * 128) if c16 < 8 else (O_GB + (c16 - 8) * 128)
            wt, wk = load_w(w_in, off)
            for tt8 in range(8):
                b = proj_tile(wt, wk, 128, tt8, grp="all")
                i = cnt % 4
                cnt += 1
                P.op("act", lambda e, b=b, i=i: e.activation(out=sg[i], in_=ps[b][:, :], func=ACT.Sigmoid),
                     writes=[("sg", i), ("ps", b)])
                P.dma(lambda q, i=i, c16=c16, tt8=tt8: q.dma_start(out=sg_scr[c16, :, tt8 * 512:(tt8 + 1) * 512], in_=sg[i]),
                      reads=[("sg", i)], writes=[("sg_scr", tt8)])
        cursor[0] = ph

    def phase_c1():
      with P.fds(pe=512):
        ph = cursor[0]
        wodn = alloc([8, 1024], BF16)
        wodil = alloc([4, 1024], BF16)
        wout = alloc([8, 1024], BF16)
        fnw_b = alloc([D], F32)
        for (dst, src, kc, key) in ((wodn, w_odn, 8, "wodn"), (wodil, w_odil, 4, "wodil"), (wout, w_out, 8, "wout")):
            for half in range(2):
                P.dma(lambda q, dst=dst, src=src, kc=kc, half=half: q.dma_start(
                    out=dst[:, 0:kc, half * 512:(half + 1) * 512],
                    in_=src[:, half * 512:(half + 1) * 512].rearrange("(k p) c -> p k c", p=128)),
                    writes=[(key, half)], q="pool")
        P.dma(lambda q: q.dma_start(out=fnw_b, in_=fnw_d.partition_broadcast(128)), writes=["fnw_b"])
        oat = [alloc([8, 512], BF16) for _ in range(2)]
        obt = [alloc([4, 512], BF16) for _ in range(2)]
        sgt = [alloc([16, 512], BF16) for _ in range(2)]
        mT = alloc([8, 512], BF16)
        m1 = [alloc([512], F32) for _ in range(2)]
        m2 = [alloc([512], F32) for _ in range(2)]
        xr = [alloc([D], F32) for _ in range(2)]
        xo = [alloc([D], F32) for _ in range(2)]
        yo = [alloc([D], F32) for _ in range(2)]
        ss = [alloc([1], F32) for _ in range(2)]
        rs = [alloc([1], F32) for _ in range(2)]
        junk2 = alloc([D], BF16)
        cnt = 0
        for tt8 in range(8):
            i = tt8 % 2
            sl = slice(tt8 * 512, (tt8 + 1) * 512)
            P.dma(lambda q, i=i, sl=sl: q.dma_start(out=oat[i], in_=oa_scr[:, :, sl].rearrange("h p t -> p h t")),
                  reads=[("oa_scr", hh, tt8) for hh in range(8)], writes=[("oat", i)])
            P.dma(lambda q, i=i, sl=sl: q.dma_start(out=obt[i], in_=ob_scr[:, :, sl].rearrange("h p t -> p h t")),
                  reads=[("ob_scr", hh, tt8) for hh in range(4)], writes=[("obt", i)])
            P.dma(lambda q, i=i, sl=sl: q.dma_start(out=sgt[i], in_=sg_scr[:, :, sl].rearrange("h p t -> p h t")),
                  reads=[("sg_scr", tt8)], writes=[("sgt", i)])
            for c in range(8):
                cs = slice(c * 128, (c + 1) * 128)
                ba = bank("all")
                for k in range(8):
                    P.op("pe", lambda e, b=ba, k=k, cs=cs, i=i: e.matmul(ps[b][:, :], lhsT=wodn[:, k, cs], rhs=oat[i][:, k, :],
                                                                         start=(k == 0), stop=(k == 7)),
                         reads=[("wodn", c // 4), ("oat", i)], writes=[("ps", ba)])
                bb = bank("all")
                for k in range(4):
                    P.op("pe", lambda e, b=bb, k=k, cs=cs, i=i: e.matmul(ps[b][:, :], lhsT=wodil[:, k, cs], rhs=obt[i][:, k, :],
                                                                         start=(k == 0), stop=(k == 3)),
                         reads=[("wodil", c // 4), ("obt", i)], writes=[("ps", bb)])
                mi = cnt % 2
                cnt += 1
                P.op("dve", lambda e, b=ba, mi=mi, i=i, c=c: e.tensor_tensor(out=m1[mi], in0=ps[b][:, :], in1=sgt[i][:, c, :], op=ALU.mult),
                     reads=[("sgt", i)], writes=[("m1", mi), ("ps", ba)])
                P.op("dve", lambda e, b=bb, mi=mi, i=i, c=c: e.tensor_tensor(out=m2[mi], in0=ps[b][:, :], in1=sgt[i][:, 8 + c, :], op=ALU.mult),
                     reads=[("sgt", i)], writes=[("m2", mi), ("ps", bb)])
                P.op("pool", lambda e, mi=mi, c=c: e.tensor_tensor(out=mT[:, c, :], in0=m1[mi], in1=m2[mi], op=ALU.add),
                     reads=[("m1", mi), ("m2", mi)], writes=[("mT", c)])
            for sub in range(4):
                tok0 = tt8 * 512 + sub * 128
                xi = sub % 2
                P.dma(lambda q, xi=xi, tok0=tok0: q.dma_start(out=xr[xi], in_=x_d[tok0:tok0 + 128, :]), writes=[("xr", xi)])
                for half in range(2):
                    b = bank("all")
                    hs = slice(half * 512, (half + 1) * 512)
                    for c in range(8):
                        P.op("pe", lambda e, b=b, c=c, sub=sub, hs=hs: e.matmul(
                            ps[b][:, :], lhsT=mT[:, c, sub * 128:(sub + 1) * 128], rhs=wout[:, c, hs], start=(c == 0), stop=(c == 7)),
                            reads=[("mT", c), ("wout", half)], writes=[("ps", b)])
                    P.op("dve", lambda e, b=b, xi=xi, hs=hs: e.tensor_tensor(out=xo[xi][:, hs], in0=ps[b][:, :], in1=xr[xi][:, hs], op=ALU.add),
                         reads=[("xr", xi)], writes=[("xo", xi, half), ("ps", b)])
                P.op("pool", lambda e, xi=xi: e.memset(ss[xi], 0.0), writes=[("ss", xi)])
                P.op("act", lambda e, xi=xi: e.activation(out=junk2, in_=xo[xi], func=ACT.Square, accum_out=ss[xi]),
                     reads=[("xo", xi, 0), ("xo", xi, 1), ("ss", xi)], writes=["junk2", ("ss", xi)])
                P.op("act", lambda e, xi=xi: e.activation(out=ss[xi], in_=ss[xi], func=ACT.Sqrt, scale=1.0 / D, bias=EPS),
                     reads=[("ss", xi)], writes=[("ss", xi)])
                P.op("dve", lambda e, xi=xi: e.reciprocal(out=rs[xi], in_=ss[xi]), reads=[("ss", xi)], writes=[("rs", xi)])
                P.op("dve", lambda e, xi=xi: e.scalar_tensor_tensor(out=yo[xi], in0=xo[xi], scalar=rs[xi], in1=fnw_b,
                                                                    op0=ALU.mult, op1=ALU.mult),
                     reads=[("xo", xi, 0), ("xo", xi, 1), ("rs", xi), "fnw_b"], writes=[("yo", xi)])
                P.dma(lambda q, xi=xi, tok0=tok0: q.dma_start(out=out_d[tok0:tok0 + 128, :], in_=yo[xi]),
                      reads=[("yo", xi)], writes=[("out", tok0)])
        cursor[0] = ph

    if stop_after is None:
        phase_c0()
        P.fence()
        dump("sg_scr", sg_scr, [])
        cursor[0] = 0
        phase_c1()

    if RESCHEDULE:
        P.sim_time = P.reschedule()
    P.emit(final_wait_ops=[o for o in P.dma_last if o is not None])
    return nc


def _consts():
    c = np.zeros((128, 128 + 12 * 256 + 256), np.float32)
    c[:, 0:128] = np.eye(128, dtype=np.float32)
    slopes = (2.0 ** (-8.0 * np.arange(1, 13, dtype=np.float32) / 12)).reshape(3, 4)
    jk = np.arange(128)[:, None]
    iq = np.arange(128)[None, :]
    for gi in range(3):
        for h in range(4):
            s = slopes[gi, h] * DIL[gi]
            d0 = (iq - jk).astype(np.float32)
            b0 = np.where(iq >= jk, -s * d0, NEG)
            d1 = (128 + iq - jk).astype(np.float32)
            b1 = np.where(iq <= jk, -s * d1, NEG)
            g = gi * 4 + h
            c[:, 128 + g * 256:128 + g * 256 + 128] = b0
            c[:, 128 + g * 256 + 128:128 + g * 256 + 256] = b1
    MK = 128 + 3072
    j = np.arange(64)[:, None]
    i = np.arange(64)[None, :]
    c[0:64, MK:MK + 64] = np.where(i >= j, 0.0, NEG)
    c[0:64, MK + 64:MK + 128] = np.where(i > j, 0.0, NEG)
    c[0:64, MK + 128:MK + 192] = (j <= i).astype(np.float32)
    c[0:64, MK + 192:MK + 256] = 1.0
    return c


_NC_CACHE = {}


def _host_inputs(x, norm_w, w_in, conv_w, a_log, dt_bias, dn_norm_w, w_o_dn, w_o_dil, w_out, final_norm_w):
    f = lambda a: np.ascontiguousarray(np.asarray(a, dtype=np.float32))
    cw = f(conv_w)[0].reshape(4, 24, 128).transpose(2, 1, 0).reshape(128, 96)
    shared = {
        "w_in": f(w_in)[0], "w_o_dn": f(w_o_dn)[0], "w_o_dil": f(w_o_dil)[0], "w_out": f(w_out)[0],
        "norm_w": f(norm_w).reshape(1, D), "final_norm_w": f(final_norm_w).reshape(1, D),
        "conv_w_l": np.ascontiguousarray(cw), "a_log": f(a_log).reshape(1, 8), "dt_bias": f(dt_bias).reshape(1, 8),
        "dn_norm_w_l": f(dn_norm_w).reshape(128, 1), "consts": _consts(),
    }
    xs = f(x)
    return [dict(shared, x=xs[b]) for b in range(xs.shape[0])]


def kernel(x, norm_w, w_in, conv_w, a_log, dt_bias, dn_norm_w, w_o_dn, w_o_dil, w_out, final_norm_w):
    in_maps = _host_inputs(x, norm_w, w_in, conv_w, a_log, dt_bias, dn_norm_w, w_o_dn, w_o_dil, w_out, final_norm_w)
    if "nc" not in _NC_CACHE:
        _NC_CACHE["nc"] = build_nc()
    res = run_bass_kernel_spmd(_NC_CACHE["nc"], in_maps, core_ids=list(range(len(in_maps))))
    return np.stack([np.asarray(r["out"], dtype=np.float32).reshape(T, D) for r in res.results], axis=0)
```

```python
import contextlib
import numpy as np
import concourse.bass as bass
import concourse.mybir as mybir
from concourse.bass_utils import run_bass_kernel_spmd

ACT = mybir.ActivationFunctionType
ALU = mybir.AluOpType
F32 = mybir.dt.float32
BF16 = mybir.dt.bfloat16

T = 4096
D = 1024
NEG = -30000.0
PIPELINE_A = True
RESCHEDULE = True
SCHED_IDENTITY = True
XLAT = 0.0
CPW = 0.0
EPS = 1e-6
O_QA, O_KA, O_VA, O_ZA, O_BA, O_QB, O_KB, O_VB, O_ZB, O_GA, O_GB = (
    0, 1024, 2048, 3072, 4096, 4112, 5648, 7184, 8720, 9232, 10256)
DIL = (1, 4, 16)


class _Op:
    __slots__ = ("eng", "fn", "deps", "signal", "ticket", "is_dma", "dsem", "dval", "odeps", "cost", "idx", "seg", "war", "pos", "tset")

    def __init__(self, eng, fn, is_dma=False):
        self.eng = eng
        self.fn = fn
        self.odeps = []
        self.cost = 0.0
        self.idx = 0
        self.seg = 0
        self.war = []
        self.pos = 0
        self.tset = None
        self.deps = []
        self.signal = False
        self.ticket = None
        self.is_dma = is_dma
        self.dsem = None
        self.dval = None


class _Res:
    __slots__ = ("w", "r", "rd")

    def __init__(self):
        self.w = None
        self.r = []
        self.rd = []


class Prog:
    ENGS = ("pe", "act", "dve", "pool", "sp")

    def __init__(self, nc, n_dma_sems=32):
        self.nc = nc
        self.streams = {e: [] for e in self.ENGS}
        self.res = {}
        self.n_dma_sems = n_dma_sems
        self.dma_cnt = [0] * n_dma_sems
        self.dma_last = [None] * n_dma_sems
        self.dma_rr = 0
        self.fence_ops = []
        self.fence_dma = []
        self.seg = 0
        self.all_ops = []
        self.fd_default = {}

    @contextlib.contextmanager
    def fds(self, **kw):
        old = dict(self.fd_default)
        self.fd_default.update(kw)
        try:
            yield
        finally:
            self.fd_default = old

    def fence(self):
        self.fence_dma.append([d for d in self.dma_last if d is not None])
        self.seg += 1

    def _r(self, k):
        r = self.res.get(k)
        if r is None:
            r = self.res[k] = _Res()
        return r

    def op(self, eng, fn, reads=(), writes=(), is_dma=False, fd=None, tset=None):
        o = _Op(eng, fn, is_dma)
        o.tset = tset
        fd = fd or self.fd_default.get(eng)
        if is_dma:
            o.cost = 0.15
        elif eng == "pe":
            o.cost = max(64, fd or 64) / 2400.0 + 0.004
        elif eng == "act":
            o.cost = (224 + (fd or 512)) / 1200.0
        elif eng == "dve":
            o.cost = (110 + (fd or 512)) / 960.0
        else:
            o.cost = 0.2 + (fd or 512) / 1000.0
        o.idx = len(self.all_ops)
        o.seg = self.seg
        self.all_ops.append(o)
        deps = []
        for k in reads:
            r = self._r(k)
            if r.w is not None:
                deps.append((r.w, "raw"))
        for k in writes:
            r = self._r(k)
            if r.w is not None:
                deps.append((r.w, "waw"))
            for rd in r.r:
                if rd is not o:
                    o.war.append(rd)
            for rd in r.rd:
                if rd is not o:
                    o.war.append(rd)
        if is_dma:
            i = self.dma_rr
            self.dma_rr = (i + 1) % self.n_dma_sems
            o.dsem = i
            self.dma_cnt[i] += 1
            o.dval = 16 * self.dma_cnt[i]
            if self.dma_last[i] is not None:
                deps.append((self.dma_last[i], "raw"))
            self.dma_last[i] = o
        seen = set()
        oseen = set()
        for d, kind in deps:
            if d is not o and id(d) not in oseen:
                oseen.add(id(d))
                o.odeps.append(d)
        for d in o.war:
            if id(d) not in oseen:
                oseen.add(id(d))
                o.odeps.append(d)
        for d, kind in deps:
            if d is o or id(d) in seen:
                continue
            if not d.is_dma and d.eng == eng and not is_dma:
                if eng == "pe":
                    continue
                if kind != "raw":
                    continue
            seen.add(id(d))
            d.signal = True
            o.deps.append(d)
        for k in reads:
            r = self._r(k)
            if is_dma:
                r.rd.append(o)
            else:
                r.r.append(o)
        for k in writes:
            r = self._r(k)
            r.w = o
            r.r = []
            r.rd = []
        self.streams[eng].append(o)
        return o

    def dma(self, fn, reads=(), writes=(), q="sp"):
        return self.op(q, fn, reads, writes, is_dma=True)

    def reschedule(self, dma_latency=3.0):
        import heapq
        ops = self.all_ops
        n = len(ops)
        succ = [[] for _ in range(n)]
        indeg = [0] * n
        for o in ops:
            for d in o.odeps:
                if d.seg == o.seg:
                    succ[d.idx].append(o.idx)
                    indeg[o.idx] += 1
        prio = [0.0] * n
        for i in range(n - 1, -1, -1):
            o = ops[i]
            m = 0.0
            for j in succ[i]:
                if prio[j] > m:
                    m = prio[j]
            prio[i] = m + (dma_latency if o.is_dma else o.cost)
        if SCHED_IDENTITY:
            prio = [float(n - i) + CPW * prio[i] for i in range(n)]
        new_streams = {e: [] for e in self.ENGS}
        now = 0.0
        cur_set = [None]
        free_at = {e: 0.0 for e in self.ENGS}
        nseg = self.seg + 1
        byseg = [[] for _ in range(nseg)]
        for o in ops:
            byseg[o.seg].append(o.idx)
        for sg in range(nseg):
            idxs = byseg[sg]
            ready = {e: [] for e in self.ENGS}
            for i in idxs:
                if indeg[i] == 0:
                    heapq.heappush(ready[ops[i].eng], (-prio[i], i))
            events = []
            done = 0
            tot = len(idxs)
            while done < tot:
                started = False
                for e in self.ENGS:
                    if free_at[e] <= now and ready[e]:
                        _, i = heapq.heappop(ready[e])
                        o = ops[i]
                        extra = 0.0
                        if e == "act":
                            if o.tset is not None and o.tset != cur_set[0]:
                                held = [(-prio[i], i)]
                                found = None
                                for _ in range(6):
                                    if not ready[e]:
                                        break
                                    c = heapq.heappop(ready[e])
                                    oc = ops[c[1]]
                                    if (oc.tset is None or oc.tset == cur_set[0]) and prio[c[1]] > prio[i] - 6.0:
                                        found = c
                                        break
                                    held.append(c)
                                for c in held:
                                    if found is not None or c[1] != i:
                                        heapq.heappush(ready[e], c)
                                if found is not None:
                                    i = found[1]
                                    o = ops[i]
                                else:
                                    cur_set[0] = o.tset
                                    extra = 1.3
                        new_streams[e].append(o)
                        free_at[e] = now + o.cost + extra
                        heapq.heappush(events, (now + (dma_latency if o.is_dma else o.cost + extra + XLAT), i))
                        started = True
                if started:
                    continue
                cand = []
                if events:
                    cand.append(events[0][0])
                for e in self.ENGS:
                    if ready[e] and free_at[e] > now:
                        cand.append(free_at[e])
                now = min(cand)
                while events and events[0][0] <= now:
                    _, i = heapq.heappop(events)
                    done += 1
                    for j in succ[i]:
                        indeg[j] -= 1
                        if indeg[j] == 0:
                            heapq.heappush(ready[ops[j].eng], (-prio[j], j))
        for e in self.ENGS:
            assert len(new_streams[e]) == len(self.streams[e])
        self.streams = new_streams
        return now

    def apply_fences(self):
        last = {}
        pos = {e: 0 for e in self.ENGS}
        for sg in range(1, self.seg + 1):
            for e in self.ENGS:
                st = self.streams[e]
                while pos[e] < len(st) and st[pos[e]].seg < sg:
                    if not st[pos[e]].is_dma:
                        last[e] = st[pos[e]]
                    pos[e] += 1
            for e in self.ENGS:
                st = self.streams[e]
                if pos[e] < len(st) and st[pos[e]].seg == sg:
                    o = st[pos[e]]
                    extra = [d for d in last.values()] + list(self.fence_dma[sg - 1])
                    have = set(id(d) for d in o.deps)
                    for d in extra:
                        if d is o or id(d) in have:
                            continue
                        if not d.is_dma and d.eng == e and e == "pe":
                            continue
                        d.signal = True
                        o.deps.append(d)

    def emit(self, final_wait_ops=()):
        nc = self.nc
        for e in self.ENGS:
            for p_, o in enumerate(self.streams[e]):
                o.pos = p_
        for o in self.all_ops:
            if not o.war:
                continue
            best = {}
            have = set(id(d) for d in o.deps)
            for d in o.war:
                if d.is_dma:
                    if id(d) not in have:
                        have.add(id(d))
                        d.signal = True
                        o.deps.append(d)
                    continue
                b = best.get(d.eng)
                if b is None or d.pos > b.pos:
                    best[d.eng] = d
            for e, d in best.items():
                if e == o.eng and not o.is_dma:
                    continue
                if id(d) in have:
                    continue
                d.signal = True
                o.deps.append(d)
        self.apply_fences()
        for e in self.ENGS:
            c = 0
            for o in self.streams[e]:
                if o.is_dma:
                    continue
                if o.signal:
                    c += 1
                    o.ticket = c
        with contextlib.ExitStack() as es:
            esem = {e: es.enter_context(nc.semaphore("s_" + e)) for e in self.ENGS}
            dsem = [es.enter_context(nc.semaphore("d_%d" % i)) for i in range(self.n_dma_sems)]
            block = es.enter_context(nc.Block())

            def run(e, engobj):
                waited = {}

                def wait_for(d):
                    if d.is_dma:
                        key, sem, val = ("d", d.dsem), dsem[d.dsem], d.dval
                    else:
                        key, sem, val = ("e", d.eng), esem[d.eng], d.ticket
                    if waited.get(key, 0) >= val:
                        return
                    waited[key] = val
                    engobj.wait_ge(sem, val)

                for o in self.streams[e]:
                    for d in o.deps:
                        wait_for(d)
                    ins = o.fn(engobj)
                    if o.is_dma:
                        ins.then_inc(dsem[o.dsem], 16)
                    elif o.signal:
                        ins.then_inc(esem[e], 1)
                if e == "sp":
                    for d in final_wait_ops:
                        wait_for(d)

            @block.tensor
            def _(eng):
                run("pe", eng)

            @block.scalar
            def _(eng):
                run("act", eng)

            @block.vector
            def _(eng):
                run("dve", eng)

            @block.gpsimd
            def _(eng):
                run("pool", eng)

            @block.sync
            def _(eng):
                run("sp", eng)


def build_nc(dbg=None, stop_after=None):
    dbg = dbg or {}
    nc = bass.Bass("TRN2", target_bir_lowering=False)
    dt = nc.dram_tensor
    x_d = dt("x", [T, D], F32, kind="ExternalInput").ap()
    w_in = dt("w_in", [D, 11280], F32, kind="ExternalInput").ap()
    w_odn = dt("w_o_dn", [1024, 1024], F32, kind="ExternalInput").ap()
    w_odil = dt("w_o_dil", [512, 1024], F32, kind="ExternalInput").ap()
    w_out = dt("w_out", [1024, 1024], F32, kind="ExternalInput").ap()
    normw_d = dt("norm_w", [1, D], F32, kind="ExternalInput").ap()
    fnw_d = dt("final_norm_w", [1, D], F32, kind="ExternalInput").ap()
    cw_d = dt("conv_w_l", [128, 96], F32, kind="ExternalInput").ap()
    alog_d = dt("a_log", [1, 8], F32, kind="ExternalInput").ap()
    dtb_d = dt("dt_bias", [1, 8], F32, kind="ExternalInput").ap()
    dnw_d = dt("dn_norm_w_l", [128, 1], F32, kind="ExternalInput").ap()
    cst_d = dt("consts", [128, 128 + 12 * 256 + 64 * 4], F32, kind="ExternalInput").ap()
    out_d = dt("out", [T, D], F32, kind="ExternalOutput").ap()
    ob_scr = dt("ob_scr", [4, 128, T], BF16).ap()
    oa_scr = dt("oa_scr", [8, 128, T], BF16).ap()
    sg_scr = dt("sg_scr", [16, 128, T], BF16).ap()
    gc_scr = dt("gc_scr", [8, T], F32).ap()
    gcl_scr = dt("gcl_scr", [8, T], F32).ap()
    dbg_out = {}
    for name, (shape, dtype) in dbg.items():
        dbg_out[name] = dt("dbg_" + name, list(shape), dtype, kind="ExternalOutput").ap()

    P = Prog(nc)
    ARENA = 212000
    arena = nc.alloc_sbuf_tensor("arena", [128, ARENA // 2], BF16)
    cursor = [0]

    def alloc(free_shape, dtype, parts=128):
        n = 1
        for s in free_shape:
            n *= s
        esz = 4 if dtype == F32 else 2
        nbytes = (n * esz + 63) // 64 * 64
        off = cursor[0]
        cursor[0] += nbytes
        assert cursor[0] <= ARENA, ("SBUF overflow", cursor[0])
        ap = arena[0:parts, off // 2: off // 2 + n * esz // 2]
        if dtype == F32:
            ap = ap.bitcast(F32)
        if len(free_shape) == 2:
            ap = ap.rearrange("p (a b) -> p a b", b=free_shape[1])
        elif len(free_shape) == 3:
            ap = ap.rearrange("p (a b c) -> p a b c", b=free_shape[1], c=free_shape[2])
        return ap

    ps = [nc.alloc_psum_tensor("ps%d" % i, [128, 512], F32) for i in range(8)]
    psb = [p[:].bitcast(BF16) for p in ps]
    bank_rr = {}

    def bank(group):
        lst = {"proj": (0, 1), "misc": (2,), "pc": (0, 1, 2), "pre": (3, 4), "q1": (5,), "q4": (6,), "po": (7,),
               "all": tuple(range(8)), "s": (3, 4), "a": (5, 7), "b": (6, 2)}[group]
        i = bank_rr.get(group, 0)
        bank_rr[group] = i + 1
        return lst[i % len(lst)]

    def dump(name, src_ap, reads):
        if name in dbg_out:
            P.dma(lambda q, a=src_ap, o=dbg_out[name]: q.dma_start(out=o, in_=a), reads=reads, writes=[("dbg", name)])

    hT = alloc([8, T], BF16)
    cst = alloc([128 + 12 * 256 + 256], F32)
    identf = cst[:, 0:128]
    alibi = cst[:, 128:128 + 3072].rearrange("p (g w) -> p g w", w=256)
    MK = 128 + 3072
    mask_incl = cst[0:64, MK:MK + 64]
    mask_strict = cst[0:64, MK + 64:MK + 128]
    triu_f = cst[0:64, MK + 128:MK + 192]
    ones64f = cst[0:64, MK + 192:MK + 256]
    ident_bf = alloc([128], BF16)
    ones_bf = alloc([128], BF16)
    ones_f = alloc([128], F32)
    mhalf = alloc([512], F32)
    cw = alloc([96], F32)
    dnw = alloc([1], F32)
    WB_N = 3
    wb = [alloc([8, 128], BF16) for _ in range(WB_N)]
    wb_rr = [0]
    beta_t = alloc([64, 8], F32, 64)
    gc_t = alloc([64, 8], F32, 64)
    negeg_t = alloc([64, 8], F32, 64)
    wdec_t = alloc([64, 8], F32, 64)
    dl_t = alloc([64, 8], F32)
    persist_end = cursor[0]

    P.dma(lambda q: q.dma_start(out=cst, in_=cst_d), writes=["cst"])
    P.dma(lambda q: q.dma_start(out=cw, in_=cw_d), writes=["cw"])
    P.dma(lambda q: q.dma_start(out=dnw, in_=dnw_d), writes=["dnw"])
    P.op("dve", lambda e: e.tensor_copy(out=ident_bf, in_=identf), reads=["cst"], writes=["ident_bf"])
    P.op("pool", lambda e: e.memset(ones_bf, 1.0), writes=["ones_bf"])
    P.op("pool", lambda e: e.memset(ones_f, 1.0), writes=["ones_f"])
    P.op("pool", lambda e: e.memset(mhalf, -0.5), writes=["mhalf"])

    def load_w(src, c0, ncols=128, kchunks=8, dst=None, key=None):
        if dst is None:
            i = wb_rr[0] % WB_N
            wb_rr[0] += 1
            dst, key = wb[i], ("wb", i)
        P.dma(lambda q, d=dst, s=src, c0=c0, n=ncols, kc=kchunks: q.dma_start(
            out=d[:, 0:kc, 0:n], in_=s[:, c0:c0 + n].rearrange("(k p) c -> p k c", p=128)),
            writes=[key], q="pool")
        return dst, key

    def hkeys(t0, t1):
        return [("hT", i) for i in range(t0 // 128, (t1 + 127) // 128)]

    def proj_tile(wt, wkey, ncols, tt8, grp="proj"):
        b = bank(grp)
        for k in range(8):
            P.op("pe", lambda e, b=b, k=k, wt=wt, n=ncols, tt8=tt8: e.matmul(
                ps[b][0:n, :], lhsT=wt[:, k, 0:n], rhs=hT[:, k, tt8 * 512:(tt8 + 1) * 512],
                start=(k == 0), stop=(k == 7)),
                reads=[wkey] + hkeys(tt8 * 512, tt8 * 512 + 512), writes=[("ps", b)], fd=512)
        return b

    ph = cursor[0]
    normw_b = alloc([D], F32)
    xs = [alloc([D], F32) for _ in range(2)]
    junk = alloc([D], BF16)
    xb = [alloc([D], BF16) for _ in range(2)]
    ss0 = [alloc([1], F32) for _ in range(2)]
    rs0 = [alloc([1], F32) for _ in range(2)]
    P.dma(lambda q: q.dma_start(out=normw_b, in_=normw_d.partition_broadcast(128)), writes=["normw_b"])
    for tt in range(32):
        i = tt % 2
        P.dma(lambda q, i=i, tt=tt: q.dma_start(out=xs[i], in_=x_d[tt * 128:(tt + 1) * 128, :]), writes=[("xs", i)])
        P.op("pool", lambda e, i=i: e.memset(ss0[i], 0.0), writes=[("ss0", i)])
        P.op("act", lambda e, i=i: e.activation(out=junk, in_=xs[i], func=ACT.Square, accum_out=ss0[i]),
             reads=[("xs", i), ("ss0", i)], writes=["junk", ("ss0", i)])
        P.op("dve", lambda e, i=i: e.tensor_scalar(out=ss0[i], in0=ss0[i], scalar1=1.0 / D, scalar2=EPS, op0=ALU.mult, op1=ALU.add),
             reads=[("ss0", i)], writes=[("ss0", i)])
        P.op("act", lambda e, i=i: e.activation(out=ss0[i], in_=ss0[i], func=ACT.Ln), reads=[("ss0", i)], writes=[("ss0", i)], tset="L", fd=1)
        P.op("act", lambda e, i=i: e.activation(out=rs0[i], in_=ss0[i], func=ACT.Exp, scale=-0.5), reads=[("ss0", i)], writes=[("rs0", i)], fd=1)
        P.op("dve", lambda e, i=i: e.scalar_tensor_tensor(out=xb[i], in0=xs[i], scalar=rs0[i], in1=normw_b,
                                                          op0=ALU.mult, op1=ALU.mult),
             reads=[("xs", i), ("rs0", i), "normw_b"], writes=[("xb", i)])
        b = bank("all")
        for k in range(8):
            P.op("pe", lambda e, b=b, k=k, i=i: e.transpose(out=psb[b][:, k * 128:(k + 1) * 128],
                                                           in_=xb[i][:, k * 128:(k + 1) * 128], identity=ident_bf),
                 reads=[("xb", i), "ident_bf"], writes=[("ps", b)])
        eng = "act" if tt % 2 == 0 else "dve"
        if eng == "act":
            fn = lambda e, b=b, tt=tt: e.activation(out=hT[:, :, tt * 128:(tt + 1) * 128],
                                                    in_=psb[b].rearrange("p (k t) -> p k t", t=128), func=ACT.Copy)
        else:
            fn = lambda e, b=b, tt=tt: e.tensor_copy(out=hT[:, :, tt * 128:(tt + 1) * 128],
                                                     in_=psb[b].rearrange("p (k t) -> p k t", t=128))
        P.op(eng, fn, writes=[("hT", tt), ("ps", b)])
    dump("hT", hT, hkeys(0, T))
    cursor[0] = ph
    P.fence()

    def phase_a0():
        ph = cursor[0]
        w16, w16k = load_w(w_in, O_BA, 16)
        alog_b = alloc([8], F32, 64)
        dtb_b = alloc([8], F32, 64)
        P.dma(lambda q: q.dma_start(out=alog_b, in_=alog_d.partition_broadcast(64)), writes=["alog_b"])
        P.dma(lambda q: q.dma_start(out=dtb_b, in_=dtb_d.partition_broadcast(64)), writes=["dtb_b"])
        Gsb = alloc([64, 16], F32, 64)
        names = ["xa", "ax", "ee", "ll", "sp", "g", "lb", "gcl", "tmp"]
        A = {n: alloc([64, 8], F32, 64) for n in names}
        glast = alloc([64, 8], F32)
        tb = alloc([8, 64], F32, 64)
        for half in range(2):
            b = bank("all")
            for n in range(32 * half, 32 * half + 32):
                for k in range(8):
                    P.op("pe", lambda e, b=b, n=n, k=k: e.matmul(
                        ps[b][0:64, (n % 32) * 16:(n % 32) * 16 + 16], lhsT=hT[:, k, n * 64:(n + 1) * 64],
                        rhs=w16[:, k, 0:16], start=(k == 0), stop=(k == 7)),
                        reads=[w16k] + hkeys(n * 64, n * 64 + 64), writes=[("ps", b)])
            P.op("act", lambda e, b=b, half=half: e.activation(
                out=Gsb[:, 32 * half:32 * half + 32, :], in_=ps[b][0:64, :].rearrange("p (n c) -> p n c", c=16),
                func=ACT.Copy), writes=["Gsb", ("ps", b)])
        bb = Gsb[:, :, 0:8]
        aa = Gsb[:, :, 8:16]
        bc = lambda v: v.unsqueeze(1).to_broadcast([64, 64, 8])
        P.op("act", lambda e: e.activation(out=beta_t, in_=bb, func=ACT.Sigmoid), reads=["Gsb"], writes=["beta_t"])
        P.op("act", lambda e: e.activation(out=A["lb"], in_=beta_t, func=ACT.Ln), reads=["beta_t"], writes=["lb"])
        P.op("dve", lambda e: e.tensor_tensor(out=A["xa"], in0=aa, in1=bc(dtb_b), op=ALU.add),
             reads=["Gsb", "dtb_b"], writes=["xa"])
        P.op("act", lambda e: e.activation(out=A["ax"], in_=A["xa"], func=ACT.Abs), reads=["xa"], writes=["ax"])
        P.op("act", lambda e: e.activation(out=A["ee"], in_=A["ax"], func=ACT.Exp, scale=-1.0), reads=["ax"], writes=["ee"])
        P.op("act", lambda e: e.activation(out=A["ll"], in_=A["ee"], func=ACT.Ln, bias=1.0), reads=["ee"], writes=["ll"])
        P.op("dve", lambda e: e.scalar_tensor_tensor(out=A["sp"], in0=A["xa"], scalar=0.0, in1=A["ll"],
                                                     op0=ALU.max, op1=ALU.add), reads=["xa", "ll"], writes=["sp"])
        P.op("act", lambda e: e.activation(out=alog_b, in_=alog_b, func=ACT.Exp), reads=["alog_b"], writes=["alog_b"])
        P.op("dve", lambda e: e.scalar_tensor_tensor(out=A["g"], in0=A["sp"], scalar=-1.0, in1=bc(alog_b),
                                                     op0=ALU.mult, op1=ALU.mult), reads=["sp", "alog_b"], writes=["g"])
        gflat = A["g"].rearrange("p n h -> p (n h)")
        b1 = bank("all")
        P.op("pe", lambda e: e.matmul(ps[b1][0:64, :], lhsT=triu_f, rhs=gflat, start=True, stop=True),
             reads=["g", "cst"], writes=[("ps", b1)])
        b2 = bank("all")
        P.op("pe", lambda e: e.matmul(ps[b2][:, :], lhsT=ones_f[0:64, :], rhs=gflat, start=True, stop=True),
             reads=["g", "ones_f"], writes=[("ps", b2)])
        fl = lambda v: v.rearrange("p n h -> p (n h)")
        P.op("act", lambda e: e.activation(out=fl(gc_t), in_=ps[b1][0:64, :], func=ACT.Copy), writes=["gc_t", ("ps", b1)])
        P.op("dve", lambda e: e.tensor_copy(out=fl(glast), in_=ps[b2][:, :]), writes=["glast", ("ps", b2)])
        P.op("act", lambda e: e.activation(out=negeg_t, in_=gc_t, func=ACT.Exp), reads=["gc_t"], writes=["negeg_t"])
        P.op("dve", lambda e: e.tensor_scalar(out=negeg_t, in0=negeg_t, scalar1=-1.0, scalar2=None, op0=ALU.mult),
             reads=["negeg_t"], writes=["negeg_t"])
        P.op("dve", lambda e: e.tensor_tensor(out=A["tmp"], in0=glast[0:64], in1=gc_t, op=ALU.subtract),
             reads=["glast", "gc_t"], writes=["tmp"])
        P.op("act", lambda e: e.activation(out=wdec_t, in_=A["tmp"], func=ACT.Exp), reads=["tmp"], writes=["wdec_t"])
        P.op("act", lambda e: e.activation(out=dl_t, in_=glast, func=ACT.Exp), reads=["glast"], writes=["dl_t"])
        P.op("dve", lambda e: e.tensor_tensor(out=A["gcl"], in0=gc_t, in1=A["lb"], op=ALU.add),
             reads=["gc_t", "lb"], writes=["gcl"])
        for nm, src, scr in (("gc", gc_t, gc_scr), ("gcl", A["gcl"], gcl_scr)):
            b = bank("all")
            for h in range(8):
                P.op("pe", lambda e, b=b, h=h, src=src: e.transpose(out=ps[b][0:64, h * 64:(h + 1) * 64],
                                                                   in_=src[:, :, h], identity=identf[0:64, 0:64]),
                     reads=["gc_t" if nm == "gc" else "gcl", "cst"], writes=[("ps", b)])
            P.op("dve", lambda e, b=b: e.tensor_copy(out=tb, in_=ps[b][0:64, :].rearrange("p (h c) -> p h c", c=64)),
                 writes=["tb", ("ps", b)])
            P.dma(lambda q, scr=scr: q.dma_start(out=scr.rearrange("h (n c) -> n h c", c=64), in_=tb),
                  reads=["tb"], writes=[nm + "_scr"])
        dump("gc_t", gc_t, ["gc_t"])
        dump("beta_t", beta_t, ["beta_t"])
        dump("g_t", A["g"], ["g"])
        cursor[0] = ph

    phase_a0()
    P.fence()

    def interleave(ta, tb):
        na, nb_ = len(ta), len(tb)
        j = 0
        for i, t in enumerate(ta):
            t()
            tgt = (nb_ * (i + 1) + na - 1) // max(na, 1)
            while j < min(tgt, nb_):
                tb[j]()
                j += 1
        while j < nb_:
            tb[j]()
            j += 1

    def phase_b():
        ph = cursor[0]
        qTs = [alloc([T], BF16) for _ in range(2)]
        kTs = [alloc([T], BF16) for _ in range(2)]
        vsbs = [alloc([32, 128], BF16) for _ in range(2)]
        vT = alloc([T], BF16)
        acc_n = alloc([T], F32)
        acc_d = alloc([T], F32)
        NS = 3
        s_sb = [alloc([256], F32) for _ in range(NS)]
        p_sb = [alloc([256], BF16) for _ in range(4)]
        zs = [alloc([512], F32) for _ in range(2)]
        rc = [alloc([512], F32) for _ in range(2)]
        ob = [alloc([512], BF16) for _ in range(2)]
        mone = alloc([512], F32)
        P.op("pool", lambda e: e.memset(mone, -1.0), writes=["mone"])
        scale = 128 ** -0.5
        jobs = [(h, gi) for h in range(4) for gi in range(3)]
        cnt = [0]

        def proj_tasks(jn):
            h, gi = jobs[jn]
            bi = jn % 2
            qT, kT, v_sb = qTs[bi], kTs[bi], vsbs[bi]
            d = DIL[gi]
            L = T // d
            nb = L // 128
            M = 512 // d
            tasks = []
            hold = {}

            def t_w():
                hold["q"] = load_w(w_in, O_QB + gi * 512 + h * 128)
                hold["k"] = load_w(w_in, O_KB + gi * 512 + h * 128)
                hold["v"] = load_w(w_in, O_VB + gi * 512 + h * 128)
            tasks.append(t_w)
            q3 = qT.rearrange("p (r m) -> p r m", r=d)
            k3 = kT.rearrange("p (r m) -> p r m", r=d)
            for tt8 in range(8):
                def t_q(tt8=tt8):
                    wq, wqk = hold["q"]
                    b = proj_tile(wq, wqk, 128, tt8)
                    P.op("act", lambda e: e.activation(out=qT[:, tt8 * 512:(tt8 + 1) * 512], in_=ps[b][:, :],
                                                       func=ACT.Copy, scale=scale),
                         writes=[("qT", bi), ("ps", b)])
                tasks.append(t_q)

                def t_k(tt8=tt8):
                    wk, wkk = hold["k"]
                    b = proj_tile(wk, wkk, 128, tt8)
                    P.op("dve", lambda e: e.tensor_copy(out=kT[:, tt8 * 512:(tt8 + 1) * 512], in_=ps[b][:, :]),
                         writes=[("kT", bi), ("ps", b)])
                tasks.append(t_k)

                def t_v(tt8=tt8):
                    wv, wvk = hold["v"]
                    b = proj_tile(wv, wvk, 128, tt8)
                    P.op("act", lambda e: e.activation(out=vT[:, tt8 * 512:(tt8 + 1) * 512], in_=ps[b][:, :], func=ACT.Copy),
                         writes=[("vT", tt8), ("ps", b)])
                tasks.append(t_v)
            for t8 in range(4):
                def t_vt(t8=t8):
                    b = bank("proj")
                    for s in range(8):
                        tid = t8 * 8 + s
                        r, j = tid // nb, tid % nb
                        t0 = 128 * j * d + r
                        P.op("pe", lambda e, s=s, t0=t0: e.transpose(
                            out=psb[b][:, s * 128:(s + 1) * 128], in_=vT[:, t0:t0 + 127 * d + 1:d], identity=ident_bf),
                            reads=[("vT", i) for i in range((128 * j * d) // 512, (128 * (j + 1) * d + 511) // 512)] + ["ident_bf"],
                            writes=[("ps", b)])
                    P.op("dve", lambda e: e.tensor_copy(
                        out=v_sb[:, t8 * 8:(t8 + 1) * 8, :], in_=psb[b][:, :].rearrange("p (s c) -> p s c", c=128)),
                        writes=[("v_sb", bi), ("ps", b)])
                tasks.append(t_vt)
            return tasks

        def core_tasks(jn):
            h, gi = jobs[jn]
            bi = jn % 2
            qT, kT, v_sb = qTs[bi], kTs[bi], vsbs[bi]
            d = DIL[gi]
            L = T // d
            nb = L // 128
            gidx = gi * 4 + h
            tasks = []
            st = {"bn": None, "bd": None}
            pis = {}

            def tok(r, j0, nblk):
                a = (128 * j0) * d + r
                return slice(a, a + (128 * nblk - 1) * d + 1, d)

            def t_qk(r, j):
              with P.fds(pe=256, act=256, dve=256):
                W = 2 if j + 1 < nb else 1
                bs = bank("s")
                P.op("pe", lambda e: e.matmul(ps[bs][:, 0:128 * W], lhsT=kT[:, tok(r, j, 1)], rhs=qT[:, tok(r, j, W)],
                                              start=True, stop=True),
                     reads=[("qT", bi), ("kT", bi)], writes=[("ps", bs)])
                si = cnt[0] % NS
                pi = cnt[0] % 4
                cnt[0] += 1
                pis[(r, j)] = pi
                P.op("dve", lambda e: e.tensor_tensor(out=s_sb[si][:, 0:128 * W], in0=ps[bs][:, 0:128 * W],
                                                      in1=alibi[:, gidx, 0:128 * W], op=ALU.add),
                     reads=["cst"], writes=[("s_sb", si), ("ps", bs)])
                P.op("act", lambda e: e.activation(out=p_sb[pi][:, 0:128 * W], in_=s_sb[si][:, 0:128 * W], func=ACT.Exp),
                     reads=[("s_sb", si)], writes=[("p_sb", pi)])

            def t_pv(r, j):
              with P.fds(pe=128):
                if j % 4 == 0:
                    st["bn"], st["bd"] = bank("a"), bank("b")
                bn, bd = st["bn"], st["bd"]
                pi = pis[(r, j)]
                prev = pis[(r, j - 1)] if j > 0 else None
                col = (j % 4) * 128
                tid = r * nb + j
                for (bk, is_den, lk) in ((bn, False, ("v_sb", bi)), (bd, True, "ones_bf")):
                    first = True
                    if j > 0:
                        lp = ones_bf if is_den else v_sb[:, tid - 1, :]
                        P.op("pe", lambda e, bk=bk, lp=lp: e.matmul(
                            ps[bk][:, col:col + 128], lhsT=lp, rhs=p_sb[prev][:, 128:256], start=True, stop=False),
                            reads=[lk, ("p_sb", prev)], writes=[("ps", bk)])
                        first = False
                    lc = ones_bf if is_den else v_sb[:, tid, :]
                    P.op("pe", lambda e, bk=bk, lc=lc, first=first: e.matmul(
                        ps[bk][:, col:col + 128], lhsT=lc, rhs=p_sb[pi][:, 0:128], start=first, stop=True),
                        reads=[lk, ("p_sb", pi)], writes=[("ps", bk)])
                if j % 4 == 3 or j == nb - 1:
                    n0 = (j // 4) * 4
                    nq = (j - n0 + 1) * 128
                    sl = slice(128 * n0 * d + r, 128 * n0 * d + r + (nq - 1) * d + 1, d)
                    for (bk, acc, key, eng) in ((bn, acc_n, "acc_n", "act"), (bd, acc_d, "acc_d", "dve")):
                        if gi == 0:
                            if eng == "act":
                                P.op("act", lambda e, bk=bk, acc=acc: e.activation(
                                    out=acc[:, sl], in_=ps[bk][:, 0:nq], func=ACT.Copy), writes=[key, ("ps", bk)])
                            else:
                                P.op("dve", lambda e, bk=bk, acc=acc: e.tensor_copy(
                                    out=acc[:, sl], in_=ps[bk][:, 0:nq]), writes=[key, ("ps", bk)])
                        else:
                            P.op("dve", lambda e, bk=bk, acc=acc: e.tensor_tensor(
                                out=acc[:, sl], in0=ps[bk][:, 0:nq], in1=acc[:, sl], op=ALU.add),
                                reads=[key], writes=[key, ("ps", bk)])

            seq = [(r, j) for r in range(d) for j in range(nb)]
            SK = 2
            for idx in range(len(seq) + SK):
                def t_blk(idx=idx):
                    if idx < len(seq):
                        t_qk(*seq[idx])
                    if idx >= SK:
                        t_pv(*seq[idx - SK])
                tasks.append(t_blk)
            if gi == 2:
                hold = {}

                def t_wz():
                    hold["z"] = load_w(w_in, O_ZB + h * 128)
                tasks.append(t_wz)
                for tt8 in range(8):
                    def t_fin(tt8=tt8):
                        wz, wzk = hold["z"]
                        i = tt8 % 2
                        sl = slice(tt8 * 512, (tt8 + 1) * 512)
                        b = proj_tile(wz, wzk, 128, tt8)
                        P.op("act", lambda e: e.activation(out=zs[i], in_=ps[b][:, :], func=ACT.Tanh, scale=0.5),
                             writes=[("zs", i), ("ps", b)], tset="T")
                        P.op("dve", lambda e: e.scalar_tensor_tensor(out=zs[i], in0=zs[i], scalar=1.0, in1=ps[b][:, :],
                                                                     op0=ALU.add, op1=ALU.mult),
                             reads=[("zs", i)], writes=[("zs", i), ("ps", b)])
                        P.op("dve", lambda e: e.reciprocal(out=rc[i], in_=acc_d[:, sl]), reads=["acc_d"], writes=[("rc", i)], fd=1500)
                        P.op("dve", lambda e: e.tensor_tensor(out=rc[i], in0=rc[i], in1=acc_n[:, sl], op=ALU.mult),
                             reads=["acc_n", ("rc", i)], writes=[("rc", i)])
                        P.op("dve", lambda e: e.scalar_tensor_tensor(out=ob[i], in0=rc[i], scalar=0.5, in1=zs[i],
                                                                     op0=ALU.mult, op1=ALU.mult),
                             reads=[("rc", i), ("zs", i)], writes=[("ob", i)])
                        P.dma(lambda q: q.dma_start(out=ob_scr[h, :, sl], in_=ob[i]),
                              reads=[("ob", i)], writes=[("ob_scr", h, tt8)])
                    tasks.append(t_fin)
            return tasks

        for t in proj_tasks(0):
            t()
        for jn in range(len(jobs)):
            interleave(core_tasks(jn), proj_tasks(jn + 1) if jn + 1 < len(jobs) else [])
        cursor[0] = ph

    if stop_after != "a0":
        phase_b()
        P.fence()
        dump("ob_scr", ob_scr, [])

    def phase_a():
        ph = cursor[0]
        NB2 = 2
        upre = {t: [alloc([516], BF16) for _ in range(2)] for t in "qkv"}
        for t in "qkv":
            P.op("pool", lambda e, t=t: e.memset(upre[t][1][:, 512:515], 0.0), writes=[("upre", t, 1)])
        dg = alloc([12, 128], BF16)
        th = {t: alloc([512], F32) for t in "qkvz"}
        yq = alloc([512], F32)
        yk = alloc([512], F32)
        sq = {t: alloc([512], BF16) for t in "qk"}
        rin = {t: alloc([512], F32) for t in "qk"}
        vTt = alloc([512], BF16)
        khT = [alloc([512], BF16) for _ in range(NB2)]
        qhT = [alloc([512], BF16) for _ in range(NB2)]
        qgT = [alloc([512], BF16) for _ in range(3)]
        zsT = [alloc([512], BF16) for _ in range(3)]
        Ktok = [alloc([8, 128], BF16, 64) for _ in range(NB2)]
        Vtok = [alloc([8, 128], BF16, 64) for _ in range(NB2)]
        AqkT = [alloc([8, 64], BF16, 64) for _ in range(NB2)]
        TT = alloc([8, 64], BF16, 64)
        gcB = [alloc([512], F32) for _ in range(2)]
        gclB = [alloc([512], F32, 64) for _ in range(2)]
        egB = alloc([512], F32)
        E1 = alloc([8, 64], F32, 64)
        E2 = alloc([8, 64], F32, 64)
        GT = alloc([8, 64], F32, 64)
        GTb = alloc([8, 64], F32, 64)
        Pk = [alloc([8, 64], BF16, 64) for _ in range(2)]
        PkT = [alloc([8, 64], BF16, 64) for _ in range(2)]
        Xb = [alloc([8, 64], BF16, 64) for _ in range(2)]
        S_f = alloc([128], F32)
        S_b = alloc([128], BF16)
        vnew = [alloc([128], BF16, 64) for _ in range(2)]
        WnT = [alloc([8, 64], BF16) for _ in range(NB2)]
        Ubf = [alloc([8, 128], BF16, 64) for _ in range(NB2)]
        Kd = [alloc([8, 128], BF16, 64) for _ in range(NB2)]
        Kgn = alloc([8, 128], BF16, 64)
        oraw = alloc([512], F32)
        osq = alloc([512], BF16)
        orst = alloc([512], F32)
        oa = [alloc([512], BF16) for _ in range(2)]
        wsets = [[alloc([8, 128], BF16) for _ in range(4)] for _ in range(2)]

        def load_head_w(h):
            ws = wsets[h % 2]
            for ti, off in enumerate((O_QA, O_KA, O_VA, O_ZA)):
                load_w(w_in, off + h * 128, dst=ws[ti], key=("wh", h % 2, ti))

        m8 = lambda m: m.unsqueeze(1).to_broadcast([64, 8, 64])
        v3 = lambda a: a.rearrange("p (c i) -> p c i", i=64)
        fl = lambda a: a.rearrange("p c i -> p (c i)")

        def rsqrt_from_psum(b, dst, key, scale):
            P.op("act", lambda e: e.activation(out=dst, in_=ps[b][:, :], func=ACT.Ln, scale=scale, bias=EPS),
                 writes=[key, ("ps", b)], tset="L")
            P.op("act", lambda e: e.activation(out=dst, in_=dst, func=ACT.Exp, scale=-0.5), reads=[key], writes=[key])

        def stage12_tasks(st):
            h, tt8 = st // 8, st % 8
            bi = st % NB2
            b3 = st % 3
            ui = st % 2
            t0 = tt8 * 512
            ws = wsets[h % 2]
            tasks = []
            A = tasks.append

            def t_pre():
                if tt8 == 0:
                    if h + 1 < 8:
                        load_head_w(h + 1)
                    for ti in range(3):
                        for kk in range(4):
                            g = ti * 8 + h
                            P.op("dve", lambda e, ti=ti, kk=kk, g=g: e.tensor_scalar(
                                out=dg[:, ti * 4 + kk, :], in0=identf, scalar1=cw[:, g * 4 + kk:g * 4 + kk + 1], scalar2=None,
                                op0=ALU.mult), reads=["cst", "cw"], writes=["dg"])
                P.dma(lambda q: q.dma_start(out=gcB[ui], in_=gc_scr[h, t0:t0 + 512].partition_broadcast(128)),
                      reads=["gc_scr"], writes=[("gcB", ui)])
                P.dma(lambda q: q.dma_start(out=gclB[ui], in_=gcl_scr[h, t0:t0 + 512].partition_broadcast(64)),
                      reads=["gcl_scr"], writes=[("gclB", ui)])
            A(t_pre)

            def proj_split(wt, wkey, hb):
                def mk(k0):
                    def f():
                        if k0 == 0:
                            hb["b"] = bank("pc")
                        b = hb["b"]
                        for k in range(k0, k0 + 2):
                            P.op("pe", lambda e, k=k: e.matmul(ps[b][:, :], lhsT=wt[:, k, :], rhs=hT[:, k, t0:t0 + 512],
                                                               start=(k == 0), stop=(k == 7)),
                                 reads=[wkey] + hkeys(t0, t0 + 512), writes=[("ps", b)], fd=512)
                    return f
                for k0 in (0, 2, 4):
                    A(mk(k0))
                return mk(6)

            hbz = {}
            last_z = proj_split(ws[3], ("wh", h % 2, 3), hbz)

            def t_projz():
                last_z()
                b = hbz["b"]
                P.op("act", lambda e: e.activation(out=th["z"], in_=ps[b][:, :], func=ACT.Tanh, scale=0.5),
                     writes=[("th", "z"), ("ps", b)], tset="T")
                P.op("dve", lambda e: e.scalar_tensor_tensor(out=zsT[b3], in0=th["z"], scalar=1.0, in1=ps[b][:, :],
                                                             op0=ALU.add, op1=ALU.mult),
                     reads=[("th", "z")], writes=[("zsT", b3), ("ps", b)])
            A(t_projz)

            for ti, t in enumerate("qkv"):
                hbp = {}
                last_p = proj_split(ws[ti], ("wh", h % 2, ti), hbp)

                def t_proj(ti=ti, t=t, hbp=hbp, last_p=last_p):
                    u = upre[t][ui]
                    up = upre[t][1 - ui]
                    last_p()
                    b = hbp["b"]
                    if tt8 == 0:
                        P.op("pool", lambda e: e.memset(u[:, 0:3], 0.0), writes=[("upre", t, ui)])
                    else:
                        P.op("pool", lambda e: e.tensor_copy(out=u[:, 0:3], in_=up[:, 512:515]),
                             reads=[("upre", t, 1 - ui)], writes=[("upre", t, ui)])
                    P.op("act", lambda e: e.activation(out=u[:, 3:515], in_=ps[b][:, :], func=ACT.Copy),
                         writes=[("upre", t, ui), ("ps", b)])
                A(t_proj)

            for ti, t in enumerate("qkv"):
                hbc = {}

                def t_conv0(ti=ti, t=t, hbc=hbc):
                    u = upre[t][ui]
                    hbc["b"] = b = bank("pc")
                    for kk in range(2):
                        P.op("pe", lambda e, kk=kk: e.matmul(ps[b][:, :], lhsT=dg[:, ti * 4 + kk, :], rhs=u[:, kk:kk + 512],
                                                             start=(kk == 0), stop=False),
                             reads=["dg", ("upre", t, ui)], writes=[("ps", b)], fd=512)
                A(t_conv0)

                def t_conv(ti=ti, t=t, hbc=hbc):
                    u = upre[t][ui]
                    b = hbc["b"]
                    for kk in range(2, 4):
                        P.op("pe", lambda e, kk=kk: e.matmul(ps[b][:, :], lhsT=dg[:, ti * 4 + kk, :], rhs=u[:, kk:kk + 512],
                                                             start=False, stop=(kk == 3)),
                             reads=["dg", ("upre", t, ui)], writes=[("ps", b)], fd=512)
                    P.op("act", lambda e: e.activation(out=th[t], in_=ps[b][:, :], func=ACT.Tanh, scale=0.5),
                         writes=[("th", t), ("ps", b)], tset="T")
                    dst = {"q": yq, "k": yk, "v": vTt}[t]
                    P.op("dve", lambda e: e.scalar_tensor_tensor(out=dst, in0=th[t], scalar=1.0, in1=ps[b][:, :],
                                                                 op0=ALU.add, op1=ALU.mult),
                         reads=[("th", t)], writes=[("y", t), ("ps", b)])
                A(t_conv)

            for t, y in (("q", yq), ("k", yk)):
                def t_norm(t=t, y=y):
                    P.op("act", lambda e: e.activation(out=sq[t], in_=y, func=ACT.Square), reads=[("y", t)], writes=[("sq", t)])
                    b = bank("pc")
                    P.op("pe", lambda e: e.matmul(ps[b][:, :], lhsT=ones_bf, rhs=sq[t], start=True, stop=True),
                         reads=[("sq", t), "ones_bf"], writes=[("ps", b)], fd=512)
                    rsqrt_from_psum(b, rin[t], ("rin", t), 0.25)
                A(t_norm)

            def t_hat():
                P.op("dve", lambda e: e.scalar_tensor_tensor(out=khT[bi], in0=yk, scalar=0.5, in1=rin["k"],
                                                             op0=ALU.mult, op1=ALU.mult),
                     reads=[("y", "k"), ("rin", "k")], writes=[("khT", bi)])
                P.op("dve", lambda e: e.scalar_tensor_tensor(out=qhT[bi], in0=yq, scalar=0.5 * 128 ** -0.5, in1=rin["q"],
                                                             op0=ALU.mult, op1=ALU.mult),
                     reads=[("y", "q"), ("rin", "q")], writes=[("qhT", bi)])
                P.op("act", lambda e: e.activation(out=egB, in_=gcB[ui], func=ACT.Exp), reads=[("gcB", ui)], writes=["egB"])
                P.op("pool", lambda e: e.tensor_tensor(out=qgT[b3], in0=qhT[bi], in1=egB, op=ALU.mult),
                     reads=[("qhT", bi), "egB"], writes=[("qgT", b3)])
            A(t_hat)

            for (src, skey, dst, dkey, sc) in ((khT[bi], ("khT", bi), Ktok[bi], ("Ktok", bi), 1.0),
                                               (vTt, ("y", "v"), Vtok[bi], ("Vtok", bi), 0.5)):
                def t_tok(src=src, skey=skey, dst=dst, dkey=dkey, sc=sc):
                    b = bank("pc")
                    for c in range(8):
                        P.op("pe", lambda e, c=c: e.transpose(out=psb[b][0:64, c * 128:(c + 1) * 128],
                                                              in_=src[:, c * 64:(c + 1) * 64], identity=ident_bf),
                             reads=[skey, "ident_bf"], writes=[("ps", b)])
                    P.op("act", lambda e: e.activation(out=dst, in_=psb[b][0:64, :].rearrange("p (c k) -> p c k", k=128),
                                                       func=ACT.Copy, scale=sc), writes=[dkey, ("ps", b)])
                A(t_tok)

            n0 = tt8 * 8
            gcJ = gc_t[:, n0:n0 + 8, h].unsqueeze(2).to_broadcast([64, 8, 64])
            bJ = beta_t[:, n0:n0 + 8, h].unsqueeze(2).to_broadcast([64, 8, 64])
            hold = {}

            split = [len(tasks)]

            def t_gates():
                P.op("pool", lambda e: e.tensor_tensor(out=E1, in0=v3(gcB[ui][0:64, :]), in1=gcJ, op=ALU.subtract),
                     reads=[("gcB", ui), "gc_t"], writes=["E1"])
                P.op("pool", lambda e: e.tensor_tensor(out=E1, in0=E1, in1=m8(mask_incl), op=ALU.add),
                     reads=["E1", "cst"], writes=["E1"])
                P.op("act", lambda e: e.activation(out=GT, in_=E1, func=ACT.Exp), reads=["E1"], writes=["GT"])
                P.op("pool", lambda e: e.tensor_tensor(out=E2, in0=v3(gclB[ui]), in1=gcJ, op=ALU.subtract),
                     reads=[("gclB", ui), "gc_t"], writes=["E2"])
                P.op("pool", lambda e: e.tensor_tensor(out=E2, in0=E2, in1=m8(mask_strict), op=ALU.add),
                     reads=["E2", "cst"], writes=["E2"])
                P.op("act", lambda e: e.activation(out=GTb, in_=E2, func=ACT.Exp), reads=["E2"], writes=["GTb"])
            A(t_gates)

            def t_kkqk():
                bkk, bqk = bank("pre"), bank("pre")
                for c in range(8):
                    cs = slice(c * 64, (c + 1) * 64)
                    P.op("pe", lambda e, cs=cs: e.matmul(ps[bkk][0:64, cs], lhsT=khT[bi][:, cs], rhs=khT[bi][:, cs],
                                                         start=True, stop=True), reads=[("khT", bi)], writes=[("ps", bkk)])
                for c in range(8):
                    cs = slice(c * 64, (c + 1) * 64)
                    P.op("pe", lambda e, cs=cs: e.matmul(ps[bqk][0:64, cs], lhsT=khT[bi][:, cs], rhs=qhT[bi][:, cs],
                                                         start=True, stop=True), reads=[("khT", bi), ("qhT", bi)], writes=[("ps", bqk)])
                P.op("dve", lambda e: e.tensor_tensor(out=AqkT[bi], in0=v3(ps[bqk][0:64, :]), in1=GT, op=ALU.mult),
                     reads=["GT"], writes=[("AqkT", bi), ("ps", bqk)])
                P.op("dve", lambda e: e.scalar_tensor_tensor(out=Pk[0], in0=v3(ps[bkk][0:64, :]), scalar=-1.0, in1=GTb,
                                                             op0=ALU.mult, op1=ALU.mult),
                     reads=["GTb"], writes=[("Pk", 0), ("ps", bkk)])
            A(t_kkqk)

            def t_pt():
                b = bank("pre")
                for c in range(8):
                    P.op("pe", lambda e, c=c: e.transpose(out=psb[b][0:64, c * 64:(c + 1) * 64], in_=Pk[0][:, c, :],
                                                          identity=ident_bf[0:64, 0:64]),
                         reads=[("Pk", 0), "ident_bf"], writes=[("ps", b)])
                P.op("act", lambda e: e.activation(out=fl(PkT[0]), in_=psb[b][0:64, 0:512], func=ACT.Copy),
                     writes=[("PkT", 0), ("ps", b)])
                P.op("pool", lambda e: e.tensor_tensor(out=Xb[0], in0=Pk[0], in1=m8(identf[0:64, 0:64]), op=ALU.add),
                     reads=[("Pk", 0), "cst"], writes=[("Xb", 0)])
            A(t_pt)

            for lvl in range(5):
                cur = lvl % 2
                nxt = 1 - cur

                def t_sq(lvl=lvl, cur=cur, nxt=nxt):
                    if lvl < 4:
                        ba = bank("pre")
                        for c in range(8):
                            cs = slice(c * 64, (c + 1) * 64)
                            P.op("pe", lambda e, c=c, cs=cs: e.matmul(ps[ba][0:64, cs], lhsT=PkT[cur][:, c, :], rhs=Pk[cur][:, c, :],
                                                                      start=True, stop=True),
                                 reads=[("Pk", cur), ("PkT", cur)], writes=[("ps", ba)])
                    bt = bank("pre")
                    for c in range(8):
                        cs = slice(c * 64, (c + 1) * 64)
                        P.op("pe", lambda e, c=c, cs=cs: e.matmul(ps[bt][0:64, cs], lhsT=Pk[cur][:, c, :], rhs=PkT[cur][:, c, :],
                                                                  start=True, stop=True),
                             reads=[("Pk", cur), ("PkT", cur)], writes=[("ps", bt)])
                    if lvl < 4:
                        P.op("act", lambda e: e.activation(out=fl(Pk[nxt]), in_=ps[ba][0:64, :], func=ACT.Copy),
                             writes=[("Pk", nxt), ("ps", ba)])
                    P.op("dve", lambda e: e.tensor_copy(out=fl(PkT[nxt]), in_=ps[bt][0:64, :]),
                         writes=[("PkT", nxt), ("ps", bt)])
                A(t_sq)

                def t_x(lvl=lvl, cur=cur, nxt=nxt):
                    bx = bank("pre")
                    for c in range(8):
                        cs = slice(c * 64, (c + 1) * 64)
                        P.op("pe", lambda e, c=c, cs=cs: e.matmul(ps[bx][0:64, cs], lhsT=PkT[nxt][:, c, :], rhs=Xb[cur][:, c, :],
                                                                  start=True, stop=True),
                             reads=[("PkT", nxt), ("Xb", cur)], writes=[("ps", bx)])
                    P.op("dve", lambda e: e.tensor_tensor(out=fl(Xb[nxt]), in0=ps[bx][0:64, :], in1=fl(Xb[cur]), op=ALU.add),
                         reads=[("Xb", cur)], writes=[("Xb", nxt), ("ps", bx)])
                    if lvl == 4:
                        P.op("pool", lambda e: e.tensor_tensor(out=TT, in0=Xb[nxt], in1=bJ, op=ALU.mult),
                             reads=[("Xb", nxt), "beta_t"], writes=["TT"])
                A(t_x)

            ngJ = negeg_t[:, n0:n0 + 8, h].unsqueeze(2).to_broadcast([64, 8, 128])
            wdJ = wdec_t[:, n0:n0 + 8, h].unsqueeze(2).to_broadcast([64, 8, 128])

            def t_kg():
                P.op("pool", lambda e: e.tensor_tensor(out=Kgn, in0=Ktok[bi], in1=ngJ, op=ALU.mult),
                     reads=[("Ktok", bi), "negeg_t"], writes=["Kgn"])
                P.op("pool", lambda e: e.tensor_tensor(out=Kd[bi], in0=Ktok[bi], in1=wdJ, op=ALU.mult),
                     reads=[("Ktok", bi), "wdec_t"], writes=[("Kd", bi)])
            tasks.insert(len(tasks) - 6, t_kg)

            def t_w():
                b = bank("pre")
                for c in range(8):
                    P.op("pe", lambda e, c=c: e.matmul(ps[b][:, c * 64:(c + 1) * 64], lhsT=Kgn[:, c, :], rhs=TT[:, c, :],
                                                       start=True, stop=True),
                         reads=["Kgn", "TT"], writes=[("ps", b)])
                P.op("act", lambda e: e.activation(out=fl(WnT[bi]), in_=ps[b][:, :], func=ACT.Copy),
                     writes=[("WnT", bi), ("ps", b)])
            A(t_w)

            for half in range(2):
                def t_u(half=half):
                    b = bank("pre")
                    for c4 in range(4):
                        c = half * 4 + c4
                        P.op("pe", lambda e, c=c, c4=c4: e.matmul(ps[b][0:64, c4 * 128:(c4 + 1) * 128], lhsT=TT[:, c, :],
                                                                  rhs=Vtok[bi][:, c, :], start=True, stop=True),
                             reads=["TT", ("Vtok", bi)], writes=[("ps", b)])
                    P.op("dve", lambda e: e.tensor_copy(out=Ubf[bi][:, half * 4:half * 4 + 4, :],
                                                        in_=ps[b][0:64, :].rearrange("p (c k) -> p c k", k=128)),
                         writes=[("Ubf", bi), ("ps", b)])
                A(t_u)
            return tasks[:split[0]], tasks[split[0]:]

        def stage3_tasks(st):
            h, tt8 = st // 8, st % 8
            bi = st % NB2
            b3 = st % 3
            t0 = tt8 * 512
            n0 = tt8 * 8
            tasks = []
            A = tasks.append
            hold = {}

            def t_begin():
                hold["bo"] = bank("po")
            A(t_begin)
            for c in range(8):
                n = n0 + c
                cs = slice(c * 64, (c + 1) * 64)
                ri = n % 2
                first = (n == 0)

                def t_a(c=c, n=n, cs=cs, ri=ri, first=first):
                  with P.fds(pe=128, act=128, dve=128):
                    b1 = bank("q1")
                    P.op("pe", lambda e: e.matmul(ps[b1][0:64, 0:128], lhsT=ident_bf[0:64, 0:64], rhs=Ubf[bi][:, c, :],
                                                  start=True, stop=first),
                         reads=["ident_bf", ("Ubf", bi)], writes=[("ps", b1)])
                    if not first:
                        P.op("pe", lambda e: e.matmul(ps[b1][0:64, 0:128], lhsT=WnT[bi][:, c, :], rhs=S_b, start=False, stop=True),
                             reads=[("WnT", bi), "S_b"], writes=[("ps", b1)])
                    P.op("act", lambda e: e.activation(out=vnew[ri], in_=ps[b1][0:64, 0:128], func=ACT.Copy),
                         writes=[("vnew", ri), ("ps", b1)])
                A(t_a)

                def t_c(c=c, n=n, cs=cs, ri=ri, first=first):
                  with P.fds(pe=128, act=128, dve=128):
                    bo = hold["bo"]
                    b4 = bank("q4")
                    P.op("pe", lambda e: e.matmul(ps[b4][:, 0:128], lhsT=Kd[bi][:, c, :], rhs=vnew[ri], start=True, stop=True),
                         reads=[("Kd", bi), ("vnew", ri)], writes=[("ps", b4)])
                    if not first:
                        P.op("pe", lambda e: e.matmul(ps[bo][:, cs], lhsT=S_b, rhs=qgT[b3][:, cs], start=True, stop=False),
                             reads=["S_b", ("qgT", b3)], writes=[("ps", bo)])
                    P.op("pe", lambda e: e.matmul(ps[bo][:, cs], lhsT=vnew[ri], rhs=AqkT[bi][:, c, :], start=first, stop=True),
                         reads=[("vnew", ri), ("AqkT", bi)], writes=[("ps", bo)])
                    if first:
                        P.op("dve", lambda e: e.tensor_copy(out=S_b, in_=ps[b4][:, 0:128]), writes=["S_b", ("ps", b4)])
                        P.op("dve", lambda e: e.tensor_copy(out=S_f, in_=ps[b4][:, 0:128]), writes=["S_f", ("ps", b4)])
                    else:
                        P.op("dve", lambda e: e.scalar_tensor_tensor(
                            out=S_b, in0=S_f, scalar=dl_t[:, n, h:h + 1], in1=ps[b4][:, 0:128], op0=ALU.mult, op1=ALU.add),
                            reads=["S_f", "dl_t"], writes=["S_b", ("ps", b4)])
                        P.op("dve", lambda e: e.scalar_tensor_tensor(
                            out=S_f, in0=S_f, scalar=dl_t[:, n, h:h + 1], in1=ps[b4][:, 0:128], op0=ALU.mult, op1=ALU.add),
                            reads=["S_f", "dl_t"], writes=["S_f", ("ps", b4)])
                A(t_c)

            def t_epi():
                bo = hold["bo"]
                oi = st % 2
                P.op("act", lambda e: e.activation(out=oraw, in_=ps[bo][:, :], func=ACT.Copy), writes=["oraw", ("ps", bo)])
                if st == 0:
                    dump("khT0", khT[bi], [("khT", bi)])
                    dump("oraw0", oraw, ["oraw"])
                P.op("act", lambda e: e.activation(out=osq, in_=oraw, func=ACT.Square), reads=["oraw"], writes=["osq"])
                b = bank("pc")
                P.op("pe", lambda e: e.matmul(ps[b][:, :], lhsT=ones_bf, rhs=osq, start=True, stop=True),
                     reads=["osq", "ones_bf"], writes=[("ps", b)], fd=512)
                rsqrt_from_psum(b, orst, "orst", 1.0 / 128)
                P.op("dve", lambda e: e.scalar_tensor_tensor(out=oraw, in0=oraw, scalar=dnw[:, 0:1], in1=orst,
                                                             op0=ALU.mult, op1=ALU.mult),
                     reads=["oraw", "orst", "dnw"], writes=["oraw"])
                P.op("dve", lambda e: e.scalar_tensor_tensor(out=oa[oi], in0=oraw, scalar=0.5, in1=zsT[b3],
                                                             op0=ALU.mult, op1=ALU.mult),
                     reads=["oraw", ("zsT", b3)], writes=[("oa", oi)])
                P.dma(lambda q: q.dma_start(out=oa_scr[h, :, t0:t0 + 512], in_=oa[oi]),
                      reads=[("oa", oi)], writes=[("oa_scr", h, tt8)])
            A(t_epi)
            return tasks

        load_head_w(0)
        NSTEP = 64

        def merge(ta, tb):
            out = []
            na, nb_ = len(ta), len(tb)
            jj = 0
            for ii, t in enumerate(ta):
                out.append(t)
                tgt = (nb_ * (ii + 1) + na - 1) // max(na, 1)
                while jj < min(tgt, nb_):
                    out.append(tb[jj])
                    jj += 1
            out.extend(tb[jj:])
            return out

        s1 = {}
        s2 = {}
        for st in range(NSTEP):
            s1[st], s2[st] = None, None

        def get12(st):
            if st >= NSTEP:
                return [], []
            return stage12_tasks(st)

        a1, a2 = get12(0)
        for t in a1:
            t()
        b1_, b2_ = get12(1)
        for t in merge(a2, b1_):
            t()
        pend2 = b2_
        for st in range(NSTEP):
            t3 = stage3_tasks(st)
            n1, n2 = get12(st + 2)
            filler = merge(pend2, n1) if len(pend2) >= len(n1) else merge(n1, pend2)
            pend2 = n2
            for t in merge(t3, filler):
                t()
        cursor[0] = ph

    if stop_after not in ("a0", "b"):
        phase_a()
        P.fence()
        dump("oa_scr", oa_scr, [])

    def phase_c0():
        ph = cursor[0]
        sg = [alloc([512], BF16) for _ in range(4)]
        sgf = [alloc([512], F32) for _ in range(4)]
        cnt = 0
        for c16 in range(16):
            off = (O_GA + c16 * 128) if c16 < 8 else (O_GB + (c16 - 8) * 128)
            wt, wk = load_w(w_in, off)
            for tt8 in range(8):
                b = proj_tile(wt, wk, 128, tt8, grp="all")
                i = cnt % 4
                cnt += 1
                P.op("act", lambda e, b=b, i=i: e.activation(out=sgf[i], in_=ps[b][:, :], func=ACT.Tanh, scale=0.5),
                     writes=[("sgf", i), ("ps", b)], tset="T")
                P.op("dve", lambda e, i=i: e.tensor_scalar(out=sg[i], in0=sgf[i], scalar1=0.5, scalar2=0.5, op0=ALU.mult, op1=ALU.add),
                     reads=[("sgf", i)], writes=[("sg", i)])
                P.dma(lambda q, i=i, c16=c16, tt8=tt8: q.dma_start(out=sg_scr[c16, :, tt8 * 512:(tt8 + 1) * 512], in_=sg[i]),
                      reads=[("sg", i)], writes=[("sg_scr", tt8)])
        cursor[0] = ph

    def phase_c1():
      with P.fds(pe=512):
        ph = cursor[0]
        wodn = alloc([8, 1024], BF16)
        wodil = alloc([4, 1024], BF16)
        wout = alloc([8, 1024], BF16)
        fnw_b = alloc([D], F32)
        mhalf_c1 = alloc([1], F32)
        P.op("pool", lambda e: e.memset(mhalf_c1, -0.5), writes=["mhalf_c1"])
        for (dst, src, kc, key) in ((wodn, w_odn, 8, "wodn"), (wodil, w_odil, 4, "wodil"), (wout, w_out, 8, "wout")):
            for half in range(2):
                P.dma(lambda q, dst=dst, src=src, kc=kc, half=half: q.dma_start(
                    out=dst[:, 0:kc, half * 512:(half + 1) * 512],
                    in_=src[:, half * 512:(half + 1) * 512].rearrange("(k p) c -> p k c", p=128)),
                    writes=[(key, half)], q="pool")
        P.dma(lambda q: q.dma_start(out=fnw_b, in_=fnw_d.partition_broadcast(128)), writes=["fnw_b"])
        oat = [alloc([8, 512], BF16) for _ in range(2)]
        obt = [alloc([4, 512], BF16) for _ in range(2)]
        sgt = [alloc([16, 512], BF16) for _ in range(2)]
        mT = alloc([8, 512], BF16)
        m1 = [alloc([512], F32) for _ in range(2)]
        m2 = [alloc([512], F32) for _ in range(2)]
        xr = [alloc([D], F32) for _ in range(2)]
        xo = [alloc([D], F32) for _ in range(2)]
        yo = [alloc([D], F32) for _ in range(2)]
        ss = [alloc([1], F32) for _ in range(2)]
        rs = [alloc([1], F32) for _ in range(2)]
        junk2 = alloc([D], BF16)
        cnt = 0
        for tt8 in range(8):
            i = tt8 % 2
            sl = slice(tt8 * 512, (tt8 + 1) * 512)
            P.dma(lambda q, i=i, sl=sl: q.dma_start(out=oat[i], in_=oa_scr[:, :, sl].rearrange("h p t -> p h t")),
                  reads=[("oa_scr", hh, tt8) for hh in range(8)], writes=[("oat", i)])
            P.dma(lambda q, i=i, sl=sl: q.dma_start(out=obt[i], in_=ob_scr[:, :, sl].rearrange("h p t -> p h t")),
                  reads=[("ob_scr", hh, tt8) for hh in range(4)], writes=[("obt", i)])
            P.dma(lambda q, i=i, sl=sl: q.dma_start(out=sgt[i], in_=sg_scr[:, :, sl].rearrange("h p t -> p h t")),
                  reads=[("sg_scr", tt8)], writes=[("sgt", i)])
            for c in range(8):
                cs = slice(c * 128, (c + 1) * 128)
                ba = bank("all")
                for k in range(8):
                    P.op("pe", lambda e, b=ba, k=k, cs=cs, i=i: e.matmul(ps[b][:, :], lhsT=wodn[:, k, cs], rhs=oat[i][:, k, :],
                                                                         start=(k == 0), stop=(k == 7)),
                         reads=[("wodn", c // 4), ("oat", i)], writes=[("ps", ba)])
                bb = bank("all")
                for k in range(4):
                    P.op("pe", lambda e, b=bb, k=k, cs=cs, i=i: e.matmul(ps[b][:, :], lhsT=wodil[:, k, cs], rhs=obt[i][:, k, :],
                                                                         start=(k == 0), stop=(k == 3)),
                         reads=[("wodil", c // 4), ("obt", i)], writes=[("ps", bb)])
                mi = cnt % 2
                cnt += 1
                P.op("dve", lambda e, b=ba, mi=mi, i=i, c=c: e.tensor_tensor(out=m1[mi], in0=ps[b][:, :], in1=sgt[i][:, c, :], op=ALU.mult),
                     reads=[("sgt", i)], writes=[("m1", mi), ("ps", ba)])
                P.op("dve", lambda e, b=bb, mi=mi, i=i, c=c: e.tensor_tensor(out=m2[mi], in0=ps[b][:, :], in1=sgt[i][:, 8 + c, :], op=ALU.mult),
                     reads=[("sgt", i)], writes=[("m2", mi), ("ps", bb)])
                P.op("pool", lambda e, mi=mi, c=c: e.tensor_tensor(out=mT[:, c, :], in0=m1[mi], in1=m2[mi], op=ALU.add),
                     reads=[("m1", mi), ("m2", mi)], writes=[("mT", c)])
            for sub in range(4):
                tok0 = tt8 * 512 + sub * 128
                xi = sub % 2
                P.dma(lambda q, xi=xi, tok0=tok0: q.dma_start(out=xr[xi], in_=x_d[tok0:tok0 + 128, :]), writes=[("xr", xi)])
                for half in range(2):
                    b = bank("all")
                    hs = slice(half * 512, (half + 1) * 512)
                    for c in range(8):
                        P.op("pe", lambda e, b=b, c=c, sub=sub, hs=hs: e.matmul(
                            ps[b][:, :], lhsT=mT[:, c, sub * 128:(sub + 1) * 128], rhs=wout[:, c, hs], start=(c == 0), stop=(c == 7)),
                            reads=[("mT", c), ("wout", half)], writes=[("ps", b)])
                    P.op("dve", lambda e, b=b, xi=xi, hs=hs: e.tensor_tensor(out=xo[xi][:, hs], in0=ps[b][:, :], in1=xr[xi][:, hs], op=ALU.add),
                         reads=[("xr", xi)], writes=[("xo", xi, half), ("ps", b)])
                P.op("pool", lambda e, xi=xi: e.memset(ss[xi], 0.0), writes=[("ss", xi)])
                P.op("act", lambda e, xi=xi: e.activation(out=junk2, in_=xo[xi], func=ACT.Square, accum_out=ss[xi]),
                     reads=[("xo", xi, 0), ("xo", xi, 1), ("ss", xi)], writes=["junk2", ("ss", xi)])
                P.op("dve", lambda e, xi=xi: e.tensor_scalar(out=ss[xi], in0=ss[xi], scalar1=1.0 / D, scalar2=EPS, op0=ALU.mult, op1=ALU.add),
                     reads=[("ss", xi)], writes=[("ss", xi)])
                P.op("act", lambda e, xi=xi: e.activation(out=ss[xi], in_=ss[xi], func=ACT.Ln), reads=[("ss", xi)], writes=[("ss", xi)], tset="L", fd=1)
                P.op("act", lambda e, xi=xi: e.activation(out=rs[xi], in_=ss[xi], func=ACT.Exp, scale=-0.5), reads=[("ss", xi)], writes=[("rs", xi)], fd=1)
                P.op("dve", lambda e, xi=xi: e.scalar_tensor_tensor(out=yo[xi], in0=xo[xi], scalar=rs[xi], in1=fnw_b,
                                                                    op0=ALU.mult, op1=ALU.mult),
                     reads=[("xo", xi, 0), ("xo", xi, 1), ("rs", xi), "fnw_b"], writes=[("yo", xi)])
                P.dma(lambda q, xi=xi, tok0=tok0: q.dma_start(out=out_d[tok0:tok0 + 128, :], in_=yo[xi]),
                      reads=[("yo", xi)], writes=[("out", tok0)])
        cursor[0] = ph

    if stop_after is None:
        phase_c0()
        P.fence()
        dump("sg_scr", sg_scr, [])
        cursor[0] = 0
        phase_c1()

    if RESCHEDULE:
        P.sim_time = P.reschedule()
    P.emit(final_wait_ops=[o for o in P.dma_last if o is not None])
    return nc


def _consts():
    c = np.zeros((128, 128 + 12 * 256 + 256), np.float32)
    c[:, 0:128] = np.eye(128, dtype=np.float32)
    slopes = (2.0 ** (-8.0 * np.arange(1, 13, dtype=np.float32) / 12)).reshape(3, 4)
    jk = np.arange(128)[:, None]
    iq = np.arange(128)[None, :]
    for gi in range(3):
        for h in range(4):
            s = slopes[gi, h] * DIL[gi]
            d0 = (iq - jk).astype(np.float32)
            b0 = np.where(iq >= jk, -s * d0, NEG)
            d1 = (128 + iq - jk).astype(np.float32)
            b1 = np.where(iq <= jk, -s * d1, NEG)
            g = gi * 4 + h
            c[:, 128 + g * 256:128 + g * 256 + 128] = b0
            c[:, 128 + g * 256 + 128:128 + g * 256 + 256] = b1
    MK = 128 + 3072
    j = np.arange(64)[:, None]
    i = np.arange(64)[None, :]
    c[0:64, MK:MK + 64] = np.where(i >= j, 0.0, NEG)
    c[0:64, MK + 64:MK + 128] = np.where(i > j, 0.0, NEG)
    c[0:64, MK + 128:MK + 192] = (j <= i).astype(np.float32)
    c[0:64, MK + 192:MK + 256] = 1.0
    return c


_NC_CACHE = {}


def _host_inputs(x, norm_w, w_in, conv_w, a_log, dt_bias, dn_norm_w, w_o_dn, w_o_dil, w_out, final_norm_w):
    f = lambda a: np.ascontiguousarray(np.asarray(a, dtype=np.float32))
    cw = f(conv_w)[0].reshape(4, 24, 128).transpose(2, 1, 0).reshape(128, 96)
    shared = {
        "w_in": f(w_in)[0], "w_o_dn": f(w_o_dn)[0], "w_o_dil": f(w_o_dil)[0], "w_out": f(w_out)[0],
        "norm_w": f(norm_w).reshape(1, D), "final_norm_w": f(final_norm_w).reshape(1, D),
        "conv_w_l": np.ascontiguousarray(cw), "a_log": f(a_log).reshape(1, 8), "dt_bias": f(dt_bias).reshape(1, 8),
        "dn_norm_w_l": f(dn_norm_w).reshape(128, 1), "consts": _consts(),
    }
    xs = f(x)
    return [dict(shared, x=xs[b]) for b in range(xs.shape[0])]


def kernel(x, norm_w, w_in, conv_w, a_log, dt_bias, dn_norm_w, w_o_dn, w_o_dil, w_out, final_norm_w):
    in_maps = _host_inputs(x, norm_w, w_in, conv_w, a_log, dt_bias, dn_norm_w, w_o_dn, w_o_dil, w_out, final_norm_w)
    if "nc" not in _NC_CACHE:
        _NC_CACHE["nc"] = build_nc()
    res = run_bass_kernel_spmd(_NC_CACHE["nc"], in_maps, core_ids=list(range(len(in_maps))))
    return np.stack([np.asarray(r["out"], dtype=np.float32).reshape(T, D) for r in res.results], axis=0)
```

```python
import contextlib
import numpy as np
import concourse.bass as bass
import concourse.mybir as mybir
from concourse.bass_utils import run_bass_kernel_spmd

ACT = mybir.ActivationFunctionType
ALU = mybir.AluOpType
F32 = mybir.dt.float32
BF16 = mybir.dt.bfloat16

T = 4096
D = 1024
NEG = -30000.0
PIPELINE_A = True
RESCHEDULE = True
SCHED_IDENTITY = True
XLAT = 0.0
CPW = 0.0
EPS = 1e-6
O_QA, O_KA, O_VA, O_ZA, O_BA, O_QB, O_KB, O_VB, O_ZB, O_GA, O_GB = (
    0, 1024, 2048, 3072, 4096, 4112, 5648, 7184, 8720, 9232, 10256)
DIL = (1, 4, 16)


class _Op:
    __slots__ = ("eng", "fn", "deps", "signal", "ticket", "is_dma", "dsem", "dval", "odeps", "cost", "idx", "seg", "war", "pos", "tset")

    def __init__(self, eng, fn, is_dma=False):
        self.eng = eng
        self.fn = fn
        self.odeps = []
        self.cost = 0.0
        self.idx = 0
        self.seg = 0
        self.war = []
        self.pos = 0
        self.tset = None
        self.deps = []
        self.signal = False
        self.ticket = None
        self.is_dma = is_dma
        self.dsem = None
        self.dval = None


class _Res:
    __slots__ = ("w", "r", "rd")

    def __init__(self):
        self.w = None
        self.r = []
        self.rd = []


class Prog:
    ENGS = ("pe", "act", "dve", "pool", "sp")

    def __init__(self, nc, n_dma_sems=32):
        self.nc = nc
        self.streams = {e: [] for e in self.ENGS}
        self.res = {}
        self.n_dma_sems = n_dma_sems
        self.dma_cnt = [0] * n_dma_sems
        self.dma_last = [None] * n_dma_sems
        self.dma_rr = 0
        self.fence_ops = []
        self.fence_dma = []
        self.seg = 0
        self.all_ops = []
        self.fd_default = {}

    @contextlib.contextmanager
    def fds(self, **kw):
        old = dict(self.fd_default)
        self.fd_default.update(kw)
        try:
            yield
        finally:
            self.fd_default = old

    def fence(self):
        self.fence_dma.append([d for d in self.dma_last if d is not None])
        self.seg += 1

    def _r(self, k):
        r = self.res.get(k)
        if r is None:
            r = self.res[k] = _Res()
        return r

    def op(self, eng, fn, reads=(), writes=(), is_dma=False, fd=None, tset=None):
        o = _Op(eng, fn, is_dma)
        o.tset = tset
        fd = fd or self.fd_default.get(eng)
        if is_dma:
            o.cost = 0.15
        elif eng == "pe":
            o.cost = max(64, fd or 64) / 2400.0 + 0.004
        elif eng == "act":
            o.cost = (224 + (fd or 512)) / 1200.0
        elif eng == "dve":
            o.cost = (110 + (fd or 512)) / 960.0
        else:
            o.cost = 0.2 + (fd or 512) / 1000.0
        o.idx = len(self.all_ops)
        o.seg = self.seg
        self.all_ops.append(o)
        deps = []
        for k in reads:
            r = self._r(k)
            if r.w is not None:
                deps.append((r.w, "raw"))
        for k in writes:
            r = self._r(k)
            if r.w is not None:
                deps.append((r.w, "waw"))
            for rd in r.r:
                if rd is not o:
                    o.war.append(rd)
            for rd in r.rd:
                if rd is not o:
                    o.war.append(rd)
        if is_dma:
            i = self.dma_rr
            self.dma_rr = (i + 1) % self.n_dma_sems
            o.dsem = i
            self.dma_cnt[i] += 1
            o.dval = 16 * self.dma_cnt[i]
            if self.dma_last[i] is not None:
                deps.append((self.dma_last[i], "raw"))
            self.dma_last[i] = o
        seen = set()
        oseen = set()
        for d, kind in deps:
            if d is not o and id(d) not in oseen:
                oseen.add(id(d))
                o.odeps.append(d)
        for d in o.war:
            if id(d) not in oseen:
                oseen.add(id(d))
                o.odeps.append(d)
        for d, kind in deps:
            if d is o or id(d) in seen:
                continue
            if not d.is_dma and d.eng == eng and not is_dma:
                if eng == "pe":
                    continue
                if kind != "raw":
                    continue
            seen.add(id(d))
            d.signal = True
            o.deps.append(d)
        for k in reads:
            r = self._r(k)
            if is_dma:
                r.rd.append(o)
            else:
                r.r.append(o)
        for k in writes:
            r = self._r(k)
            r.w = o
            r.r = []
            r.rd = []
        self.streams[eng].append(o)
        return o

    def dma(self, fn, reads=(), writes=(), q="sp"):
        return self.op(q, fn, reads, writes, is_dma=True)

    def reschedule(self, dma_latency=3.0):
        import heapq
        ops = self.all_ops
        n = len(ops)
        succ = [[] for _ in range(n)]
        indeg = [0] * n
        for o in ops:
            for d in o.odeps:
                if d.seg == o.seg:
                    succ[d.idx].append(o.idx)
                    indeg[o.idx] += 1
        prio = [0.0] * n
        for i in range(n - 1, -1, -1):
            o = ops[i]
            m = 0.0
            for j in succ[i]:
                if prio[j] > m:
                    m = prio[j]
            prio[i] = m + (dma_latency if o.is_dma else o.cost)
        if SCHED_IDENTITY:
            prio = [float(n - i) + CPW * prio[i] for i in range(n)]
        new_streams = {e: [] for e in self.ENGS}
        now = 0.0
        cur_set = [None]
        free_at = {e: 0.0 for e in self.ENGS}
        nseg = self.seg + 1
        byseg = [[] for _ in range(nseg)]
        for o in ops:
            byseg[o.seg].append(o.idx)
        for sg in range(nseg):
            idxs = byseg[sg]
            ready = {e: [] for e in self.ENGS}
            for i in idxs:
                if indeg[i] == 0:
                    heapq.heappush(ready[ops[i].eng], (-prio[i], i))
            events = []
            done = 0
            tot = len(idxs)
            while done < tot:
                started = False
                for e in self.ENGS:
                    if free_at[e] <= now and ready[e]:
                        _, i = heapq.heappop(ready[e])
                        o = ops[i]
                        extra = 0.0
                        if e == "act":
                            if o.tset is not None and o.tset != cur_set[0]:
                                held = [(-prio[i], i)]
                                found = None
                                for _ in range(6):
                                    if not ready[e]:
                                        break
                                    c = heapq.heappop(ready[e])
                                    oc = ops[c[1]]
                                    if (oc.tset is None or oc.tset == cur_set[0]) and prio[c[1]] > prio[i] - 6.0:
                                        found = c
                                        break
                                    held.append(c)
                                for c in held:
                                    if found is not None or c[1] != i:
                                        heapq.heappush(ready[e], c)
                                if found is not None:
                                    i = found[1]
                                    o = ops[i]
                                else:
                                    cur_set[0] = o.tset
                                    extra = 1.3
                        new_streams[e].append(o)
                        free_at[e] = now + o.cost + extra
                        heapq.heappush(events, (now + (dma_latency if o.is_dma else o.cost + extra + XLAT), i))
                        started = True
                if started:
                    continue
                cand = []
                if events:
                    cand.append(events[0][0])
                for e in self.ENGS:
                    if ready[e] and free_at[e] > now:
                        cand.append(free_at[e])
                now = min(cand)
                while events and events[0][0] <= now:
                    _, i = heapq.heappop(events)
                    done += 1
                    for j in succ[i]:
                        indeg[j] -= 1
                        if indeg[j] == 0:
                            heapq.heappush(ready[ops[j].eng], (-prio[j], j))
        for e in self.ENGS:
            assert len(new_streams[e]) == len(self.streams[e])
        self.streams = new_streams
        return now

    def apply_fences(self):
        last = {}
        pos = {e: 0 for e in self.ENGS}
        for sg in range(1, self.seg + 1):
            for e in self.ENGS:
                st = self.streams[e]
                while pos[e] < len(st) and st[pos[e]].seg < sg:
                    if not st[pos[e]].is_dma:
                        last[e] = st[pos[e]]
                    pos[e] += 1
            for e in self.ENGS:
                st = self.streams[e]
                if pos[e] < len(st) and st[pos[e]].seg == sg:
                    o = st[pos[e]]
                    extra = [d for d in last.values()] + list(self.fence_dma[sg - 1])
                    have = set(id(d) for d in o.deps)
                    for d in extra:
                        if d is o or id(d) in have:
                            continue
                        if not d.is_dma and d.eng == e and e == "pe":
                            continue
                        d.signal = True
                        o.deps.append(d)

    def emit(self, final_wait_ops=()):
        nc = self.nc
        for e in self.ENGS:
            for p_, o in enumerate(self.streams[e]):
                o.pos = p_
        for o in self.all_ops:
            if not o.war:
                continue
            best = {}
            have = set(id(d) for d in o.deps)
            for d in o.war:
                if d.is_dma:
                    if id(d) not in have:
                        have.add(id(d))
                        d.signal = True
                        o.deps.append(d)
                    continue
                b = best.get(d.eng)
                if b is None or d.pos > b.pos:
                    best[d.eng] = d
            for e, d in best.items():
                if e == o.eng and not o.is_dma:
                    continue
                if id(d) in have:
                    continue
                d.signal = True
                o.deps.append(d)
        self.apply_fences()
        for e in self.ENGS:
            c = 0
            for o in self.streams[e]:
                if o.is_dma:
                    continue
                if o.signal:
                    c += 1
                    o.ticket = c
        with contextlib.ExitStack() as es:
            esem = {e: es.enter_context(nc.semaphore("s_" + e)) for e in self.ENGS}
            dsem = [es.enter_context(nc.semaphore("d_%d" % i)) for i in range(self.n_dma_sems)]
            block = es.enter_context(nc.Block())

            def run(e, engobj):
                waited = {}

                def wait_for(d):
                    if d.is_dma:
                        key, sem, val = ("d", d.dsem), dsem[d.dsem], d.dval
                    else:
                        key, sem, val = ("e", d.eng), esem[d.eng], d.ticket
                    if waited.get(key, 0) >= val:
                        return
                    waited[key] = val
                    engobj.wait_ge(sem, val)

                for o in self.streams[e]:
                    for d in o.deps:
                        wait_for(d)
                    ins = o.fn(engobj)
                    if o.is_dma:
                        ins.then_inc(dsem[o.dsem], 16)
                    elif o.signal:
                        ins.then_inc(esem[e], 1)
                if e == "sp":
                    for d in final_wait_ops:
                        wait_for(d)

            @block.tensor
            def _(eng):
                run("pe", eng)

            @block.scalar
            def _(eng):
                run("act", eng)

            @block.vector
            def _(eng):
                run("dve", eng)

            @block.gpsimd
            def _(eng):
                run("pool", eng)

            @block.sync
            def _(eng):
                run("sp", eng)


def build_nc(dbg=None, stop_after=None):
    dbg = dbg or {}
    nc = bass.Bass("TRN2", target_bir_lowering=False)
    dt = nc.dram_tensor
    x_d = dt("x", [T, D], F32, kind="ExternalInput").ap()
    w_in = dt("w_in", [D, 11280], F32, kind="ExternalInput").ap()
    w_odn = dt("w_o_dn", [1024, 1024], F32, kind="ExternalInput").ap()
    w_odil = dt("w_o_dil", [512, 1024], F32, kind="ExternalInput").ap()
    w_out = dt("w_out", [1024, 1024], F32, kind="ExternalInput").ap()
    normw_d = dt("norm_w", [1, D], F32, kind="ExternalInput").ap()
    fnw_d = dt("final_norm_w", [1, D], F32, kind="ExternalInput").ap()
    cw_d = dt("conv_w_l", [128, 96], F32, kind="ExternalInput").ap()
    alog_d = dt("a_log", [1, 8], F32, kind="ExternalInput").ap()
    dtb_d = dt("dt_bias", [1, 8], F32, kind="ExternalInput").ap()
    dnw_d = dt("dn_norm_w_l", [128, 1], F32, kind="ExternalInput").ap()
    cst_d = dt("consts", [128, 128 + 12 * 256 + 64 * 4], F32, kind="ExternalInput").ap()
    out_d = dt("out", [T, D], F32, kind="ExternalOutput").ap()
    ob_scr = dt("ob_scr", [4, 128, T], BF16).ap()
    oa_scr = dt("oa_scr", [8, 128, T], BF16).ap()
    sg_scr = dt("sg_scr", [16, 128, T], BF16).ap()
    gc_scr = dt("gc_scr", [8, T], F32).ap()
    gcl_scr = dt("gcl_scr", [8, T], F32).ap()
    dbg_out = {}
    for name, (shape, dtype) in dbg.items():
        dbg_out[name] = dt("dbg_" + name, list(shape), dtype, kind="ExternalOutput").ap()

    P = Prog(nc)
    ARENA = 212000
    arena = nc.alloc_sbuf_tensor("arena", [128, ARENA // 2], BF16)
    cursor = [0]

    def alloc(free_shape, dtype, parts=128):
        n = 1
        for s in free_shape:
            n *= s
        esz = 4 if dtype == F32 else 2
        nbytes = (n * esz + 63) // 64 * 64
        off = cursor[0]
        cursor[0] += nbytes
        assert cursor[0] <= ARENA, ("SBUF overflow", cursor[0])
        ap = arena[0:parts, off // 2: off // 2 + n * esz // 2]
        if dtype == F32:
            ap = ap.bitcast(F32)
        if len(free_shape) == 2:
            ap = ap.rearrange("p (a b) -> p a b", b=free_shape[1])
        elif len(free_shape) == 3:
            ap = ap.rearrange("p (a b c) -> p a b c", b=free_shape[1], c=free_shape[2])
        return ap

    ps = [nc.alloc_psum_tensor("ps%d" % i, [128, 512], F32) for i in range(8)]
    psb = [p[:].bitcast(BF16) for p in ps]
    bank_rr = {}

    def bank(group):
        lst = {"proj": (0, 1), "misc": (2,), "pc": (0, 1, 2), "pre": (3, 4), "q1": (5,), "q4": (6,), "po": (7,),
               "all": tuple(range(8)), "s": (3, 4), "a": (5, 7), "b": (6, 2)}[group]
        i = bank_rr.get(group, 0)
        bank_rr[group] = i + 1
        return lst[i % len(lst)]

    def dump(name, src_ap, reads):
        if name in dbg_out:
            P.dma(lambda q, a=src_ap, o=dbg_out[name]: q.dma_start(out=o, in_=a), reads=reads, writes=[("dbg", name)])

    hT = alloc([8, T], BF16)
    cst = alloc([128 + 12 * 256 + 256], F32)
    identf = cst[:, 0:128]
    alibi = cst[:, 128:128 + 3072].rearrange("p (g w) -> p g w", w=256)
    MK = 128 + 3072
    mask_incl = cst[0:64, MK:MK + 64]
    mask_strict = cst[0:64, MK + 64:MK + 128]
    triu_f = cst[0:64, MK + 128:MK + 192]
    ones64f = cst[0:64, MK + 192:MK + 256]
    ident_bf = alloc([128], BF16)
    ones_bf = alloc([128], BF16)
    ones_f = alloc([128], F32)
    cw = alloc([96], F32)
    dnw = alloc([1], F32)
    WB_N = 3
    wb = [alloc([8, 128], BF16) for _ in range(WB_N)]
    wb_rr = [0]
    beta_t = alloc([64, 8], F32, 64)
    gc_t = alloc([64, 8], F32, 64)
    negeg_t = alloc([64, 8], F32, 64)
    wdec_t = alloc([64, 8], F32, 64)
    dl_t = alloc([64, 8], F32)
    persist_end = cursor[0]

    P.dma(lambda q: q.dma_start(out=cst, in_=cst_d), writes=["cst"])
    P.dma(lambda q: q.dma_start(out=cw, in_=cw_d), writes=["cw"])
    P.dma(lambda q: q.dma_start(out=dnw, in_=dnw_d), writes=["dnw"])
    P.op("dve", lambda e: e.tensor_copy(out=ident_bf, in_=identf), reads=["cst"], writes=["ident_bf"])
    P.op("pool", lambda e: e.memset(ones_bf, 1.0), writes=["ones_bf"])
    P.op("pool", lambda e: e.memset(ones_f, 1.0), writes=["ones_f"])

    def load_w(src, c0, ncols=128, kchunks=8, dst=None, key=None):
        if dst is None:
            i = wb_rr[0] % WB_N
            wb_rr[0] += 1
            dst, key = wb[i], ("wb", i)
        P.dma(lambda q, d=dst, s=src, c0=c0, n=ncols, kc=kchunks: q.dma_start(
            out=d[:, 0:kc, 0:n], in_=s[:, c0:c0 + n].rearrange("(k p) c -> p k c", p=128)),
            writes=[key], q="pool")
        return dst, key

    def hkeys(t0, t1):
        return [("hT", i) for i in range(t0 // 128, (t1 + 127) // 128)]

    def proj_tile(wt, wkey, ncols, tt8, grp="proj"):
        b = bank(grp)
        for k in range(8):
            P.op("pe", lambda e, b=b, k=k, wt=wt, n=ncols, tt8=tt8: e.matmul(
                ps[b][0:n, :], lhsT=wt[:, k, 0:n], rhs=hT[:, k, tt8 * 512:(tt8 + 1) * 512],
                start=(k == 0), stop=(k == 7)),
                reads=[wkey] + hkeys(tt8 * 512, tt8 * 512 + 512), writes=[("ps", b)], fd=512)
        return b

    ph = cursor[0]
    normw_b = alloc([D], F32)
    xs = [alloc([D], F32) for _ in range(2)]
    junk = alloc([D], BF16)
    xb = [alloc([D], BF16) for _ in range(2)]
    ss0 = [alloc([1], F32) for _ in range(2)]
    rs0 = [alloc([1], F32) for _ in range(2)]
    P.dma(lambda q: q.dma_start(out=normw_b, in_=normw_d.partition_broadcast(128)), writes=["normw_b"])
    for tt in range(32):
        i = tt % 2
        P.dma(lambda q, i=i, tt=tt: q.dma_start(out=xs[i], in_=x_d[tt * 128:(tt + 1) * 128, :]), writes=[("xs", i)])
        P.op("pool", lambda e, i=i: e.memset(ss0[i], 0.0), writes=[("ss0", i)])
        P.op("act", lambda e, i=i: e.activation(out=junk, in_=xs[i], func=ACT.Square, accum_out=ss0[i]),
             reads=[("xs", i), ("ss0", i)], writes=["junk", ("ss0", i)])
        P.op("dve", lambda e, i=i: e.tensor_scalar(out=ss0[i], in0=ss0[i], scalar1=1.0 / D, scalar2=EPS, op0=ALU.mult, op1=ALU.add),
             reads=[("ss0", i)], writes=[("ss0", i)])
        P.op("act", lambda e, i=i: e.activation(out=ss0[i], in_=ss0[i], func=ACT.Ln), reads=[("ss0", i)], writes=[("ss0", i)], tset="L", fd=1)
        P.op("act", lambda e, i=i: e.activation(out=rs0[i], in_=ss0[i], func=ACT.Exp, scale=-0.5), reads=[("ss0", i)], writes=[("rs0", i)], fd=1)
        P.op("dve", lambda e, i=i: e.scalar_tensor_tensor(out=xb[i], in0=xs[i], scalar=rs0[i], in1=normw_b,
                                                          op0=ALU.mult, op1=ALU.mult),
             reads=[("xs", i), ("rs0", i), "normw_b"], writes=[("xb", i)])
        b = bank("all")
        for k in range(8):
            P.op("pe", lambda e, b=b, k=k, i=i: e.transpose(out=psb[b][:, k * 128:(k + 1) * 128],
                                                           in_=xb[i][:, k * 128:(k + 1) * 128], identity=ident_bf),
                 reads=[("xb", i), "ident_bf"], writes=[("ps", b)])
        eng = "act" if tt % 2 == 0 else "dve"
        if eng == "act":
            fn = lambda e, b=b, tt=tt: e.activation(out=hT[:, :, tt * 128:(tt + 1) * 128],
                                                    in_=psb[b].rearrange("p (k t) -> p k t", t=128), func=ACT.Copy)
        else:
            fn = lambda e, b=b, tt=tt: e.tensor_copy(out=hT[:, :, tt * 128:(tt + 1) * 128],
                                                     in_=psb[b].rearrange("p (k t) -> p k t", t=128))
        P.op(eng, fn, writes=[("hT", tt), ("ps", b)])
    dump("hT", hT, hkeys(0, T))
    cursor[0] = ph
    P.fence()

    def phase_a0():
        ph = cursor[0]
        w16, w16k = load_w(w_in, O_BA, 16)
        alog_b = alloc([8], F32, 64)
        dtb_b = alloc([8], F32, 64)
        P.dma(lambda q: q.dma_start(out=alog_b, in_=alog_d.partition_broadcast(64)), writes=["alog_b"])
        P.dma(lambda q: q.dma_start(out=dtb_b, in_=dtb_d.partition_broadcast(64)), writes=["dtb_b"])
        Gsb = alloc([64, 16], F32, 64)
        names = ["xa", "ax", "ee", "ll", "sp", "g", "lb", "gcl", "tmp"]
        A = {n: alloc([64, 8], F32, 64) for n in names}
        glast = alloc([64, 8], F32)
        tb = alloc([8, 64], F32, 64)
        for half in range(2):
            b = bank("all")
            for n in range(32 * half, 32 * half + 32):
                for k in range(8):
                    P.op("pe", lambda e, b=b, n=n, k=k: e.matmul(
                        ps[b][0:64, (n % 32) * 16:(n % 32) * 16 + 16], lhsT=hT[:, k, n * 64:(n + 1) * 64],
                        rhs=w16[:, k, 0:16], start=(k == 0), stop=(k == 7)),
                        reads=[w16k] + hkeys(n * 64, n * 64 + 64), writes=[("ps", b)])
            P.op("act", lambda e, b=b, half=half: e.activation(
                out=Gsb[:, 32 * half:32 * half + 32, :], in_=ps[b][0:64, :].rearrange("p (n c) -> p n c", c=16),
                func=ACT.Copy), writes=["Gsb", ("ps", b)])
        bb = Gsb[:, :, 0:8]
        aa = Gsb[:, :, 8:16]
        bc = lambda v: v.unsqueeze(1).to_broadcast([64, 64, 8])
        P.op("act", lambda e: e.activation(out=beta_t, in_=bb, func=ACT.Sigmoid), reads=["Gsb"], writes=["beta_t"])
        P.op("act", lambda e: e.activation(out=A["lb"], in_=beta_t, func=ACT.Ln), reads=["beta_t"], writes=["lb"])
        P.op("dve", lambda e: e.tensor_tensor(out=A["xa"], in0=aa, in1=bc(dtb_b), op=ALU.add),
             reads=["Gsb", "dtb_b"], writes=["xa"])
        P.op("act", lambda e: e.activation(out=A["ax"], in_=A["xa"], func=ACT.Abs), reads=["xa"], writes=["ax"])
        P.op("act", lambda e: e.activation(out=A["ee"], in_=A["ax"], func=ACT.Exp, scale=-1.0), reads=["ax"], writes=["ee"])
        P.op("act", lambda e: e.activation(out=A["ll"], in_=A["ee"], func=ACT.Ln, bias=1.0), reads=["ee"], writes=["ll"])
        P.op("dve", lambda e: e.scalar_tensor_tensor(out=A["sp"], in0=A["xa"], scalar=0.0, in1=A["ll"],
                                                     op0=ALU.max, op1=ALU.add), reads=["xa", "ll"], writes=["sp"])
        P.op("act", lambda e: e.activation(out=alog_b, in_=alog_b, func=ACT.Exp), reads=["alog_b"], writes=["alog_b"])
        P.op("dve", lambda e: e.scalar_tensor_tensor(out=A["g"], in0=A["sp"], scalar=-1.0, in1=bc(alog_b),
                                                     op0=ALU.mult, op1=ALU.mult), reads=["sp", "alog_b"], writes=["g"])
        gflat = A["g"].rearrange("p n h -> p (n h)")
        b1 = bank("all")
        P.op("pe", lambda e: e.matmul(ps[b1][0:64, :], lhsT=triu_f, rhs=gflat, start=True, stop=True),
             reads=["g", "cst"], writes=[("ps", b1)])
        b2 = bank("all")
        P.op("pe", lambda e: e.matmul(ps[b2][:, :], lhsT=ones_f[0:64, :], rhs=gflat, start=True, stop=True),
             reads=["g", "ones_f"], writes=[("ps", b2)])
        fl = lambda v: v.rearrange("p n h -> p (n h)")
        P.op("act", lambda e: e.activation(out=fl(gc_t), in_=ps[b1][0:64, :], func=ACT.Copy), writes=["gc_t", ("ps", b1)])
        P.op("dve", lambda e: e.tensor_copy(out=fl(glast), in_=ps[b2][:, :]), writes=["glast", ("ps", b2)])
        P.op("act", lambda e: e.activation(out=negeg_t, in_=gc_t, func=ACT.Exp), reads=["gc_t"], writes=["negeg_t"])
        P.op("dve", lambda e: e.tensor_scalar(out=negeg_t, in0=negeg_t, scalar1=-1.0, scalar2=None, op0=ALU.mult),
             reads=["negeg_t"], writes=["negeg_t"])
        P.op("dve", lambda e: e.tensor_tensor(out=A["tmp"], in0=glast[0:64], in1=gc_t, op=ALU.subtract),
             reads=["glast", "gc_t"], writes=["tmp"])
        P.op("act", lambda e: e.activation(out=wdec_t, in_=A["tmp"], func=ACT.Exp), reads=["tmp"], writes=["wdec_t"])
        P.op("act", lambda e: e.activation(out=dl_t, in_=glast, func=ACT.Exp), reads=["glast"], writes=["dl_t"])
        P.op("dve", lambda e: e.tensor_tensor(out=A["gcl"], in0=gc_t, in1=A["lb"], op=ALU.add),
             reads=["gc_t", "lb"], writes=["gcl"])
        for nm, src, scr in (("gc", gc_t, gc_scr), ("gcl", A["gcl"], gcl_scr)):
            b = bank("all")
            for h in range(8):
                P.op("pe", lambda e, b=b, h=h, src=src: e.transpose(out=ps[b][0:64, h * 64:(h + 1) * 64],
                                                                   in_=src[:, :, h], identity=identf[0:64, 0:64]),
                     reads=["gc_t" if nm == "gc" else "gcl", "cst"], writes=[("ps", b)])
            P.op("dve", lambda e, b=b: e.tensor_copy(out=tb, in_=ps[b][0:64, :].rearrange("p (h c) -> p h c", c=64)),
                 writes=["tb", ("ps", b)])
            P.dma(lambda q, scr=scr: q.dma_start(out=scr.rearrange("h (n c) -> n h c", c=64), in_=tb),
                  reads=["tb"], writes=[nm + "_scr"])
        dump("gc_t", gc_t, ["gc_t"])
        dump("beta_t", beta_t, ["beta_t"])
        dump("g_t", A["g"], ["g"])
        cursor[0] = ph

    phase_a0()
    P.fence()

    def interleave(ta, tb):
        na, nb_ = len(ta), len(tb)
        j = 0
        for i, t in enumerate(ta):
            t()
            tgt = (nb_ * (i + 1) + na - 1) // max(na, 1)
            while j < min(tgt, nb_):
                tb[j]()
                j += 1
        while j < nb_:
            tb[j]()
            j += 1

    def phase_b():
        ph = cursor[0]
        qTs = [alloc([T], BF16) for _ in range(2)]
        kTs = [alloc([T], BF16) for _ in range(2)]
        vsbs = [alloc([32, 128], BF16) for _ in range(2)]
        vT = alloc([T], BF16)
        acc_n = alloc([T], F32)
        acc_d = alloc([T], F32)
        NS = 3
        s_sb = [alloc([256], F32) for _ in range(NS)]
        p_sb = [alloc([256], BF16) for _ in range(4)]
        zs = [alloc([512], F32) for _ in range(2)]
        rc = [alloc([512], F32) for _ in range(2)]
        ob = [alloc([512], BF16) for _ in range(2)]
        mone = alloc([512], F32)
        P.op("pool", lambda e: e.memset(mone, -1.0), writes=["mone"])
        scale = 128 ** -0.5
        jobs = [(h, gi) for h in range(4) for gi in range(3)]
        cnt = [0]

        def proj_tasks(jn):
            h, gi = jobs[jn]
            bi = jn % 2
            qT, kT, v_sb = qTs[bi], kTs[bi], vsbs[bi]
            d = DIL[gi]
            L = T // d
            nb = L // 128
            M = 512 // d
            tasks = []
            hold = {}

            def t_w():
                hold["q"] = load_w(w_in, O_QB + gi * 512 + h * 128)
                hold["k"] = load_w(w_in, O_KB + gi * 512 + h * 128)
                hold["v"] = load_w(w_in, O_VB + gi * 512 + h * 128)
            tasks.append(t_w)
            q3 = qT.rearrange("p (r m) -> p r m", r=d)
            k3 = kT.rearrange("p (r m) -> p r m", r=d)
            for tt8 in range(8):
                def t_q(tt8=tt8):
                    wq, wqk = hold["q"]
                    b = proj_tile(wq, wqk, 128, tt8)
                    P.op("act", lambda e: e.activation(out=qT[:, tt8 * 512:(tt8 + 1) * 512], in_=ps[b][:, :],
                                                       func=ACT.Copy, scale=scale),
                         writes=[("qT", bi), ("ps", b)])
                tasks.append(t_q)

                def t_k(tt8=tt8):
                    wk, wkk = hold["k"]
                    b = proj_tile(wk, wkk, 128, tt8)
                    P.op("dve", lambda e: e.tensor_copy(out=kT[:, tt8 * 512:(tt8 + 1) * 512], in_=ps[b][:, :]),
                         writes=[("kT", bi), ("ps", b)])
                tasks.append(t_k)

                def t_v(tt8=tt8):
                    wv, wvk = hold["v"]
                    b = proj_tile(wv, wvk, 128, tt8)
                    P.op("act", lambda e: e.activation(out=vT[:, tt8 * 512:(tt8 + 1) * 512], in_=ps[b][:, :], func=ACT.Copy),
                         writes=[("vT", tt8), ("ps", b)])
                tasks.append(t_v)
            for t8 in range(4):
                def t_vt(t8=t8):
                    b = bank("proj")
                    for s in range(8):
                        tid = t8 * 8 + s
                        r, j = tid // nb, tid % nb
                        t0 = 128 * j * d + r
                        P.op("pe", lambda e, s=s, t0=t0: e.transpose(
                            out=psb[b][:, s * 128:(s + 1) * 128], in_=vT[:, t0:t0 + 127 * d + 1:d], identity=ident_bf),
                            reads=[("vT", i) for i in range((128 * j * d) // 512, (128 * (j + 1) * d + 511) // 512)] + ["ident_bf"],
                            writes=[("ps", b)])
                    P.op("dve", lambda e: e.tensor_copy(
                        out=v_sb[:, t8 * 8:(t8 + 1) * 8, :], in_=psb[b][:, :].rearrange("p (s c) -> p s c", c=128)),
                        writes=[("v_sb", bi), ("ps", b)])
                tasks.append(t_vt)
            return tasks

        def core_tasks(jn):
            h, gi = jobs[jn]
            bi = jn % 2
            qT, kT, v_sb = qTs[bi], kTs[bi], vsbs[bi]
            d = DIL[gi]
            L = T // d
            nb = L // 128
            gidx = gi * 4 + h
            tasks = []
            st = {"bn": None, "bd": None}
            pis = {}

            def tok(r, j0, nblk):
                a = (128 * j0) * d + r
                return slice(a, a + (128 * nblk - 1) * d + 1, d)

            def t_qk(r, j):
              with P.fds(pe=256, act=256, dve=256):
                W = 2 if j + 1 < nb else 1
                bs = bank("s")
                P.op("pe", lambda e: e.matmul(ps[bs][:, 0:128 * W], lhsT=kT[:, tok(r, j, 1)], rhs=qT[:, tok(r, j, W)],
                                              start=True, stop=True),
                     reads=[("qT", bi), ("kT", bi)], writes=[("ps", bs)])
                si = cnt[0] % NS
                pi = cnt[0] % 4
                cnt[0] += 1
                pis[(r, j)] = pi
                P.op("dve", lambda e: e.tensor_tensor(out=s_sb[si][:, 0:128 * W], in0=ps[bs][:, 0:128 * W],
                                                      in1=alibi[:, gidx, 0:128 * W], op=ALU.add),
                     reads=["cst"], writes=[("s_sb", si), ("ps", bs)])
                P.op("act", lambda e: e.activation(out=p_sb[pi][:, 0:128 * W], in_=s_sb[si][:, 0:128 * W], func=ACT.Exp),
                     reads=[("s_sb", si)], writes=[("p_sb", pi)])

            def t_pv(r, j):
              with P.fds(pe=128):
                if j % 4 == 0:
                    st["bn"], st["bd"] = bank("a"), bank("b")
                bn, bd = st["bn"], st["bd"]
                pi = pis[(r, j)]
                prev = pis[(r, j - 1)] if j > 0 else None
                col = (j % 4) * 128
                tid = r * nb + j
                for (bk, is_den, lk) in ((bn, False, ("v_sb", bi)), (bd, True, "ones_bf")):
                    first = True
                    if j > 0:
                        lp = ones_bf if is_den else v_sb[:, tid - 1, :]
                        P.op("pe", lambda e, bk=bk, lp=lp: e.matmul(
                            ps[bk][:, col:col + 128], lhsT=lp, rhs=p_sb[prev][:, 128:256], start=True, stop=False),
                            reads=[lk, ("p_sb", prev)], writes=[("ps", bk)])
                        first = False
                    lc = ones_bf if is_den else v_sb[:, tid, :]
                    P.op("pe", lambda e, bk=bk, lc=lc, first=first: e.matmul(
                        ps[bk][:, col:col + 128], lhsT=lc, rhs=p_sb[pi][:, 0:128], start=first, stop=True),
                        reads=[lk, ("p_sb", pi)], writes=[("ps", bk)])
                if j % 4 == 3 or j == nb - 1:
                    n0 = (j // 4) * 4
                    nq = (j - n0 + 1) * 128
                    sl = slice(128 * n0 * d + r, 128 * n0 * d + r + (nq - 1) * d + 1, d)
                    for (bk, acc, key, eng) in ((bn, acc_n, "acc_n", "act"), (bd, acc_d, "acc_d", "dve")):
                        if gi == 0:
                            if eng == "act":
                                P.op("act", lambda e, bk=bk, acc=acc: e.activation(
                                    out=acc[:, sl], in_=ps[bk][:, 0:nq], func=ACT.Copy), writes=[key, ("ps", bk)])
                            else:
                                P.op("dve", lambda e, bk=bk, acc=acc: e.tensor_copy(
                                    out=acc[:, sl], in_=ps[bk][:, 0:nq]), writes=[key, ("ps", bk)])
                        else:
                            P.op("dve", lambda e, bk=bk, acc=acc: e.tensor_tensor(
                                out=acc[:, sl], in0=ps[bk][:, 0:nq], in1=acc[:, sl], op=ALU.add),
                                reads=[key], writes=[key, ("ps", bk)])

            seq = [(r, j) for r in range(d) for j in range(nb)]
            SK = 2
            for idx in range(len(seq) + SK):
                def t_blk(idx=idx):
                    if idx < len(seq):
                        t_qk(*seq[idx])
                    if idx >= SK:
                        t_pv(*seq[idx - SK])
                tasks.append(t_blk)
            if gi == 2:
                hold = {}

                def t_wz():
                    hold["z"] = load_w(w_in, O_ZB + h * 128)
                tasks.append(t_wz)
                for tt8 in range(8):
                    def t_fin(tt8=tt8):
                        wz, wzk = hold["z"]
                        i = tt8 % 2
                        sl = slice(tt8 * 512, (tt8 + 1) * 512)
                        b = proj_tile(wz, wzk, 128, tt8)
                        P.op("act", lambda e: e.activation(out=zs[i], in_=ps[b][:, :], func=ACT.Tanh, scale=0.5),
                             writes=[("zs", i), ("ps", b)], tset="T")
                        P.op("dve", lambda e: e.scalar_tensor_tensor(out=zs[i], in0=zs[i], scalar=1.0, in1=ps[b][:, :],
                                                                     op0=ALU.add, op1=ALU.mult),
                             reads=[("zs", i)], writes=[("zs", i), ("ps", b)])
                        P.op("dve", lambda e: e.reciprocal(out=rc[i], in_=acc_d[:, sl]), reads=["acc_d"], writes=[("rc", i)], fd=1500)
                        P.op("dve", lambda e: e.tensor_tensor(out=rc[i], in0=rc[i], in1=acc_n[:, sl], op=ALU.mult),
                             reads=["acc_n", ("rc", i)], writes=[("rc", i)])
                        P.op("dve", lambda e: e.scalar_tensor_tensor(out=ob[i], in0=rc[i], scalar=0.5, in1=zs[i],
                                                                     op0=ALU.mult, op1=ALU.mult),
                             reads=[("rc", i), ("zs", i)], writes=[("ob", i)])
                        P.dma(lambda q: q.dma_start(out=ob_scr[h, :, sl], in_=ob[i]),
                              reads=[("ob", i)], writes=[("ob_scr", h, tt8)])
                    tasks.append(t_fin)
            return tasks

        for t in proj_tasks(0):
            t()
        for jn in range(len(jobs)):
            interleave(core_tasks(jn), proj_tasks(jn + 1) if jn + 1 < len(jobs) else [])
        cursor[0] = ph

    if stop_after != "a0":
        phase_b()
        P.fence()
        dump("ob_scr", ob_scr, [])

    def phase_a():
        ph = cursor[0]
        NB2 = 2
        upre = {t: [alloc([516], BF16) for _ in range(2)] for t in "qkv"}
        for t in "qkv":
            P.op("pool", lambda e, t=t: e.memset(upre[t][1][:, 512:515], 0.0), writes=[("upre", t, 1)])
        dg = alloc([12, 128], BF16)
        _thb = [alloc([512], F32) for _ in range(2)]
        th = {"q": _thb[0], "k": _thb[1], "v": _thb[0], "z": _thb[1]}
        thk = {"q": ("th", 0), "k": ("th", 1), "v": ("th", 0), "z": ("th", 1)}
        yq = alloc([512], F32)
        yk = alloc([512], F32)
        sq = {t: alloc([512], BF16) for t in "qk"}
        rin = {t: alloc([512], F32) for t in "qk"}
        vTt = alloc([512], BF16)
        khT = [alloc([512], BF16) for _ in range(NB2)]
        qhT = [alloc([512], BF16) for _ in range(NB2)]
        qgT = [alloc([512], BF16) for _ in range(3)]
        zsT = [alloc([512], BF16) for _ in range(3)]
        Ktok = [alloc([8, 128], BF16, 64) for _ in range(NB2)]
        Vtok = [alloc([8, 128], BF16, 64) for _ in range(NB2)]
        AqkT = [alloc([8, 64], BF16, 64) for _ in range(NB2)]
        TT = alloc([8, 64], BF16, 64)
        gcB = [alloc([512], F32) for _ in range(2)]
        gclB = [alloc([512], F32, 64) for _ in range(2)]
        egB = alloc([512], F32)
        E1 = alloc([8, 64], F32, 64)
        E2 = alloc([8, 64], F32, 64)
        GT = alloc([8, 64], F32, 64)
        GTb = alloc([8, 64], F32, 64)
        Pk = [alloc([8, 64], BF16, 64) for _ in range(2)]
        PkT = [alloc([8, 64], BF16, 64) for _ in range(2)]
        Xb = [alloc([8, 64], BF16, 64) for _ in range(2)]
        S_f = alloc([128], F32)
        S_b = alloc([128], BF16)
        vnew = [alloc([128], BF16, 64) for _ in range(2)]
        WnT = [alloc([8, 64], BF16) for _ in range(NB2)]
        Ubf = [alloc([8, 128], BF16, 64) for _ in range(NB2)]
        Kd = [alloc([8, 128], BF16, 64) for _ in range(NB2)]
        Kgn = alloc([8, 128], BF16, 64)
        oraw = alloc([512], F32)
        osq = alloc([512], BF16)
        orst = alloc([512], F32)
        oa = [alloc([512], BF16) for _ in range(2)]
        wsets = [[alloc([8, 128], BF16) for _ in range(4)] for _ in range(2)]

        def load_head_w(h):
            ws = wsets[h % 2]
            for ti, off in enumerate((O_QA, O_KA, O_VA, O_ZA)):
                load_w(w_in, off + h * 128, dst=ws[ti], key=("wh", h % 2, ti))

        m8 = lambda m: m.unsqueeze(1).to_broadcast([64, 8, 64])
        v3 = lambda a: a.rearrange("p (c i) -> p c i", i=64)
        fl = lambda a: a.rearrange("p c i -> p (c i)")

        def rsqrt_from_psum(b, dst, key, scale):
            P.op("act", lambda e: e.activation(out=dst, in_=ps[b][:, :], func=ACT.Ln, scale=scale, bias=EPS),
                 writes=[key, ("ps", b)], tset="L")
            P.op("act", lambda e: e.activation(out=dst, in_=dst, func=ACT.Exp, scale=-0.5), reads=[key], writes=[key])

        def stage12_tasks(st):
            h, tt8 = st // 8, st % 8
            bi = st % NB2
            b3 = st % 3
            ui = st % 2
            t0 = tt8 * 512
            ws = wsets[h % 2]
            tasks = []
            A = tasks.append

            def t_pre():
                if tt8 == 0:
                    if h + 1 < 8:
                        load_head_w(h + 1)
                    for ti in range(3):
                        for kk in range(4):
                            g = ti * 8 + h
                            P.op("dve", lambda e, ti=ti, kk=kk, g=g: e.tensor_scalar(
                                out=dg[:, ti * 4 + kk, :], in0=identf, scalar1=cw[:, g * 4 + kk:g * 4 + kk + 1], scalar2=None,
                                op0=ALU.mult), reads=["cst", "cw"], writes=["dg"])
                P.dma(lambda q: q.dma_start(out=gcB[ui], in_=gc_scr[h, t0:t0 + 512].partition_broadcast(128)),
                      reads=["gc_scr"], writes=[("gcB", ui)])
                P.dma(lambda q: q.dma_start(out=gclB[ui], in_=gcl_scr[h, t0:t0 + 512].partition_broadcast(64)),
                      reads=["gcl_scr"], writes=[("gclB", ui)])
            A(t_pre)

            def proj_split(wt, wkey, hb):
                def mk(k0):
                    def f():
                        if k0 == 0:
                            hb["b"] = bank("pc")
                        b = hb["b"]
                        for k in range(k0, k0 + 2):
                            P.op("pe", lambda e, k=k: e.matmul(ps[b][:, :], lhsT=wt[:, k, :], rhs=hT[:, k, t0:t0 + 512],
                                                               start=(k == 0), stop=(k == 7)),
                                 reads=[wkey] + hkeys(t0, t0 + 512), writes=[("ps", b)], fd=512)
                    return f
                for k0 in (0, 2, 4):
                    A(mk(k0))
                return mk(6)

            hbz = {}
            last_z = proj_split(ws[3], ("wh", h % 2, 3), hbz)

            def t_projz():
                last_z()
                b = hbz["b"]
                P.op("act", lambda e: e.activation(out=th["z"], in_=ps[b][:, :], func=ACT.Tanh, scale=0.5),
                     writes=[thk["z"], ("ps", b)], tset="T")
                P.op("dve", lambda e: e.scalar_tensor_tensor(out=zsT[b3], in0=th["z"], scalar=1.0, in1=ps[b][:, :],
                                                             op0=ALU.add, op1=ALU.mult),
                     reads=[thk["z"]], writes=[("zsT", b3), ("ps", b)])
            A(t_projz)

            for ti, t in enumerate("qkv"):
                hbp = {}
                last_p = proj_split(ws[ti], ("wh", h % 2, ti), hbp)

                def t_proj(ti=ti, t=t, hbp=hbp, last_p=last_p):
                    u = upre[t][ui]
                    up = upre[t][1 - ui]
                    last_p()
                    b = hbp["b"]
                    if tt8 == 0:
                        P.op("pool", lambda e: e.memset(u[:, 0:3], 0.0), writes=[("upre", t, ui)])
                    else:
                        P.op("pool", lambda e: e.tensor_copy(out=u[:, 0:3], in_=up[:, 512:515]),
                             reads=[("upre", t, 1 - ui)], writes=[("upre", t, ui)])
                    P.op("act", lambda e: e.activation(out=u[:, 3:515], in_=ps[b][:, :], func=ACT.Copy),
                         writes=[("upre", t, ui), ("ps", b)])
                A(t_proj)

            for ti, t in enumerate("qkv"):
                hbc = {}

                def t_conv0(ti=ti, t=t, hbc=hbc):
                    u = upre[t][ui]
                    hbc["b"] = b = bank("pc")
                    for kk in range(2):
                        P.op("pe", lambda e, kk=kk: e.matmul(ps[b][:, :], lhsT=dg[:, ti * 4 + kk, :], rhs=u[:, kk:kk + 512],
                                                             start=(kk == 0), stop=False),
                             reads=["dg", ("upre", t, ui)], writes=[("ps", b)], fd=512)
                A(t_conv0)

                def t_conv(ti=ti, t=t, hbc=hbc):
                    u = upre[t][ui]
                    b = hbc["b"]
                    for kk in range(2, 4):
                        P.op("pe", lambda e, kk=kk: e.matmul(ps[b][:, :], lhsT=dg[:, ti * 4 + kk, :], rhs=u[:, kk:kk + 512],
                                                             start=False, stop=(kk == 3)),
                             reads=["dg", ("upre", t, ui)], writes=[("ps", b)], fd=512)
                    P.op("act", lambda e: e.activation(out=th[t], in_=ps[b][:, :], func=ACT.Tanh, scale=0.5),
                         writes=[thk[t], ("ps", b)], tset="T")
                    dst = {"q": yq, "k": yk, "v": vTt}[t]
                    P.op("dve", lambda e: e.scalar_tensor_tensor(out=dst, in0=th[t], scalar=1.0, in1=ps[b][:, :],
                                                                 op0=ALU.add, op1=ALU.mult),
                         reads=[thk[t]], writes=[("y", t), ("ps", b)])
                A(t_conv)

            for t, y in (("q", yq), ("k", yk)):
                def t_norm(t=t, y=y):
                    P.op("act", lambda e: e.activation(out=sq[t], in_=y, func=ACT.Square), reads=[("y", t)], writes=[("sq", t)])
                    b = bank("pc")
                    P.op("pe", lambda e: e.matmul(ps[b][:, :], lhsT=ones_bf, rhs=sq[t], start=True, stop=True),
                         reads=[("sq", t), "ones_bf"], writes=[("ps", b)], fd=512)
                    rsqrt_from_psum(b, rin[t], ("rin", t), 0.25)
                A(t_norm)

            def t_hat():
                P.op("dve", lambda e: e.scalar_tensor_tensor(out=khT[bi], in0=yk, scalar=0.5, in1=rin["k"],
                                                             op0=ALU.mult, op1=ALU.mult),
                     reads=[("y", "k"), ("rin", "k")], writes=[("khT", bi)])
                P.op("dve", lambda e: e.scalar_tensor_tensor(out=qhT[bi], in0=yq, scalar=0.5 * 128 ** -0.5, in1=rin["q"],
                                                             op0=ALU.mult, op1=ALU.mult),
                     reads=[("y", "q"), ("rin", "q")], writes=[("qhT", bi)])
                P.op("act", lambda e: e.activation(out=egB, in_=gcB[ui], func=ACT.Exp), reads=[("gcB", ui)], writes=["egB"])
                P.op("pool", lambda e: e.tensor_tensor(out=qgT[b3], in0=qhT[bi], in1=egB, op=ALU.mult),
                     reads=[("qhT", bi), "egB"], writes=[("qgT", b3)])
            A(t_hat)

            for (src, skey, dst, dkey, sc) in ((khT[bi], ("khT", bi), Ktok[bi], ("Ktok", bi), 1.0),
                                               (vTt, ("y", "v"), Vtok[bi], ("Vtok", bi), 0.5)):
                def t_tok(src=src, skey=skey, dst=dst, dkey=dkey, sc=sc):
                    b = bank("pc")
                    for c in range(8):
                        P.op("pe", lambda e, c=c: e.transpose(out=psb[b][0:64, c * 128:(c + 1) * 128],
                                                              in_=src[:, c * 64:(c + 1) * 64], identity=ident_bf),
                             reads=[skey, "ident_bf"], writes=[("ps", b)])
                    P.op("act", lambda e: e.activation(out=dst, in_=psb[b][0:64, :].rearrange("p (c k) -> p c k", k=128),
                                                       func=ACT.Copy, scale=sc), writes=[dkey, ("ps", b)])
                A(t_tok)

            n0 = tt8 * 8
            gcJ = gc_t[:, n0:n0 + 8, h].unsqueeze(2).to_broadcast([64, 8, 64])
            bJ = beta_t[:, n0:n0 + 8, h].unsqueeze(2).to_broadcast([64, 8, 64])
            hold = {}

            split = [len(tasks)]

            def t_gates():
                P.op("pool", lambda e: e.tensor_tensor(out=E1, in0=v3(gcB[ui][0:64, :]), in1=gcJ, op=ALU.subtract),
                     reads=[("gcB", ui), "gc_t"], writes=["E1"])
                P.op("pool", lambda e: e.tensor_tensor(out=E1, in0=E1, in1=m8(mask_incl), op=ALU.add),
                     reads=["E1", "cst"], writes=["E1"])
                P.op("act", lambda e: e.activation(out=GT, in_=E1, func=ACT.Exp), reads=["E1"], writes=["GT"])
                P.op("pool", lambda e: e.tensor_tensor(out=E2, in0=v3(gclB[ui]), in1=gcJ, op=ALU.subtract),
                     reads=[("gclB", ui), "gc_t"], writes=["E2"])
                P.op("pool", lambda e: e.tensor_tensor(out=E2, in0=E2, in1=m8(mask_strict), op=ALU.add),
                     reads=["E2", "cst"], writes=["E2"])
                P.op("act", lambda e: e.activation(out=GTb, in_=E2, func=ACT.Exp), reads=["E2"], writes=["GTb"])
            A(t_gates)

            def t_kkqk():
                bkk, bqk = bank("pre"), bank("pre")
                for c in range(8):
                    cs = slice(c * 64, (c + 1) * 64)
                    P.op("pe", lambda e, cs=cs: e.matmul(ps[bkk][0:64, cs], lhsT=khT[bi][:, cs], rhs=khT[bi][:, cs],
                                                         start=True, stop=True), reads=[("khT", bi)], writes=[("ps", bkk)])
                for c in range(8):
                    cs = slice(c * 64, (c + 1) * 64)
                    P.op("pe", lambda e, cs=cs: e.matmul(ps[bqk][0:64, cs], lhsT=khT[bi][:, cs], rhs=qhT[bi][:, cs],
                                                         start=True, stop=True), reads=[("khT", bi), ("qhT", bi)], writes=[("ps", bqk)])
                P.op("dve", lambda e: e.tensor_tensor(out=AqkT[bi], in0=v3(ps[bqk][0:64, :]), in1=GT, op=ALU.mult),
                     reads=["GT"], writes=[("AqkT", bi), ("ps", bqk)])
                P.op("dve", lambda e: e.scalar_tensor_tensor(out=Pk[0], in0=v3(ps[bkk][0:64, :]), scalar=-1.0, in1=GTb,
                                                             op0=ALU.mult, op1=ALU.mult),
                     reads=["GTb"], writes=[("Pk", 0), ("ps", bkk)])
            A(t_kkqk)

            def t_pt():
                b = bank("pre")
                for c in range(8):
                    P.op("pe", lambda e, c=c: e.transpose(out=psb[b][0:64, c * 64:(c + 1) * 64], in_=Pk[0][:, c, :],
                                                          identity=ident_bf[0:64, 0:64]),
                         reads=[("Pk", 0), "ident_bf"], writes=[("ps", b)])
                P.op("act", lambda e: e.activation(out=fl(PkT[0]), in_=psb[b][0:64, 0:512], func=ACT.Copy),
                     writes=[("PkT", 0), ("ps", b)])
                P.op("pool", lambda e: e.tensor_tensor(out=Xb[0], in0=Pk[0], in1=m8(identf[0:64, 0:64]), op=ALU.add),
                     reads=[("Pk", 0), "cst"], writes=[("Xb", 0)])
            A(t_pt)

            for lvl in range(5):
                cur = lvl % 2
                nxt = 1 - cur

                def t_sq(lvl=lvl, cur=cur, nxt=nxt):
                    if lvl < 4:
                        ba = bank("pre")
                        for c in range(8):
                            cs = slice(c * 64, (c + 1) * 64)
                            P.op("pe", lambda e, c=c, cs=cs: e.matmul(ps[ba][0:64, cs], lhsT=PkT[cur][:, c, :], rhs=Pk[cur][:, c, :],
                                                                      start=True, stop=True),
                                 reads=[("Pk", cur), ("PkT", cur)], writes=[("ps", ba)])
                    bt = bank("pre")
                    for c in range(8):
                        cs = slice(c * 64, (c + 1) * 64)
                        P.op("pe", lambda e, c=c, cs=cs: e.matmul(ps[bt][0:64, cs], lhsT=Pk[cur][:, c, :], rhs=PkT[cur][:, c, :],
                                                                  start=True, stop=True),
                             reads=[("Pk", cur), ("PkT", cur)], writes=[("ps", bt)])
                    if lvl < 4:
                        P.op("act", lambda e: e.activation(out=fl(Pk[nxt]), in_=ps[ba][0:64, :], func=ACT.Copy),
                             writes=[("Pk", nxt), ("ps", ba)])
                    P.op("dve", lambda e: e.tensor_copy(out=fl(PkT[nxt]), in_=ps[bt][0:64, :]),
                         writes=[("PkT", nxt), ("ps", bt)])
                A(t_sq)

                def t_x(lvl=lvl, cur=cur, nxt=nxt):
                    bx = bank("pre")
                    for c in range(8):
                        cs = slice(c * 64, (c + 1) * 64)
                        P.op("pe", lambda e, c=c, cs=cs: e.matmul(ps[bx][0:64, cs], lhsT=PkT[nxt][:, c, :], rhs=Xb[cur][:, c, :],
                                                                  start=True, stop=True),
                             reads=[("PkT", nxt), ("Xb", cur)], writes=[("ps", bx)])
                    P.op("dve", lambda e: e.tensor_tensor(out=fl(Xb[nxt]), in0=ps[bx][0:64, :], in1=fl(Xb[cur]), op=ALU.add),
                         reads=[("Xb", cur)], writes=[("Xb", nxt), ("ps", bx)])
                    if lvl == 4:
                        P.op("pool", lambda e: e.tensor_tensor(out=TT, in0=Xb[nxt], in1=bJ, op=ALU.mult),
                             reads=[("Xb", nxt), "beta_t"], writes=["TT"])
                A(t_x)

            ngJ = negeg_t[:, n0:n0 + 8, h].unsqueeze(2).to_broadcast([64, 8, 128])
            wdJ = wdec_t[:, n0:n0 + 8, h].unsqueeze(2).to_broadcast([64, 8, 128])

            def t_kg():
                P.op("pool", lambda e: e.tensor_tensor(out=Kgn, in0=Ktok[bi], in1=ngJ, op=ALU.mult),
                     reads=[("Ktok", bi), "negeg_t"], writes=["Kgn"])
                P.op("pool", lambda e: e.tensor_tensor(out=Kd[bi], in0=Ktok[bi], in1=wdJ, op=ALU.mult),
                     reads=[("Ktok", bi), "wdec_t"], writes=[("Kd", bi)])
            tasks.insert(len(tasks) - 6, t_kg)

            def t_w():
                b = bank("pre")
                for c in range(8):
                    P.op("pe", lambda e, c=c: e.matmul(ps[b][:, c * 64:(c + 1) * 64], lhsT=Kgn[:, c, :], rhs=TT[:, c, :],
                                                       start=True, stop=True),
                         reads=["Kgn", "TT"], writes=[("ps", b)])
                P.op("act", lambda e: e.activation(out=fl(WnT[bi]), in_=ps[b][:, :], func=ACT.Copy),
                     writes=[("WnT", bi), ("ps", b)])
            A(t_w)

            for half in range(2):
                def t_u(half=half):
                    b = bank("pre")
                    for c4 in range(4):
                        c = half * 4 + c4
                        P.op("pe", lambda e, c=c, c4=c4: e.matmul(ps[b][0:64, c4 * 128:(c4 + 1) * 128], lhsT=TT[:, c, :],
                                                                  rhs=Vtok[bi][:, c, :], start=True, stop=True),
                             reads=["TT", ("Vtok", bi)], writes=[("ps", b)])
                    P.op("dve", lambda e: e.tensor_copy(out=Ubf[bi][:, half * 4:half * 4 + 4, :],
                                                        in_=ps[b][0:64, :].rearrange("p (c k) -> p c k", k=128)),
                         writes=[("Ubf", bi), ("ps", b)])
                A(t_u)
            return tasks[:split[0]], tasks[split[0]:]

        def stage3_tasks(st):
            h, tt8 = st // 8, st % 8
            bi = st % NB2
            b3 = st % 3
            t0 = tt8 * 512
            n0 = tt8 * 8
            tasks = []
            A = tasks.append
            hold = {}

            def t_begin():
                hold["bo"] = bank("po")
            A(t_begin)
            for c in range(8):
                n = n0 + c
                cs = slice(c * 64, (c + 1) * 64)
                ri = n % 2
                first = (n == 0)

                def t_a(c=c, n=n, cs=cs, ri=ri, first=first):
                  with P.fds(pe=128, act=128, dve=128):
                    b1 = bank("q1")
                    P.op("pe", lambda e: e.matmul(ps[b1][0:64, 0:128], lhsT=ident_bf[0:64, 0:64], rhs=Ubf[bi][:, c, :],
                                                  start=True, stop=first),
                         reads=["ident_bf", ("Ubf", bi)], writes=[("ps", b1)])
                    if not first:
                        P.op("pe", lambda e: e.matmul(ps[b1][0:64, 0:128], lhsT=WnT[bi][:, c, :], rhs=S_b, start=False, stop=True),
                             reads=[("WnT", bi), "S_b"], writes=[("ps", b1)])
                    P.op("act", lambda e: e.activation(out=vnew[ri], in_=ps[b1][0:64, 0:128], func=ACT.Copy),
                         writes=[("vnew", ri), ("ps", b1)])
                A(t_a)

                def t_c(c=c, n=n, cs=cs, ri=ri, first=first):
                  with P.fds(pe=128, act=128, dve=128):
                    bo = hold["bo"]
                    b4 = bank("q4")
                    P.op("pe", lambda e: e.matmul(ps[b4][:, 0:128], lhsT=Kd[bi][:, c, :], rhs=vnew[ri], start=True, stop=True),
                         reads=[("Kd", bi), ("vnew", ri)], writes=[("ps", b4)])
                    if not first:
                        P.op("pe", lambda e: e.matmul(ps[bo][:, cs], lhsT=S_b, rhs=qgT[b3][:, cs], start=True, stop=False),
                             reads=["S_b", ("qgT", b3)], writes=[("ps", bo)])
                    P.op("pe", lambda e: e.matmul(ps[bo][:, cs], lhsT=vnew[ri], rhs=AqkT[bi][:, c, :], start=first, stop=True),
                         reads=[("vnew", ri), ("AqkT", bi)], writes=[("ps", bo)])
                    if first:
                        P.op("dve", lambda e: e.tensor_copy(out=S_b, in_=ps[b4][:, 0:128]), writes=["S_b", ("ps", b4)])
                        P.op("dve", lambda e: e.tensor_copy(out=S_f, in_=ps[b4][:, 0:128]), writes=["S_f", ("ps", b4)])
                    else:
                        P.op("dve", lambda e: e.scalar_tensor_tensor(
                            out=S_b, in0=S_f, scalar=dl_t[:, n, h:h + 1], in1=ps[b4][:, 0:128], op0=ALU.mult, op1=ALU.add),
                            reads=["S_f", "dl_t"], writes=["S_b", ("ps", b4)])
                        P.op("dve", lambda e: e.scalar_tensor_tensor(
                            out=S_f, in0=S_f, scalar=dl_t[:, n, h:h + 1], in1=ps[b4][:, 0:128], op0=ALU.mult, op1=ALU.add),
                            reads=["S_f", "dl_t"], writes=["S_f", ("ps", b4)])
                A(t_c)

            def t_epi():
                bo = hold["bo"]
                oi = st % 2
                P.op("act", lambda e: e.activation(out=oraw, in_=ps[bo][:, :], func=ACT.Copy), writes=["oraw", ("ps", bo)])
                if st == 0:
                    dump("khT0", khT[bi], [("khT", bi)])
                    dump("oraw0", oraw, ["oraw"])
                P.op("act", lambda e: e.activation(out=osq, in_=oraw, func=ACT.Square), reads=["oraw"], writes=["osq"])
                b = bank("pc")
                P.op("pe", lambda e: e.matmul(ps[b][:, :], lhsT=ones_bf, rhs=osq, start=True, stop=True),
                     reads=["osq", "ones_bf"], writes=[("ps", b)], fd=512)
                rsqrt_from_psum(b, orst, "orst", 1.0 / 128)
                P.op("dve", lambda e: e.scalar_tensor_tensor(out=oraw, in0=oraw, scalar=dnw[:, 0:1], in1=orst,
                                                             op0=ALU.mult, op1=ALU.mult),
                     reads=["oraw", "orst", "dnw"], writes=["oraw"])
                P.op("dve", lambda e: e.scalar_tensor_tensor(out=oa[oi], in0=oraw, scalar=0.5, in1=zsT[b3],
                                                             op0=ALU.mult, op1=ALU.mult),
                     reads=["oraw", ("zsT", b3)], writes=[("oa", oi)])
                P.dma(lambda q: q.dma_start(out=oa_scr[h, :, t0:t0 + 512], in_=oa[oi]),
                      reads=[("oa", oi)], writes=[("oa_scr", h, tt8)])
            A(t_epi)
            return tasks

        sg = [alloc([512], BF16) for _ in range(2)]
        sgf = [alloc([512], F32) for _ in range(2)]
        c0_tasks = []
        c0_hold = {}
        for c16 in range(16):
            for tt8 in range(8):
                def t_c0(c16=c16, tt8=tt8):
                    if tt8 == 0:
                        off = (O_GA + c16 * 128) if c16 < 8 else (O_GB + (c16 - 8) * 128)
                        c0_hold["w"] = load_w(w_in, off)
                    wt, wk = c0_hold["w"]
                    i = (c16 * 8 + tt8) % 2
                    b = proj_tile(wt, wk, 128, tt8, grp="pc")
                    P.op("act", lambda e: e.activation(out=sgf[i], in_=ps[b][:, :], func=ACT.Tanh, scale=0.5),
                         writes=[("sgf", i), ("ps", b)], tset="T")
                    P.op("dve", lambda e: e.tensor_scalar(out=sg[i], in0=sgf[i], scalar1=0.5, scalar2=0.5, op0=ALU.mult, op1=ALU.add),
                         reads=[("sgf", i)], writes=[("sg", i)])
                    P.dma(lambda q: q.dma_start(out=sg_scr[c16, :, tt8 * 512:(tt8 + 1) * 512], in_=sg[i]),
                          reads=[("sg", i)], writes=[("sg_scr", tt8)])
                c0_tasks.append(t_c0)

        load_head_w(0)
        NSTEP = 64

        def merge(ta, tb):
            out = []
            na, nb_ = len(ta), len(tb)
            jj = 0
            for ii, t in enumerate(ta):
                out.append(t)
                tgt = (nb_ * (ii + 1) + na - 1) // max(na, 1)
                while jj < min(tgt, nb_):
                    out.append(tb[jj])
                    jj += 1
            out.extend(tb[jj:])
            return out

        s1 = {}
        s2 = {}
        for st in range(NSTEP):
            s1[st], s2[st] = None, None

        def get12(st):
            if st >= NSTEP:
                return [], []
            return stage12_tasks(st)

        a1, a2 = get12(0)
        for t in a1:
            t()
        b1_, b2_ = get12(1)
        for t in merge(a2, b1_):
            t()
        pend2 = b2_
        for st in range(NSTEP):
            t3 = stage3_tasks(st)
            n1, n2 = get12(st + 2)
            filler = merge(pend2, n1) if len(pend2) >= len(n1) else merge(n1, pend2)
            pend2 = n2
            for t in merge(t3, filler):
                t()
            for _ in range(2):
                if c0_tasks:
                    c0_tasks.pop(0)()
        while c0_tasks:
            c0_tasks.pop(0)()
        cursor[0] = ph

    if stop_after not in ("a0", "b"):
        phase_a()
        P.fence()
        dump("oa_scr", oa_scr, [])

    def phase_c0():
        ph = cursor[0]
        sg = [alloc([512], BF16) for _ in range(4)]
        sgf = [alloc([512], F32) for _ in range(4)]
        cnt = 0
        for c16 in range(16):
            off = (O_GA + c16 * 128) if c16 < 8 else (O_GB + (c16 - 8) * 128)
            wt, wk = load_w(w_in, off)
            for tt8 in range(8):
                b = proj_tile(wt, wk, 128, tt8, grp="all")
                i = cnt % 4
                cnt += 1
                P.op("act", lambda e, b=b, i=i: e.activation(out=sgf[i], in_=ps[b][:, :], func=ACT.Tanh, scale=0.5),
                     writes=[("sgf", i), ("ps", b)], tset="T")
                P.op("dve", lambda e, i=i: e.tensor_scalar(out=sg[i], in0=sgf[i], scalar1=0.5, scalar2=0.5, op0=ALU.mult, op1=ALU.add),
                     reads=[("sgf", i)], writes=[("sg", i)])
                P.dma(lambda q, i=i, c16=c16, tt8=tt8: q.dma_start(out=sg_scr[c16, :, tt8 * 512:(tt8 + 1) * 512], in_=sg[i]),
                      reads=[("sg", i)], writes=[("sg_scr", tt8)])
        cursor[0] = ph

    def phase_c1():
      with P.fds(pe=512):
        ph = cursor[0]
        wodn = alloc([8, 1024], BF16)
        wodil = alloc([4, 1024], BF16)
        wout = alloc([8, 1024], BF16)
        fnw_b = alloc([D], F32)
        for (dst, src, kc, key) in ((wodn, w_odn, 8, "wodn"), (wodil, w_odil, 4, "wodil"), (wout, w_out, 8, "wout")):
            for half in range(2):
                P.dma(lambda q, dst=dst, src=src, kc=kc, half=half: q.dma_start(
                    out=dst[:, 0:kc, half * 512:(half + 1) * 512],
                    in_=src[:, half * 512:(half + 1) * 512].rearrange("(k p) c -> p k c", p=128)),
                    writes=[(key, half)], q="pool")
        P.dma(lambda q: q.dma_start(out=fnw_b, in_=fnw_d.partition_broadcast(128)), writes=["fnw_b"])
        oat = [alloc([8, 512], BF16) for _ in range(2)]
        obt = [alloc([4, 512], BF16) for _ in range(2)]
        sgt = [alloc([16, 512], BF16) for _ in range(2)]
        mT = alloc([8, 512], BF16)
        m1 = [alloc([512], F32) for _ in range(2)]
        m2 = [alloc([512], F32) for _ in range(2)]
        xr = [alloc([D], F32) for _ in range(2)]
        xo = [alloc([D], F32) for _ in range(2)]
        yo = [alloc([D], F32) for _ in range(2)]
        ss = [alloc([1], F32) for _ in range(2)]
        rs = [alloc([1], F32) for _ in range(2)]
        junk2 = alloc([D], BF16)
        cnt = 0
        for tt8 in range(8):
            i = tt8 % 2
            sl = slice(tt8 * 512, (tt8 + 1) * 512)
            P.dma(lambda q, i=i, sl=sl: q.dma_start(out=oat[i], in_=oa_scr[:, :, sl].rearrange("h p t -> p h t")),
                  reads=[("oa_scr", hh, tt8) for hh in range(8)], writes=[("oat", i)])
            P.dma(lambda q, i=i, sl=sl: q.dma_start(out=obt[i], in_=ob_scr[:, :, sl].rearrange("h p t -> p h t")),
                  reads=[("ob_scr", hh, tt8) for hh in range(4)], writes=[("obt", i)])
            P.dma(lambda q, i=i, sl=sl: q.dma_start(out=sgt[i], in_=sg_scr[:, :, sl].rearrange("h p t -> p h t")),
                  reads=[("sg_scr", tt8)], writes=[("sgt", i)])
            for c in range(8):
                cs = slice(c * 128, (c + 1) * 128)
                ba = bank("all")
                for k in range(8):
                    P.op("pe", lambda e, b=ba, k=k, cs=cs, i=i: e.matmul(ps[b][:, :], lhsT=wodn[:, k, cs], rhs=oat[i][:, k, :],
                                                                         start=(k == 0), stop=(k == 7)),
                         reads=[("wodn", c // 4), ("oat", i)], writes=[("ps", ba)])
                bb = bank("all")
                for k in range(4):
                    P.op("pe", lambda e, b=bb, k=k, cs=cs, i=i: e.matmul(ps[b][:, :], lhsT=wodil[:, k, cs], rhs=obt[i][:, k, :],
                                                                         start=(k == 0), stop=(k == 3)),
                         reads=[("wodil", c // 4), ("obt", i)], writes=[("ps", bb)])
                mi = cnt % 2
                cnt += 1
                P.op("dve", lambda e, b=ba, mi=mi, i=i, c=c: e.tensor_tensor(out=m1[mi], in0=ps[b][:, :], in1=sgt[i][:, c, :], op=ALU.mult),
                     reads=[("sgt", i)], writes=[("m1", mi), ("ps", ba)])
                P.op("dve", lambda e, b=bb, mi=mi, i=i, c=c: e.tensor_tensor(out=m2[mi], in0=ps[b][:, :], in1=sgt[i][:, 8 + c, :], op=ALU.mult),
                     reads=[("sgt", i)], writes=[("m2", mi), ("ps", bb)])
                P.op("pool", lambda e, mi=mi, c=c: e.tensor_tensor(out=mT[:, c, :], in0=m1[mi], in1=m2[mi], op=ALU.add),
                     reads=[("m1", mi), ("m2", mi)], writes=[("mT", c)])
            for sub in range(4):
                tok0 = tt8 * 512 + sub * 128
                xi = sub % 2
                P.dma(lambda q, xi=xi, tok0=tok0: q.dma_start(out=xr[xi], in_=x_d[tok0:tok0 + 128, :]), writes=[("xr", xi)])
                for half in range(2):
                    b = bank("all")
                    hs = slice(half * 512, (half + 1) * 512)
                    for c in range(8):
                        P.op("pe", lambda e, b=b, c=c, sub=sub, hs=hs: e.matmul(
                            ps[b][:, :], lhsT=mT[:, c, sub * 128:(sub + 1) * 128], rhs=wout[:, c, hs], start=(c == 0), stop=(c == 7)),
                            reads=[("mT", c), ("wout", half)], writes=[("ps", b)])
                    P.op("dve", lambda e, b=b, xi=xi, hs=hs: e.tensor_tensor(out=xo[xi][:, hs], in0=ps[b][:, :], in1=xr[xi][:, hs], op=ALU.add),
                         reads=[("xr", xi)], writes=[("xo", xi, half), ("ps", b)])
                P.op("pool", lambda e, xi=xi: e.memset(ss[xi], 0.0), writes=[("ss", xi)])
                P.op("act", lambda e, xi=xi: e.activation(out=junk2, in_=xo[xi], func=ACT.Square, accum_out=ss[xi]),
                     reads=[("xo", xi, 0), ("xo", xi, 1), ("ss", xi)], writes=["junk2", ("ss", xi)])
                P.op("dve", lambda e, xi=xi: e.tensor_scalar(out=ss[xi], in0=ss[xi], scalar1=1.0 / D, scalar2=EPS, op0=ALU.mult, op1=ALU.add),
                     reads=[("ss", xi)], writes=[("ss", xi)])
                P.op("act", lambda e, xi=xi: e.activation(out=ss[xi], in_=ss[xi], func=ACT.Ln), reads=[("ss", xi)], writes=[("ss", xi)], tset="L", fd=1)
                P.op("act", lambda e, xi=xi: e.activation(out=rs[xi], in_=ss[xi], func=ACT.Exp, scale=-0.5), reads=[("ss", xi)], writes=[("rs", xi)], fd=1)
                P.op("dve", lambda e, xi=xi: e.scalar_tensor_tensor(out=yo[xi], in0=xo[xi], scalar=rs[xi], in1=fnw_b,
                                                                    op0=ALU.mult, op1=ALU.mult),
                     reads=[("xo", xi, 0), ("xo", xi, 1), ("rs", xi), "fnw_b"], writes=[("yo", xi)])
                P.dma(lambda q, xi=xi, tok0=tok0: q.dma_start(out=out_d[tok0:tok0 + 128, :], in_=yo[xi]),
                      reads=[("yo", xi)], writes=[("out", tok0)])
        cursor[0] = ph

    if stop_after is None:
        dump("sg_scr", sg_scr, [])
        cursor[0] = 0
        phase_c1()

    if RESCHEDULE:
        P.sim_time = P.reschedule()
    P.emit(final_wait_ops=[o for o in P.dma_last if o is not None])
    return nc


def _consts():
    c = np.zeros((128, 128 + 12 * 256 + 256), np.float32)
    c[:, 0:128] = np.eye(128, dtype=np.float32)
    slopes = (2.0 ** (-8.0 * np.arange(1, 13, dtype=np.float32) / 12)).reshape(3, 4)
    jk = np.arange(128)[:, None]
    iq = np.arange(128)[None, :]
    for gi in range(3):
        for h in range(4):
            s = slopes[gi, h] * DIL[gi]
            d0 = (iq - jk).astype(np.float32)
            b0 = np.where(iq >= jk, -s * d0, NEG)
            d1 = (128 + iq - jk).astype(np.float32)
            b1 = np.where(iq <= jk, -s * d1, NEG)
            g = gi * 4 + h
            c[:, 128 + g * 256:128 + g * 256 + 128] = b0
            c[:, 128 + g * 256 + 128:128 + g * 256 + 256] = b1
    MK = 128 + 3072
    j = np.arange(64)[:, None]
    i = np.arange(64)[None, :]
    c[0:64, MK:MK + 64] = np.where(i >= j, 0.0, NEG)
    c[0:64, MK + 64:MK + 128] = np.where(i > j, 0.0, NEG)
    c[0:64, MK + 128:MK + 192] = (j <= i).astype(np.float32)
    c[0:64, MK + 192:MK + 256] = 1.0
    return c


_NC_CACHE = {}


def _host_inputs(x, norm_w, w_in, conv_w, a_log, dt_bias, dn_norm_w, w_o_dn, w_o_dil, w_out, final_norm_w):
    f = lambda a: np.ascontiguousarray(np.asarray(a, dtype=np.float32))
    cw = f(conv_w)[0].reshape(4, 24, 128).transpose(2, 1, 0).reshape(128, 96)
    shared = {
        "w_in": f(w_in)[0], "w_o_dn": f(w_o_dn)[0], "w_o_dil": f(w_o_dil)[0], "w_out": f(w_out)[0],
        "norm_w": f(norm_w).reshape(1, D), "final_norm_w": f(final_norm_w).reshape(1, D),
        "conv_w_l": np.ascontiguousarray(cw), "a_log": f(a_log).reshape(1, 8), "dt_bias": f(dt_bias).reshape(1, 8),
        "dn_norm_w_l": f(dn_norm_w).reshape(128, 1), "consts": _consts(),
    }
    xs = f(x)
    return [dict(shared, x=xs[b]) for b in range(xs.shape[0])]


def kernel(x, norm_w, w_in, conv_w, a_log, dt_bias, dn_norm_w, w_o_dn, w_o_dil, w_out, final_norm_w):
    in_maps = _host_inputs(x, norm_w, w_in, conv_w, a_log, dt_bias, dn_norm_w, w_o_dn, w_o_dil, w_out, final_norm_w)
    if "nc" not in _NC_CACHE:
        _NC_CACHE["nc"] = build_nc()
    res = run_bass_kernel_spmd(_NC_CACHE["nc"], in_maps, core_ids=list(range(len(in_maps))))
    return np.stack([np.asarray(r["out"], dtype=np.float32).reshape(T, D) for r in res.results], axis=0)
```

```python
import contextlib
import numpy as np
import concourse.bass as bass
import concourse.mybir as mybir
from concourse.bass_utils import run_bass_kernel_spmd

ACT = mybir.ActivationFunctionType
ALU = mybir.AluOpType
F32 = mybir.dt.float32
BF16 = mybir.dt.bfloat16

T = 4096
D = 1024
NEG = -30000.0
PIPELINE_A = True
RESCHEDULE = True
SCHED_IDENTITY = True
XLAT = 0.0
CPW = 0.0
CHAIN_FIRST = False
S2_FIRST = False
EPS = 1e-6
O_QA, O_KA, O_VA, O_ZA, O_BA, O_QB, O_KB, O_VB, O_ZB, O_GA, O_GB = (
    0, 1024, 2048, 3072, 4096, 4112, 5648, 7184, 8720, 9232, 10256)
DIL = (1, 4, 16)


class _Op:
    __slots__ = ("eng", "fn", "deps", "signal", "ticket", "is_dma", "dsem", "dval", "odeps", "cost", "idx", "seg", "war", "pos", "tset")

    def __init__(self, eng, fn, is_dma=False):
        self.eng = eng
        self.fn = fn
        self.odeps = []
        self.cost = 0.0
        self.idx = 0
        self.seg = 0
        self.war = []
        self.pos = 0
        self.tset = None
        self.deps = []
        self.signal = False
        self.ticket = None
        self.is_dma = is_dma
        self.dsem = None
        self.dval = None


class _Res:
    __slots__ = ("w", "r", "rd")

    def __init__(self):
        self.w = None
        self.r = []
        self.rd = []


class Prog:
    ENGS = ("pe", "act", "dve", "pool", "sp")

    def __init__(self, nc, n_dma_sems=32):
        self.nc = nc
        self.streams = {e: [] for e in self.ENGS}
        self.res = {}
        self.n_dma_sems = n_dma_sems
        self.dma_cnt = [0] * n_dma_sems
        self.dma_last = [None] * n_dma_sems
        self.dma_rr = 0
        self.fence_ops = []
        self.fence_dma = []
        self.seg = 0
        self.all_ops = []
        self.fd_default = {}

    @contextlib.contextmanager
    def fds(self, **kw):
        old = dict(self.fd_default)
        self.fd_default.update(kw)
        try:
            yield
        finally:
            self.fd_default = old

    def fence(self):
        self.fence_dma.append([d for d in self.dma_last if d is not None])
        self.seg += 1

    def _r(self, k):
        r = self.res.get(k)
        if r is None:
            r = self.res[k] = _Res()
        return r

    def op(self, eng, fn, reads=(), writes=(), is_dma=False, fd=None, tset=None):
        o = _Op(eng, fn, is_dma)
        o.tset = tset
        fd = fd or self.fd_default.get(eng)
        if is_dma:
            o.cost = 0.15
        elif eng == "pe":
            o.cost = max(64, fd or 64) / 2400.0 + 0.004
        elif eng == "act":
            o.cost = (224 + (fd or 512)) / 1200.0
        elif eng == "dve":
            o.cost = (110 + (fd or 512)) / 960.0
        else:
            o.cost = 0.2 + (fd or 512) / 1000.0
        o.idx = len(self.all_ops)
        o.seg = self.seg
        self.all_ops.append(o)
        deps = []
        for k in reads:
            r = self._r(k)
            if r.w is not None:
                deps.append((r.w, "raw"))
        for k in writes:
            r = self._r(k)
            if r.w is not None:
                deps.append((r.w, "waw"))
            for rd in r.r:
                if rd is not o:
                    o.war.append(rd)
            for rd in r.rd:
                if rd is not o:
                    o.war.append(rd)
        if is_dma:
            i = self.dma_rr
            self.dma_rr = (i + 1) % self.n_dma_sems
            o.dsem = i
            self.dma_cnt[i] += 1
            o.dval = 16 * self.dma_cnt[i]
            if self.dma_last[i] is not None:
                deps.append((self.dma_last[i], "raw"))
            self.dma_last[i] = o
        seen = set()
        oseen = set()
        for d, kind in deps:
            if d is not o and id(d) not in oseen:
                oseen.add(id(d))
                o.odeps.append(d)
        for d in o.war:
            if id(d) not in oseen:
                oseen.add(id(d))
                o.odeps.append(d)
        for d, kind in deps:
            if d is o or id(d) in seen:
                continue
            if not d.is_dma and d.eng == eng and not is_dma:
                if eng == "pe":
                    continue
                if kind != "raw":
                    continue
            seen.add(id(d))
            d.signal = True
            o.deps.append(d)
        for k in reads:
            r = self._r(k)
            if is_dma:
                r.rd.append(o)
            else:
                r.r.append(o)
        for k in writes:
            r = self._r(k)
            r.w = o
            r.r = []
            r.rd = []
        self.streams[eng].append(o)
        return o

    def dma(self, fn, reads=(), writes=(), q="sp"):
        return self.op(q, fn, reads, writes, is_dma=True)

    def reschedule(self, dma_latency=3.0):
        import heapq
        ops = self.all_ops
        n = len(ops)
        succ = [[] for _ in range(n)]
        indeg = [0] * n
        for o in ops:
            for d in o.odeps:
                if d.seg == o.seg:
                    succ[d.idx].append(o.idx)
                    indeg[o.idx] += 1
        prio = [0.0] * n
        for i in range(n - 1, -1, -1):
            o = ops[i]
            m = 0.0
            for j in succ[i]:
                if prio[j] > m:
                    m = prio[j]
            prio[i] = m + (dma_latency if o.is_dma else o.cost)
        if SCHED_IDENTITY:
            prio = [float(n - i) + CPW * prio[i] for i in range(n)]
        new_streams = {e: [] for e in self.ENGS}
        now = 0.0
        cur_set = [None]
        free_at = {e: 0.0 for e in self.ENGS}
        nseg = self.seg + 1
        byseg = [[] for _ in range(nseg)]
        for o in ops:
            byseg[o.seg].append(o.idx)
        for sg in range(nseg):
            idxs = byseg[sg]
            ready = {e: [] for e in self.ENGS}
            for i in idxs:
                if indeg[i] == 0:
                    heapq.heappush(ready[ops[i].eng], (-prio[i], i))
            events = []
            done = 0
            tot = len(idxs)
            while done < tot:
                started = False
                for e in self.ENGS:
                    if free_at[e] <= now and ready[e]:
                        _, i = heapq.heappop(ready[e])
                        o = ops[i]
                        extra = 0.0
                        if e == "act":
                            if o.tset is not None and o.tset != cur_set[0]:
                                held = [(-prio[i], i)]
                                found = None
                                for _ in range(6):
                                    if not ready[e]:
                                        break
                                    c = heapq.heappop(ready[e])
                                    oc = ops[c[1]]
                                    if (oc.tset is None or oc.tset == cur_set[0]) and prio[c[1]] > prio[i] - 6.0:
                                        found = c
                                        break
                                    held.append(c)
                                for c in held:
                                    if found is not None or c[1] != i:
                                        heapq.heappush(ready[e], c)
                                if found is not None:
                                    i = found[1]
                                    o = ops[i]
                                else:
                                    cur_set[0] = o.tset
                                    extra = 1.3
                        new_streams[e].append(o)
                        free_at[e] = now + o.cost + extra
                        heapq.heappush(events, (now + (dma_latency if o.is_dma else o.cost + extra + XLAT), i))
                        started = True
                if started:
                    continue
                cand = []
                if events:
                    cand.append(events[0][0])
                for e in self.ENGS:
                    if ready[e] and free_at[e] > now:
                        cand.append(free_at[e])
                now = min(cand)
                while events and events[0][0] <= now:
                    _, i = heapq.heappop(events)
                    done += 1
                    for j in succ[i]:
                        indeg[j] -= 1
                        if indeg[j] == 0:
                            heapq.heappush(ready[ops[j].eng], (-prio[j], j))
        for e in self.ENGS:
            assert len(new_streams[e]) == len(self.streams[e])
        self.streams = new_streams
        return now

    def apply_fences(self):
        last = {}
        pos = {e: 0 for e in self.ENGS}
        for sg in range(1, self.seg + 1):
            for e in self.ENGS:
                st = self.streams[e]
                while pos[e] < len(st) and st[pos[e]].seg < sg:
                    if not st[pos[e]].is_dma:
                        last[e] = st[pos[e]]
                    pos[e] += 1
            for e in self.ENGS:
                st = self.streams[e]
                if pos[e] < len(st) and st[pos[e]].seg == sg:
                    o = st[pos[e]]
                    extra = [d for d in last.values()] + list(self.fence_dma[sg - 1])
                    have = set(id(d) for d in o.deps)
                    for d in extra:
                        if d is o or id(d) in have:
                            continue
                        if not d.is_dma and d.eng == e and e == "pe":
                            continue
                        d.signal = True
                        o.deps.append(d)

    def emit(self, final_wait_ops=()):
        nc = self.nc
        for e in self.ENGS:
            for p_, o in enumerate(self.streams[e]):
                o.pos = p_
        for o in self.all_ops:
            if not o.war:
                continue
            best = {}
            have = set(id(d) for d in o.deps)
            for d in o.war:
                if d.is_dma:
                    if id(d) not in have:
                        have.add(id(d))
                        d.signal = True
                        o.deps.append(d)
                    continue
                b = best.get(d.eng)
                if b is None or d.pos > b.pos:
                    best[d.eng] = d
            for e, d in best.items():
                if e == o.eng and not o.is_dma:
                    continue
                if id(d) in have:
                    continue
                d.signal = True
                o.deps.append(d)
        self.apply_fences()
        for e in self.ENGS:
            c = 0
            for o in self.streams[e]:
                if o.is_dma:
                    continue
                if o.signal:
                    c += 1
                    o.ticket = c
        with contextlib.ExitStack() as es:
            esem = {e: es.enter_context(nc.semaphore("s_" + e)) for e in self.ENGS}
            dsem = [es.enter_context(nc.semaphore("d_%d" % i)) for i in range(self.n_dma_sems)]
            block = es.enter_context(nc.Block())

            def run(e, engobj):
                waited = {}

                def wait_for(d):
                    if d.is_dma:
                        key, sem, val = ("d", d.dsem), dsem[d.dsem], d.dval
                    else:
                        key, sem, val = ("e", d.eng), esem[d.eng], d.ticket
                    if waited.get(key, 0) >= val:
                        return
                    waited[key] = val
                    engobj.wait_ge(sem, val)

                for o in self.streams[e]:
                    for d in o.deps:
                        wait_for(d)
                    ins = o.fn(engobj)
                    if o.is_dma:
                        ins.then_inc(dsem[o.dsem], 16)
                    elif o.signal:
                        ins.then_inc(esem[e], 1)
                if e == "sp":
                    for d in final_wait_ops:
                        wait_for(d)

            @block.tensor
            def _(eng):
                run("pe", eng)

            @block.scalar
            def _(eng):
                run("act", eng)

            @block.vector
            def _(eng):
                run("dve", eng)

            @block.gpsimd
            def _(eng):
                run("pool", eng)

            @block.sync
            def _(eng):
                run("sp", eng)


def build_nc(dbg=None, stop_after=None):
    dbg = dbg or {}
    nc = bass.Bass("TRN2", target_bir_lowering=False)
    dt = nc.dram_tensor
    x_d = dt("x", [T, D], F32, kind="ExternalInput").ap()
    w_in = dt("w_in", [D, 11280], F32, kind="ExternalInput").ap()
    w_odn = dt("w_o_dn", [1024, 1024], F32, kind="ExternalInput").ap()
    w_odil = dt("w_o_dil", [512, 1024], F32, kind="ExternalInput").ap()
    w_out = dt("w_out", [1024, 1024], F32, kind="ExternalInput").ap()
    normw_d = dt("norm_w", [1, D], F32, kind="ExternalInput").ap()
    fnw_d = dt("final_norm_w", [1, D], F32, kind="ExternalInput").ap()
    cw_d = dt("conv_w_l", [128, 96], F32, kind="ExternalInput").ap()
    alog_d = dt("a_log", [1, 8], F32, kind="ExternalInput").ap()
    dtb_d = dt("dt_bias", [1, 8], F32, kind="ExternalInput").ap()
    dnw_d = dt("dn_norm_w_l", [128, 1], F32, kind="ExternalInput").ap()
    cst_d = dt("consts", [128, 128 + 12 * 256 + 64 * 4], F32, kind="ExternalInput").ap()
    out_d = dt("out", [T, D], F32, kind="ExternalOutput").ap()
    ob_scr = dt("ob_scr", [4, 128, T], BF16).ap()
    oa_scr = dt("oa_scr", [8, 128, T], BF16).ap()
    sg_scr = dt("sg_scr", [16, 128, T], BF16).ap()
    gc_scr = dt("gc_scr", [8, T], F32).ap()
    gcl_scr = dt("gcl_scr", [8, T], F32).ap()
    dbg_out = {}
    for name, (shape, dtype) in dbg.items():
        dbg_out[name] = dt("dbg_" + name, list(shape), dtype, kind="ExternalOutput").ap()

    P = Prog(nc)
    ARENA = 212000
    arena = nc.alloc_sbuf_tensor("arena", [128, ARENA // 2], BF16)
    cursor = [0]

    def alloc(free_shape, dtype, parts=128):
        n = 1
        for s in free_shape:
            n *= s
        esz = 4 if dtype == F32 else 2
        nbytes = (n * esz + 63) // 64 * 64
        off = cursor[0]
        cursor[0] += nbytes
        assert cursor[0] <= ARENA, ("SBUF overflow", cursor[0])
        ap = arena[0:parts, off // 2: off // 2 + n * esz // 2]
        if dtype == F32:
            ap = ap.bitcast(F32)
        if len(free_shape) == 2:
            ap = ap.rearrange("p (a b) -> p a b", b=free_shape[1])
        elif len(free_shape) == 3:
            ap = ap.rearrange("p (a b c) -> p a b c", b=free_shape[1], c=free_shape[2])
        return ap

    ps = [nc.alloc_psum_tensor("ps%d" % i, [128, 512], F32) for i in range(8)]
    psb = [p[:].bitcast(BF16) for p in ps]
    bank_rr = {}

    def bank(group):
        lst = {"proj": (0, 1), "misc": (2,), "pc": (0, 1, 2), "pre": (3, 4), "q1": (5,), "q4": (6,), "po": (7,),
               "all": tuple(range(8)), "s": (3, 4), "a": (5, 7), "b": (6, 2)}[group]
        i = bank_rr.get(group, 0)
        bank_rr[group] = i + 1
        return lst[i % len(lst)]

    def dump(name, src_ap, reads):
        if name in dbg_out:
            P.dma(lambda q, a=src_ap, o=dbg_out[name]: q.dma_start(out=o, in_=a), reads=reads, writes=[("dbg", name)])

    hT = alloc([8, T], BF16)
    cst = alloc([128 + 12 * 256 + 256], F32)
    identf = cst[:, 0:128]
    alibi = cst[:, 128:128 + 3072].rearrange("p (g w) -> p g w", w=256)
    MK = 128 + 3072
    mask_incl = cst[0:64, MK:MK + 64]
    mask_strict = cst[0:64, MK + 64:MK + 128]
    triu_f = cst[0:64, MK + 128:MK + 192]
    ones64f = cst[0:64, MK + 192:MK + 256]
    ident_bf = alloc([128], BF16)
    ones_bf = alloc([128], BF16)
    ones_f = alloc([128], F32)
    cw = alloc([96], F32)
    dnw = alloc([1], F32)
    WB_N = 3
    wb = [alloc([8, 128], BF16) for _ in range(WB_N)]
    wb_rr = [0]
    beta_t = alloc([64, 8], F32, 64)
    gc_t = alloc([64, 8], F32, 64)
    negeg_t = alloc([64, 8], F32, 64)
    wdec_t = alloc([64, 8], F32, 64)
    dl_t = alloc([64, 8], F32)
    persist_end = cursor[0]

    P.dma(lambda q: q.dma_start(out=cst, in_=cst_d), writes=["cst"])
    P.dma(lambda q: q.dma_start(out=cw, in_=cw_d), writes=["cw"])
    P.dma(lambda q: q.dma_start(out=dnw, in_=dnw_d), writes=["dnw"])
    P.op("dve", lambda e: e.tensor_copy(out=ident_bf, in_=identf), reads=["cst"], writes=["ident_bf"])
    P.op("pool", lambda e: e.memset(ones_bf, 1.0), writes=["ones_bf"])
    P.op("pool", lambda e: e.memset(ones_f, 1.0), writes=["ones_f"])

    def load_w(src, c0, ncols=128, kchunks=8, dst=None, key=None):
        if dst is None:
            i = wb_rr[0] % WB_N
            wb_rr[0] += 1
            dst, key = wb[i], ("wb", i)
        P.dma(lambda q, d=dst, s=src, c0=c0, n=ncols, kc=kchunks: q.dma_start(
            out=d[:, 0:kc, 0:n], in_=s[:, c0:c0 + n].rearrange("(k p) c -> p k c", p=128)),
            writes=[key], q="pool")
        return dst, key

    def hkeys(t0, t1):
        return [("hT", i) for i in range(t0 // 128, (t1 + 127) // 128)]

    def proj_tile(wt, wkey, ncols, tt8, grp="proj"):
        b = bank(grp)
        for k in range(8):
            P.op("pe", lambda e, b=b, k=k, wt=wt, n=ncols, tt8=tt8: e.matmul(
                ps[b][0:n, :], lhsT=wt[:, k, 0:n], rhs=hT[:, k, tt8 * 512:(tt8 + 1) * 512],
                start=(k == 0), stop=(k == 7)),
                reads=[wkey] + hkeys(tt8 * 512, tt8 * 512 + 512), writes=[("ps", b)], fd=512)
        return b

    ph = cursor[0]
    normw_b = alloc([D], F32)
    xs = [alloc([D], F32) for _ in range(4)]
    junk = [alloc([D], BF16) for _ in range(2)]
    xb = [alloc([D], BF16) for _ in range(4)]
    ss0 = [alloc([1], F32) for _ in range(4)]
    rs0 = [alloc([1], F32) for _ in range(4)]
    P.dma(lambda q: q.dma_start(out=normw_b, in_=normw_d.partition_broadcast(128)), writes=["normw_b"])
    for tt in range(32):
        i = tt % 4
        P.dma(lambda q, i=i, tt=tt: q.dma_start(out=xs[i], in_=x_d[tt * 128:(tt + 1) * 128, :]), writes=[("xs", i)])
        P.op("pool", lambda e, i=i: e.memset(ss0[i], 0.0), writes=[("ss0", i)])
        P.op("act", lambda e, i=i: e.activation(out=junk[i % 2], in_=xs[i], func=ACT.Square, accum_out=ss0[i]),
             reads=[("xs", i), ("ss0", i)], writes=[("junk", i % 2), ("ss0", i)])
        P.op("dve", lambda e, i=i: e.tensor_scalar(out=ss0[i], in0=ss0[i], scalar1=1.0 / D, scalar2=EPS, op0=ALU.mult, op1=ALU.add),
             reads=[("ss0", i)], writes=[("ss0", i)])
        P.op("act", lambda e, i=i: e.activation(out=ss0[i], in_=ss0[i], func=ACT.Ln), reads=[("ss0", i)], writes=[("ss0", i)], tset="L", fd=1)
        P.op("act", lambda e, i=i: e.activation(out=rs0[i], in_=ss0[i], func=ACT.Exp, scale=-0.5), reads=[("ss0", i)], writes=[("rs0", i)], fd=1)
        P.op("dve", lambda e, i=i: e.scalar_tensor_tensor(out=xb[i], in0=xs[i], scalar=rs0[i], in1=normw_b,
                                                          op0=ALU.mult, op1=ALU.mult),
             reads=[("xs", i), ("rs0", i), "normw_b"], writes=[("xb", i)])
        b = bank("all")
        for k in range(8):
            P.op("pe", lambda e, b=b, k=k, i=i: e.transpose(out=psb[b][:, k * 128:(k + 1) * 128],
                                                           in_=xb[i][:, k * 128:(k + 1) * 128], identity=ident_bf),
                 reads=[("xb", i), "ident_bf"], writes=[("ps", b)])
        eng = "act" if tt % 2 == 0 else "dve"
        if eng == "act":
            fn = lambda e, b=b, tt=tt: e.activation(out=hT[:, :, tt * 128:(tt + 1) * 128],
                                                    in_=psb[b].rearrange("p (k t) -> p k t", t=128), func=ACT.Copy)
        else:
            fn = lambda e, b=b, tt=tt: e.tensor_copy(out=hT[:, :, tt * 128:(tt + 1) * 128],
                                                     in_=psb[b].rearrange("p (k t) -> p k t", t=128))
        P.op(eng, fn, writes=[("hT", tt), ("ps", b)])
    dump("hT", hT, hkeys(0, T))
    cursor[0] = ph
    P.fence()

    def phase_a0():
        ph = cursor[0]
        w16, w16k = load_w(w_in, O_BA, 16)
        alog_b = alloc([8], F32, 64)
        dtb_b = alloc([8], F32, 64)
        P.dma(lambda q: q.dma_start(out=alog_b, in_=alog_d.partition_broadcast(64)), writes=["alog_b"])
        P.dma(lambda q: q.dma_start(out=dtb_b, in_=dtb_d.partition_broadcast(64)), writes=["dtb_b"])
        Gsb = alloc([64, 16], F32, 64)
        names = ["xa", "ax", "ee", "ll", "sp", "g", "lb", "gcl", "tmp"]
        A = {n: alloc([64, 8], F32, 64) for n in names}
        glast = alloc([64, 8], F32)
        tb = alloc([8, 64], F32, 64)
        for half in range(2):
            b = bank("all")
            for n in range(32 * half, 32 * half + 32):
                for k in range(8):
                    P.op("pe", lambda e, b=b, n=n, k=k: e.matmul(
                        ps[b][0:64, (n % 32) * 16:(n % 32) * 16 + 16], lhsT=hT[:, k, n * 64:(n + 1) * 64],
                        rhs=w16[:, k, 0:16], start=(k == 0), stop=(k == 7)),
                        reads=[w16k] + hkeys(n * 64, n * 64 + 64), writes=[("ps", b)])
            P.op("act", lambda e, b=b, half=half: e.activation(
                out=Gsb[:, 32 * half:32 * half + 32, :], in_=ps[b][0:64, :].rearrange("p (n c) -> p n c", c=16),
                func=ACT.Copy), writes=["Gsb", ("ps", b)])
        bb = Gsb[:, :, 0:8]
        aa = Gsb[:, :, 8:16]
        bc = lambda v: v.unsqueeze(1).to_broadcast([64, 64, 8])
        P.op("act", lambda e: e.activation(out=beta_t, in_=bb, func=ACT.Sigmoid), reads=["Gsb"], writes=["beta_t"])
        P.op("act", lambda e: e.activation(out=A["lb"], in_=beta_t, func=ACT.Ln), reads=["beta_t"], writes=["lb"])
        P.op("dve", lambda e: e.tensor_tensor(out=A["xa"], in0=aa, in1=bc(dtb_b), op=ALU.add),
             reads=["Gsb", "dtb_b"], writes=["xa"])
        P.op("act", lambda e: e.activation(out=A["ax"], in_=A["xa"], func=ACT.Abs), reads=["xa"], writes=["ax"])
        P.op("act", lambda e: e.activation(out=A["ee"], in_=A["ax"], func=ACT.Exp, scale=-1.0), reads=["ax"], writes=["ee"])
        P.op("act", lambda e: e.activation(out=A["ll"], in_=A["ee"], func=ACT.Ln, bias=1.0), reads=["ee"], writes=["ll"])
        P.op("dve", lambda e: e.scalar_tensor_tensor(out=A["sp"], in0=A["xa"], scalar=0.0, in1=A["ll"],
                                                     op0=ALU.max, op1=ALU.add), reads=["xa", "ll"], writes=["sp"])
        P.op("act", lambda e: e.activation(out=alog_b, in_=alog_b, func=ACT.Exp), reads=["alog_b"], writes=["alog_b"])
        P.op("dve", lambda e: e.scalar_tensor_tensor(out=A["g"], in0=A["sp"], scalar=-1.0, in1=bc(alog_b),
                                                     op0=ALU.mult, op1=ALU.mult), reads=["sp", "alog_b"], writes=["g"])
        gflat = A["g"].rearrange("p n h -> p (n h)")
        b1 = bank("all")
        P.op("pe", lambda e: e.matmul(ps[b1][0:64, :], lhsT=triu_f, rhs=gflat, start=True, stop=True),
             reads=["g", "cst"], writes=[("ps", b1)])
        b2 = bank("all")
        P.op("pe", lambda e: e.matmul(ps[b2][:, :], lhsT=ones_f[0:64, :], rhs=gflat, start=True, stop=True),
             reads=["g", "ones_f"], writes=[("ps", b2)])
        fl = lambda v: v.rearrange("p n h -> p (n h)")
        P.op("act", lambda e: e.activation(out=fl(gc_t), in_=ps[b1][0:64, :], func=ACT.Copy), writes=["gc_t", ("ps", b1)])
        P.op("dve", lambda e: e.tensor_copy(out=fl(glast), in_=ps[b2][:, :]), writes=["glast", ("ps", b2)])
        P.op("act", lambda e: e.activation(out=negeg_t, in_=gc_t, func=ACT.Exp), reads=["gc_t"], writes=["negeg_t"])
        P.op("dve", lambda e: e.tensor_scalar(out=negeg_t, in0=negeg_t, scalar1=-1.0, scalar2=None, op0=ALU.mult),
             reads=["negeg_t"], writes=["negeg_t"])
        P.op("dve", lambda e: e.tensor_tensor(out=A["tmp"], in0=glast[0:64], in1=gc_t, op=ALU.subtract),
             reads=["glast", "gc_t"], writes=["tmp"])
        P.op("act", lambda e: e.activation(out=wdec_t, in_=A["tmp"], func=ACT.Exp), reads=["tmp"], writes=["wdec_t"])
        P.op("act", lambda e: e.activation(out=dl_t, in_=glast, func=ACT.Exp), reads=["glast"], writes=["dl_t"])
        P.op("dve", lambda e: e.tensor_tensor(out=A["gcl"], in0=gc_t, in1=A["lb"], op=ALU.add),
             reads=["gc_t", "lb"], writes=["gcl"])
        for nm, src, scr in (("gc", gc_t, gc_scr), ("gcl", A["gcl"], gcl_scr)):
            b = bank("all")
            for h in range(8):
                P.op("pe", lambda e, b=b, h=h, src=src: e.transpose(out=ps[b][0:64, h * 64:(h + 1) * 64],
                                                                   in_=src[:, :, h], identity=identf[0:64, 0:64]),
                     reads=["gc_t" if nm == "gc" else "gcl", "cst"], writes=[("ps", b)])
            P.op("dve", lambda e, b=b: e.tensor_copy(out=tb, in_=ps[b][0:64, :].rearrange("p (h c) -> p h c", c=64)),
                 writes=["tb", ("ps", b)])
            P.dma(lambda q, scr=scr: q.dma_start(out=scr.rearrange("h (n c) -> n h c", c=64), in_=tb),
                  reads=["tb"], writes=[nm + "_scr"])
        dump("gc_t", gc_t, ["gc_t"])
        dump("beta_t", beta_t, ["beta_t"])
        dump("g_t", A["g"], ["g"])
        cursor[0] = ph

    phase_a0()
    P.fence()

    def interleave(ta, tb):
        na, nb_ = len(ta), len(tb)
        j = 0
        for i, t in enumerate(ta):
            t()
            tgt = (nb_ * (i + 1) + na - 1) // max(na, 1)
            while j < min(tgt, nb_):
                tb[j]()
                j += 1
        while j < nb_:
            tb[j]()
            j += 1

    def phase_b():
        ph = cursor[0]
        qTs = [alloc([T], BF16) for _ in range(2)]
        kTs = [alloc([T], BF16) for _ in range(2)]
        vsbs = [alloc([32, 128], BF16) for _ in range(2)]
        vT = alloc([T], BF16)
        acc_n = alloc([T], F32)
        acc_d = alloc([T], F32)
        NS = 3
        s_sb = [alloc([256], F32) for _ in range(NS)]
        p_sb = [alloc([256], BF16) for _ in range(4)]
        zs = [alloc([512], F32) for _ in range(2)]
        rc = [alloc([512], F32) for _ in range(2)]
        ob = [alloc([512], BF16) for _ in range(2)]
        mone = alloc([512], F32)
        P.op("pool", lambda e: e.memset(mone, -1.0), writes=["mone"])
        scale = 128 ** -0.5
        jobs = [(h, gi) for h in range(4) for gi in range(3)]
        cnt = [0]

        def proj_tasks(jn):
            h, gi = jobs[jn]
            bi = jn % 2
            qT, kT, v_sb = qTs[bi], kTs[bi], vsbs[bi]
            d = DIL[gi]
            L = T // d
            nb = L // 128
            M = 512 // d
            tasks = []
            hold = {}

            def t_w():
                hold["q"] = load_w(w_in, O_QB + gi * 512 + h * 128)
                hold["k"] = load_w(w_in, O_KB + gi * 512 + h * 128)
                hold["v"] = load_w(w_in, O_VB + gi * 512 + h * 128)
            tasks.append(t_w)
            q3 = qT.rearrange("p (r m) -> p r m", r=d)
            k3 = kT.rearrange("p (r m) -> p r m", r=d)
            for tt8 in range(8):
                def t_q(tt8=tt8):
                    wq, wqk = hold["q"]
                    b = proj_tile(wq, wqk, 128, tt8)
                    P.op("act", lambda e: e.activation(out=qT[:, tt8 * 512:(tt8 + 1) * 512], in_=ps[b][:, :],
                                                       func=ACT.Copy, scale=scale),
                         writes=[("qT", bi), ("ps", b)])
                tasks.append(t_q)

                def t_k(tt8=tt8):
                    wk, wkk = hold["k"]
                    b = proj_tile(wk, wkk, 128, tt8)
                    P.op("dve", lambda e: e.tensor_copy(out=kT[:, tt8 * 512:(tt8 + 1) * 512], in_=ps[b][:, :]),
                         writes=[("kT", bi), ("ps", b)])
                tasks.append(t_k)

                def t_v(tt8=tt8):
                    wv, wvk = hold["v"]
                    b = proj_tile(wv, wvk, 128, tt8)
                    P.op("act", lambda e: e.activation(out=vT[:, tt8 * 512:(tt8 + 1) * 512], in_=ps[b][:, :], func=ACT.Copy),
                         writes=[("vT", tt8), ("ps", b)])
                tasks.append(t_v)
            for t8 in range(4):
                def t_vt(t8=t8):
                    b = bank("proj")
                    for s in range(8):
                        tid = t8 * 8 + s
                        r, j = tid // nb, tid % nb
                        t0 = 128 * j * d + r
                        P.op("pe", lambda e, s=s, t0=t0: e.transpose(
                            out=psb[b][:, s * 128:(s + 1) * 128], in_=vT[:, t0:t0 + 127 * d + 1:d], identity=ident_bf),
                            reads=[("vT", i) for i in range((128 * j * d) // 512, (128 * (j + 1) * d + 511) // 512)] + ["ident_bf"],
                            writes=[("ps", b)])
                    P.op("dve", lambda e: e.tensor_copy(
                        out=v_sb[:, t8 * 8:(t8 + 1) * 8, :], in_=psb[b][:, :].rearrange("p (s c) -> p s c", c=128)),
                        writes=[("v_sb", bi), ("ps", b)])
                tasks.append(t_vt)
            return tasks

        def core_tasks(jn):
            h, gi = jobs[jn]
            bi = jn % 2
            qT, kT, v_sb = qTs[bi], kTs[bi], vsbs[bi]
            d = DIL[gi]
            L = T // d
            nb = L // 128
            gidx = gi * 4 + h
            tasks = []
            st = {"bn": None, "bd": None}
            pis = {}

            def tok(r, j0, nblk):
                a = (128 * j0) * d + r
                return slice(a, a + (128 * nblk - 1) * d + 1, d)

            def t_qk(r, j):
              with P.fds(pe=256, act=256, dve=256):
                W = 2 if j + 1 < nb else 1
                bs = bank("s")
                P.op("pe", lambda e: e.matmul(ps[bs][:, 0:128 * W], lhsT=kT[:, tok(r, j, 1)], rhs=qT[:, tok(r, j, W)],
                                              start=True, stop=True),
                     reads=[("qT", bi), ("kT", bi)], writes=[("ps", bs)])
                si = cnt[0] % NS
                pi = cnt[0] % 4
                cnt[0] += 1
                pis[(r, j)] = pi
                P.op("dve", lambda e: e.tensor_tensor(out=s_sb[si][:, 0:128 * W], in0=ps[bs][:, 0:128 * W],
                                                      in1=alibi[:, gidx, 0:128 * W], op=ALU.add),
                     reads=["cst"], writes=[("s_sb", si), ("ps", bs)])
                P.op("act", lambda e: e.activation(out=p_sb[pi][:, 0:128 * W], in_=s_sb[si][:, 0:128 * W], func=ACT.Exp),
                     reads=[("s_sb", si)], writes=[("p_sb", pi)])

            def t_pv(r, j):
              with P.fds(pe=128):
                if j % 4 == 0:
                    st["bn"], st["bd"] = bank("a"), bank("b")
                bn, bd = st["bn"], st["bd"]
                pi = pis[(r, j)]
                prev = pis[(r, j - 1)] if j > 0 else None
                col = (j % 4) * 128
                tid = r * nb + j
                for (bk, is_den, lk) in ((bn, False, ("v_sb", bi)), (bd, True, "ones_bf")):
                    first = True
                    if j > 0:
                        lp = ones_bf if is_den else v_sb[:, tid - 1, :]
                        P.op("pe", lambda e, bk=bk, lp=lp: e.matmul(
                            ps[bk][:, col:col + 128], lhsT=lp, rhs=p_sb[prev][:, 128:256], start=True, stop=False),
                            reads=[lk, ("p_sb", prev)], writes=[("ps", bk)])
                        first = False
                    lc = ones_bf if is_den else v_sb[:, tid, :]
                    P.op("pe", lambda e, bk=bk, lc=lc, first=first: e.matmul(
                        ps[bk][:, col:col + 128], lhsT=lc, rhs=p_sb[pi][:, 0:128], start=first, stop=True),
                        reads=[lk, ("p_sb", pi)], writes=[("ps", bk)])
                if j % 4 == 3 or j == nb - 1:
                    n0 = (j // 4) * 4
                    nq = (j - n0 + 1) * 128
                    sl = slice(128 * n0 * d + r, 128 * n0 * d + r + (nq - 1) * d + 1, d)
                    for (bk, acc, key, eng) in ((bn, acc_n, "acc_n", "act"), (bd, acc_d, "acc_d", "dve")):
                        if gi == 0:
                            if eng == "act":
                                P.op("act", lambda e, bk=bk, acc=acc: e.activation(
                                    out=acc[:, sl], in_=ps[bk][:, 0:nq], func=ACT.Copy), writes=[key, ("ps", bk)])
                            else:
                                P.op("dve", lambda e, bk=bk, acc=acc: e.tensor_copy(
                                    out=acc[:, sl], in_=ps[bk][:, 0:nq]), writes=[key, ("ps", bk)])
                        else:
                            P.op("dve", lambda e, bk=bk, acc=acc: e.tensor_tensor(
                                out=acc[:, sl], in0=ps[bk][:, 0:nq], in1=acc[:, sl], op=ALU.add),
                                reads=[key], writes=[key, ("ps", bk)])

            seq = [(r, j) for r in range(d) for j in range(nb)]
            SK = 2
            for idx in range(len(seq) + SK):
                def t_blk(idx=idx):
                    if idx < len(seq):
                        t_qk(*seq[idx])
                    if idx >= SK:
                        t_pv(*seq[idx - SK])
                tasks.append(t_blk)
            if gi == 2:
                hold = {}

                def t_wz():
                    hold["z"] = load_w(w_in, O_ZB + h * 128)
                tasks.append(t_wz)
                for tt8 in range(8):
                    def t_fin(tt8=tt8):
                        wz, wzk = hold["z"]
                        i = tt8 % 2
                        sl = slice(tt8 * 512, (tt8 + 1) * 512)
                        b = proj_tile(wz, wzk, 128, tt8)
                        P.op("act", lambda e: e.activation(out=zs[i], in_=ps[b][:, :], func=ACT.Tanh, scale=0.5),
                             writes=[("zs", i), ("ps", b)], tset="T")
                        P.op("dve", lambda e: e.scalar_tensor_tensor(out=zs[i], in0=zs[i], scalar=1.0, in1=ps[b][:, :],
                                                                     op0=ALU.add, op1=ALU.mult),
                             reads=[("zs", i)], writes=[("zs", i), ("ps", b)])
                        P.op("dve", lambda e: e.reciprocal(out=rc[i], in_=acc_d[:, sl]), reads=["acc_d"], writes=[("rc", i)], fd=1500)
                        P.op("dve", lambda e: e.tensor_tensor(out=rc[i], in0=rc[i], in1=acc_n[:, sl], op=ALU.mult),
                             reads=["acc_n", ("rc", i)], writes=[("rc", i)])
                        P.op("dve", lambda e: e.scalar_tensor_tensor(out=ob[i], in0=rc[i], scalar=0.5, in1=zs[i],
                                                                     op0=ALU.mult, op1=ALU.mult),
                             reads=[("rc", i), ("zs", i)], writes=[("ob", i)])
                        P.dma(lambda q: q.dma_start(out=ob_scr[h, :, sl], in_=ob[i]),
                              reads=[("ob", i)], writes=[("ob_scr", h, tt8)])
                    tasks.append(t_fin)
            return tasks

        for t in proj_tasks(0):
            t()
        for jn in range(len(jobs)):
            interleave(core_tasks(jn), proj_tasks(jn + 1) if jn + 1 < len(jobs) else [])
        cursor[0] = ph

    if stop_after != "a0":
        phase_b()
        P.fence()
        dump("ob_scr", ob_scr, [])

    def phase_a():
        ph = cursor[0]
        NB2 = 2
        upre = {t: [alloc([516], BF16) for _ in range(2)] for t in "qkv"}
        for t in "qkv":
            P.op("pool", lambda e, t=t: e.memset(upre[t][1][:, 512:515], 0.0), writes=[("upre", t, 1)])
        dg = alloc([12, 128], BF16)
        _thb = [alloc([512], F32) for _ in range(2)]
        th = {"q": _thb[0], "k": _thb[1], "v": _thb[0], "z": _thb[1]}
        thk = {"q": ("th", 0), "k": ("th", 1), "v": ("th", 0), "z": ("th", 1)}
        yq = alloc([512], F32)
        yk = alloc([512], F32)
        sq = {t: alloc([512], BF16) for t in "qk"}
        rin = {t: alloc([512], F32) for t in "qk"}
        vTt = alloc([512], BF16)
        khT = [alloc([512], BF16) for _ in range(NB2)]
        qhT = [alloc([512], BF16) for _ in range(NB2)]
        qgT = [alloc([512], BF16) for _ in range(3)]
        zsT = [alloc([512], BF16) for _ in range(3)]
        Ktok = [alloc([8, 128], BF16, 64) for _ in range(NB2)]
        Vtok = [alloc([8, 128], BF16, 64) for _ in range(NB2)]
        AqkT = [alloc([8, 64], BF16, 64) for _ in range(NB2)]
        TT = alloc([8, 64], BF16, 64)
        gcB = [alloc([512], F32) for _ in range(2)]
        gclB = [alloc([512], F32, 64) for _ in range(2)]
        egB = alloc([512], F32)
        E1 = alloc([8, 64], F32, 64)
        E2 = alloc([8, 64], F32, 64)
        GT = alloc([8, 64], F32, 64)
        GTb = alloc([8, 64], F32, 64)
        Pk = [alloc([8, 64], BF16, 64) for _ in range(2)]
        PkT = [alloc([8, 64], BF16, 64) for _ in range(2)]
        Xb = [alloc([8, 64], BF16, 64) for _ in range(2)]
        S_f = alloc([128], F32)
        S_b = alloc([128], BF16)
        vnew = [alloc([128], BF16, 64) for _ in range(2)]
        WnT = [alloc([8, 64], BF16) for _ in range(NB2)]
        Ubf = [alloc([8, 128], BF16, 64) for _ in range(NB2)]
        Kd = [alloc([8, 128], BF16, 64) for _ in range(NB2)]
        Kgn = alloc([8, 128], BF16, 64)
        oraw = alloc([512], F32)
        osq = alloc([512], BF16)
        orst = alloc([512], F32)
        oa = [alloc([512], BF16) for _ in range(2)]
        wsets = [[alloc([8, 128], BF16) for _ in range(4)] for _ in range(2)]

        def load_head_w(h):
            ws = wsets[h % 2]
            for ti, off in enumerate((O_QA, O_KA, O_VA, O_ZA)):
                load_w(w_in, off + h * 128, dst=ws[ti], key=("wh", h % 2, ti))

        m8 = lambda m: m.unsqueeze(1).to_broadcast([64, 8, 64])
        v3 = lambda a: a.rearrange("p (c i) -> p c i", i=64)
        fl = lambda a: a.rearrange("p c i -> p (c i)")

        def rsqrt_from_psum(b, dst, key, scale):
            P.op("act", lambda e: e.activation(out=dst, in_=ps[b][:, :], func=ACT.Ln, scale=scale, bias=EPS),
                 writes=[key, ("ps", b)], tset="L")
            P.op("act", lambda e: e.activation(out=dst, in_=dst, func=ACT.Exp, scale=-0.5), reads=[key], writes=[key])

        def stage12_tasks(st):
            h, tt8 = st // 8, st % 8
            bi = st % NB2
            b3 = st % 3
            ui = st % 2
            t0 = tt8 * 512
            ws = wsets[h % 2]
            tasks = []
            A = tasks.append

            def t_pre():
                if tt8 == 0:
                    if h + 1 < 8:
                        load_head_w(h + 1)
                    for ti in range(3):
                        for kk in range(4):
                            g = ti * 8 + h
                            P.op("dve", lambda e, ti=ti, kk=kk, g=g: e.tensor_scalar(
                                out=dg[:, ti * 4 + kk, :], in0=identf, scalar1=cw[:, g * 4 + kk:g * 4 + kk + 1], scalar2=None,
                                op0=ALU.mult), reads=["cst", "cw"], writes=["dg"])
                P.dma(lambda q: q.dma_start(out=gcB[ui], in_=gc_scr[h, t0:t0 + 512].partition_broadcast(128)),
                      reads=["gc_scr"], writes=[("gcB", ui)])
                P.dma(lambda q: q.dma_start(out=gclB[ui], in_=gcl_scr[h, t0:t0 + 512].partition_broadcast(64)),
                      reads=["gcl_scr"], writes=[("gclB", ui)])
            A(t_pre)

            def proj_split(wt, wkey, hb):
                def mk(k0):
                    def f():
                        if k0 == 0:
                            hb["b"] = bank("pc")
                        b = hb["b"]
                        for k in range(k0, k0 + 2):
                            P.op("pe", lambda e, k=k: e.matmul(ps[b][:, :], lhsT=wt[:, k, :], rhs=hT[:, k, t0:t0 + 512],
                                                               start=(k == 0), stop=(k == 7)),
                                 reads=[wkey] + hkeys(t0, t0 + 512), writes=[("ps", b)], fd=512)
                    return f
                for k0 in (0, 2, 4):
                    A(mk(k0))
                return mk(6)

            hbz = {}
            last_z = proj_split(ws[3], ("wh", h % 2, 3), hbz)

            def t_projz():
                last_z()
                b = hbz["b"]
                P.op("act", lambda e: e.activation(out=th["z"], in_=ps[b][:, :], func=ACT.Tanh, scale=0.5),
                     writes=[thk["z"], ("ps", b)], tset="T")
                P.op("dve", lambda e: e.scalar_tensor_tensor(out=zsT[b3], in0=th["z"], scalar=1.0, in1=ps[b][:, :],
                                                             op0=ALU.add, op1=ALU.mult),
                     reads=[thk["z"]], writes=[("zsT", b3), ("ps", b)])
            A(t_projz)

            for ti, t in enumerate("qkv"):
                hbp = {}
                last_p = proj_split(ws[ti], ("wh", h % 2, ti), hbp)

                def t_proj(ti=ti, t=t, hbp=hbp, last_p=last_p):
                    u = upre[t][ui]
                    up = upre[t][1 - ui]
                    last_p()
                    b = hbp["b"]
                    if tt8 == 0:
                        P.op("pool", lambda e: e.memset(u[:, 0:3], 0.0), writes=[("upre", t, ui)])
                    else:
                        P.op("pool", lambda e: e.tensor_copy(out=u[:, 0:3], in_=up[:, 512:515]),
                             reads=[("upre", t, 1 - ui)], writes=[("upre", t, ui)])
                    P.op("act", lambda e: e.activation(out=u[:, 3:515], in_=ps[b][:, :], func=ACT.Copy),
                         writes=[("upre", t, ui), ("ps", b)])
                A(t_proj)

            for ti, t in enumerate("qkv"):
                hbc = {}

                def t_conv0(ti=ti, t=t, hbc=hbc):
                    u = upre[t][ui]
                    hbc["b"] = b = bank("pc")
                    for kk in range(2):
                        P.op("pe", lambda e, kk=kk: e.matmul(ps[b][:, :], lhsT=dg[:, ti * 4 + kk, :], rhs=u[:, kk:kk + 512],
                                                             start=(kk == 0), stop=False),
                             reads=["dg", ("upre", t, ui)], writes=[("ps", b)], fd=512)
                A(t_conv0)

                def t_conv(ti=ti, t=t, hbc=hbc):
                    u = upre[t][ui]
                    b = hbc["b"]
                    for kk in range(2, 4):
                        P.op("pe", lambda e, kk=kk: e.matmul(ps[b][:, :], lhsT=dg[:, ti * 4 + kk, :], rhs=u[:, kk:kk + 512],
                                                             start=False, stop=(kk == 3)),
                             reads=["dg", ("upre", t, ui)], writes=[("ps", b)], fd=512)
                    P.op("act", lambda e: e.activation(out=th[t], in_=ps[b][:, :], func=ACT.Tanh, scale=0.5),
                         writes=[thk[t], ("ps", b)], tset="T")
                    dst = {"q": yq, "k": yk, "v": vTt}[t]
                    P.op("dve", lambda e: e.scalar_tensor_tensor(out=dst, in0=th[t], scalar=1.0, in1=ps[b][:, :],
                                                                 op0=ALU.add, op1=ALU.mult),
                         reads=[thk[t]], writes=[("y", t), ("ps", b)])
                A(t_conv)

            for t, y in (("q", yq), ("k", yk)):
                def t_norm(t=t, y=y):
                    P.op("act", lambda e: e.activation(out=sq[t], in_=y, func=ACT.Square), reads=[("y", t)], writes=[("sq", t)])
                    b = bank("pc")
                    P.op("pe", lambda e: e.matmul(ps[b][:, :], lhsT=ones_bf, rhs=sq[t], start=True, stop=True),
                         reads=[("sq", t), "ones_bf"], writes=[("ps", b)], fd=512)
                    rsqrt_from_psum(b, rin[t], ("rin", t), 0.25)
                A(t_norm)

            def t_hat():
                P.op("dve", lambda e: e.scalar_tensor_tensor(out=khT[bi], in0=yk, scalar=0.5, in1=rin["k"],
                                                             op0=ALU.mult, op1=ALU.mult),
                     reads=[("y", "k"), ("rin", "k")], writes=[("khT", bi)])
                P.op("dve", lambda e: e.scalar_tensor_tensor(out=qhT[bi], in0=yq, scalar=0.5 * 128 ** -0.5, in1=rin["q"],
                                                             op0=ALU.mult, op1=ALU.mult),
                     reads=[("y", "q"), ("rin", "q")], writes=[("qhT", bi)])
                P.op("act", lambda e: e.activation(out=egB, in_=gcB[ui], func=ACT.Exp), reads=[("gcB", ui)], writes=["egB"])
                P.op("pool", lambda e: e.tensor_tensor(out=qgT[b3], in0=qhT[bi], in1=egB, op=ALU.mult),
                     reads=[("qhT", bi), "egB"], writes=[("qgT", b3)])
            A(t_hat)

            for (src, skey, dst, dkey, sc) in ((khT[bi], ("khT", bi), Ktok[bi], ("Ktok", bi), 1.0),
                                               (vTt, ("y", "v"), Vtok[bi], ("Vtok", bi), 0.5)):
                def t_tok(src=src, skey=skey, dst=dst, dkey=dkey, sc=sc):
                    b = bank("pc")
                    for c in range(8):
                        P.op("pe", lambda e, c=c: e.transpose(out=psb[b][0:64, c * 128:(c + 1) * 128],
                                                              in_=src[:, c * 64:(c + 1) * 64], identity=ident_bf),
                             reads=[skey, "ident_bf"], writes=[("ps", b)])
                    P.op("act", lambda e: e.activation(out=dst, in_=psb[b][0:64, :].rearrange("p (c k) -> p c k", k=128),
                                                       func=ACT.Copy, scale=sc), writes=[dkey, ("ps", b)])
                A(t_tok)

            n0 = tt8 * 8
            gcJ = gc_t[:, n0:n0 + 8, h].unsqueeze(2).to_broadcast([64, 8, 64])
            bJ = beta_t[:, n0:n0 + 8, h].unsqueeze(2).to_broadcast([64, 8, 64])
            hold = {}

            split = [len(tasks)]

            def t_gates():
                P.op("pool", lambda e: e.tensor_tensor(out=E1, in0=v3(gcB[ui][0:64, :]), in1=gcJ, op=ALU.subtract),
                     reads=[("gcB", ui), "gc_t"], writes=["E1"])
                P.op("pool", lambda e: e.tensor_tensor(out=E1, in0=E1, in1=m8(mask_incl), op=ALU.add),
                     reads=["E1", "cst"], writes=["E1"])
                P.op("act", lambda e: e.activation(out=GT, in_=E1, func=ACT.Exp), reads=["E1"], writes=["GT"])
                P.op("pool", lambda e: e.tensor_tensor(out=E2, in0=v3(gclB[ui]), in1=gcJ, op=ALU.subtract),
                     reads=[("gclB", ui), "gc_t"], writes=["E2"])
                P.op("pool", lambda e: e.tensor_tensor(out=E2, in0=E2, in1=m8(mask_strict), op=ALU.add),
                     reads=["E2", "cst"], writes=["E2"])
                P.op("act", lambda e: e.activation(out=GTb, in_=E2, func=ACT.Exp), reads=["E2"], writes=["GTb"])
            A(t_gates)

            def t_kkqk():
                bkk, bqk = bank("pre"), bank("pre")
                for c in range(8):
                    cs = slice(c * 64, (c + 1) * 64)
                    P.op("pe", lambda e, cs=cs: e.matmul(ps[bkk][0:64, cs], lhsT=khT[bi][:, cs], rhs=khT[bi][:, cs],
                                                         start=True, stop=True), reads=[("khT", bi)], writes=[("ps", bkk)])
                for c in range(8):
                    cs = slice(c * 64, (c + 1) * 64)
                    P.op("pe", lambda e, cs=cs: e.matmul(ps[bqk][0:64, cs], lhsT=khT[bi][:, cs], rhs=qhT[bi][:, cs],
                                                         start=True, stop=True), reads=[("khT", bi), ("qhT", bi)], writes=[("ps", bqk)])
                P.op("dve", lambda e: e.tensor_tensor(out=AqkT[bi], in0=v3(ps[bqk][0:64, :]), in1=GT, op=ALU.mult),
                     reads=["GT"], writes=[("AqkT", bi), ("ps", bqk)])
                P.op("dve", lambda e: e.scalar_tensor_tensor(out=Pk[0], in0=v3(ps[bkk][0:64, :]), scalar=-1.0, in1=GTb,
                                                             op0=ALU.mult, op1=ALU.mult),
                     reads=["GTb"], writes=[("Pk", 0), ("ps", bkk)])
            A(t_kkqk)

            def t_pt():
                b = bank("pre")
                for c in range(8):
                    P.op("pe", lambda e, c=c: e.transpose(out=psb[b][0:64, c * 64:(c + 1) * 64], in_=Pk[0][:, c, :],
                                                          identity=ident_bf[0:64, 0:64]),
                         reads=[("Pk", 0), "ident_bf"], writes=[("ps", b)])
                P.op("act", lambda e: e.activation(out=fl(PkT[0]), in_=psb[b][0:64, 0:512], func=ACT.Copy),
                     writes=[("PkT", 0), ("ps", b)])
                P.op("pool", lambda e: e.tensor_tensor(out=Xb[0], in0=Pk[0], in1=m8(identf[0:64, 0:64]), op=ALU.add),
                     reads=[("Pk", 0), "cst"], writes=[("Xb", 0)])
            A(t_pt)

            for lvl in range(5):
                cur = lvl % 2
                nxt = 1 - cur

                def t_sq(lvl=lvl, cur=cur, nxt=nxt):
                    if lvl < 4:
                        ba = bank("pre")
                        for c in range(8):
                            cs = slice(c * 64, (c + 1) * 64)
                            P.op("pe", lambda e, c=c, cs=cs: e.matmul(ps[ba][0:64, cs], lhsT=PkT[cur][:, c, :], rhs=Pk[cur][:, c, :],
                                                                      start=True, stop=True),
                                 reads=[("Pk", cur), ("PkT", cur)], writes=[("ps", ba)])
                    bt = bank("pre")
                    for c in range(8):
                        cs = slice(c * 64, (c + 1) * 64)
                        P.op("pe", lambda e, c=c, cs=cs: e.matmul(ps[bt][0:64, cs], lhsT=Pk[cur][:, c, :], rhs=PkT[cur][:, c, :],
                                                                  start=True, stop=True),
                             reads=[("Pk", cur), ("PkT", cur)], writes=[("ps", bt)])
                    if lvl < 4:
                        P.op("act", lambda e: e.activation(out=fl(Pk[nxt]), in_=ps[ba][0:64, :], func=ACT.Copy),
                             writes=[("Pk", nxt), ("ps", ba)])
                    P.op("dve", lambda e: e.tensor_copy(out=fl(PkT[nxt]), in_=ps[bt][0:64, :]),
                         writes=[("PkT", nxt), ("ps", bt)])
                A(t_sq)

                def t_x(lvl=lvl, cur=cur, nxt=nxt):
                    bx = bank("pre")
                    for c in range(8):
                        cs = slice(c * 64, (c + 1) * 64)
                        P.op("pe", lambda e, c=c, cs=cs: e.matmul(ps[bx][0:64, cs], lhsT=PkT[nxt][:, c, :], rhs=Xb[cur][:, c, :],
                                                                  start=True, stop=True),
                             reads=[("PkT", nxt), ("Xb", cur)], writes=[("ps", bx)])
                    P.op("dve", lambda e: e.tensor_tensor(out=fl(Xb[nxt]), in0=ps[bx][0:64, :], in1=fl(Xb[cur]), op=ALU.add),
                         reads=[("Xb", cur)], writes=[("Xb", nxt), ("ps", bx)])
                    if lvl == 4:
                        P.op("pool", lambda e: e.tensor_tensor(out=TT, in0=Xb[nxt], in1=bJ, op=ALU.mult),
                             reads=[("Xb", nxt), "beta_t"], writes=["TT"])
                A(t_x)

            ngJ = negeg_t[:, n0:n0 + 8, h].unsqueeze(2).to_broadcast([64, 8, 128])
            wdJ = wdec_t[:, n0:n0 + 8, h].unsqueeze(2).to_broadcast([64, 8, 128])

            def t_kg():
                P.op("pool", lambda e: e.tensor_tensor(out=Kgn, in0=Ktok[bi], in1=ngJ, op=ALU.mult),
                     reads=[("Ktok", bi), "negeg_t"], writes=["Kgn"])
                P.op("pool", lambda e: e.tensor_tensor(out=Kd[bi], in0=Ktok[bi], in1=wdJ, op=ALU.mult),
                     reads=[("Ktok", bi), "wdec_t"], writes=[("Kd", bi)])
            tasks.insert(len(tasks) - 6, t_kg)

            def t_w():
                b = bank("pre")
                for c in range(8):
                    P.op("pe", lambda e, c=c: e.matmul(ps[b][:, c * 64:(c + 1) * 64], lhsT=Kgn[:, c, :], rhs=TT[:, c, :],
                                                       start=True, stop=True),
                         reads=["Kgn", "TT"], writes=[("ps", b)])
                P.op("act", lambda e: e.activation(out=fl(WnT[bi]), in_=ps[b][:, :], func=ACT.Copy),
                     writes=[("WnT", bi), ("ps", b)])
            A(t_w)

            for half in range(2):
                def t_u(half=half):
                    b = bank("pre")
                    for c4 in range(4):
                        c = half * 4 + c4
                        P.op("pe", lambda e, c=c, c4=c4: e.matmul(ps[b][0:64, c4 * 128:(c4 + 1) * 128], lhsT=TT[:, c, :],
                                                                  rhs=Vtok[bi][:, c, :], start=True, stop=True),
                             reads=["TT", ("Vtok", bi)], writes=[("ps", b)])
                    P.op("dve", lambda e: e.tensor_copy(out=Ubf[bi][:, half * 4:half * 4 + 4, :],
                                                        in_=ps[b][0:64, :].rearrange("p (c k) -> p c k", k=128)),
                         writes=[("Ubf", bi), ("ps", b)])
                A(t_u)
            return tasks[:split[0]], tasks[split[0]:]

        def stage3_tasks(st):
            h, tt8 = st // 8, st % 8
            bi = st % NB2
            b3 = st % 3
            t0 = tt8 * 512
            n0 = tt8 * 8
            tasks = []
            A = tasks.append
            hold = {}

            def t_begin():
                hold["bo"] = bank("po")
            A(t_begin)
            for c in range(8):
                n = n0 + c
                cs = slice(c * 64, (c + 1) * 64)
                ri = n % 2
                first = (n == 0)

                def t_a(c=c, n=n, cs=cs, ri=ri, first=first):
                  with P.fds(pe=128, act=128, dve=128):
                    b1 = bank("q1")
                    P.op("pe", lambda e: e.matmul(ps[b1][0:64, 0:128], lhsT=ident_bf[0:64, 0:64], rhs=Ubf[bi][:, c, :],
                                                  start=True, stop=first),
                         reads=["ident_bf", ("Ubf", bi)], writes=[("ps", b1)])
                    if not first:
                        P.op("pe", lambda e: e.matmul(ps[b1][0:64, 0:128], lhsT=WnT[bi][:, c, :], rhs=S_b, start=False, stop=True),
                             reads=[("WnT", bi), "S_b"], writes=[("ps", b1)])
                    P.op("act", lambda e: e.activation(out=vnew[ri], in_=ps[b1][0:64, 0:128], func=ACT.Copy),
                         writes=[("vnew", ri), ("ps", b1)])
                A(t_a)

                def t_c(c=c, n=n, cs=cs, ri=ri, first=first):
                  with P.fds(pe=128, act=128, dve=128):
                    bo = hold["bo"]
                    b4 = bank("q4")
                    P.op("pe", lambda e: e.matmul(ps[b4][:, 0:128], lhsT=Kd[bi][:, c, :], rhs=vnew[ri], start=True, stop=True),
                         reads=[("Kd", bi), ("vnew", ri)], writes=[("ps", b4)])
                    if not first:
                        P.op("pe", lambda e: e.matmul(ps[bo][:, cs], lhsT=S_b, rhs=qgT[b3][:, cs], start=True, stop=False),
                             reads=["S_b", ("qgT", b3)], writes=[("ps", bo)])
                    P.op("pe", lambda e: e.matmul(ps[bo][:, cs], lhsT=vnew[ri], rhs=AqkT[bi][:, c, :], start=first, stop=True),
                         reads=[("vnew", ri), ("AqkT", bi)], writes=[("ps", bo)])
                    if first:
                        P.op("dve", lambda e: e.tensor_copy(out=S_b, in_=ps[b4][:, 0:128]), writes=["S_b", ("ps", b4)])
                        P.op("dve", lambda e: e.tensor_copy(out=S_f, in_=ps[b4][:, 0:128]), writes=["S_f", ("ps", b4)])
                    else:
                        P.op("dve", lambda e: e.scalar_tensor_tensor(
                            out=S_b, in0=S_f, scalar=dl_t[:, n, h:h + 1], in1=ps[b4][:, 0:128], op0=ALU.mult, op1=ALU.add),
                            reads=["S_f", "dl_t"], writes=["S_b", ("ps", b4)])
                        P.op("dve", lambda e: e.scalar_tensor_tensor(
                            out=S_f, in0=S_f, scalar=dl_t[:, n, h:h + 1], in1=ps[b4][:, 0:128], op0=ALU.mult, op1=ALU.add),
                            reads=["S_f", "dl_t"], writes=["S_f", ("ps", b4)])
                A(t_c)

            def t_epi():
                bo = hold["bo"]
                oi = st % 2
                P.op("act", lambda e: e.activation(out=oraw, in_=ps[bo][:, :], func=ACT.Copy), writes=["oraw", ("ps", bo)])
                if st == 0:
                    dump("khT0", khT[bi], [("khT", bi)])
                    dump("oraw0", oraw, ["oraw"])
                P.op("act", lambda e: e.activation(out=osq, in_=oraw, func=ACT.Square), reads=["oraw"], writes=["osq"])
                b = bank("pc")
                P.op("pe", lambda e: e.matmul(ps[b][:, :], lhsT=ones_bf, rhs=osq, start=True, stop=True),
                     reads=["osq", "ones_bf"], writes=[("ps", b)], fd=512)
                rsqrt_from_psum(b, orst, "orst", 1.0 / 128)
                P.op("dve", lambda e: e.scalar_tensor_tensor(out=oraw, in0=oraw, scalar=dnw[:, 0:1], in1=orst,
                                                             op0=ALU.mult, op1=ALU.mult),
                     reads=["oraw", "orst", "dnw"], writes=["oraw"])
                P.op("dve", lambda e: e.scalar_tensor_tensor(out=oa[oi], in0=oraw, scalar=0.5, in1=zsT[b3],
                                                             op0=ALU.mult, op1=ALU.mult),
                     reads=["oraw", ("zsT", b3)], writes=[("oa", oi)])
                P.dma(lambda q: q.dma_start(out=oa_scr[h, :, t0:t0 + 512], in_=oa[oi]),
                      reads=[("oa", oi)], writes=[("oa_scr", h, tt8)])
            A(t_epi)
            return tasks

        sg = [alloc([512], BF16) for _ in range(2)]
        sgf = [alloc([512], F32) for _ in range(2)]
        c0_tasks = []
        c0_hold = {}
        for c16 in range(16):
            for tt8 in range(8):
                def t_c0(c16=c16, tt8=tt8):
                    if tt8 == 0:
                        off = (O_GA + c16 * 128) if c16 < 8 else (O_GB + (c16 - 8) * 128)
                        c0_hold["w"] = load_w(w_in, off)
                    wt, wk = c0_hold["w"]
                    i = (c16 * 8 + tt8) % 2
                    b = proj_tile(wt, wk, 128, tt8, grp="pc")
                    P.op("act", lambda e: e.activation(out=sgf[i], in_=ps[b][:, :], func=ACT.Tanh, scale=0.5),
                         writes=[("sgf", i), ("ps", b)], tset="T")
                    P.op("dve", lambda e: e.tensor_scalar(out=sg[i], in0=sgf[i], scalar1=0.5, scalar2=0.5, op0=ALU.mult, op1=ALU.add),
                         reads=[("sgf", i)], writes=[("sg", i)])
                    P.dma(lambda q: q.dma_start(out=sg_scr[c16, :, tt8 * 512:(tt8 + 1) * 512], in_=sg[i]),
                          reads=[("sg", i)], writes=[("sg_scr", tt8)])
                c0_tasks.append(t_c0)

        load_head_w(0)
        NSTEP = 64

        def merge(ta, tb):
            out = []
            na, nb_ = len(ta), len(tb)
            jj = 0
            for ii, t in enumerate(ta):
                out.append(t)
                tgt = (nb_ * (ii + 1) + na - 1) // max(na, 1)
                while jj < min(tgt, nb_):
                    out.append(tb[jj])
                    jj += 1
            out.extend(tb[jj:])
            return out

        s1 = {}
        s2 = {}
        for st in range(NSTEP):
            s1[st], s2[st] = None, None

        def get12(st):
            if st >= NSTEP:
                return [], []
            return stage12_tasks(st)

        a1, a2 = get12(0)
        for t in a1:
            t()
        b1_, b2_ = get12(1)
        for t in merge(a2, b1_):
            t()
        pend2 = b2_
        for st in range(NSTEP):
            t3 = stage3_tasks(st)
            n1, n2 = get12(st + 2)
            filler = (pend2 + n1) if S2_FIRST else (merge(pend2, n1) if len(pend2) >= len(n1) else merge(n1, pend2))
            pend2 = n2
            for t in (t3 + filler if CHAIN_FIRST else merge(t3, filler)):
                t()
            for _ in range(2):
                if c0_tasks:
                    c0_tasks.pop(0)()
        while c0_tasks:
            c0_tasks.pop(0)()
        cursor[0] = ph

    if stop_after not in ("a0", "b"):
        phase_a()
        P.fence()
        dump("oa_scr", oa_scr, [])

    def phase_c0():
        ph = cursor[0]
        sg = [alloc([512], BF16) for _ in range(4)]
        sgf = [alloc([512], F32) for _ in range(4)]
        cnt = 0
        for c16 in range(16):
            off = (O_GA + c16 * 128) if c16 < 8 else (O_GB + (c16 - 8) * 128)
            wt, wk = load_w(w_in, off)
            for tt8 in range(8):
                b = proj_tile(wt, wk, 128, tt8, grp="all")
                i = cnt % 4
                cnt += 1
                P.op("act", lambda e, b=b, i=i: e.activation(out=sgf[i], in_=ps[b][:, :], func=ACT.Tanh, scale=0.5),
                     writes=[("sgf", i), ("ps", b)], tset="T")
                P.op("dve", lambda e, i=i: e.tensor_scalar(out=sg[i], in0=sgf[i], scalar1=0.5, scalar2=0.5, op0=ALU.mult, op1=ALU.add),
                     reads=[("sgf", i)], writes=[("sg", i)])
                P.dma(lambda q, i=i, c16=c16, tt8=tt8: q.dma_start(out=sg_scr[c16, :, tt8 * 512:(tt8 + 1) * 512], in_=sg[i]),
                      reads=[("sg", i)], writes=[("sg_scr", tt8)])
        cursor[0] = ph

    def phase_c1():
      with P.fds(pe=512):
        ph = cursor[0]
        wodn = alloc([8, 1024], BF16)
        wodil = alloc([4, 1024], BF16)
        wout = alloc([8, 1024], BF16)
        fnw_b = alloc([D], F32)
        for (dst, src, kc, key) in ((wodn, w_odn, 8, "wodn"), (wodil, w_odil, 4, "wodil"), (wout, w_out, 8, "wout")):
            for half in range(2):
                P.dma(lambda q, dst=dst, src=src, kc=kc, half=half: q.dma_start(
                    out=dst[:, 0:kc, half * 512:(half + 1) * 512],
                    in_=src[:, half * 512:(half + 1) * 512].rearrange("(k p) c -> p k c", p=128)),
                    writes=[(key, half)], q="pool")
        P.dma(lambda q: q.dma_start(out=fnw_b, in_=fnw_d.partition_broadcast(128)), writes=["fnw_b"])
        oat = [alloc([8, 512], BF16) for _ in range(2)]
        obt = [alloc([4, 512], BF16) for _ in range(2)]
        sgt = [alloc([16, 512], BF16) for _ in range(2)]
        mT = alloc([8, 512], BF16)
        m1 = [alloc([512], F32) for _ in range(2)]
        m2 = [alloc([512], F32) for _ in range(2)]
        xr = [alloc([D], F32) for _ in range(2)]
        xo = [alloc([D], F32) for _ in range(2)]
        yo = [alloc([D], F32) for _ in range(2)]
        ss = [alloc([1], F32) for _ in range(2)]
        rs = [alloc([1], F32) for _ in range(2)]
        junk2 = alloc([D], BF16)
        cnt = 0
        for tt8 in range(8):
            i = tt8 % 2
            sl = slice(tt8 * 512, (tt8 + 1) * 512)
            P.dma(lambda q, i=i, sl=sl: q.dma_start(out=oat[i], in_=oa_scr[:, :, sl].rearrange("h p t -> p h t")),
                  reads=[("oa_scr", hh, tt8) for hh in range(8)], writes=[("oat", i)])
            P.dma(lambda q, i=i, sl=sl: q.dma_start(out=obt[i], in_=ob_scr[:, :, sl].rearrange("h p t -> p h t")),
                  reads=[("ob_scr", hh, tt8) for hh in range(4)], writes=[("obt", i)])
            P.dma(lambda q, i=i, sl=sl: q.dma_start(out=sgt[i], in_=sg_scr[:, :, sl].rearrange("h p t -> p h t")),
                  reads=[("sg_scr", tt8)], writes=[("sgt", i)])
            for c in range(8):
                cs = slice(c * 128, (c + 1) * 128)
                ba = bank("all")
                for k in range(8):
                    P.op("pe", lambda e, b=ba, k=k, cs=cs, i=i: e.matmul(ps[b][:, :], lhsT=wodn[:, k, cs], rhs=oat[i][:, k, :],
                                                                         start=(k == 0), stop=(k == 7)),
                         reads=[("wodn", c // 4), ("oat", i)], writes=[("ps", ba)])
                bb = bank("all")
                for k in range(4):
                    P.op("pe", lambda e, b=bb, k=k, cs=cs, i=i: e.matmul(ps[b][:, :], lhsT=wodil[:, k, cs], rhs=obt[i][:, k, :],
                                                                         start=(k == 0), stop=(k == 3)),
                         reads=[("wodil", c // 4), ("obt", i)], writes=[("ps", bb)])
                mi = cnt % 2
                cnt += 1
                P.op("dve", lambda e, b=ba, mi=mi, i=i, c=c: e.tensor_tensor(out=m1[mi], in0=ps[b][:, :], in1=sgt[i][:, c, :], op=ALU.mult),
                     reads=[("sgt", i)], writes=[("m1", mi), ("ps", ba)])
                P.op("dve", lambda e, b=bb, mi=mi, i=i, c=c: e.tensor_tensor(out=m2[mi], in0=ps[b][:, :], in1=sgt[i][:, 8 + c, :], op=ALU.mult),
                     reads=[("sgt", i)], writes=[("m2", mi), ("ps", bb)])
                P.op("pool", lambda e, mi=mi, c=c: e.tensor_tensor(out=mT[:, c, :], in0=m1[mi], in1=m2[mi], op=ALU.add),
                     reads=[("m1", mi), ("m2", mi)], writes=[("mT", c)])
            for sub in range(4):
                tok0 = tt8 * 512 + sub * 128
                xi = sub % 2
                P.dma(lambda q, xi=xi, tok0=tok0: q.dma_start(out=xr[xi], in_=x_d[tok0:tok0 + 128, :]), writes=[("xr", xi)])
                for half in range(2):
                    b = bank("all")
                    hs = slice(half * 512, (half + 1) * 512)
                    for c in range(8):
                        P.op("pe", lambda e, b=b, c=c, sub=sub, hs=hs: e.matmul(
                            ps[b][:, :], lhsT=mT[:, c, sub * 128:(sub + 1) * 128], rhs=wout[:, c, hs], start=(c == 0), stop=(c == 7)),
                            reads=[("mT", c), ("wout", half)], writes=[("ps", b)])
                    P.op("dve", lambda e, b=b, xi=xi, hs=hs: e.tensor_tensor(out=xo[xi][:, hs], in0=ps[b][:, :], in1=xr[xi][:, hs], op=ALU.add),
                         reads=[("xr", xi)], writes=[("xo", xi, half), ("ps", b)])
                P.op("pool", lambda e, xi=xi: e.memset(ss[xi], 0.0), writes=[("ss", xi)])
                P.op("act", lambda e, xi=xi: e.activation(out=junk2, in_=xo[xi], func=ACT.Square, accum_out=ss[xi]),
                     reads=[("xo", xi, 0), ("xo", xi, 1), ("ss", xi)], writes=["junk2", ("ss", xi)])
                P.op("dve", lambda e, xi=xi: e.tensor_scalar(out=ss[xi], in0=ss[xi], scalar1=1.0 / D, scalar2=EPS, op0=ALU.mult, op1=ALU.add),
                     reads=[("ss", xi)], writes=[("ss", xi)])
                P.op("act", lambda e, xi=xi: e.activation(out=ss[xi], in_=ss[xi], func=ACT.Ln), reads=[("ss", xi)], writes=[("ss", xi)], tset="L", fd=1)
                P.op("act", lambda e, xi=xi: e.activation(out=rs[xi], in_=ss[xi], func=ACT.Exp, scale=-0.5), reads=[("ss", xi)], writes=[("rs", xi)], fd=1)
                P.op("dve", lambda e, xi=xi: e.scalar_tensor_tensor(out=yo[xi], in0=xo[xi], scalar=rs[xi], in1=fnw_b,
                                                                    op0=ALU.mult, op1=ALU.mult),
                     reads=[("xo", xi, 0), ("xo", xi, 1), ("rs", xi), "fnw_b"], writes=[("yo", xi)])
                P.dma(lambda q, xi=xi, tok0=tok0: q.dma_start(out=out_d[tok0:tok0 + 128, :], in_=yo[xi]),
                      reads=[("yo", xi)], writes=[("out", tok0)])
        cursor[0] = ph

    if stop_after is None:
        dump("sg_scr", sg_scr, [])
        cursor[0] = 0
        phase_c1()

    if RESCHEDULE:
        P.sim_time = P.reschedule()
    P.emit(final_wait_ops=[o for o in P.dma_last if o is not None])
    return nc


def _consts():
    c = np.zeros((128, 128 + 12 * 256 + 256), np.float32)
    c[:, 0:128] = np.eye(128, dtype=np.float32)
    slopes = (2.0 ** (-8.0 * np.arange(1, 13, dtype=np.float32) / 12)).reshape(3, 4)
    jk = np.arange(128)[:, None]
    iq = np.arange(128)[None, :]
    for gi in range(3):
        for h in range(4):
            s = slopes[gi, h] * DIL[gi]
            d0 = (iq - jk).astype(np.float32)
            b0 = np.where(iq >= jk, -s * d0, NEG)
            d1 = (128 + iq - jk).astype(np.float32)
            b1 = np.where(iq <= jk, -s * d1, NEG)
            g = gi * 4 + h
            c[:, 128 + g * 256:128 + g * 256 + 128] = b0
            c[:, 128 + g * 256 + 128:128 + g * 256 + 256] = b1
    MK = 128 + 3072
    j = np.arange(64)[:, None]
    i = np.arange(64)[None, :]
    c[0:64, MK:MK + 64] = np.where(i >= j, 0.0, NEG)
    c[0:64, MK + 64:MK + 128] = np.where(i > j, 0.0, NEG)
    c[0:64, MK + 128:MK + 192] = (j <= i).astype(np.float32)
    c[0:64, MK + 192:MK + 256] = 1.0
    return c


_NC_CACHE = {}


def _host_inputs(x, norm_w, w_in, conv_w, a_log, dt_bias, dn_norm_w, w_o_dn, w_o_dil, w_out, final_norm_w):
    f = lambda a: np.ascontiguousarray(np.asarray(a, dtype=np.float32))
    cw = f(conv_w)[0].reshape(4, 24, 128).transpose(2, 1, 0).reshape(128, 96)
    shared = {
        "w_in": f(w_in)[0], "w_o_dn": f(w_o_dn)[0], "w_o_dil": f(w_o_dil)[0], "w_out": f(w_out)[0],
        "norm_w": f(norm_w).reshape(1, D), "final_norm_w": f(final_norm_w).reshape(1, D),
        "conv_w_l": np.ascontiguousarray(cw), "a_log": f(a_log).reshape(1, 8), "dt_bias": f(dt_bias).reshape(1, 8),
        "dn_norm_w_l": f(dn_norm_w).reshape(128, 1), "consts": _consts(),
    }
    xs = f(x)
    return [dict(shared, x=xs[b]) for b in range(xs.shape[0])]


def kernel(x, norm_w, w_in, conv_w, a_log, dt_bias, dn_norm_w, w_o_dn, w_o_dil, w_out, final_norm_w):
    in_maps = _host_inputs(x, norm_w, w_in, conv_w, a_log, dt_bias, dn_norm_w, w_o_dn, w_o_dil, w_out, final_norm_w)
    if "nc" not in _NC_CACHE:
        _NC_CACHE["nc"] = build_nc()
    res = run_bass_kernel_spmd(_NC_CACHE["nc"], in_maps, core_ids=list(range(len(in_maps))))
    return np.stack([np.asarray(r["out"], dtype=np.float32).reshape(T, D) for r in res.results], axis=0)
```

```python
import contextlib
import numpy as np
import concourse.bass as bass
import concourse.mybir as mybir
from concourse.bass_utils import run_bass_kernel_spmd

ACT = mybir.ActivationFunctionType
ALU = mybir.AluOpType
F32 = mybir.dt.float32
BF16 = mybir.dt.bfloat16

T = 4096
D = 1024
NEG = -30000.0
PIPELINE_A = True
RESCHEDULE = True
SCHED_IDENTITY = True
XLAT = 0.0
CPW = 0.0
CHAIN_FIRST = False
S2_FIRST = False
EPS = 1e-6
O_QA, O_KA, O_VA, O_ZA, O_BA, O_QB, O_KB, O_VB, O_ZB, O_GA, O_GB = (
    0, 1024, 2048, 3072, 4096, 4112, 5648, 7184, 8720, 9232, 10256)
DIL = (1, 4, 16)


class _Op:
    __slots__ = ("eng", "fn", "deps", "signal", "ticket", "is_dma", "dsem", "dval", "odeps", "cost", "idx", "seg", "war", "pos", "tset")

    def __init__(self, eng, fn, is_dma=False):
        self.eng = eng
        self.fn = fn
        self.odeps = []
        self.cost = 0.0
        self.idx = 0
        self.seg = 0
        self.war = []
        self.pos = 0
        self.tset = None
        self.deps = []
        self.signal = False
        self.ticket = None
        self.is_dma = is_dma
        self.dsem = None
        self.dval = None


class _Res:
    __slots__ = ("w", "r", "rd")

    def __init__(self):
        self.w = None
        self.r = []
        self.rd = []


class Prog:
    ENGS = ("pe", "act", "dve", "pool", "sp")

    def __init__(self, nc, n_dma_sems=32):
        self.nc = nc
        self.streams = {e: [] for e in self.ENGS}
        self.res = {}
        self.n_dma_sems = n_dma_sems
        self.dma_cnt = [0] * n_dma_sems
        self.dma_last = [None] * n_dma_sems
        self.dma_rr = 0
        self.fence_ops = []
        self.fence_dma = []
        self.seg = 0
        self.all_ops = []
        self.fd_default = {}

    @contextlib.contextmanager
    def fds(self, **kw):
        old = dict(self.fd_default)
        self.fd_default.update(kw)
        try:
            yield
        finally:
            self.fd_default = old

    def fence(self):
        self.fence_dma.append([d for d in self.dma_last if d is not None])
        self.seg += 1

    def _r(self, k):
        r = self.res.get(k)
        if r is None:
            r = self.res[k] = _Res()
        return r

    def op(self, eng, fn, reads=(), writes=(), is_dma=False, fd=None, tset=None):
        o = _Op(eng, fn, is_dma)
        o.tset = tset
        fd = fd or self.fd_default.get(eng)
        if is_dma:
            o.cost = 0.15
        elif eng == "pe":
            o.cost = max(64, fd or 64) / 2400.0 + 0.004
        elif eng == "act":
            o.cost = (224 + (fd or 512)) / 1200.0
        elif eng == "dve":
            o.cost = (110 + (fd or 512)) / 960.0
        else:
            o.cost = 0.2 + (fd or 512) / 1000.0
        o.idx = len(self.all_ops)
        o.seg = self.seg
        self.all_ops.append(o)
        deps = []
        for k in reads:
            r = self._r(k)
            if r.w is not None:
                deps.append((r.w, "raw"))
        for k in writes:
            r = self._r(k)
            if r.w is not None:
                deps.append((r.w, "waw"))
            for rd in r.r:
                if rd is not o:
                    o.war.append(rd)
            for rd in r.rd:
                if rd is not o:
                    o.war.append(rd)
        if is_dma:
            i = self.dma_rr
            self.dma_rr = (i + 1) % self.n_dma_sems
            o.dsem = i
            self.dma_cnt[i] += 1
            o.dval = 16 * self.dma_cnt[i]
            if self.dma_last[i] is not None:
                deps.append((self.dma_last[i], "raw"))
            self.dma_last[i] = o
        seen = set()
        oseen = set()
        for d, kind in deps:
            if d is not o and id(d) not in oseen:
                oseen.add(id(d))
                o.odeps.append(d)
        for d in o.war:
            if id(d) not in oseen:
                oseen.add(id(d))
                o.odeps.append(d)
        for d, kind in deps:
            if d is o or id(d) in seen:
                continue
            if not d.is_dma and d.eng == eng and not is_dma:
                if eng == "pe":
                    continue
                if kind != "raw":
                    continue
            seen.add(id(d))
            d.signal = True
            o.deps.append(d)
        for k in reads:
            r = self._r(k)
            if is_dma:
                r.rd.append(o)
            else:
                r.r.append(o)
        for k in writes:
            r = self._r(k)
            r.w = o
            r.r = []
            r.rd = []
        self.streams[eng].append(o)
        return o

    def dma(self, fn, reads=(), writes=(), q="sp"):
        return self.op(q, fn, reads, writes, is_dma=True)

    def reschedule(self, dma_latency=3.0):
        import heapq
        ops = self.all_ops
        n = len(ops)
        succ = [[] for _ in range(n)]
        indeg = [0] * n
        for o in ops:
            for d in o.odeps:
                if d.seg == o.seg:
                    succ[d.idx].append(o.idx)
                    indeg[o.idx] += 1
        prio = [0.0] * n
        for i in range(n - 1, -1, -1):
            o = ops[i]
            m = 0.0
            for j in succ[i]:
                if prio[j] > m:
                    m = prio[j]
            prio[i] = m + (dma_latency if o.is_dma else o.cost)
        if SCHED_IDENTITY:
            prio = [float(n - i) + CPW * prio[i] for i in range(n)]
        new_streams = {e: [] for e in self.ENGS}
        now = 0.0
        cur_set = [None]
        free_at = {e: 0.0 for e in self.ENGS}
        nseg = self.seg + 1
        byseg = [[] for _ in range(nseg)]
        for o in ops:
            byseg[o.seg].append(o.idx)
        for sg in range(nseg):
            idxs = byseg[sg]
            ready = {e: [] for e in self.ENGS}
            for i in idxs:
                if indeg[i] == 0:
                    heapq.heappush(ready[ops[i].eng], (-prio[i], i))
            events = []
            done = 0
            tot = len(idxs)
            while done < tot:
                started = False
                for e in self.ENGS:
                    if free_at[e] <= now and ready[e]:
                        _, i = heapq.heappop(ready[e])
                        o = ops[i]
                        extra = 0.0
                        if e == "act":
                            if o.tset is not None and o.tset != cur_set[0]:
                                held = [(-prio[i], i)]
                                found = None
                                for _ in range(6):
                                    if not ready[e]:
                                        break
                                    c = heapq.heappop(ready[e])
                                    oc = ops[c[1]]
                                    if (oc.tset is None or oc.tset == cur_set[0]) and prio[c[1]] > prio[i] - 6.0:
                                        found = c
                                        break
                                    held.append(c)
                                for c in held:
                                    if found is not None or c[1] != i:
                                        heapq.heappush(ready[e], c)
                                if found is not None:
                                    i = found[1]
                                    o = ops[i]
                                else:
                                    cur_set[0] = o.tset
                                    extra = 1.3
                        new_streams[e].append(o)
                        free_at[e] = now + o.cost + extra
                        heapq.heappush(events, (now + (dma_latency if o.is_dma else o.cost + extra + XLAT), i))
                        started = True
                if started:
                    continue
                cand = []
                if events:
                    cand.append(events[0][0])
                for e in self.ENGS:
                    if ready[e] and free_at[e] > now:
                        cand.append(free_at[e])
                now = min(cand)
                while events and events[0][0] <= now:
                    _, i = heapq.heappop(events)
                    done += 1
                    for j in succ[i]:
                        indeg[j] -= 1
                        if indeg[j] == 0:
                            heapq.heappush(ready[ops[j].eng], (-prio[j], j))
        for e in self.ENGS:
            assert len(new_streams[e]) == len(self.streams[e])
        self.streams = new_streams
        return now

    def apply_fences(self):
        last = {}
        pos = {e: 0 for e in self.ENGS}
        for sg in range(1, self.seg + 1):
            for e in self.ENGS:
                st = self.streams[e]
                while pos[e] < len(st) and st[pos[e]].seg < sg:
                    if not st[pos[e]].is_dma:
                        last[e] = st[pos[e]]
                    pos[e] += 1
            for e in self.ENGS:
                st = self.streams[e]
                if pos[e] < len(st) and st[pos[e]].seg == sg:
                    o = st[pos[e]]
                    extra = [d for d in last.values()] + list(self.fence_dma[sg - 1])
                    have = set(id(d) for d in o.deps)
                    for d in extra:
                        if d is o or id(d) in have:
                            continue
                        if not d.is_dma and d.eng == e and e == "pe":
                            continue
                        d.signal = True
                        o.deps.append(d)

    def emit(self, final_wait_ops=()):
        nc = self.nc
        for e in self.ENGS:
            for p_, o in enumerate(self.streams[e]):
                o.pos = p_
        for o in self.all_ops:
            if not o.war:
                continue
            best = {}
            have = set(id(d) for d in o.deps)
            for d in o.war:
                if d.is_dma:
                    if id(d) not in have:
                        have.add(id(d))
                        d.signal = True
                        o.deps.append(d)
                    continue
                b = best.get(d.eng)
                if b is None or d.pos > b.pos:
                    best[d.eng] = d
            for e, d in best.items():
                if e == o.eng and not o.is_dma:
                    continue
                if id(d) in have:
                    continue
                d.signal = True
                o.deps.append(d)
        self.apply_fences()
        for e in self.ENGS:
            c = 0
            for o in self.streams[e]:
                if o.is_dma:
                    continue
                if o.signal:
                    c += 1
                    o.ticket = c
        with contextlib.ExitStack() as es:
            esem = {e: es.enter_context(nc.semaphore("s_" + e)) for e in self.ENGS}
            dsem = [es.enter_context(nc.semaphore("d_%d" % i)) for i in range(self.n_dma_sems)]
            block = es.enter_context(nc.Block())

            def run(e, engobj):
                waited = {}

                def wait_for(d):
                    if d.is_dma:
                        key, sem, val = ("d", d.dsem), dsem[d.dsem], d.dval
                    else:
                        key, sem, val = ("e", d.eng), esem[d.eng], d.ticket
                    if waited.get(key, 0) >= val:
                        return
                    waited[key] = val
                    engobj.wait_ge(sem, val)

                for o in self.streams[e]:
                    for d in o.deps:
                        wait_for(d)
                    ins = o.fn(engobj)
                    if o.is_dma:
                        ins.then_inc(dsem[o.dsem], 16)
                    elif o.signal:
                        ins.then_inc(esem[e], 1)
                if e == "sp":
                    for d in final_wait_ops:
                        wait_for(d)

            @block.tensor
            def _(eng):
                run("pe", eng)

            @block.scalar
            def _(eng):
                run("act", eng)

            @block.vector
            def _(eng):
                run("dve", eng)

            @block.gpsimd
            def _(eng):
                run("pool", eng)

            @block.sync
            def _(eng):
                run("sp", eng)


def build_nc(dbg=None, stop_after=None):
    dbg = dbg or {}
    nc = bass.Bass("TRN2", target_bir_lowering=False)
    dt = nc.dram_tensor
    x_d = dt("x", [T, D], F32, kind="ExternalInput").ap()
    w_in = dt("w_in", [D, 11280], F32, kind="ExternalInput").ap()
    w_odn = dt("w_o_dn", [1024, 1024], F32, kind="ExternalInput").ap()
    w_odil = dt("w_o_dil", [512, 1024], F32, kind="ExternalInput").ap()
    w_out = dt("w_out", [1024, 1024], F32, kind="ExternalInput").ap()
    normw_d = dt("norm_w", [1, D], F32, kind="ExternalInput").ap()
    fnw_d = dt("final_norm_w", [1, D], F32, kind="ExternalInput").ap()
    cw_d = dt("conv_w_l", [128, 96], F32, kind="ExternalInput").ap()
    alog_d = dt("a_log", [1, 8], F32, kind="ExternalInput").ap()
    dtb_d = dt("dt_bias", [1, 8], F32, kind="ExternalInput").ap()
    dnw_d = dt("dn_norm_w_l", [128, 1], F32, kind="ExternalInput").ap()
    cst_d = dt("consts", [128, 128 + 12 * 256 + 64 * 4], F32, kind="ExternalInput").ap()
    out_d = dt("out", [T, D], F32, kind="ExternalOutput").ap()
    ob_scr = dt("ob_scr", [4, 128, T], BF16).ap()
    oa_scr = dt("oa_scr", [8, 128, T], BF16).ap()
    sg_scr = dt("sg_scr", [16, 128, T], BF16).ap()
    gc_scr = dt("gc_scr", [8, T], F32).ap()
    gcl_scr = dt("gcl_scr", [8, T], F32).ap()
    dbg_out = {}
    for name, (shape, dtype) in dbg.items():
        dbg_out[name] = dt("dbg_" + name, list(shape), dtype, kind="ExternalOutput").ap()

    P = Prog(nc)
    ARENA = 212000
    arena = nc.alloc_sbuf_tensor("arena", [128, ARENA // 2], BF16)
    cursor = [0]

    def alloc(free_shape, dtype, parts=128):
        n = 1
        for s in free_shape:
            n *= s
        esz = 4 if dtype == F32 else 2
        nbytes = (n * esz + 63) // 64 * 64
        off = cursor[0]
        cursor[0] += nbytes
        assert cursor[0] <= ARENA, ("SBUF overflow", cursor[0])
        ap = arena[0:parts, off // 2: off // 2 + n * esz // 2]
        if dtype == F32:
            ap = ap.bitcast(F32)
        if len(free_shape) == 2:
            ap = ap.rearrange("p (a b) -> p a b", b=free_shape[1])
        elif len(free_shape) == 3:
            ap = ap.rearrange("p (a b c) -> p a b c", b=free_shape[1], c=free_shape[2])
        return ap

    ps = [nc.alloc_psum_tensor("ps%d" % i, [128, 512], F32) for i in range(8)]
    psb = [p[:].bitcast(BF16) for p in ps]
    bank_rr = {}

    def bank(group):
        lst = {"proj": (0, 1), "misc": (2,), "pc": (0, 1, 2), "pre": (3, 4), "q1": (5,), "q4": (6,), "po": (7,),
               "all": tuple(range(8)), "s": (3, 4), "a": (5, 7), "b": (6, 2)}[group]
        i = bank_rr.get(group, 0)
        bank_rr[group] = i + 1
        return lst[i % len(lst)]

    def dump(name, src_ap, reads):
        if name in dbg_out:
            P.dma(lambda q, a=src_ap, o=dbg_out[name]: q.dma_start(out=o, in_=a), reads=reads, writes=[("dbg", name)])

    hT = alloc([8, T], BF16)
    cst = alloc([128 + 12 * 256 + 256], F32)
    identf = cst[:, 0:128]
    alibi = cst[:, 128:128 + 3072].rearrange("p (g w) -> p g w", w=256)
    MK = 128 + 3072
    mask_incl = cst[0:64, MK:MK + 64]
    mask_strict = cst[0:64, MK + 64:MK + 128]
    triu_f = cst[0:64, MK + 128:MK + 192]
    ones64f = cst[0:64, MK + 192:MK + 256]
    ident_bf = alloc([128], BF16)
    ones_bf = alloc([128], BF16)
    ones_f = alloc([128], F32)
    cw = alloc([96], F32)
    dnw = alloc([1], F32)
    WB_N = 3
    wb = [alloc([8, 128], BF16) for _ in range(WB_N)]
    wb_rr = [0]
    beta_t = alloc([64, 8], F32, 64)
    gc_t = alloc([64, 8], F32, 64)
    negeg_t = alloc([64, 8], F32, 64)
    wdec_t = alloc([64, 8], F32, 64)
    dl_t = alloc([64, 8], F32)
    persist_end = cursor[0]

    P.dma(lambda q: q.dma_start(out=cst, in_=cst_d), writes=["cst"])
    P.dma(lambda q: q.dma_start(out=cw, in_=cw_d), writes=["cw"])
    P.dma(lambda q: q.dma_start(out=dnw, in_=dnw_d), writes=["dnw"])
    P.op("dve", lambda e: e.tensor_copy(out=ident_bf, in_=identf), reads=["cst"], writes=["ident_bf"])
    P.op("pool", lambda e: e.memset(ones_bf, 1.0), writes=["ones_bf"])
    P.op("pool", lambda e: e.memset(ones_f, 1.0), writes=["ones_f"])

    def load_w(src, c0, ncols=128, kchunks=8, dst=None, key=None):
        if dst is None:
            i = wb_rr[0] % WB_N
            wb_rr[0] += 1
            dst, key = wb[i], ("wb", i)
        P.dma(lambda q, d=dst, s=src, c0=c0, n=ncols, kc=kchunks: q.dma_start(
            out=d[:, 0:kc, 0:n], in_=s[:, c0:c0 + n].rearrange("(k p) c -> p k c", p=128)),
            writes=[key], q="pool")
        return dst, key

    def hkeys(t0, t1):
        return [("hT", i) for i in range(t0 // 128, (t1 + 127) // 128)]

    def proj_tile(wt, wkey, ncols, tt8, grp="proj"):
        b = bank(grp)
        for k in range(8):
            P.op("pe", lambda e, b=b, k=k, wt=wt, n=ncols, tt8=tt8: e.matmul(
                ps[b][0:n, :], lhsT=wt[:, k, 0:n], rhs=hT[:, k, tt8 * 512:(tt8 + 1) * 512],
                start=(k == 0), stop=(k == 7)),
                reads=[wkey] + hkeys(tt8 * 512, tt8 * 512 + 512), writes=[("ps", b)], fd=512)
        return b

    ph = cursor[0]
    normw_b = alloc([D], F32)
    xs = [alloc([D], F32) for _ in range(4)]
    junk = [alloc([D], BF16) for _ in range(2)]
    xb = [alloc([D], BF16) for _ in range(4)]
    ss0 = [alloc([1], F32) for _ in range(4)]
    rs0 = [alloc([1], F32) for _ in range(4)]
    P.dma(lambda q: q.dma_start(out=normw_b, in_=normw_d.partition_broadcast(128)), writes=["normw_b"])
    for tt in range(32):
        i = tt % 4
        P.dma(lambda q, i=i, tt=tt: q.dma_start(out=xs[i], in_=x_d[tt * 128:(tt + 1) * 128, :]), writes=[("xs", i)])
        P.op("pool", lambda e, i=i: e.memset(ss0[i], 0.0), writes=[("ss0", i)])
        P.op("act", lambda e, i=i: e.activation(out=junk[i % 2], in_=xs[i], func=ACT.Square, accum_out=ss0[i]),
             reads=[("xs", i), ("ss0", i)], writes=[("junk", i % 2), ("ss0", i)])
        P.op("dve", lambda e, i=i: e.tensor_scalar(out=ss0[i], in0=ss0[i], scalar1=1.0 / D, scalar2=EPS, op0=ALU.mult, op1=ALU.add),
             reads=[("ss0", i)], writes=[("ss0", i)])
        P.op("act", lambda e, i=i: e.activation(out=ss0[i], in_=ss0[i], func=ACT.Ln), reads=[("ss0", i)], writes=[("ss0", i)], tset="L", fd=1)
        P.op("act", lambda e, i=i: e.activation(out=rs0[i], in_=ss0[i], func=ACT.Exp, scale=-0.5), reads=[("ss0", i)], writes=[("rs0", i)], fd=1)
        P.op("dve", lambda e, i=i: e.scalar_tensor_tensor(out=xb[i], in0=xs[i], scalar=rs0[i], in1=normw_b,
                                                          op0=ALU.mult, op1=ALU.mult),
             reads=[("xs", i), ("rs0", i), "normw_b"], writes=[("xb", i)])
        b = bank("all")
        for k in range(8):
            P.op("pe", lambda e, b=b, k=k, i=i: e.transpose(out=psb[b][:, k * 128:(k + 1) * 128],
                                                           in_=xb[i][:, k * 128:(k + 1) * 128], identity=ident_bf),
                 reads=[("xb", i), "ident_bf"], writes=[("ps", b)])
        eng = "act" if tt % 2 == 0 else "dve"
        if eng == "act":
            fn = lambda e, b=b, tt=tt: e.activation(out=hT[:, :, tt * 128:(tt + 1) * 128],
                                                    in_=psb[b].rearrange("p (k t) -> p k t", t=128), func=ACT.Copy)
        else:
            fn = lambda e, b=b, tt=tt: e.tensor_copy(out=hT[:, :, tt * 128:(tt + 1) * 128],
                                                     in_=psb[b].rearrange("p (k t) -> p k t", t=128))
        P.op(eng, fn, writes=[("hT", tt), ("ps", b)])
    dump("hT", hT, hkeys(0, T))
    ph0_end = cursor[0]

    def phase_a0():
        ph = cursor[0]
        w16, w16k = load_w(w_in, O_BA, 16)
        alog_b = alloc([8], F32, 64)
        dtb_b = alloc([8], F32, 64)
        P.dma(lambda q: q.dma_start(out=alog_b, in_=alog_d.partition_broadcast(64)), writes=["alog_b"])
        P.dma(lambda q: q.dma_start(out=dtb_b, in_=dtb_d.partition_broadcast(64)), writes=["dtb_b"])
        Gsb = alloc([64, 16], F32, 64)
        names = ["xa", "ax", "ee", "ll", "sp", "g", "lb", "gcl", "tmp"]
        A = {n: alloc([64, 8], F32, 64) for n in names}
        glast = alloc([64, 8], F32)
        tb = alloc([8, 64], F32, 64)
        for half in range(2):
            b = bank("all")
            for n in range(32 * half, 32 * half + 32):
                for k in range(8):
                    P.op("pe", lambda e, b=b, n=n, k=k: e.matmul(
                        ps[b][0:64, (n % 32) * 16:(n % 32) * 16 + 16], lhsT=hT[:, k, n * 64:(n + 1) * 64],
                        rhs=w16[:, k, 0:16], start=(k == 0), stop=(k == 7)),
                        reads=[w16k] + hkeys(n * 64, n * 64 + 64), writes=[("ps", b)])
            P.op("act", lambda e, b=b, half=half: e.activation(
                out=Gsb[:, 32 * half:32 * half + 32, :], in_=ps[b][0:64, :].rearrange("p (n c) -> p n c", c=16),
                func=ACT.Copy), writes=["Gsb", ("ps", b)])
        bb = Gsb[:, :, 0:8]
        aa = Gsb[:, :, 8:16]
        bc = lambda v: v.unsqueeze(1).to_broadcast([64, 64, 8])
        P.op("act", lambda e: e.activation(out=beta_t, in_=bb, func=ACT.Sigmoid), reads=["Gsb"], writes=["beta_t"])
        P.op("act", lambda e: e.activation(out=A["lb"], in_=beta_t, func=ACT.Ln), reads=["beta_t"], writes=["lb"])
        P.op("dve", lambda e: e.tensor_tensor(out=A["xa"], in0=aa, in1=bc(dtb_b), op=ALU.add),
             reads=["Gsb", "dtb_b"], writes=["xa"])
        P.op("act", lambda e: e.activation(out=A["ax"], in_=A["xa"], func=ACT.Abs), reads=["xa"], writes=["ax"])
        P.op("act", lambda e: e.activation(out=A["ee"], in_=A["ax"], func=ACT.Exp, scale=-1.0), reads=["ax"], writes=["ee"])
        P.op("act", lambda e: e.activation(out=A["ll"], in_=A["ee"], func=ACT.Ln, bias=1.0), reads=["ee"], writes=["ll"])
        P.op("dve", lambda e: e.scalar_tensor_tensor(out=A["sp"], in0=A["xa"], scalar=0.0, in1=A["ll"],
                                                     op0=ALU.max, op1=ALU.add), reads=["xa", "ll"], writes=["sp"])
        P.op("act", lambda e: e.activation(out=alog_b, in_=alog_b, func=ACT.Exp), reads=["alog_b"], writes=["alog_b"])
        P.op("dve", lambda e: e.scalar_tensor_tensor(out=A["g"], in0=A["sp"], scalar=-1.0, in1=bc(alog_b),
                                                     op0=ALU.mult, op1=ALU.mult), reads=["sp", "alog_b"], writes=["g"])
        gflat = A["g"].rearrange("p n h -> p (n h)")
        b1 = bank("all")
        P.op("pe", lambda e: e.matmul(ps[b1][0:64, :], lhsT=triu_f, rhs=gflat, start=True, stop=True),
             reads=["g", "cst"], writes=[("ps", b1)])
        b2 = bank("all")
        P.op("pe", lambda e: e.matmul(ps[b2][:, :], lhsT=ones_f[0:64, :], rhs=gflat, start=True, stop=True),
             reads=["g", "ones_f"], writes=[("ps", b2)])
        fl = lambda v: v.rearrange("p n h -> p (n h)")
        P.op("act", lambda e: e.activation(out=fl(gc_t), in_=ps[b1][0:64, :], func=ACT.Copy), writes=["gc_t", ("ps", b1)])
        P.op("dve", lambda e: e.tensor_copy(out=fl(glast), in_=ps[b2][:, :]), writes=["glast", ("ps", b2)])
        P.op("act", lambda e: e.activation(out=negeg_t, in_=gc_t, func=ACT.Exp), reads=["gc_t"], writes=["negeg_t"])
        P.op("dve", lambda e: e.tensor_scalar(out=negeg_t, in0=negeg_t, scalar1=-1.0, scalar2=None, op0=ALU.mult),
             reads=["negeg_t"], writes=["negeg_t"])
        P.op("dve", lambda e: e.tensor_tensor(out=A["tmp"], in0=glast[0:64], in1=gc_t, op=ALU.subtract),
             reads=["glast", "gc_t"], writes=["tmp"])
        P.op("act", lambda e: e.activation(out=wdec_t, in_=A["tmp"], func=ACT.Exp), reads=["tmp"], writes=["wdec_t"])
        P.op("act", lambda e: e.activation(out=dl_t, in_=glast, func=ACT.Exp), reads=["glast"], writes=["dl_t"])
        P.op("dve", lambda e: e.tensor_tensor(out=A["gcl"], in0=gc_t, in1=A["lb"], op=ALU.add),
             reads=["gc_t", "lb"], writes=["gcl"])
        for nm, src, scr in (("gc", gc_t, gc_scr), ("gcl", A["gcl"], gcl_scr)):
            b = bank("all")
            for h in range(8):
                P.op("pe", lambda e, b=b, h=h, src=src: e.transpose(out=ps[b][0:64, h * 64:(h + 1) * 64],
                                                                   in_=src[:, :, h], identity=identf[0:64, 0:64]),
                     reads=["gc_t" if nm == "gc" else "gcl", "cst"], writes=[("ps", b)])
            P.op("dve", lambda e, b=b: e.tensor_copy(out=tb, in_=ps[b][0:64, :].rearrange("p (h c) -> p h c", c=64)),
                 writes=["tb", ("ps", b)])
            P.dma(lambda q, scr=scr: q.dma_start(out=scr.rearrange("h (n c) -> n h c", c=64), in_=tb),
                  reads=["tb"], writes=[nm + "_scr"])
        dump("gc_t", gc_t, ["gc_t"])
        dump("beta_t", beta_t, ["beta_t"])
        dump("g_t", A["g"], ["g"])
        cursor[0] = ph

    phase_a0()
    cursor[0] = ph
    P.fence()

    def interleave(ta, tb):
        na, nb_ = len(ta), len(tb)
        j = 0
        for i, t in enumerate(ta):
            t()
            tgt = (nb_ * (i + 1) + na - 1) // max(na, 1)
            while j < min(tgt, nb_):
                tb[j]()
                j += 1
        while j < nb_:
            tb[j]()
            j += 1

    def phase_b():
        ph = cursor[0]
        qTs = [alloc([T], BF16) for _ in range(2)]
        kTs = [alloc([T], BF16) for _ in range(2)]
        vsbs = [alloc([32, 128], BF16) for _ in range(2)]
        vT = alloc([T], BF16)
        acc_n = alloc([T], F32)
        acc_d = alloc([T], F32)
        NS = 3
        s_sb = [alloc([256], F32) for _ in range(NS)]
        p_sb = [alloc([256], BF16) for _ in range(4)]
        zs = [alloc([512], F32) for _ in range(2)]
        rc = [alloc([512], F32) for _ in range(2)]
        ob = [alloc([512], BF16) for _ in range(2)]
        mone = alloc([512], F32)
        P.op("pool", lambda e: e.memset(mone, -1.0), writes=["mone"])
        scale = 128 ** -0.5
        jobs = [(h, gi) for h in range(4) for gi in range(3)]
        cnt = [0]

        def proj_tasks(jn):
            h, gi = jobs[jn]
            bi = jn % 2
            qT, kT, v_sb = qTs[bi], kTs[bi], vsbs[bi]
            d = DIL[gi]
            L = T // d
            nb = L // 128
            M = 512 // d
            tasks = []
            hold = {}

            def t_w():
                hold["q"] = load_w(w_in, O_QB + gi * 512 + h * 128)
                hold["k"] = load_w(w_in, O_KB + gi * 512 + h * 128)
                hold["v"] = load_w(w_in, O_VB + gi * 512 + h * 128)
            tasks.append(t_w)
            q3 = qT.rearrange("p (r m) -> p r m", r=d)
            k3 = kT.rearrange("p (r m) -> p r m", r=d)
            for tt8 in range(8):
                def t_q(tt8=tt8):
                    wq, wqk = hold["q"]
                    b = proj_tile(wq, wqk, 128, tt8)
                    P.op("act", lambda e: e.activation(out=qT[:, tt8 * 512:(tt8 + 1) * 512], in_=ps[b][:, :],
                                                       func=ACT.Copy, scale=scale),
                         writes=[("qT", bi), ("ps", b)])
                tasks.append(t_q)

                def t_k(tt8=tt8):
                    wk, wkk = hold["k"]
                    b = proj_tile(wk, wkk, 128, tt8)
                    P.op("dve", lambda e: e.tensor_copy(out=kT[:, tt8 * 512:(tt8 + 1) * 512], in_=ps[b][:, :]),
                         writes=[("kT", bi), ("ps", b)])
                tasks.append(t_k)

                def t_v(tt8=tt8):
                    wv, wvk = hold["v"]
                    b = proj_tile(wv, wvk, 128, tt8)
                    P.op("act", lambda e: e.activation(out=vT[:, tt8 * 512:(tt8 + 1) * 512], in_=ps[b][:, :], func=ACT.Copy),
                         writes=[("vT", tt8), ("ps", b)])
                tasks.append(t_v)
            for t8 in range(4):
                def t_vt(t8=t8):
                    b = bank("proj")
                    for s in range(8):
                        tid = t8 * 8 + s
                        r, j = tid // nb, tid % nb
                        t0 = 128 * j * d + r
                        P.op("pe", lambda e, s=s, t0=t0: e.transpose(
                            out=psb[b][:, s * 128:(s + 1) * 128], in_=vT[:, t0:t0 + 127 * d + 1:d], identity=ident_bf),
                            reads=[("vT", i) for i in range((128 * j * d) // 512, (128 * (j + 1) * d + 511) // 512)] + ["ident_bf"],
                            writes=[("ps", b)])
                    P.op("dve", lambda e: e.tensor_copy(
                        out=v_sb[:, t8 * 8:(t8 + 1) * 8, :], in_=psb[b][:, :].rearrange("p (s c) -> p s c", c=128)),
                        writes=[("v_sb", bi), ("ps", b)])
                tasks.append(t_vt)
            return tasks

        def core_tasks(jn):
            h, gi = jobs[jn]
            bi = jn % 2
            qT, kT, v_sb = qTs[bi], kTs[bi], vsbs[bi]
            d = DIL[gi]
            L = T // d
            nb = L // 128
            gidx = gi * 4 + h
            tasks = []
            st = {"bn": None, "bd": None}
            pis = {}

            def tok(r, j0, nblk):
                a = (128 * j0) * d + r
                return slice(a, a + (128 * nblk - 1) * d + 1, d)

            def t_qk(r, j):
              with P.fds(pe=256, act=256, dve=256):
                W = 2 if j + 1 < nb else 1
                bs = bank("s")
                P.op("pe", lambda e: e.matmul(ps[bs][:, 0:128 * W], lhsT=kT[:, tok(r, j, 1)], rhs=qT[:, tok(r, j, W)],
                                              start=True, stop=True),
                     reads=[("qT", bi), ("kT", bi)], writes=[("ps", bs)])
                si = cnt[0] % NS
                pi = cnt[0] % 4
                cnt[0] += 1
                pis[(r, j)] = pi
                P.op("dve", lambda e: e.tensor_tensor(out=s_sb[si][:, 0:128 * W], in0=ps[bs][:, 0:128 * W],
                                                      in1=alibi[:, gidx, 0:128 * W], op=ALU.add),
                     reads=["cst"], writes=[("s_sb", si), ("ps", bs)])
                P.op("act", lambda e: e.activation(out=p_sb[pi][:, 0:128 * W], in_=s_sb[si][:, 0:128 * W], func=ACT.Exp),
                     reads=[("s_sb", si)], writes=[("p_sb", pi)])

            def t_pv(r, j):
              with P.fds(pe=128):
                if j % 4 == 0:
                    st["bn"], st["bd"] = bank("a"), bank("b")
                bn, bd = st["bn"], st["bd"]
                pi = pis[(r, j)]
                prev = pis[(r, j - 1)] if j > 0 else None
                col = (j % 4) * 128
                tid = r * nb + j
                for (bk, is_den, lk) in ((bn, False, ("v_sb", bi)), (bd, True, "ones_bf")):
                    first = True
                    if j > 0:
                        lp = ones_bf if is_den else v_sb[:, tid - 1, :]
                        P.op("pe", lambda e, bk=bk, lp=lp: e.matmul(
                            ps[bk][:, col:col + 128], lhsT=lp, rhs=p_sb[prev][:, 128:256], start=True, stop=False),
                            reads=[lk, ("p_sb", prev)], writes=[("ps", bk)])
                        first = False
                    lc = ones_bf if is_den else v_sb[:, tid, :]
                    P.op("pe", lambda e, bk=bk, lc=lc, first=first: e.matmul(
                        ps[bk][:, col:col + 128], lhsT=lc, rhs=p_sb[pi][:, 0:128], start=first, stop=True),
                        reads=[lk, ("p_sb", pi)], writes=[("ps", bk)])
                if j % 4 == 3 or j == nb - 1:
                    n0 = (j // 4) * 4
                    nq = (j - n0 + 1) * 128
                    sl = slice(128 * n0 * d + r, 128 * n0 * d + r + (nq - 1) * d + 1, d)
                    for (bk, acc, key, eng) in ((bn, acc_n, "acc_n", "act"), (bd, acc_d, "acc_d", "dve")):
                        if gi == 0:
                            if eng == "act":
                                P.op("act", lambda e, bk=bk, acc=acc: e.activation(
                                    out=acc[:, sl], in_=ps[bk][:, 0:nq], func=ACT.Copy), writes=[key, ("ps", bk)])
                            else:
                                P.op("dve", lambda e, bk=bk, acc=acc: e.tensor_copy(
                                    out=acc[:, sl], in_=ps[bk][:, 0:nq]), writes=[key, ("ps", bk)])
                        else:
                            P.op("dve", lambda e, bk=bk, acc=acc: e.tensor_tensor(
                                out=acc[:, sl], in0=ps[bk][:, 0:nq], in1=acc[:, sl], op=ALU.add),
                                reads=[key], writes=[key, ("ps", bk)])

            seq = [(r, j) for r in range(d) for j in range(nb)]
            SK = 2
            for idx in range(len(seq) + SK):
                def t_blk(idx=idx):
                    if idx < len(seq):
                        t_qk(*seq[idx])
                    if idx >= SK:
                        t_pv(*seq[idx - SK])
                tasks.append(t_blk)
            if gi == 2:
                hold = {}

                def t_wz():
                    hold["z"] = load_w(w_in, O_ZB + h * 128)
                tasks.append(t_wz)
                for tt8 in range(8):
                    def t_fin(tt8=tt8):
                        wz, wzk = hold["z"]
                        i = tt8 % 2
                        sl = slice(tt8 * 512, (tt8 + 1) * 512)
                        b = proj_tile(wz, wzk, 128, tt8)
                        P.op("act", lambda e: e.activation(out=zs[i], in_=ps[b][:, :], func=ACT.Tanh, scale=0.5),
                             writes=[("zs", i), ("ps", b)], tset="T")
                        P.op("dve", lambda e: e.scalar_tensor_tensor(out=zs[i], in0=zs[i], scalar=1.0, in1=ps[b][:, :],
                                                                     op0=ALU.add, op1=ALU.mult),
                             reads=[("zs", i)], writes=[("zs", i), ("ps", b)])
                        P.op("dve", lambda e: e.reciprocal(out=rc[i], in_=acc_d[:, sl]), reads=["acc_d"], writes=[("rc", i)], fd=1500)
                        P.op("dve", lambda e: e.tensor_tensor(out=rc[i], in0=rc[i], in1=acc_n[:, sl], op=ALU.mult),
                             reads=["acc_n", ("rc", i)], writes=[("rc", i)])
                        P.op("dve", lambda e: e.scalar_tensor_tensor(out=ob[i], in0=rc[i], scalar=0.5, in1=zs[i],
                                                                     op0=ALU.mult, op1=ALU.mult),
                             reads=[("rc", i), ("zs", i)], writes=[("ob", i)])
                        P.dma(lambda q: q.dma_start(out=ob_scr[h, :, sl], in_=ob[i]),
                              reads=[("ob", i)], writes=[("ob_scr", h, tt8)])
                    tasks.append(t_fin)
            return tasks

        for t in proj_tasks(0):
            t()
        for jn in range(len(jobs)):
            interleave(core_tasks(jn), proj_tasks(jn + 1) if jn + 1 < len(jobs) else [])
        cursor[0] = ph

    if stop_after != "a0":
        phase_b()
        P.fence()
        dump("ob_scr", ob_scr, [])

    def phase_a():
        ph = cursor[0]
        NB2 = 2
        upre = {t: [alloc([516], BF16) for _ in range(2)] for t in "qkv"}
        for t in "qkv":
            P.op("pool", lambda e, t=t: e.memset(upre[t][1][:, 512:515], 0.0), writes=[("upre", t, 1)])
        dg = alloc([12, 128], BF16)
        _thb = [alloc([512], F32) for _ in range(2)]
        th = {"q": _thb[0], "k": _thb[1], "v": _thb[0], "z": _thb[1]}
        thk = {"q": ("th", 0), "k": ("th", 1), "v": ("th", 0), "z": ("th", 1)}
        yq = alloc([512], F32)
        yk = alloc([512], F32)
        sq = {t: alloc([512], BF16) for t in "qk"}
        rin = {t: alloc([512], F32) for t in "qk"}
        vTt = alloc([512], BF16)
        khT = [alloc([512], BF16) for _ in range(NB2)]
        qhT = [alloc([512], BF16) for _ in range(NB2)]
        qgT = [alloc([512], BF16) for _ in range(3)]
        zsT = [alloc([512], BF16) for _ in range(3)]
        Ktok = [alloc([8, 128], BF16, 64) for _ in range(NB2)]
        Vtok = [alloc([8, 128], BF16, 64) for _ in range(NB2)]
        AqkT = [alloc([8, 64], BF16, 64) for _ in range(NB2)]
        TT = alloc([8, 64], BF16, 64)
        gcB = [alloc([512], F32) for _ in range(2)]
        gclB = [alloc([512], F32, 64) for _ in range(2)]
        egB = alloc([512], F32)
        E1 = alloc([8, 64], F32, 64)
        E2 = alloc([8, 64], F32, 64)
        GT = alloc([8, 64], F32, 64)
        GTb = alloc([8, 64], F32, 64)
        Pk = [alloc([8, 64], BF16, 64) for _ in range(2)]
        PkT = [alloc([8, 64], BF16, 64) for _ in range(2)]
        Xb = [alloc([8, 64], BF16, 64) for _ in range(2)]
        S_f = alloc([128], F32)
        S_b = alloc([128], BF16)
        vnew = [alloc([128], BF16, 64) for _ in range(2)]
        WnT = [alloc([8, 64], BF16) for _ in range(NB2)]
        Ubf = [alloc([8, 128], BF16, 64) for _ in range(NB2)]
        Kd = [alloc([8, 128], BF16, 64) for _ in range(NB2)]
        Kgn = alloc([8, 128], BF16, 64)
        oraw = alloc([512], F32)
        osq = alloc([512], BF16)
        orst = alloc([512], F32)
        oa = [alloc([512], BF16) for _ in range(2)]
        wsets = [[alloc([8, 128], BF16) for _ in range(4)] for _ in range(2)]

        def load_head_w(h):
            ws = wsets[h % 2]
            for ti, off in enumerate((O_QA, O_KA, O_VA, O_ZA)):
                load_w(w_in, off + h * 128, dst=ws[ti], key=("wh", h % 2, ti))

        m8 = lambda m: m.unsqueeze(1).to_broadcast([64, 8, 64])
        v3 = lambda a: a.rearrange("p (c i) -> p c i", i=64)
        fl = lambda a: a.rearrange("p c i -> p (c i)")

        def rsqrt_from_psum(b, dst, key, scale):
            P.op("act", lambda e: e.activation(out=dst, in_=ps[b][:, :], func=ACT.Ln, scale=scale, bias=EPS),
                 writes=[key, ("ps", b)], tset="L")
            P.op("act", lambda e: e.activation(out=dst, in_=dst, func=ACT.Exp, scale=-0.5), reads=[key], writes=[key])

        def stage12_tasks(st):
            h, tt8 = st // 8, st % 8
            bi = st % NB2
            b3 = st % 3
            ui = st % 2
            t0 = tt8 * 512
            ws = wsets[h % 2]
            tasks = []
            A = tasks.append

            def t_pre():
                if tt8 == 0:
                    if h + 1 < 8:
                        load_head_w(h + 1)
                    for ti in range(3):
                        for kk in range(4):
                            g = ti * 8 + h
                            P.op("dve", lambda e, ti=ti, kk=kk, g=g: e.tensor_scalar(
                                out=dg[:, ti * 4 + kk, :], in0=identf, scalar1=cw[:, g * 4 + kk:g * 4 + kk + 1], scalar2=None,
                                op0=ALU.mult), reads=["cst", "cw"], writes=["dg"])
                P.dma(lambda q: q.dma_start(out=gcB[ui], in_=gc_scr[h, t0:t0 + 512].partition_broadcast(128)),
                      reads=["gc_scr"], writes=[("gcB", ui)])
                P.dma(lambda q: q.dma_start(out=gclB[ui], in_=gcl_scr[h, t0:t0 + 512].partition_broadcast(64)),
                      reads=["gcl_scr"], writes=[("gclB", ui)])
            A(t_pre)

            def proj_split(wt, wkey, hb):
                def mk(k0):
                    def f():
                        if k0 == 0:
                            hb["b"] = bank("pc")
                        b = hb["b"]
                        for k in range(k0, k0 + 2):
                            P.op("pe", lambda e, k=k: e.matmul(ps[b][:, :], lhsT=wt[:, k, :], rhs=hT[:, k, t0:t0 + 512],
                                                               start=(k == 0), stop=(k == 7)),
                                 reads=[wkey] + hkeys(t0, t0 + 512), writes=[("ps", b)], fd=512)
                    return f
                for k0 in (0, 2, 4):
                    A(mk(k0))
                return mk(6)

            hbz = {}
            last_z = proj_split(ws[3], ("wh", h % 2, 3), hbz)

            def t_projz():
                last_z()
                b = hbz["b"]
                P.op("act", lambda e: e.activation(out=th["z"], in_=ps[b][:, :], func=ACT.Tanh, scale=0.5),
                     writes=[thk["z"], ("ps", b)], tset="T")
                P.op("dve", lambda e: e.scalar_tensor_tensor(out=zsT[b3], in0=th["z"], scalar=1.0, in1=ps[b][:, :],
                                                             op0=ALU.add, op1=ALU.mult),
                     reads=[thk["z"]], writes=[("zsT", b3), ("ps", b)])
            A(t_projz)

            for ti, t in enumerate("qkv"):
                hbp = {}
                last_p = proj_split(ws[ti], ("wh", h % 2, ti), hbp)

                def t_proj(ti=ti, t=t, hbp=hbp, last_p=last_p):
                    u = upre[t][ui]
                    up = upre[t][1 - ui]
                    last_p()
                    b = hbp["b"]
                    if tt8 == 0:
                        P.op("pool", lambda e: e.memset(u[:, 0:3], 0.0), writes=[("upre", t, ui)])
                    else:
                        P.op("pool", lambda e: e.tensor_copy(out=u[:, 0:3], in_=up[:, 512:515]),
                             reads=[("upre", t, 1 - ui)], writes=[("upre", t, ui)])
                    P.op("act", lambda e: e.activation(out=u[:, 3:515], in_=ps[b][:, :], func=ACT.Copy),
                         writes=[("upre", t, ui), ("ps", b)])
                A(t_proj)

            for ti, t in enumerate("qkv"):
                hbc = {}

                def t_conv0(ti=ti, t=t, hbc=hbc):
                    u = upre[t][ui]
                    hbc["b"] = b = bank("pc")
                    for kk in range(2):
                        P.op("pe", lambda e, kk=kk: e.matmul(ps[b][:, :], lhsT=dg[:, ti * 4 + kk, :], rhs=u[:, kk:kk + 512],
                                                             start=(kk == 0), stop=False),
                             reads=["dg", ("upre", t, ui)], writes=[("ps", b)], fd=512)
                A(t_conv0)

                def t_conv(ti=ti, t=t, hbc=hbc):
                    u = upre[t][ui]
                    b = hbc["b"]
                    for kk in range(2, 4):
                        P.op("pe", lambda e, kk=kk: e.matmul(ps[b][:, :], lhsT=dg[:, ti * 4 + kk, :], rhs=u[:, kk:kk + 512],
                                                             start=False, stop=(kk == 3)),
                             reads=["dg", ("upre", t, ui)], writes=[("ps", b)], fd=512)
                    P.op("act", lambda e: e.activation(out=th[t], in_=ps[b][:, :], func=ACT.Tanh, scale=0.5),
                         writes=[thk[t], ("ps", b)], tset="T")
                    dst = {"q": yq, "k": yk, "v": vTt}[t]
                    P.op("dve", lambda e: e.scalar_tensor_tensor(out=dst, in0=th[t], scalar=1.0, in1=ps[b][:, :],
                                                                 op0=ALU.add, op1=ALU.mult),
                         reads=[thk[t]], writes=[("y", t), ("ps", b)])
                A(t_conv)

            for t, y in (("q", yq), ("k", yk)):
                def t_norm(t=t, y=y):
                    P.op("pool", lambda e: e.tensor_tensor(out=sq[t], in0=y, in1=y, op=ALU.mult), reads=[("y", t)], writes=[("sq", t)])
                    b = bank("pc")
                    P.op("pe", lambda e: e.matmul(ps[b][:, :], lhsT=ones_bf, rhs=sq[t], start=True, stop=True),
                         reads=[("sq", t), "ones_bf"], writes=[("ps", b)], fd=512)
                    rsqrt_from_psum(b, rin[t], ("rin", t), 0.25)
                A(t_norm)

            def t_hat():
                P.op("dve", lambda e: e.scalar_tensor_tensor(out=khT[bi], in0=yk, scalar=0.5, in1=rin["k"],
                                                             op0=ALU.mult, op1=ALU.mult),
                     reads=[("y", "k"), ("rin", "k")], writes=[("khT", bi)])
                P.op("dve", lambda e: e.scalar_tensor_tensor(out=qhT[bi], in0=yq, scalar=0.5 * 128 ** -0.5, in1=rin["q"],
                                                             op0=ALU.mult, op1=ALU.mult),
                     reads=[("y", "q"), ("rin", "q")], writes=[("qhT", bi)])
                P.op("act", lambda e: e.activation(out=egB, in_=gcB[ui], func=ACT.Exp), reads=[("gcB", ui)], writes=["egB"])
                P.op("pool", lambda e: e.tensor_tensor(out=qgT[b3], in0=qhT[bi], in1=egB, op=ALU.mult),
                     reads=[("qhT", bi), "egB"], writes=[("qgT", b3)])
            A(t_hat)

            for (src, skey, dst, dkey, sc) in ((khT[bi], ("khT", bi), Ktok[bi], ("Ktok", bi), 1.0),
                                               (vTt, ("y", "v"), Vtok[bi], ("Vtok", bi), 0.5)):
                def t_tok(src=src, skey=skey, dst=dst, dkey=dkey, sc=sc):
                    b = bank("pc")
                    for c in range(8):
                        P.op("pe", lambda e, c=c: e.transpose(out=psb[b][0:64, c * 128:(c + 1) * 128],
                                                              in_=src[:, c * 64:(c + 1) * 64], identity=ident_bf),
                             reads=[skey, "ident_bf"], writes=[("ps", b)])
                    P.op("act", lambda e: e.activation(out=dst, in_=psb[b][0:64, :].rearrange("p (c k) -> p c k", k=128),
                                                       func=ACT.Copy, scale=sc), writes=[dkey, ("ps", b)])
                A(t_tok)

            n0 = tt8 * 8
            gcJ = gc_t[:, n0:n0 + 8, h].unsqueeze(2).to_broadcast([64, 8, 64])
            bJ = beta_t[:, n0:n0 + 8, h].unsqueeze(2).to_broadcast([64, 8, 64])
            hold = {}

            split = [len(tasks)]

            def t_gates():
                P.op("pool", lambda e: e.tensor_tensor(out=E1, in0=v3(gcB[ui][0:64, :]), in1=gcJ, op=ALU.subtract),
                     reads=[("gcB", ui), "gc_t"], writes=["E1"])
                P.op("pool", lambda e: e.tensor_tensor(out=E1, in0=E1, in1=m8(mask_incl), op=ALU.add),
                     reads=["E1", "cst"], writes=["E1"])
                P.op("act", lambda e: e.activation(out=GT, in_=E1, func=ACT.Exp), reads=["E1"], writes=["GT"])
                P.op("pool", lambda e: e.tensor_tensor(out=E2, in0=v3(gclB[ui]), in1=gcJ, op=ALU.subtract),
                     reads=[("gclB", ui), "gc_t"], writes=["E2"])
                P.op("pool", lambda e: e.tensor_tensor(out=E2, in0=E2, in1=m8(mask_strict), op=ALU.add),
                     reads=["E2", "cst"], writes=["E2"])
                P.op("act", lambda e: e.activation(out=GTb, in_=E2, func=ACT.Exp), reads=["E2"], writes=["GTb"])
            A(t_gates)

            def t_kkqk():
                bkk, bqk = bank("pre"), bank("pre")
                for c in range(8):
                    cs = slice(c * 64, (c + 1) * 64)
                    P.op("pe", lambda e, cs=cs: e.matmul(ps[bkk][0:64, cs], lhsT=khT[bi][:, cs], rhs=khT[bi][:, cs],
                                                         start=True, stop=True), reads=[("khT", bi)], writes=[("ps", bkk)])
                for c in range(8):
                    cs = slice(c * 64, (c + 1) * 64)
                    P.op("pe", lambda e, cs=cs: e.matmul(ps[bqk][0:64, cs], lhsT=khT[bi][:, cs], rhs=qhT[bi][:, cs],
                                                         start=True, stop=True), reads=[("khT", bi), ("qhT", bi)], writes=[("ps", bqk)])
                P.op("dve", lambda e: e.tensor_tensor(out=AqkT[bi], in0=v3(ps[bqk][0:64, :]), in1=GT, op=ALU.mult),
                     reads=["GT"], writes=[("AqkT", bi), ("ps", bqk)])
                P.op("dve", lambda e: e.scalar_tensor_tensor(out=Pk[0], in0=v3(ps[bkk][0:64, :]), scalar=-1.0, in1=GTb,
                                                             op0=ALU.mult, op1=ALU.mult),
                     reads=["GTb"], writes=[("Pk", 0), ("ps", bkk)])
            A(t_kkqk)

            def t_pt():
                b = bank("pre")
                for c in range(8):
                    P.op("pe", lambda e, c=c: e.transpose(out=psb[b][0:64, c * 64:(c + 1) * 64], in_=Pk[0][:, c, :],
                                                          identity=ident_bf[0:64, 0:64]),
                         reads=[("Pk", 0), "ident_bf"], writes=[("ps", b)])
                P.op("act", lambda e: e.activation(out=fl(PkT[0]), in_=psb[b][0:64, 0:512], func=ACT.Copy),
                     writes=[("PkT", 0), ("ps", b)])
                P.op("pool", lambda e: e.tensor_tensor(out=Xb[0], in0=Pk[0], in1=m8(identf[0:64, 0:64]), op=ALU.add),
                     reads=[("Pk", 0), "cst"], writes=[("Xb", 0)])
            A(t_pt)

            for lvl in range(5):
                cur = lvl % 2
                nxt = 1 - cur

                def t_sq(lvl=lvl, cur=cur, nxt=nxt):
                    if lvl < 4:
                        ba = bank("pre")
                        for c in range(8):
                            cs = slice(c * 64, (c + 1) * 64)
                            P.op("pe", lambda e, c=c, cs=cs: e.matmul(ps[ba][0:64, cs], lhsT=PkT[cur][:, c, :], rhs=Pk[cur][:, c, :],
                                                                      start=True, stop=True),
                                 reads=[("Pk", cur), ("PkT", cur)], writes=[("ps", ba)])
                    bt = bank("pre")
                    for c in range(8):
                        cs = slice(c * 64, (c + 1) * 64)
                        P.op("pe", lambda e, c=c, cs=cs: e.matmul(ps[bt][0:64, cs], lhsT=Pk[cur][:, c, :], rhs=PkT[cur][:, c, :],
                                                                  start=True, stop=True),
                             reads=[("Pk", cur), ("PkT", cur)], writes=[("ps", bt)])
                    if lvl < 4:
                        P.op("act", lambda e: e.activation(out=fl(Pk[nxt]), in_=ps[ba][0:64, :], func=ACT.Copy),
                             writes=[("Pk", nxt), ("ps", ba)])
                    P.op("dve", lambda e: e.tensor_copy(out=fl(PkT[nxt]), in_=ps[bt][0:64, :]),
                         writes=[("PkT", nxt), ("ps", bt)])
                A(t_sq)

                def t_x(lvl=lvl, cur=cur, nxt=nxt):
                    bx = bank("pre")
                    for c in range(8):
                        cs = slice(c * 64, (c + 1) * 64)
                        P.op("pe", lambda e, c=c, cs=cs: e.matmul(ps[bx][0:64, cs], lhsT=PkT[nxt][:, c, :], rhs=Xb[cur][:, c, :],
                                                                  start=True, stop=True),
                             reads=[("PkT", nxt), ("Xb", cur)], writes=[("ps", bx)])
                    P.op("dve", lambda e: e.tensor_tensor(out=fl(Xb[nxt]), in0=ps[bx][0:64, :], in1=fl(Xb[cur]), op=ALU.add),
                         reads=[("Xb", cur)], writes=[("Xb", nxt), ("ps", bx)])
                    if lvl == 4:
                        P.op("pool", lambda e: e.tensor_tensor(out=TT, in0=Xb[nxt], in1=bJ, op=ALU.mult),
                             reads=[("Xb", nxt), "beta_t"], writes=["TT"])
                A(t_x)

            ngJ = negeg_t[:, n0:n0 + 8, h].unsqueeze(2).to_broadcast([64, 8, 128])
            wdJ = wdec_t[:, n0:n0 + 8, h].unsqueeze(2).to_broadcast([64, 8, 128])

            def t_kg():
                P.op("pool", lambda e: e.tensor_tensor(out=Kgn, in0=Ktok[bi], in1=ngJ, op=ALU.mult),
                     reads=[("Ktok", bi), "negeg_t"], writes=["Kgn"])
                P.op("pool", lambda e: e.tensor_tensor(out=Kd[bi], in0=Ktok[bi], in1=wdJ, op=ALU.mult),
                     reads=[("Ktok", bi), "wdec_t"], writes=[("Kd", bi)])
            tasks.insert(len(tasks) - 6, t_kg)

            def t_w():
                b = bank("pre")
                for c in range(8):
                    P.op("pe", lambda e, c=c: e.matmul(ps[b][:, c * 64:(c + 1) * 64], lhsT=Kgn[:, c, :], rhs=TT[:, c, :],
                                                       start=True, stop=True),
                         reads=["Kgn", "TT"], writes=[("ps", b)])
                P.op("act", lambda e: e.activation(out=fl(WnT[bi]), in_=ps[b][:, :], func=ACT.Copy),
                     writes=[("WnT", bi), ("ps", b)])
            A(t_w)

            for half in range(2):
                def t_u(half=half):
                    b = bank("pre")
                    for c4 in range(4):
                        c = half * 4 + c4
                        P.op("pe", lambda e, c=c, c4=c4: e.matmul(ps[b][0:64, c4 * 128:(c4 + 1) * 128], lhsT=TT[:, c, :],
                                                                  rhs=Vtok[bi][:, c, :], start=True, stop=True),
                             reads=["TT", ("Vtok", bi)], writes=[("ps", b)])
                    P.op("dve", lambda e: e.tensor_copy(out=Ubf[bi][:, half * 4:half * 4 + 4, :],
                                                        in_=ps[b][0:64, :].rearrange("p (c k) -> p c k", k=128)),
                         writes=[("Ubf", bi), ("ps", b)])
                A(t_u)
            return tasks[:split[0]], tasks[split[0]:]

        def stage3_tasks(st):
            h, tt8 = st // 8, st % 8
            bi = st % NB2
            b3 = st % 3
            t0 = tt8 * 512
            n0 = tt8 * 8
            tasks = []
            A = tasks.append
            hold = {}

            def t_begin():
                hold["bo"] = bank("po")
            A(t_begin)
            for c in range(8):
                n = n0 + c
                cs = slice(c * 64, (c + 1) * 64)
                ri = n % 2
                first = (n == 0)

                def t_a(c=c, n=n, cs=cs, ri=ri, first=first):
                  with P.fds(pe=128, act=128, dve=128):
                    b1 = bank("q1")
                    P.op("pe", lambda e: e.matmul(ps[b1][0:64, 0:128], lhsT=ident_bf[0:64, 0:64], rhs=Ubf[bi][:, c, :],
                                                  start=True, stop=first),
                         reads=["ident_bf", ("Ubf", bi)], writes=[("ps", b1)])
                    if not first:
                        P.op("pe", lambda e: e.matmul(ps[b1][0:64, 0:128], lhsT=WnT[bi][:, c, :], rhs=S_b, start=False, stop=True),
                             reads=[("WnT", bi), "S_b"], writes=[("ps", b1)])
                    P.op("act", lambda e: e.activation(out=vnew[ri], in_=ps[b1][0:64, 0:128], func=ACT.Copy),
                         writes=[("vnew", ri), ("ps", b1)])
                A(t_a)

                def t_c(c=c, n=n, cs=cs, ri=ri, first=first):
                  with P.fds(pe=128, act=128, dve=128):
                    bo = hold["bo"]
                    b4 = bank("q4")
                    P.op("pe", lambda e: e.matmul(ps[b4][:, 0:128], lhsT=Kd[bi][:, c, :], rhs=vnew[ri], start=True, stop=True),
                         reads=[("Kd", bi), ("vnew", ri)], writes=[("ps", b4)])
                    if not first:
                        P.op("pe", lambda e: e.matmul(ps[bo][:, cs], lhsT=S_b, rhs=qgT[b3][:, cs], start=True, stop=False),
                             reads=["S_b", ("qgT", b3)], writes=[("ps", bo)])
                    P.op("pe", lambda e: e.matmul(ps[bo][:, cs], lhsT=vnew[ri], rhs=AqkT[bi][:, c, :], start=first, stop=True),
                         reads=[("vnew", ri), ("AqkT", bi)], writes=[("ps", bo)])
                    if first:
                        P.op("dve", lambda e: e.tensor_copy(out=S_b, in_=ps[b4][:, 0:128]), writes=["S_b", ("ps", b4)])
                        P.op("dve", lambda e: e.tensor_copy(out=S_f, in_=ps[b4][:, 0:128]), writes=["S_f", ("ps", b4)])
                    else:
                        P.op("dve", lambda e: e.scalar_tensor_tensor(
                            out=S_b, in0=S_f, scalar=dl_t[:, n, h:h + 1], in1=ps[b4][:, 0:128], op0=ALU.mult, op1=ALU.add),
                            reads=["S_f", "dl_t"], writes=["S_b", ("ps", b4)])
                        P.op("dve", lambda e: e.scalar_tensor_tensor(
                            out=S_f, in0=S_f, scalar=dl_t[:, n, h:h + 1], in1=ps[b4][:, 0:128], op0=ALU.mult, op1=ALU.add),
                            reads=["S_f", "dl_t"], writes=["S_f", ("ps", b4)])
                A(t_c)

            def t_epi():
                bo = hold["bo"]
                oi = st % 2
                P.op("act", lambda e: e.activation(out=oraw, in_=ps[bo][:, :], func=ACT.Copy), writes=["oraw", ("ps", bo)])
                if st == 0:
                    dump("khT0", khT[bi], [("khT", bi)])
                    dump("oraw0", oraw, ["oraw"])
                P.op("act", lambda e: e.activation(out=osq, in_=oraw, func=ACT.Square), reads=["oraw"], writes=["osq"])
                b = bank("pc")
                P.op("pe", lambda e: e.matmul(ps[b][:, :], lhsT=ones_bf, rhs=osq, start=True, stop=True),
                     reads=["osq", "ones_bf"], writes=[("ps", b)], fd=512)
                rsqrt_from_psum(b, orst, "orst", 1.0 / 128)
                P.op("dve", lambda e: e.scalar_tensor_tensor(out=oraw, in0=oraw, scalar=dnw[:, 0:1], in1=orst,
                                                             op0=ALU.mult, op1=ALU.mult),
                     reads=["oraw", "orst", "dnw"], writes=["oraw"])
                P.op("dve", lambda e: e.scalar_tensor_tensor(out=oa[oi], in0=oraw, scalar=0.5, in1=zsT[b3],
                                                             op0=ALU.mult, op1=ALU.mult),
                     reads=["oraw", ("zsT", b3)], writes=[("oa", oi)])
                P.dma(lambda q: q.dma_start(out=oa_scr[h, :, t0:t0 + 512], in_=oa[oi]),
                      reads=[("oa", oi)], writes=[("oa_scr", h, tt8)])
            A(t_epi)
            return tasks

        sg = [alloc([512], BF16) for _ in range(2)]
        sgf = [alloc([512], F32) for _ in range(2)]
        c0_tasks = []
        c0_hold = {}
        for c16 in range(16):
            for tt8 in range(8):
                def t_c0(c16=c16, tt8=tt8):
                    if tt8 == 0:
                        off = (O_GA + c16 * 128) if c16 < 8 else (O_GB + (c16 - 8) * 128)
                        c0_hold["w"] = load_w(w_in, off)
                    wt, wk = c0_hold["w"]
                    i = (c16 * 8 + tt8) % 2
                    b = proj_tile(wt, wk, 128, tt8, grp="pc")
                    P.op("act", lambda e: e.activation(out=sgf[i], in_=ps[b][:, :], func=ACT.Tanh, scale=0.5),
                         writes=[("sgf", i), ("ps", b)], tset="T")
                    P.op("dve", lambda e: e.tensor_scalar(out=sg[i], in0=sgf[i], scalar1=0.5, scalar2=0.5, op0=ALU.mult, op1=ALU.add),
                         reads=[("sgf", i)], writes=[("sg", i)])
                    P.dma(lambda q: q.dma_start(out=sg_scr[c16, :, tt8 * 512:(tt8 + 1) * 512], in_=sg[i]),
                          reads=[("sg", i)], writes=[("sg_scr", tt8)])
                c0_tasks.append(t_c0)

        load_head_w(0)
        NSTEP = 64

        def merge(ta, tb):
            out = []
            na, nb_ = len(ta), len(tb)
            jj = 0
            for ii, t in enumerate(ta):
                out.append(t)
                tgt = (nb_ * (ii + 1) + na - 1) // max(na, 1)
                while jj < min(tgt, nb_):
                    out.append(tb[jj])
                    jj += 1
            out.extend(tb[jj:])
            return out

        s1 = {}
        s2 = {}
        for st in range(NSTEP):
            s1[st], s2[st] = None, None

        def get12(st):
            if st >= NSTEP:
                return [], []
            return stage12_tasks(st)

        a1, a2 = get12(0)
        for t in a1:
            t()
        b1_, b2_ = get12(1)
        for t in merge(a2, b1_):
            t()
        pend2 = b2_
        for st in range(NSTEP):
            t3 = stage3_tasks(st)
            n1, n2 = get12(st + 2)
            filler = (pend2 + n1) if S2_FIRST else (merge(pend2, n1) if len(pend2) >= len(n1) else merge(n1, pend2))
            pend2 = n2
            for t in (t3 + filler if CHAIN_FIRST else merge(t3, filler)):
                t()
            for _ in range(2):
                if c0_tasks:
                    c0_tasks.pop(0)()
        while c0_tasks:
            c0_tasks.pop(0)()
        cursor[0] = ph

    if stop_after not in ("a0", "b"):
        phase_a()
        P.fence()
        dump("oa_scr", oa_scr, [])

    def phase_c0():
        ph = cursor[0]
        sg = [alloc([512], BF16) for _ in range(4)]
        sgf = [alloc([512], F32) for _ in range(4)]
        cnt = 0
        for c16 in range(16):
            off = (O_GA + c16 * 128) if c16 < 8 else (O_GB + (c16 - 8) * 128)
            wt, wk = load_w(w_in, off)
            for tt8 in range(8):
                b = proj_tile(wt, wk, 128, tt8, grp="all")
                i = cnt % 4
                cnt += 1
                P.op("act", lambda e, b=b, i=i: e.activation(out=sgf[i], in_=ps[b][:, :], func=ACT.Tanh, scale=0.5),
                     writes=[("sgf", i), ("ps", b)], tset="T")
                P.op("dve", lambda e, i=i: e.tensor_scalar(out=sg[i], in0=sgf[i], scalar1=0.5, scalar2=0.5, op0=ALU.mult, op1=ALU.add),
                     reads=[("sgf", i)], writes=[("sg", i)])
                P.dma(lambda q, i=i, c16=c16, tt8=tt8: q.dma_start(out=sg_scr[c16, :, tt8 * 512:(tt8 + 1) * 512], in_=sg[i]),
                      reads=[("sg", i)], writes=[("sg_scr", tt8)])
        cursor[0] = ph

    def phase_c1():
      with P.fds(pe=512):
        ph = cursor[0]
        wodn = alloc([8, 1024], BF16)
        wodil = alloc([4, 1024], BF16)
        wout = alloc([8, 1024], BF16)
        fnw_b = alloc([D], F32)
        for (dst, src, kc, key) in ((wodn, w_odn, 8, "wodn"), (wodil, w_odil, 4, "wodil"), (wout, w_out, 8, "wout")):
            for half in range(2):
                P.dma(lambda q, dst=dst, src=src, kc=kc, half=half: q.dma_start(
                    out=dst[:, 0:kc, half * 512:(half + 1) * 512],
                    in_=src[:, half * 512:(half + 1) * 512].rearrange("(k p) c -> p k c", p=128)),
                    writes=[(key, half)], q="pool")
        P.dma(lambda q: q.dma_start(out=fnw_b, in_=fnw_d.partition_broadcast(128)), writes=["fnw_b"])
        oat = [alloc([8, 512], BF16) for _ in range(2)]
        obt = [alloc([4, 512], BF16) for _ in range(2)]
        sgt = [alloc([16, 512], BF16) for _ in range(2)]
        mT = alloc([8, 512], BF16)
        m1 = [alloc([512], F32) for _ in range(2)]
        m2 = [alloc([512], F32) for _ in range(2)]
        xr = [alloc([D], F32) for _ in range(2)]
        xo = [alloc([D], F32) for _ in range(2)]
        yo = [alloc([D], F32) for _ in range(2)]
        ss = [alloc([1], F32) for _ in range(2)]
        rs = [alloc([1], F32) for _ in range(2)]
        junk2 = alloc([D], BF16)
        cnt = 0
        for tt8 in range(8):
            i = tt8 % 2
            sl = slice(tt8 * 512, (tt8 + 1) * 512)
            P.dma(lambda q, i=i, sl=sl: q.dma_start(out=oat[i], in_=oa_scr[:, :, sl].rearrange("h p t -> p h t")),
                  reads=[("oa_scr", hh, tt8) for hh in range(8)], writes=[("oat", i)])
            P.dma(lambda q, i=i, sl=sl: q.dma_start(out=obt[i], in_=ob_scr[:, :, sl].rearrange("h p t -> p h t")),
                  reads=[("ob_scr", hh, tt8) for hh in range(4)], writes=[("obt", i)])
            P.dma(lambda q, i=i, sl=sl: q.dma_start(out=sgt[i], in_=sg_scr[:, :, sl].rearrange("h p t -> p h t")),
                  reads=[("sg_scr", tt8)], writes=[("sgt", i)])
            for c in range(8):
                cs = slice(c * 128, (c + 1) * 128)
                ba = bank("all")
                for k in range(8):
                    P.op("pe", lambda e, b=ba, k=k, cs=cs, i=i: e.matmul(ps[b][:, :], lhsT=wodn[:, k, cs], rhs=oat[i][:, k, :],
                                                                         start=(k == 0), stop=(k == 7)),
                         reads=[("wodn", c // 4), ("oat", i)], writes=[("ps", ba)])
                bb = bank("all")
                for k in range(4):
                    P.op("pe", lambda e, b=bb, k=k, cs=cs, i=i: e.matmul(ps[b][:, :], lhsT=wodil[:, k, cs], rhs=obt[i][:, k, :],
                                                                         start=(k == 0), stop=(k == 3)),
                         reads=[("wodil", c // 4), ("obt", i)], writes=[("ps", bb)])
                mi = cnt % 2
                cnt += 1
                P.op("dve", lambda e, b=ba, mi=mi, i=i, c=c: e.tensor_tensor(out=m1[mi], in0=ps[b][:, :], in1=sgt[i][:, c, :], op=ALU.mult),
                     reads=[("sgt", i)], writes=[("m1", mi), ("ps", ba)])
                P.op("dve", lambda e, b=bb, mi=mi, i=i, c=c: e.tensor_tensor(out=m2[mi], in0=ps[b][:, :], in1=sgt[i][:, 8 + c, :], op=ALU.mult),
                     reads=[("sgt", i)], writes=[("m2", mi), ("ps", bb)])
                P.op("pool", lambda e, mi=mi, c=c: e.tensor_tensor(out=mT[:, c, :], in0=m1[mi], in1=m2[mi], op=ALU.add),
                     reads=[("m1", mi), ("m2", mi)], writes=[("mT", c)])
            for sub in range(4):
                tok0 = tt8 * 512 + sub * 128
                xi = sub % 2
                P.dma(lambda q, xi=xi, tok0=tok0: q.dma_start(out=xr[xi], in_=x_d[tok0:tok0 + 128, :]), writes=[("xr", xi)])
                for half in range(2):
                    b = bank("all")
                    hs = slice(half * 512, (half + 1) * 512)
                    for c in range(8):
                        P.op("pe", lambda e, b=b, c=c, sub=sub, hs=hs: e.matmul(
                            ps[b][:, :], lhsT=mT[:, c, sub * 128:(sub + 1) * 128], rhs=wout[:, c, hs], start=(c == 0), stop=(c == 7)),
                            reads=[("mT", c), ("wout", half)], writes=[("ps", b)])
                    P.op("dve", lambda e, b=b, xi=xi, hs=hs: e.tensor_tensor(out=xo[xi][:, hs], in0=ps[b][:, :], in1=xr[xi][:, hs], op=ALU.add),
                         reads=[("xr", xi)], writes=[("xo", xi, half), ("ps", b)])
                P.op("pool", lambda e, xi=xi: e.memset(ss[xi], 0.0), writes=[("ss", xi)])
                P.op("act", lambda e, xi=xi: e.activation(out=junk2, in_=xo[xi], func=ACT.Square, accum_out=ss[xi]),
                     reads=[("xo", xi, 0), ("xo", xi, 1), ("ss", xi)], writes=["junk2", ("ss", xi)])
                P.op("dve", lambda e, xi=xi: e.tensor_scalar(out=ss[xi], in0=ss[xi], scalar1=1.0 / D, scalar2=EPS, op0=ALU.mult, op1=ALU.add),
                     reads=[("ss", xi)], writes=[("ss", xi)])
                P.op("act", lambda e, xi=xi: e.activation(out=ss[xi], in_=ss[xi], func=ACT.Ln), reads=[("ss", xi)], writes=[("ss", xi)], tset="L", fd=1)
                P.op("act", lambda e, xi=xi: e.activation(out=rs[xi], in_=ss[xi], func=ACT.Exp, scale=-0.5), reads=[("ss", xi)], writes=[("rs", xi)], fd=1)
                P.op("dve", lambda e, xi=xi: e.scalar_tensor_tensor(out=yo[xi], in0=xo[xi], scalar=rs[xi], in1=fnw_b,
                                                                    op0=ALU.mult, op1=ALU.mult),
                     reads=[("xo", xi, 0), ("xo", xi, 1), ("rs", xi), "fnw_b"], writes=[("yo", xi)])
                P.dma(lambda q, xi=xi, tok0=tok0: q.dma_start(out=out_d[tok0:tok0 + 128, :], in_=yo[xi]),
                      reads=[("yo", xi)], writes=[("out", tok0)])
        cursor[0] = ph

    if stop_after is None:
        dump("sg_scr", sg_scr, [])
        cursor[0] = 0
        phase_c1()

    if RESCHEDULE:
        P.sim_time = P.reschedule()
    P.emit(final_wait_ops=[o for o in P.dma_last if o is not None])
    return nc


def _consts():
    c = np.zeros((128, 128 + 12 * 256 + 256), np.float32)
    c[:, 0:128] = np.eye(128, dtype=np.float32)
    slopes = (2.0 ** (-8.0 * np.arange(1, 13, dtype=np.float32) / 12)).reshape(3, 4)
    jk = np.arange(128)[:, None]
    iq = np.arange(128)[None, :]
    for gi in range(3):
        for h in range(4):
            s = slopes[gi, h] * DIL[gi]
            d0 = (iq - jk).astype(np.float32)
            b0 = np.where(iq >= jk, -s * d0, NEG)
            d1 = (128 + iq - jk).astype(np.float32)
            b1 = np.where(iq <= jk, -s * d1, NEG)
            g = gi * 4 + h
            c[:, 128 + g * 256:128 + g * 256 + 128] = b0
            c[:, 128 + g * 256 + 128:128 + g * 256 + 256] = b1
    MK = 128 + 3072
    j = np.arange(64)[:, None]
    i = np.arange(64)[None, :]
    c[0:64, MK:MK + 64] = np.where(i >= j, 0.0, NEG)
    c[0:64, MK + 64:MK + 128] = np.where(i > j, 0.0, NEG)
    c[0:64, MK + 128:MK + 192] = (j <= i).astype(np.float32)
    c[0:64, MK + 192:MK + 256] = 1.0
    return c


_NC_CACHE = {}


def _host_inputs(x, norm_w, w_in, conv_w, a_log, dt_bias, dn_norm_w, w_o_dn, w_o_dil, w_out, final_norm_w):
    f = lambda a: np.ascontiguousarray(np.asarray(a, dtype=np.float32))
    cw = f(conv_w)[0].reshape(4, 24, 128).transpose(2, 1, 0).reshape(128, 96)
    shared = {
        "w_in": f(w_in)[0], "w_o_dn": f(w_o_dn)[0], "w_o_dil": f(w_o_dil)[0], "w_out": f(w_out)[0],
        "norm_w": f(norm_w).reshape(1, D), "final_norm_w": f(final_norm_w).reshape(1, D),
        "conv_w_l": np.ascontiguousarray(cw), "a_log": f(a_log).reshape(1, 8), "dt_bias": f(dt_bias).reshape(1, 8),
        "dn_norm_w_l": f(dn_norm_w).reshape(128, 1), "consts": _consts(),
    }
    xs = f(x)
    return [dict(shared, x=xs[b]) for b in range(xs.shape[0])]


def kernel(x, norm_w, w_in, conv_w, a_log, dt_bias, dn_norm_w, w_o_dn, w_o_dil, w_out, final_norm_w):
    in_maps = _host_inputs(x, norm_w, w_in, conv_w, a_log, dt_bias, dn_norm_w, w_o_dn, w_o_dil, w_out, final_norm_w)
    if "nc" not in _NC_CACHE:
        _NC_CACHE["nc"] = build_nc()
    res = run_bass_kernel_spmd(_NC_CACHE["nc"], in_maps, core_ids=list(range(len(in_maps))))
    return np.stack([np.asarray(r["out"], dtype=np.float32).reshape(T, D) for r in res.results], axis=0)
```

```python
import contextlib
import numpy as np
import concourse.bass as bass
import concourse.mybir as mybir
from concourse.bass_utils import run_bass_kernel_spmd

ACT = mybir.ActivationFunctionType
ALU = mybir.AluOpType
F32 = mybir.dt.float32
BF16 = mybir.dt.bfloat16

T = 4096
D = 1024
NEG = -30000.0
PIPELINE_A = True
RESCHEDULE = True
SCHED_IDENTITY = True
XLAT = 0.0
CPW = 0.0
CHAIN_FIRST = False
S2_FIRST = False
EPS = 1e-6
O_QA, O_KA, O_VA, O_ZA, O_BA, O_QB, O_KB, O_VB, O_ZB, O_GA, O_GB = (
    0, 1024, 2048, 3072, 4096, 4112, 5648, 7184, 8720, 9232, 10256)
DIL = (1, 4, 16)


class _Op:
    __slots__ = ("eng", "fn", "deps", "signal", "ticket", "is_dma", "dsem", "dval", "odeps", "cost", "idx", "seg", "war", "pos", "tset")

    def __init__(self, eng, fn, is_dma=False):
        self.eng = eng
        self.fn = fn
        self.odeps = []
        self.cost = 0.0
        self.idx = 0
        self.seg = 0
        self.war = []
        self.pos = 0
        self.tset = None
        self.deps = []
        self.signal = False
        self.ticket = None
        self.is_dma = is_dma
        self.dsem = None
        self.dval = None


class _Res:
    __slots__ = ("w", "r", "rd")

    def __init__(self):
        self.w = None
        self.r = []
        self.rd = []


class Prog:
    ENGS = ("pe", "act", "dve", "pool", "sp")

    def __init__(self, nc, n_dma_sems=32):
        self.nc = nc
        self.streams = {e: [] for e in self.ENGS}
        self.res = {}
        self.n_dma_sems = n_dma_sems
        self.dma_cnt = [0] * n_dma_sems
        self.dma_last = [None] * n_dma_sems
        self.dma_rr = 0
        self.dma_rr_q = {}
        self.fence_ops = []
        self.fence_dma = []
        self.seg = 0
        self.all_ops = []
        self.fd_default = {}

    @contextlib.contextmanager
    def fds(self, **kw):
        old = dict(self.fd_default)
        self.fd_default.update(kw)
        try:
            yield
        finally:
            self.fd_default = old

    def fence(self):
        self.fence_dma.append([d for d in self.dma_last if d is not None])
        self.seg += 1

    def _r(self, k):
        r = self.res.get(k)
        if r is None:
            r = self.res[k] = _Res()
        return r

    def op(self, eng, fn, reads=(), writes=(), is_dma=False, fd=None, tset=None):
        o = _Op(eng, fn, is_dma)
        o.tset = tset
        fd = fd or self.fd_default.get(eng)
        if is_dma:
            o.cost = 0.15
        elif eng == "pe":
            o.cost = max(64, fd or 64) / 2400.0 + 0.004
        elif eng == "act":
            o.cost = (224 + (fd or 512)) / 1200.0
        elif eng == "dve":
            o.cost = (110 + (fd or 512)) / 960.0
        else:
            o.cost = 0.2 + (fd or 512) / 1000.0
        o.idx = len(self.all_ops)
        o.seg = self.seg
        self.all_ops.append(o)
        deps = []
        for k in reads:
            r = self._r(k)
            if r.w is not None:
                deps.append((r.w, "raw"))
        for k in writes:
            r = self._r(k)
            if r.w is not None:
                deps.append((r.w, "waw"))
            for rd in r.r:
                if rd is not o:
                    o.war.append(rd)
            for rd in r.rd:
                if rd is not o:
                    o.war.append(rd)
        if is_dma:
            lo, hi = (0, self.n_dma_sems - 8) if eng != "pool" else (self.n_dma_sems - 8, self.n_dma_sems)
            i = lo + self.dma_rr_q.get(eng, 0) % (hi - lo)
            self.dma_rr_q[eng] = self.dma_rr_q.get(eng, 0) + 1
            o.dsem = i
            self.dma_cnt[i] += 1
            o.dval = 16 * self.dma_cnt[i]
            if self.dma_last[i] is not None:
                deps.append((self.dma_last[i], "raw"))
            self.dma_last[i] = o
        seen = set()
        oseen = set()
        for d, kind in deps:
            if d is not o and id(d) not in oseen:
                oseen.add(id(d))
                o.odeps.append(d)
        for d in o.war:
            if id(d) not in oseen:
                oseen.add(id(d))
                o.odeps.append(d)
        for d, kind in deps:
            if d is o or id(d) in seen:
                continue
            if not d.is_dma and d.eng == eng and not is_dma:
                if eng == "pe":
                    continue
                if kind != "raw":
                    continue
            seen.add(id(d))
            d.signal = True
            o.deps.append(d)
        for k in reads:
            r = self._r(k)
            if is_dma:
                r.rd.append(o)
            else:
                r.r.append(o)
        for k in writes:
            r = self._r(k)
            r.w = o
            r.r = []
            r.rd = []
        self.streams[eng].append(o)
        return o

    def dma(self, fn, reads=(), writes=(), q="sp"):
        return self.op(q, fn, reads, writes, is_dma=True)

    def reschedule(self, dma_latency=3.0):
        import heapq
        ops = self.all_ops
        n = len(ops)
        succ = [[] for _ in range(n)]
        indeg = [0] * n
        for o in ops:
            for d in o.odeps:
                if d.seg == o.seg:
                    succ[d.idx].append(o.idx)
                    indeg[o.idx] += 1
        prio = [0.0] * n
        for i in range(n - 1, -1, -1):
            o = ops[i]
            m = 0.0
            for j in succ[i]:
                if prio[j] > m:
                    m = prio[j]
            prio[i] = m + (dma_latency if o.is_dma else o.cost)
        if SCHED_IDENTITY:
            prio = [float(n - i) + CPW * prio[i] for i in range(n)]
        new_streams = {e: [] for e in self.ENGS}
        now = 0.0
        cur_set = [None]
        free_at = {e: 0.0 for e in self.ENGS}
        nseg = self.seg + 1
        byseg = [[] for _ in range(nseg)]
        for o in ops:
            byseg[o.seg].append(o.idx)
        for sg in range(nseg):
            idxs = byseg[sg]
            ready = {e: [] for e in self.ENGS}
            for i in idxs:
                if indeg[i] == 0:
                    heapq.heappush(ready[ops[i].eng], (-prio[i], i))
            events = []
            done = 0
            tot = len(idxs)
            while done < tot:
                started = False
                for e in self.ENGS:
                    if free_at[e] <= now and ready[e]:
                        _, i = heapq.heappop(ready[e])
                        o = ops[i]
                        extra = 0.0
                        if e == "act":
                            if o.tset is not None and o.tset != cur_set[0]:
                                held = [(-prio[i], i)]
                                found = None
                                for _ in range(6):
                                    if not ready[e]:
                                        break
                                    c = heapq.heappop(ready[e])
                                    oc = ops[c[1]]
                                    if (oc.tset is None or oc.tset == cur_set[0]) and prio[c[1]] > prio[i] - 6.0:
                                        found = c
                                        break
                                    held.append(c)
                                for c in held:
                                    if found is not None or c[1] != i:
                                        heapq.heappush(ready[e], c)
                                if found is not None:
                                    i = found[1]
                                    o = ops[i]
                                else:
                                    cur_set[0] = o.tset
                                    extra = 1.3
                        new_streams[e].append(o)
                        free_at[e] = now + o.cost + extra
                        heapq.heappush(events, (now + (dma_latency if o.is_dma else o.cost + extra + XLAT), i))
                        started = True
                if started:
                    continue
                cand = []
                if events:
                    cand.append(events[0][0])
                for e in self.ENGS:
                    if ready[e] and free_at[e] > now:
                        cand.append(free_at[e])
                now = min(cand)
                while events and events[0][0] <= now:
                    _, i = heapq.heappop(events)
                    done += 1
                    for j in succ[i]:
                        indeg[j] -= 1
                        if indeg[j] == 0:
                            heapq.heappush(ready[ops[j].eng], (-prio[j], j))
        for e in self.ENGS:
            assert len(new_streams[e]) == len(self.streams[e])
        self.streams = new_streams
        return now

    def apply_fences(self):
        last = {}
        pos = {e: 0 for e in self.ENGS}
        for sg in range(1, self.seg + 1):
            for e in self.ENGS:
                st = self.streams[e]
                while pos[e] < len(st) and st[pos[e]].seg < sg:
                    if not st[pos[e]].is_dma:
                        last[e] = st[pos[e]]
                    pos[e] += 1
            for e in self.ENGS:
                st = self.streams[e]
                if pos[e] < len(st) and st[pos[e]].seg == sg:
                    o = st[pos[e]]
                    extra = [d for d in last.values()] + list(self.fence_dma[sg - 1])
                    have = set(id(d) for d in o.deps)
                    for d in extra:
                        if d is o or id(d) in have:
                            continue
                        if not d.is_dma and d.eng == e and e == "pe":
                            continue
                        d.signal = True
                        o.deps.append(d)

    def emit(self, final_wait_ops=()):
        nc = self.nc
        for e in self.ENGS:
            for p_, o in enumerate(self.streams[e]):
                o.pos = p_
        for o in self.all_ops:
            if not o.war:
                continue
            best = {}
            have = set(id(d) for d in o.deps)
            for d in o.war:
                if d.is_dma:
                    if id(d) not in have:
                        have.add(id(d))
                        d.signal = True
                        o.deps.append(d)
                    continue
                b = best.get(d.eng)
                if b is None or d.pos > b.pos:
                    best[d.eng] = d
            for e, d in best.items():
                if e == o.eng and not o.is_dma:
                    continue
                if id(d) in have:
                    continue
                d.signal = True
                o.deps.append(d)
        self.apply_fences()
        for o in self.all_ops:
            o.signal = False
        for e in self.ENGS:
            waited = {}
            for o in self.streams[e]:
                for d in o.deps:
                    if d.is_dma:
                        continue
                    if waited.get(d.eng, -1) >= d.pos:
                        continue
                    waited[d.eng] = d.pos
                    d.signal = True
        for e in self.ENGS:
            c = 0
            for o in self.streams[e]:
                if o.is_dma:
                    continue
                if o.signal:
                    c += 1
                o.ticket = c
        with contextlib.ExitStack() as es:
            esem = {e: es.enter_context(nc.semaphore("s_" + e)) for e in self.ENGS}
            dsem = [es.enter_context(nc.semaphore("d_%d" % i)) for i in range(self.n_dma_sems)]
            block = es.enter_context(nc.Block())

            def run(e, engobj):
                waited = {}

                def wait_for(d):
                    if d.is_dma:
                        key, sem, val = ("d", d.dsem), dsem[d.dsem], d.dval
                    else:
                        key, sem, val = ("e", d.eng), esem[d.eng], d.ticket
                    if waited.get(key, 0) >= val:
                        return
                    waited[key] = val
                    engobj.wait_ge(sem, val)

                for o in self.streams[e]:
                    for d in o.deps:
                        wait_for(d)
                    ins = o.fn(engobj)
                    if o.is_dma:
                        ins.then_inc(dsem[o.dsem], 16)
                    elif o.signal:
                        ins.then_inc(esem[e], 1)
                if e == "sp":
                    for d in final_wait_ops:
                        wait_for(d)

            @block.tensor
            def _(eng):
                run("pe", eng)

            @block.scalar
            def _(eng):
                run("act", eng)

            @block.vector
            def _(eng):
                run("dve", eng)

            @block.gpsimd
            def _(eng):
                run("pool", eng)

            @block.sync
            def _(eng):
                run("sp", eng)


def build_nc(dbg=None, stop_after=None):
    dbg = dbg or {}
    nc = bass.Bass("TRN2", target_bir_lowering=False)
    dt = nc.dram_tensor
    x_d = dt("x", [T, D], F32, kind="ExternalInput").ap()
    w_in = dt("w_in", [D, 11280], F32, kind="ExternalInput").ap()
    w_odn = dt("w_o_dn", [1024, 1024], F32, kind="ExternalInput").ap()
    w_odil = dt("w_o_dil", [512, 1024], F32, kind="ExternalInput").ap()
    w_out = dt("w_out", [1024, 1024], F32, kind="ExternalInput").ap()
    normw_d = dt("norm_w", [1, D], F32, kind="ExternalInput").ap()
    fnw_d = dt("final_norm_w", [1, D], F32, kind="ExternalInput").ap()
    cw_d = dt("conv_w_l", [128, 96], F32, kind="ExternalInput").ap()
    alog_d = dt("a_log", [1, 8], F32, kind="ExternalInput").ap()
    dtb_d = dt("dt_bias", [1, 8], F32, kind="ExternalInput").ap()
    dnw_d = dt("dn_norm_w_l", [128, 1], F32, kind="ExternalInput").ap()
    cst_d = dt("consts", [128, 128 + 12 * 256 + 64 * 4], F32, kind="ExternalInput").ap()
    out_d = dt("out", [T, D], F32, kind="ExternalOutput").ap()
    ob_scr = dt("ob_scr", [4, 128, T], BF16).ap()
    oa_scr = dt("oa_scr", [8, 128, T], BF16).ap()
    sg_scr = dt("sg_scr", [16, 128, T], BF16).ap()
    gc_scr = dt("gc_scr", [8, T], F32).ap()
    gcl_scr = dt("gcl_scr", [8, T], F32).ap()
    dbg_out = {}
    for name, (shape, dtype) in dbg.items():
        dbg_out[name] = dt("dbg_" + name, list(shape), dtype, kind="ExternalOutput").ap()

    P = Prog(nc)
    ARENA = 212000
    arena = nc.alloc_sbuf_tensor("arena", [128, ARENA // 2], BF16)
    cursor = [0]

    def alloc(free_shape, dtype, parts=128):
        n = 1
        for s in free_shape:
            n *= s
        esz = 4 if dtype == F32 else 2
        nbytes = (n * esz + 63) // 64 * 64
        off = cursor[0]
        cursor[0] += nbytes
        assert cursor[0] <= ARENA, ("SBUF overflow", cursor[0])
        ap = arena[0:parts, off // 2: off // 2 + n * esz // 2]
        if dtype == F32:
            ap = ap.bitcast(F32)
        if len(free_shape) == 2:
            ap = ap.rearrange("p (a b) -> p a b", b=free_shape[1])
        elif len(free_shape) == 3:
            ap = ap.rearrange("p (a b c) -> p a b c", b=free_shape[1], c=free_shape[2])
        return ap

    ps = [nc.alloc_psum_tensor("ps%d" % i, [128, 512], F32) for i in range(8)]
    psb = [p[:].bitcast(BF16) for p in ps]
    bank_rr = {}

    def bank(group):
        lst = {"proj": (0, 1), "misc": (2,), "pc": (0, 1, 2), "pre": (3, 4), "q1": (5,), "q4": (6,), "po": (7,),
               "all": tuple(range(8)), "s": (3, 4), "a": (5, 7), "b": (6, 2)}[group]
        i = bank_rr.get(group, 0)
        bank_rr[group] = i + 1
        return lst[i % len(lst)]

    def dump(name, src_ap, reads):
        if name in dbg_out:
            P.dma(lambda q, a=src_ap, o=dbg_out[name]: q.dma_start(out=o, in_=a), reads=reads, writes=[("dbg", name)])

    hT = alloc([8, T], BF16)
    cst = alloc([128 + 12 * 256 + 256], F32)
    identf = cst[:, 0:128]
    alibi = cst[:, 128:128 + 3072].rearrange("p (g w) -> p g w", w=256)
    MK = 128 + 3072
    mask_incl = cst[0:64, MK:MK + 64]
    mask_strict = cst[0:64, MK + 64:MK + 128]
    triu_f = cst[0:64, MK + 128:MK + 192]
    ones64f = cst[0:64, MK + 192:MK + 256]
    ident_bf = alloc([128], BF16)
    ones_bf = alloc([128], BF16)
    ones_f = alloc([128], F32)
    cw = alloc([96], F32)
    dnw = alloc([1], F32)
    WB_N = 3
    wb = [alloc([8, 128], BF16) for _ in range(WB_N)]
    wb_rr = [0]
    beta_t = alloc([64, 8], F32, 64)
    gc_t = alloc([64, 8], F32, 64)
    negeg_t = alloc([64, 8], F32, 64)
    wdec_t = alloc([64, 8], F32, 64)
    dl_t = alloc([64, 8], F32)
    persist_end = cursor[0]

    P.dma(lambda q: q.dma_start(out=cst, in_=cst_d), writes=["cst"])
    P.dma(lambda q: q.dma_start(out=cw, in_=cw_d), writes=["cw"])
    P.dma(lambda q: q.dma_start(out=dnw, in_=dnw_d), writes=["dnw"])
    P.op("dve", lambda e: e.tensor_copy(out=ident_bf, in_=identf), reads=["cst"], writes=["ident_bf"])
    P.op("pool", lambda e: e.memset(ones_bf, 1.0), writes=["ones_bf"])
    P.op("pool", lambda e: e.memset(ones_f, 1.0), writes=["ones_f"])

    def load_w(src, c0, ncols=128, kchunks=8, dst=None, key=None):
        if dst is None:
            i = wb_rr[0] % WB_N
            wb_rr[0] += 1
            dst, key = wb[i], ("wb", i)
        P.dma(lambda q, d=dst, s=src, c0=c0, n=ncols, kc=kchunks: q.dma_start(
            out=d[:, 0:kc, 0:n], in_=s[:, c0:c0 + n].rearrange("(k p) c -> p k c", p=128)),
            writes=[key], q="pool")
        return dst, key

    def hkeys(t0, t1):
        return [("hT", i) for i in range(t0 // 128, (t1 + 127) // 128)]

    def proj_tile(wt, wkey, ncols, tt8, grp="proj"):
        b = bank(grp)
        for k in range(8):
            P.op("pe", lambda e, b=b, k=k, wt=wt, n=ncols, tt8=tt8: e.matmul(
                ps[b][0:n, :], lhsT=wt[:, k, 0:n], rhs=hT[:, k, tt8 * 512:(tt8 + 1) * 512],
                start=(k == 0), stop=(k == 7)),
                reads=[wkey] + hkeys(tt8 * 512, tt8 * 512 + 512), writes=[("ps", b)], fd=512)
        return b

    ph = cursor[0]
    normw_b = alloc([D], F32)
    xs = [alloc([D], F32) for _ in range(4)]
    junk = [alloc([D], BF16) for _ in range(2)]
    xb = [alloc([D], BF16) for _ in range(4)]
    ss0 = [alloc([1], F32) for _ in range(4)]
    rs0 = [alloc([1], F32) for _ in range(4)]
    P.dma(lambda q: q.dma_start(out=normw_b, in_=normw_d.partition_broadcast(128)), writes=["normw_b"])
    for tt in range(32):
        i = tt % 4
        P.dma(lambda q, i=i, tt=tt: q.dma_start(out=xs[i], in_=x_d[tt * 128:(tt + 1) * 128, :]), writes=[("xs", i)])
        P.op("pool", lambda e, i=i: e.memset(ss0[i], 0.0), writes=[("ss0", i)])
        P.op("act", lambda e, i=i: e.activation(out=junk[i % 2], in_=xs[i], func=ACT.Square, accum_out=ss0[i]),
             reads=[("xs", i), ("ss0", i)], writes=[("junk", i % 2), ("ss0", i)])
        P.op("dve", lambda e, i=i: e.tensor_scalar(out=ss0[i], in0=ss0[i], scalar1=1.0 / D, scalar2=EPS, op0=ALU.mult, op1=ALU.add),
             reads=[("ss0", i)], writes=[("ss0", i)])
        P.op("act", lambda e, i=i: e.activation(out=ss0[i], in_=ss0[i], func=ACT.Ln), reads=[("ss0", i)], writes=[("ss0", i)], tset="L", fd=1)
        P.op("act", lambda e, i=i: e.activation(out=rs0[i], in_=ss0[i], func=ACT.Exp, scale=-0.5), reads=[("ss0", i)], writes=[("rs0", i)], fd=1)
        P.op("dve", lambda e, i=i: e.scalar_tensor_tensor(out=xb[i], in0=xs[i], scalar=rs0[i], in1=normw_b,
                                                          op0=ALU.mult, op1=ALU.mult),
             reads=[("xs", i), ("rs0", i), "normw_b"], writes=[("xb", i)])
        b = bank("all")
        for k in range(8):
            P.op("pe", lambda e, b=b, k=k, i=i: e.transpose(out=psb[b][:, k * 128:(k + 1) * 128],
                                                           in_=xb[i][:, k * 128:(k + 1) * 128], identity=ident_bf),
                 reads=[("xb", i), "ident_bf"], writes=[("ps", b)])
        eng = "act" if tt % 2 == 0 else "dve"
        if eng == "act":
            fn = lambda e, b=b, tt=tt: e.activation(out=hT[:, :, tt * 128:(tt + 1) * 128],
                                                    in_=psb[b].rearrange("p (k t) -> p k t", t=128), func=ACT.Copy)
        else:
            fn = lambda e, b=b, tt=tt: e.tensor_copy(out=hT[:, :, tt * 128:(tt + 1) * 128],
                                                     in_=psb[b].rearrange("p (k t) -> p k t", t=128))
        P.op(eng, fn, writes=[("hT", tt), ("ps", b)])
    dump("hT", hT, hkeys(0, T))
    ph0_end = cursor[0]

    def phase_a0():
        ph = cursor[0]
        w16, w16k = load_w(w_in, O_BA, 16)
        alog_b = alloc([8], F32, 64)
        dtb_b = alloc([8], F32, 64)
        P.dma(lambda q: q.dma_start(out=alog_b, in_=alog_d.partition_broadcast(64)), writes=["alog_b"])
        P.dma(lambda q: q.dma_start(out=dtb_b, in_=dtb_d.partition_broadcast(64)), writes=["dtb_b"])
        Gsb = alloc([64, 16], F32, 64)
        names = ["xa", "ax", "ee", "ll", "sp", "g", "lb", "gcl", "tmp"]
        A = {n: alloc([64, 8], F32, 64) for n in names}
        glast = alloc([64, 8], F32)
        tb = alloc([8, 64], F32, 64)
        for half in range(2):
            b = bank("all")
            for n in range(32 * half, 32 * half + 32):
                for k in range(8):
                    P.op("pe", lambda e, b=b, n=n, k=k: e.matmul(
                        ps[b][0:64, (n % 32) * 16:(n % 32) * 16 + 16], lhsT=hT[:, k, n * 64:(n + 1) * 64],
                        rhs=w16[:, k, 0:16], start=(k == 0), stop=(k == 7)),
                        reads=[w16k] + hkeys(n * 64, n * 64 + 64), writes=[("ps", b)])
            P.op("act", lambda e, b=b, half=half: e.activation(
                out=Gsb[:, 32 * half:32 * half + 32, :], in_=ps[b][0:64, :].rearrange("p (n c) -> p n c", c=16),
                func=ACT.Copy), writes=["Gsb", ("ps", b)])
        bb = Gsb[:, :, 0:8]
        aa = Gsb[:, :, 8:16]
        bc = lambda v: v.unsqueeze(1).to_broadcast([64, 64, 8])
        P.op("act", lambda e: e.activation(out=beta_t, in_=bb, func=ACT.Sigmoid), reads=["Gsb"], writes=["beta_t"])
        P.op("act", lambda e: e.activation(out=A["lb"], in_=beta_t, func=ACT.Ln), reads=["beta_t"], writes=["lb"])
        P.op("dve", lambda e: e.tensor_tensor(out=A["xa"], in0=aa, in1=bc(dtb_b), op=ALU.add),
             reads=["Gsb", "dtb_b"], writes=["xa"])
        P.op("act", lambda e: e.activation(out=A["ax"], in_=A["xa"], func=ACT.Abs), reads=["xa"], writes=["ax"])
        P.op("act", lambda e: e.activation(out=A["ee"], in_=A["ax"], func=ACT.Exp, scale=-1.0), reads=["ax"], writes=["ee"])
        P.op("act", lambda e: e.activation(out=A["ll"], in_=A["ee"], func=ACT.Ln, bias=1.0), reads=["ee"], writes=["ll"])
        P.op("dve", lambda e: e.scalar_tensor_tensor(out=A["sp"], in0=A["xa"], scalar=0.0, in1=A["ll"],
                                                     op0=ALU.max, op1=ALU.add), reads=["xa", "ll"], writes=["sp"])
        P.op("act", lambda e: e.activation(out=alog_b, in_=alog_b, func=ACT.Exp), reads=["alog_b"], writes=["alog_b"])
        P.op("dve", lambda e: e.scalar_tensor_tensor(out=A["g"], in0=A["sp"], scalar=-1.0, in1=bc(alog_b),
                                                     op0=ALU.mult, op1=ALU.mult), reads=["sp", "alog_b"], writes=["g"])
        gflat = A["g"].rearrange("p n h -> p (n h)")
        b1 = bank("all")
        P.op("pe", lambda e: e.matmul(ps[b1][0:64, :], lhsT=triu_f, rhs=gflat, start=True, stop=True),
             reads=["g", "cst"], writes=[("ps", b1)])
        b2 = bank("all")
        P.op("pe", lambda e: e.matmul(ps[b2][:, :], lhsT=ones_f[0:64, :], rhs=gflat, start=True, stop=True),
             reads=["g", "ones_f"], writes=[("ps", b2)])
        fl = lambda v: v.rearrange("p n h -> p (n h)")
        P.op("act", lambda e: e.activation(out=fl(gc_t), in_=ps[b1][0:64, :], func=ACT.Copy), writes=["gc_t", ("ps", b1)])
        P.op("dve", lambda e: e.tensor_copy(out=fl(glast), in_=ps[b2][:, :]), writes=["glast", ("ps", b2)])
        P.op("act", lambda e: e.activation(out=negeg_t, in_=gc_t, func=ACT.Exp), reads=["gc_t"], writes=["negeg_t"])
        P.op("dve", lambda e: e.tensor_scalar(out=negeg_t, in0=negeg_t, scalar1=-1.0, scalar2=None, op0=ALU.mult),
             reads=["negeg_t"], writes=["negeg_t"])
        P.op("dve", lambda e: e.tensor_tensor(out=A["tmp"], in0=glast[0:64], in1=gc_t, op=ALU.subtract),
             reads=["glast", "gc_t"], writes=["tmp"])
        P.op("act", lambda e: e.activation(out=wdec_t, in_=A["tmp"], func=ACT.Exp), reads=["tmp"], writes=["wdec_t"])
        P.op("act", lambda e: e.activation(out=dl_t, in_=glast, func=ACT.Exp), reads=["glast"], writes=["dl_t"])
        P.op("dve", lambda e: e.tensor_tensor(out=A["gcl"], in0=gc_t, in1=A["lb"], op=ALU.add),
             reads=["gc_t", "lb"], writes=["gcl"])
        for nm, src, scr in (("gc", gc_t, gc_scr), ("gcl", A["gcl"], gcl_scr)):
            b = bank("all")
            for h in range(8):
                P.op("pe", lambda e, b=b, h=h, src=src: e.transpose(out=ps[b][0:64, h * 64:(h + 1) * 64],
                                                                   in_=src[:, :, h], identity=identf[0:64, 0:64]),
                     reads=["gc_t" if nm == "gc" else "gcl", "cst"], writes=[("ps", b)])
            P.op("dve", lambda e, b=b: e.tensor_copy(out=tb, in_=ps[b][0:64, :].rearrange("p (h c) -> p h c", c=64)),
                 writes=["tb", ("ps", b)])
            P.dma(lambda q, scr=scr: q.dma_start(out=scr.rearrange("h (n c) -> n h c", c=64), in_=tb),
                  reads=["tb"], writes=[nm + "_scr"])
        dump("gc_t", gc_t, ["gc_t"])
        dump("beta_t", beta_t, ["beta_t"])
        dump("g_t", A["g"], ["g"])
        cursor[0] = ph

    phase_a0()
    cursor[0] = ph
    P.fence()

    def interleave(ta, tb):
        na, nb_ = len(ta), len(tb)
        j = 0
        for i, t in enumerate(ta):
            t()
            tgt = (nb_ * (i + 1) + na - 1) // max(na, 1)
            while j < min(tgt, nb_):
                tb[j]()
                j += 1
        while j < nb_:
            tb[j]()
            j += 1

    def phase_b():
        ph = cursor[0]
        qTs = [alloc([T], BF16) for _ in range(2)]
        kTs = [alloc([T], BF16) for _ in range(2)]
        vsbs = [alloc([32, 128], BF16) for _ in range(2)]
        vT = alloc([T], BF16)
        acc_n = alloc([T], F32)
        acc_d = alloc([T], F32)
        NS = 3
        s_sb = [alloc([256], F32) for _ in range(NS)]
        p_sb = [alloc([256], BF16) for _ in range(4)]
        zs = [alloc([512], F32) for _ in range(2)]
        rc = [alloc([512], F32) for _ in range(2)]
        ob = [alloc([512], BF16) for _ in range(2)]
        mone = alloc([512], F32)
        P.op("pool", lambda e: e.memset(mone, -1.0), writes=["mone"])
        scale = 128 ** -0.5
        jobs = [(h, gi) for h in range(4) for gi in range(3)]
        cnt = [0]

        def proj_tasks(jn):
            h, gi = jobs[jn]
            bi = jn % 2
            qT, kT, v_sb = qTs[bi], kTs[bi], vsbs[bi]
            d = DIL[gi]
            L = T // d
            nb = L // 128
            M = 512 // d
            tasks = []
            hold = {}

            def t_w():
                hold["q"] = load_w(w_in, O_QB + gi * 512 + h * 128)
                hold["k"] = load_w(w_in, O_KB + gi * 512 + h * 128)
                hold["v"] = load_w(w_in, O_VB + gi * 512 + h * 128)
            tasks.append(t_w)
            q3 = qT.rearrange("p (r m) -> p r m", r=d)
            k3 = kT.rearrange("p (r m) -> p r m", r=d)
            for tt8 in range(8):
                def t_q(tt8=tt8):
                    wq, wqk = hold["q"]
                    b = proj_tile(wq, wqk, 128, tt8)
                    P.op("act", lambda e: e.activation(out=qT[:, tt8 * 512:(tt8 + 1) * 512], in_=ps[b][:, :],
                                                       func=ACT.Copy, scale=scale),
                         writes=[("qT", bi), ("ps", b)])
                tasks.append(t_q)

                def t_k(tt8=tt8):
                    wk, wkk = hold["k"]
                    b = proj_tile(wk, wkk, 128, tt8)
                    P.op("dve", lambda e: e.tensor_copy(out=kT[:, tt8 * 512:(tt8 + 1) * 512], in_=ps[b][:, :]),
                         writes=[("kT", bi), ("ps", b)])
                tasks.append(t_k)

                def t_v(tt8=tt8):
                    wv, wvk = hold["v"]
                    b = proj_tile(wv, wvk, 128, tt8)
                    P.op("act", lambda e: e.activation(out=vT[:, tt8 * 512:(tt8 + 1) * 512], in_=ps[b][:, :], func=ACT.Copy),
                         writes=[("vT", tt8), ("ps", b)])
                tasks.append(t_v)
            for t8 in range(4):
                def t_vt(t8=t8):
                    b = bank("proj")
                    for s in range(8):
                        tid = t8 * 8 + s
                        r, j = tid // nb, tid % nb
                        t0 = 128 * j * d + r
                        P.op("pe", lambda e, s=s, t0=t0: e.transpose(
                            out=psb[b][:, s * 128:(s + 1) * 128], in_=vT[:, t0:t0 + 127 * d + 1:d], identity=ident_bf),
                            reads=[("vT", i) for i in range((128 * j * d) // 512, (128 * (j + 1) * d + 511) // 512)] + ["ident_bf"],
                            writes=[("ps", b)])
                    P.op("dve", lambda e: e.tensor_copy(
                        out=v_sb[:, t8 * 8:(t8 + 1) * 8, :], in_=psb[b][:, :].rearrange("p (s c) -> p s c", c=128)),
                        writes=[("v_sb", bi), ("ps", b)])
                tasks.append(t_vt)
            return tasks

        def core_tasks(jn):
            h, gi = jobs[jn]
            bi = jn % 2
            qT, kT, v_sb = qTs[bi], kTs[bi], vsbs[bi]
            d = DIL[gi]
            L = T // d
            nb = L // 128
            gidx = gi * 4 + h
            tasks = []
            st = {"bn": None, "bd": None}
            pis = {}

            def tok(r, j0, nblk):
                a = (128 * j0) * d + r
                return slice(a, a + (128 * nblk - 1) * d + 1, d)

            def t_qk(r, j):
              with P.fds(pe=256, act=256, dve=256):
                W = 2 if j + 1 < nb else 1
                bs = bank("s")
                P.op("pe", lambda e: e.matmul(ps[bs][:, 0:128 * W], lhsT=kT[:, tok(r, j, 1)], rhs=qT[:, tok(r, j, W)],
                                              start=True, stop=True),
                     reads=[("qT", bi), ("kT", bi)], writes=[("ps", bs)])
                si = cnt[0] % NS
                pi = cnt[0] % 4
                cnt[0] += 1
                pis[(r, j)] = pi
                P.op("dve", lambda e: e.tensor_tensor(out=s_sb[si][:, 0:128 * W], in0=ps[bs][:, 0:128 * W],
                                                      in1=alibi[:, gidx, 0:128 * W], op=ALU.add),
                     reads=["cst"], writes=[("s_sb", si), ("ps", bs)])
                P.op("act", lambda e: e.activation(out=p_sb[pi][:, 0:128 * W], in_=s_sb[si][:, 0:128 * W], func=ACT.Exp),
                     reads=[("s_sb", si)], writes=[("p_sb", pi)])

            def t_pv(r, j):
              with P.fds(pe=128):
                if j % 4 == 0:
                    st["bn"], st["bd"] = bank("a"), bank("b")
                bn, bd = st["bn"], st["bd"]
                pi = pis[(r, j)]
                prev = pis[(r, j - 1)] if j > 0 else None
                col = (j % 4) * 128
                tid = r * nb + j
                for (bk, is_den, lk) in ((bn, False, ("v_sb", bi)), (bd, True, "ones_bf")):
                    first = True
                    if j > 0:
                        lp = ones_bf if is_den else v_sb[:, tid - 1, :]
                        P.op("pe", lambda e, bk=bk, lp=lp: e.matmul(
                            ps[bk][:, col:col + 128], lhsT=lp, rhs=p_sb[prev][:, 128:256], start=True, stop=False),
                            reads=[lk, ("p_sb", prev)], writes=[("ps", bk)])
                        first = False
                    lc = ones_bf if is_den else v_sb[:, tid, :]
                    P.op("pe", lambda e, bk=bk, lc=lc, first=first: e.matmul(
                        ps[bk][:, col:col + 128], lhsT=lc, rhs=p_sb[pi][:, 0:128], start=first, stop=True),
                        reads=[lk, ("p_sb", pi)], writes=[("ps", bk)])
                if j % 4 == 3 or j == nb - 1:
                    n0 = (j // 4) * 4
                    nq = (j - n0 + 1) * 128
                    sl = slice(128 * n0 * d + r, 128 * n0 * d + r + (nq - 1) * d + 1, d)
                    for (bk, acc, key, eng) in ((bn, acc_n, "acc_n", "act"), (bd, acc_d, "acc_d", "dve")):
                        if gi == 0:
                            if eng == "act":
                                P.op("act", lambda e, bk=bk, acc=acc: e.activation(
                                    out=acc[:, sl], in_=ps[bk][:, 0:nq], func=ACT.Copy), writes=[key, ("ps", bk)])
                            else:
                                P.op("dve", lambda e, bk=bk, acc=acc: e.tensor_copy(
                                    out=acc[:, sl], in_=ps[bk][:, 0:nq]), writes=[key, ("ps", bk)])
                        else:
                            P.op("dve", lambda e, bk=bk, acc=acc: e.tensor_tensor(
                                out=acc[:, sl], in0=ps[bk][:, 0:nq], in1=acc[:, sl], op=ALU.add),
                                reads=[key], writes=[key, ("ps", bk)])

            seq = [(r, j) for r in range(d) for j in range(nb)]
            SK = 2
            for idx in range(len(seq) + SK):
                def t_blk(idx=idx):
                    if idx < len(seq):
                        t_qk(*seq[idx])
                    if idx >= SK:
                        t_pv(*seq[idx - SK])
                tasks.append(t_blk)
            if gi == 2:
                hold = {}

                def t_wz():
                    hold["z"] = load_w(w_in, O_ZB + h * 128)
                tasks.append(t_wz)
                for tt8 in range(8):
                    def t_fin(tt8=tt8):
                        wz, wzk = hold["z"]
                        i = tt8 % 2
                        sl = slice(tt8 * 512, (tt8 + 1) * 512)
                        b = proj_tile(wz, wzk, 128, tt8)
                        P.op("act", lambda e: e.activation(out=zs[i], in_=ps[b][:, :], func=ACT.Tanh, scale=0.5),
                             writes=[("zs", i), ("ps", b)], tset="T")
                        P.op("dve", lambda e: e.scalar_tensor_tensor(out=zs[i], in0=zs[i], scalar=1.0, in1=ps[b][:, :],
                                                                     op0=ALU.add, op1=ALU.mult),
                             reads=[("zs", i)], writes=[("zs", i), ("ps", b)])
                        P.op("dve", lambda e: e.reciprocal(out=rc[i], in_=acc_d[:, sl]), reads=["acc_d"], writes=[("rc", i)], fd=1500)
                        P.op("dve", lambda e: e.tensor_tensor(out=rc[i], in0=rc[i], in1=acc_n[:, sl], op=ALU.mult),
                             reads=["acc_n", ("rc", i)], writes=[("rc", i)])
                        P.op("dve", lambda e: e.scalar_tensor_tensor(out=ob[i], in0=rc[i], scalar=0.5, in1=zs[i],
                                                                     op0=ALU.mult, op1=ALU.mult),
                             reads=[("rc", i), ("zs", i)], writes=[("ob", i)])
                        P.dma(lambda q: q.dma_start(out=ob_scr[h, :, sl], in_=ob[i]),
                              reads=[("ob", i)], writes=[("ob_scr", h, tt8)])
                    tasks.append(t_fin)
            return tasks

        for t in proj_tasks(0):
            t()
        for jn in range(len(jobs)):
            interleave(core_tasks(jn), proj_tasks(jn + 1) if jn + 1 < len(jobs) else [])
        cursor[0] = ph

    if stop_after != "a0":
        phase_b()
        P.fence()
        dump("ob_scr", ob_scr, [])

    def phase_a():
        ph = cursor[0]
        NB2 = 2
        upre = {t: [alloc([516], BF16) for _ in range(2)] for t in "qkv"}
        for t in "qkv":
            P.op("pool", lambda e, t=t: e.memset(upre[t][1][:, 512:515], 0.0), writes=[("upre", t, 1)])
        dg = alloc([12, 128], BF16)
        _thb = [alloc([512], F32) for _ in range(2)]
        th = {"q": _thb[0], "k": _thb[1], "v": _thb[0], "z": _thb[1]}
        thk = {"q": ("th", 0), "k": ("th", 1), "v": ("th", 0), "z": ("th", 1)}
        yq = alloc([512], F32)
        yk = alloc([512], F32)
        sq = {t: alloc([512], BF16) for t in "qk"}
        rin = {t: alloc([512], F32) for t in "qk"}
        vTt = alloc([512], BF16)
        khT = [alloc([512], BF16) for _ in range(NB2)]
        qhT = [alloc([512], BF16) for _ in range(NB2)]
        qgT = [alloc([512], BF16) for _ in range(3)]
        zsT = [alloc([512], BF16) for _ in range(3)]
        Ktok = [alloc([8, 128], BF16, 64) for _ in range(NB2)]
        Vtok = [alloc([8, 128], BF16, 64) for _ in range(NB2)]
        AqkT = [alloc([8, 64], BF16, 64) for _ in range(NB2)]
        TT = alloc([8, 64], BF16, 64)
        gcB = [alloc([512], F32) for _ in range(2)]
        gclB = [alloc([512], F32, 64) for _ in range(2)]
        egB = alloc([512], F32)
        E1 = alloc([8, 64], F32, 64)
        E2 = alloc([8, 64], F32, 64)
        GT = alloc([8, 64], F32, 64)
        GTb = alloc([8, 64], F32, 64)
        Pk = [alloc([8, 64], BF16, 64) for _ in range(2)]
        PkT = [alloc([8, 64], BF16, 64) for _ in range(2)]
        Xb = [alloc([8, 64], BF16, 64) for _ in range(2)]
        S_f = alloc([128], F32)
        S_b = alloc([128], BF16)
        vnew = [alloc([128], BF16, 64) for _ in range(2)]
        WnT = [alloc([8, 64], BF16) for _ in range(NB2)]
        Ubf = [alloc([8, 128], BF16, 64) for _ in range(NB2)]
        Kd = [alloc([8, 128], BF16, 64) for _ in range(NB2)]
        Kgn = alloc([8, 128], BF16, 64)
        oraw = alloc([512], F32)
        osq = alloc([512], BF16)
        orst = alloc([512], F32)
        oa = [alloc([512], BF16) for _ in range(2)]
        wsets = [[alloc([8, 128], BF16) for _ in range(4)] for _ in range(2)]

        def load_head_w(h):
            ws = wsets[h % 2]
            for ti, off in enumerate((O_QA, O_KA, O_VA, O_ZA)):
                load_w(w_in, off + h * 128, dst=ws[ti], key=("wh", h % 2, ti))

        m8 = lambda m: m.unsqueeze(1).to_broadcast([64, 8, 64])
        v3 = lambda a: a.rearrange("p (c i) -> p c i", i=64)
        fl = lambda a: a.rearrange("p c i -> p (c i)")

        def rsqrt_from_psum(b, dst, key, scale):
            P.op("act", lambda e: e.activation(out=dst, in_=ps[b][:, :], func=ACT.Ln, scale=scale, bias=EPS),
                 writes=[key, ("ps", b)], tset="L")
            P.op("act", lambda e: e.activation(out=dst, in_=dst, func=ACT.Exp, scale=-0.5), reads=[key], writes=[key])

        def stage12_tasks(st):
            h, tt8 = st // 8, st % 8
            bi = st % NB2
            b3 = st % 3
            ui = st % 2
            t0 = tt8 * 512
            ws = wsets[h % 2]
            tasks = []
            A = tasks.append

            def t_pre():
                if tt8 == 0:
                    if h + 1 < 8:
                        load_head_w(h + 1)
                    for ti in range(3):
                        for kk in range(4):
                            g = ti * 8 + h
                            P.op("dve", lambda e, ti=ti, kk=kk, g=g: e.tensor_scalar(
                                out=dg[:, ti * 4 + kk, :], in0=identf, scalar1=cw[:, g * 4 + kk:g * 4 + kk + 1], scalar2=None,
                                op0=ALU.mult), reads=["cst", "cw"], writes=["dg"])
                P.dma(lambda q: q.dma_start(out=gcB[ui], in_=gc_scr[h, t0:t0 + 512].partition_broadcast(128)),
                      reads=["gc_scr"], writes=[("gcB", ui)])
                P.dma(lambda q: q.dma_start(out=gclB[ui], in_=gcl_scr[h, t0:t0 + 512].partition_broadcast(64)),
                      reads=["gcl_scr"], writes=[("gclB", ui)])
            A(t_pre)

            def proj_split(wt, wkey, hb):
                def mk(k0):
                    def f():
                        if k0 == 0:
                            hb["b"] = bank("pc")
                        b = hb["b"]
                        for k in range(k0, k0 + 2):
                            P.op("pe", lambda e, k=k: e.matmul(ps[b][:, :], lhsT=wt[:, k, :], rhs=hT[:, k, t0:t0 + 512],
                                                               start=(k == 0), stop=(k == 7)),
                                 reads=[wkey] + hkeys(t0, t0 + 512), writes=[("ps", b)], fd=512)
                    return f
                for k0 in (0, 2, 4):
                    A(mk(k0))
                return mk(6)

            hbz = {}
            last_z = proj_split(ws[3], ("wh", h % 2, 3), hbz)

            def t_projz():
                last_z()
                b = hbz["b"]
                P.op("act", lambda e: e.activation(out=th["z"], in_=ps[b][:, :], func=ACT.Tanh, scale=0.5),
                     writes=[thk["z"], ("ps", b)], tset="T")
                P.op("dve", lambda e: e.scalar_tensor_tensor(out=zsT[b3], in0=th["z"], scalar=1.0, in1=ps[b][:, :],
                                                             op0=ALU.add, op1=ALU.mult),
                     reads=[thk["z"]], writes=[("zsT", b3), ("ps", b)])
            A(t_projz)

            for ti, t in enumerate("qkv"):
                hbp = {}
                last_p = proj_split(ws[ti], ("wh", h % 2, ti), hbp)

                def t_proj(ti=ti, t=t, hbp=hbp, last_p=last_p):
                    u = upre[t][ui]
                    up = upre[t][1 - ui]
                    last_p()
                    b = hbp["b"]
                    if tt8 == 0:
                        P.op("pool", lambda e: e.memset(u[:, 0:3], 0.0), writes=[("upre", t, ui)])
                    else:
                        P.op("pool", lambda e: e.tensor_copy(out=u[:, 0:3], in_=up[:, 512:515]),
                             reads=[("upre", t, 1 - ui)], writes=[("upre", t, ui)])
                    P.op("act", lambda e: e.activation(out=u[:, 3:515], in_=ps[b][:, :], func=ACT.Copy),
                         writes=[("upre", t, ui), ("ps", b)])
                A(t_proj)

            for ti, t in enumerate("qkv"):
                hbc = {}

                def t_conv0(ti=ti, t=t, hbc=hbc):
                    u = upre[t][ui]
                    hbc["b"] = b = bank("pc")
                    for kk in range(2):
                        P.op("pe", lambda e, kk=kk: e.matmul(ps[b][:, :], lhsT=dg[:, ti * 4 + kk, :], rhs=u[:, kk:kk + 512],
                                                             start=(kk == 0), stop=False),
                             reads=["dg", ("upre", t, ui)], writes=[("ps", b)], fd=512)
                A(t_conv0)

                def t_conv(ti=ti, t=t, hbc=hbc):
                    u = upre[t][ui]
                    b = hbc["b"]
                    for kk in range(2, 4):
                        P.op("pe", lambda e, kk=kk: e.matmul(ps[b][:, :], lhsT=dg[:, ti * 4 + kk, :], rhs=u[:, kk:kk + 512],
                                                             start=False, stop=(kk == 3)),
                             reads=["dg", ("upre", t, ui)], writes=[("ps", b)], fd=512)
                    P.op("act", lambda e: e.activation(out=th[t], in_=ps[b][:, :], func=ACT.Tanh, scale=0.5),
                         writes=[thk[t], ("ps", b)], tset="T")
                    dst = {"q": yq, "k": yk, "v": vTt}[t]
                    P.op("dve", lambda e: e.scalar_tensor_tensor(out=dst, in0=th[t], scalar=1.0, in1=ps[b][:, :],
                                                                 op0=ALU.add, op1=ALU.mult),
                         reads=[thk[t]], writes=[("y", t), ("ps", b)])
                A(t_conv)

            for t, y in (("q", yq), ("k", yk)):
                def t_norm(t=t, y=y):
                    P.op("pool", lambda e: e.tensor_tensor(out=sq[t], in0=y, in1=y, op=ALU.mult), reads=[("y", t)], writes=[("sq", t)])
                    b = bank("pc")
                    P.op("pe", lambda e: e.matmul(ps[b][:, :], lhsT=ones_bf, rhs=sq[t], start=True, stop=True),
                         reads=[("sq", t), "ones_bf"], writes=[("ps", b)], fd=512)
                    rsqrt_from_psum(b, rin[t], ("rin", t), 0.25)
                A(t_norm)

            def t_hat():
                P.op("dve", lambda e: e.scalar_tensor_tensor(out=khT[bi], in0=yk, scalar=0.5, in1=rin["k"],
                                                             op0=ALU.mult, op1=ALU.mult),
                     reads=[("y", "k"), ("rin", "k")], writes=[("khT", bi)])
                P.op("dve", lambda e: e.scalar_tensor_tensor(out=qhT[bi], in0=yq, scalar=0.5 * 128 ** -0.5, in1=rin["q"],
                                                             op0=ALU.mult, op1=ALU.mult),
                     reads=[("y", "q"), ("rin", "q")], writes=[("qhT", bi)])
                P.op("act", lambda e: e.activation(out=egB, in_=gcB[ui], func=ACT.Exp), reads=[("gcB", ui)], writes=["egB"])
                P.op("pool", lambda e: e.tensor_tensor(out=qgT[b3], in0=qhT[bi], in1=egB, op=ALU.mult),
                     reads=[("qhT", bi), "egB"], writes=[("qgT", b3)])
            A(t_hat)

            for (src, skey, dst, dkey, sc) in ((khT[bi], ("khT", bi), Ktok[bi], ("Ktok", bi), 1.0),
                                               (vTt, ("y", "v"), Vtok[bi], ("Vtok", bi), 0.5)):
                def t_tok(src=src, skey=skey, dst=dst, dkey=dkey, sc=sc):
                    b = bank("pc")
                    for c in range(8):
                        P.op("pe", lambda e, c=c: e.transpose(out=psb[b][0:64, c * 128:(c + 1) * 128],
                                                              in_=src[:, c * 64:(c + 1) * 64], identity=ident_bf),
                             reads=[skey, "ident_bf"], writes=[("ps", b)])
                    P.op("act", lambda e: e.activation(out=dst, in_=psb[b][0:64, :].rearrange("p (c k) -> p c k", k=128),
                                                       func=ACT.Copy, scale=sc), writes=[dkey, ("ps", b)])
                A(t_tok)

            n0 = tt8 * 8
            gcJ = gc_t[:, n0:n0 + 8, h].unsqueeze(2).to_broadcast([64, 8, 64])
            bJ = beta_t[:, n0:n0 + 8, h].unsqueeze(2).to_broadcast([64, 8, 64])
            hold = {}

            split = [len(tasks)]

            def t_gates():
                P.op("pool", lambda e: e.tensor_tensor(out=E1, in0=v3(gcB[ui][0:64, :]), in1=gcJ, op=ALU.subtract),
                     reads=[("gcB", ui), "gc_t"], writes=["E1"])
                P.op("pool", lambda e: e.tensor_tensor(out=E1, in0=E1, in1=m8(mask_incl), op=ALU.add),
                     reads=["E1", "cst"], writes=["E1"])
                P.op("act", lambda e: e.activation(out=GT, in_=E1, func=ACT.Exp), reads=["E1"], writes=["GT"])
                P.op("pool", lambda e: e.tensor_tensor(out=E2, in0=v3(gclB[ui]), in1=gcJ, op=ALU.subtract),
                     reads=[("gclB", ui), "gc_t"], writes=["E2"])
                P.op("pool", lambda e: e.tensor_tensor(out=E2, in0=E2, in1=m8(mask_strict), op=ALU.add),
                     reads=["E2", "cst"], writes=["E2"])
                P.op("act", lambda e: e.activation(out=GTb, in_=E2, func=ACT.Exp), reads=["E2"], writes=["GTb"])
            A(t_gates)

            def t_kkqk():
                bkk, bqk = bank("pre"), bank("pre")
                for c in range(8):
                    cs = slice(c * 64, (c + 1) * 64)
                    P.op("pe", lambda e, cs=cs: e.matmul(ps[bkk][0:64, cs], lhsT=khT[bi][:, cs], rhs=khT[bi][:, cs],
                                                         start=True, stop=True), reads=[("khT", bi)], writes=[("ps", bkk)])
                for c in range(8):
                    cs = slice(c * 64, (c + 1) * 64)
                    P.op("pe", lambda e, cs=cs: e.matmul(ps[bqk][0:64, cs], lhsT=khT[bi][:, cs], rhs=qhT[bi][:, cs],
                                                         start=True, stop=True), reads=[("khT", bi), ("qhT", bi)], writes=[("ps", bqk)])
                P.op("dve", lambda e: e.tensor_tensor(out=AqkT[bi], in0=v3(ps[bqk][0:64, :]), in1=GT, op=ALU.mult),
                     reads=["GT"], writes=[("AqkT", bi), ("ps", bqk)])
                P.op("dve", lambda e: e.scalar_tensor_tensor(out=Pk[0], in0=v3(ps[bkk][0:64, :]), scalar=-1.0, in1=GTb,
                                                             op0=ALU.mult, op1=ALU.mult),
                     reads=["GTb"], writes=[("Pk", 0), ("ps", bkk)])
            A(t_kkqk)

            def t_pt():
                b = bank("pre")
                for c in range(8):
                    P.op("pe", lambda e, c=c: e.transpose(out=psb[b][0:64, c * 64:(c + 1) * 64], in_=Pk[0][:, c, :],
                                                          identity=ident_bf[0:64, 0:64]),
                         reads=[("Pk", 0), "ident_bf"], writes=[("ps", b)])
                P.op("act", lambda e: e.activation(out=fl(PkT[0]), in_=psb[b][0:64, 0:512], func=ACT.Copy),
                     writes=[("PkT", 0), ("ps", b)])
                P.op("pool", lambda e: e.tensor_tensor(out=Xb[0], in0=Pk[0], in1=m8(identf[0:64, 0:64]), op=ALU.add),
                     reads=[("Pk", 0), "cst"], writes=[("Xb", 0)])
            A(t_pt)

            for lvl in range(5):
                cur = lvl % 2
                nxt = 1 - cur

                def t_sq(lvl=lvl, cur=cur, nxt=nxt):
                    if lvl < 4:
                        ba = bank("pre")
                        for c in range(8):
                            cs = slice(c * 64, (c + 1) * 64)
                            P.op("pe", lambda e, c=c, cs=cs: e.matmul(ps[ba][0:64, cs], lhsT=PkT[cur][:, c, :], rhs=Pk[cur][:, c, :],
                                                                      start=True, stop=True),
                                 reads=[("Pk", cur), ("PkT", cur)], writes=[("ps", ba)])
                    bt = bank("pre")
                    for c in range(8):
                        cs = slice(c * 64, (c + 1) * 64)
                        P.op("pe", lambda e, c=c, cs=cs: e.matmul(ps[bt][0:64, cs], lhsT=Pk[cur][:, c, :], rhs=PkT[cur][:, c, :],
                                                                  start=True, stop=True),
                             reads=[("Pk", cur), ("PkT", cur)], writes=[("ps", bt)])
                    if lvl < 4:
                        P.op("act", lambda e: e.activation(out=fl(Pk[nxt]), in_=ps[ba][0:64, :], func=ACT.Copy),
                             writes=[("Pk", nxt), ("ps", ba)])
                    P.op("dve", lambda e: e.tensor_copy(out=fl(PkT[nxt]), in_=ps[bt][0:64, :]),
                         writes=[("PkT", nxt), ("ps", bt)])
                A(t_sq)

                def t_x(lvl=lvl, cur=cur, nxt=nxt):
                    bx = bank("pre")
                    for c in range(8):
                        cs = slice(c * 64, (c + 1) * 64)
                        P.op("pe", lambda e, c=c, cs=cs: e.matmul(ps[bx][0:64, cs], lhsT=PkT[nxt][:, c, :], rhs=Xb[cur][:, c, :],
                                                                  start=True, stop=True),
                             reads=[("PkT", nxt), ("Xb", cur)], writes=[("ps", bx)])
                    P.op("dve", lambda e: e.tensor_tensor(out=fl(Xb[nxt]), in0=ps[bx][0:64, :], in1=fl(Xb[cur]), op=ALU.add),
                         reads=[("Xb", cur)], writes=[("Xb", nxt), ("ps", bx)])
                    if lvl == 4:
                        P.op("pool", lambda e: e.tensor_tensor(out=TT, in0=Xb[nxt], in1=bJ, op=ALU.mult),
                             reads=[("Xb", nxt), "beta_t"], writes=["TT"])
                A(t_x)

            ngJ = negeg_t[:, n0:n0 + 8, h].unsqueeze(2).to_broadcast([64, 8, 128])
            wdJ = wdec_t[:, n0:n0 + 8, h].unsqueeze(2).to_broadcast([64, 8, 128])

            def t_kg():
                P.op("pool", lambda e: e.tensor_tensor(out=Kgn, in0=Ktok[bi], in1=ngJ, op=ALU.mult),
                     reads=[("Ktok", bi), "negeg_t"], writes=["Kgn"])
                P.op("pool", lambda e: e.tensor_tensor(out=Kd[bi], in0=Ktok[bi], in1=wdJ, op=ALU.mult),
                     reads=[("Ktok", bi), "wdec_t"], writes=[("Kd", bi)])
            tasks.insert(len(tasks) - 6, t_kg)

            def t_w():
                b = bank("pre")
                for c in range(8):
                    P.op("pe", lambda e, c=c: e.matmul(ps[b][:, c * 64:(c + 1) * 64], lhsT=Kgn[:, c, :], rhs=TT[:, c, :],
                                                       start=True, stop=True),
                         reads=["Kgn", "TT"], writes=[("ps", b)])
                P.op("act", lambda e: e.activation(out=fl(WnT[bi]), in_=ps[b][:, :], func=ACT.Copy),
                     writes=[("WnT", bi), ("ps", b)])
            A(t_w)

            for half in range(2):
                def t_u(half=half):
                    b = bank("pre")
                    for c4 in range(4):
                        c = half * 4 + c4
                        P.op("pe", lambda e, c=c, c4=c4: e.matmul(ps[b][0:64, c4 * 128:(c4 + 1) * 128], lhsT=TT[:, c, :],
                                                                  rhs=Vtok[bi][:, c, :], start=True, stop=True),
                             reads=["TT", ("Vtok", bi)], writes=[("ps", b)])
                    P.op("dve", lambda e: e.tensor_copy(out=Ubf[bi][:, half * 4:half * 4 + 4, :],
                                                        in_=ps[b][0:64, :].rearrange("p (c k) -> p c k", k=128)),
                         writes=[("Ubf", bi), ("ps", b)])
                A(t_u)
            return tasks[:split[0]], tasks[split[0]:]

        def stage3_tasks(st):
            h, tt8 = st // 8, st % 8
            bi = st % NB2
            b3 = st % 3
            t0 = tt8 * 512
            n0 = tt8 * 8
            tasks = []
            A = tasks.append
            hold = {}

            def t_begin():
                hold["bo"] = bank("po")
            A(t_begin)
            for c in range(8):
                n = n0 + c
                cs = slice(c * 64, (c + 1) * 64)
                ri = n % 2
                first = (n == 0)

                def t_a(c=c, n=n, cs=cs, ri=ri, first=first):
                  with P.fds(pe=128, act=128, dve=128):
                    b1 = bank("q1")
                    P.op("pe", lambda e: e.matmul(ps[b1][0:64, 0:128], lhsT=ident_bf[0:64, 0:64], rhs=Ubf[bi][:, c, :],
                                                  start=True, stop=first),
                         reads=["ident_bf", ("Ubf", bi)], writes=[("ps", b1)])
                    if not first:
                        P.op("pe", lambda e: e.matmul(ps[b1][0:64, 0:128], lhsT=WnT[bi][:, c, :], rhs=S_b, start=False, stop=True),
                             reads=[("WnT", bi), "S_b"], writes=[("ps", b1)])
                    P.op("act", lambda e: e.activation(out=vnew[ri], in_=ps[b1][0:64, 0:128], func=ACT.Copy),
                         writes=[("vnew", ri), ("ps", b1)])
                A(t_a)

                def t_c(c=c, n=n, cs=cs, ri=ri, first=first):
                  with P.fds(pe=128, act=128, dve=128):
                    bo = hold["bo"]
                    b4 = bank("q4")
                    P.op("pe", lambda e: e.matmul(ps[b4][:, 0:128], lhsT=Kd[bi][:, c, :], rhs=vnew[ri], start=True, stop=True),
                         reads=[("Kd", bi), ("vnew", ri)], writes=[("ps", b4)])
                    if not first:
                        P.op("pe", lambda e: e.matmul(ps[bo][:, cs], lhsT=S_b, rhs=qgT[b3][:, cs], start=True, stop=False),
                             reads=["S_b", ("qgT", b3)], writes=[("ps", bo)])
                    P.op("pe", lambda e: e.matmul(ps[bo][:, cs], lhsT=vnew[ri], rhs=AqkT[bi][:, c, :], start=first, stop=True),
                         reads=[("vnew", ri), ("AqkT", bi)], writes=[("ps", bo)])
                    if first:
                        P.op("dve", lambda e: e.tensor_copy(out=S_b, in_=ps[b4][:, 0:128]), writes=["S_b", ("ps", b4)])
                        P.op("dve", lambda e: e.tensor_copy(out=S_f, in_=ps[b4][:, 0:128]), writes=["S_f", ("ps", b4)])
                    else:
                        P.op("dve", lambda e: e.scalar_tensor_tensor(
                            out=S_b, in0=S_f, scalar=dl_t[:, n, h:h + 1], in1=ps[b4][:, 0:128], op0=ALU.mult, op1=ALU.add),
                            reads=["S_f", "dl_t"], writes=["S_b", ("ps", b4)])
                        P.op("dve", lambda e: e.scalar_tensor_tensor(
                            out=S_f, in0=S_f, scalar=dl_t[:, n, h:h + 1], in1=ps[b4][:, 0:128], op0=ALU.mult, op1=ALU.add),
                            reads=["S_f", "dl_t"], writes=["S_f", ("ps", b4)])
                A(t_c)

            def t_epi():
                bo = hold["bo"]
                oi = st % 2
                P.op("act", lambda e: e.activation(out=oraw, in_=ps[bo][:, :], func=ACT.Copy), writes=["oraw", ("ps", bo)])
                if st == 0:
                    dump("khT0", khT[bi], [("khT", bi)])
                    dump("oraw0", oraw, ["oraw"])
                P.op("act", lambda e: e.activation(out=osq, in_=oraw, func=ACT.Square), reads=["oraw"], writes=["osq"])
                b = bank("pc")
                P.op("pe", lambda e: e.matmul(ps[b][:, :], lhsT=ones_bf, rhs=osq, start=True, stop=True),
                     reads=["osq", "ones_bf"], writes=[("ps", b)], fd=512)
                rsqrt_from_psum(b, orst, "orst", 1.0 / 128)
                P.op("dve", lambda e: e.scalar_tensor_tensor(out=oraw, in0=oraw, scalar=dnw[:, 0:1], in1=orst,
                                                             op0=ALU.mult, op1=ALU.mult),
                     reads=["oraw", "orst", "dnw"], writes=["oraw"])
                P.op("dve", lambda e: e.scalar_tensor_tensor(out=oa[oi], in0=oraw, scalar=0.5, in1=zsT[b3],
                                                             op0=ALU.mult, op1=ALU.mult),
                     reads=["oraw", ("zsT", b3)], writes=[("oa", oi)])
                P.dma(lambda q: q.dma_start(out=oa_scr[h, :, t0:t0 + 512], in_=oa[oi]),
                      reads=[("oa", oi)], writes=[("oa_scr", h, tt8)])
            A(t_epi)
            return tasks

        sg = [alloc([512], BF16) for _ in range(2)]
        sgf = [alloc([512], F32) for _ in range(2)]
        c0_tasks = []
        c0_hold = {}
        for c16 in range(16):
            for tt8 in range(8):
                def t_c0(c16=c16, tt8=tt8):
                    if tt8 == 0:
                        off = (O_GA + c16 * 128) if c16 < 8 else (O_GB + (c16 - 8) * 128)
                        c0_hold["w"] = load_w(w_in, off)
                    wt, wk = c0_hold["w"]
                    i = (c16 * 8 + tt8) % 2
                    b = proj_tile(wt, wk, 128, tt8, grp="pc")
                    P.op("act", lambda e: e.activation(out=sgf[i], in_=ps[b][:, :], func=ACT.Tanh, scale=0.5),
                         writes=[("sgf", i), ("ps", b)], tset="T")
                    P.op("dve", lambda e: e.tensor_scalar(out=sg[i], in0=sgf[i], scalar1=0.5, scalar2=0.5, op0=ALU.mult, op1=ALU.add),
                         reads=[("sgf", i)], writes=[("sg", i)])
                    P.dma(lambda q: q.dma_start(out=sg_scr[c16, :, tt8 * 512:(tt8 + 1) * 512], in_=sg[i]),
                          reads=[("sg", i)], writes=[("sg_scr", tt8)])
                c0_tasks.append(t_c0)

        load_head_w(0)
        NSTEP = 64

        def merge(ta, tb):
            out = []
            na, nb_ = len(ta), len(tb)
            jj = 0
            for ii, t in enumerate(ta):
                out.append(t)
                tgt = (nb_ * (ii + 1) + na - 1) // max(na, 1)
                while jj < min(tgt, nb_):
                    out.append(tb[jj])
                    jj += 1
            out.extend(tb[jj:])
            return out

        s1 = {}
        s2 = {}
        for st in range(NSTEP):
            s1[st], s2[st] = None, None

        def get12(st):
            if st >= NSTEP:
                return [], []
            return stage12_tasks(st)

        a1, a2 = get12(0)
        for t in a1:
            t()
        b1_, b2_ = get12(1)
        for t in merge(a2, b1_):
            t()
        pend2 = b2_
        for st in range(NSTEP):
            t3 = stage3_tasks(st)
            n1, n2 = get12(st + 2)
            filler = (pend2 + n1) if S2_FIRST else (merge(pend2, n1) if len(pend2) >= len(n1) else merge(n1, pend2))
            pend2 = n2
            for t in (t3 + filler if CHAIN_FIRST else merge(t3, filler)):
                t()
            for _ in range(2):
                if c0_tasks:
                    c0_tasks.pop(0)()
        while c0_tasks:
            c0_tasks.pop(0)()
        cursor[0] = ph

    if stop_after not in ("a0", "b"):
        phase_a()
        P.fence()
        dump("oa_scr", oa_scr, [])

    def phase_c0():
        ph = cursor[0]
        sg = [alloc([512], BF16) for _ in range(4)]
        sgf = [alloc([512], F32) for _ in range(4)]
        cnt = 0
        for c16 in range(16):
            off = (O_GA + c16 * 128) if c16 < 8 else (O_GB + (c16 - 8) * 128)
            wt, wk = load_w(w_in, off)
            for tt8 in range(8):
                b = proj_tile(wt, wk, 128, tt8, grp="all")
                i = cnt % 4
                cnt += 1
                P.op("act", lambda e, b=b, i=i: e.activation(out=sgf[i], in_=ps[b][:, :], func=ACT.Tanh, scale=0.5),
                     writes=[("sgf", i), ("ps", b)], tset="T")
                P.op("dve", lambda e, i=i: e.tensor_scalar(out=sg[i], in0=sgf[i], scalar1=0.5, scalar2=0.5, op0=ALU.mult, op1=ALU.add),
                     reads=[("sgf", i)], writes=[("sg", i)])
                P.dma(lambda q, i=i, c16=c16, tt8=tt8: q.dma_start(out=sg_scr[c16, :, tt8 * 512:(tt8 + 1) * 512], in_=sg[i]),
                      reads=[("sg", i)], writes=[("sg_scr", tt8)])
        cursor[0] = ph

    def phase_c1():
      with P.fds(pe=512):
        ph = cursor[0]
        wodn = alloc([8, 1024], BF16)
        wodil = alloc([4, 1024], BF16)
        wout = alloc([8, 1024], BF16)
        fnw_b = alloc([D], F32)
        for (dst, src, kc, key) in ((wodn, w_odn, 8, "wodn"), (wodil, w_odil, 4, "wodil"), (wout, w_out, 8, "wout")):
            for half in range(2):
                P.dma(lambda q, dst=dst, src=src, kc=kc, half=half: q.dma_start(
                    out=dst[:, 0:kc, half * 512:(half + 1) * 512],
                    in_=src[:, half * 512:(half + 1) * 512].rearrange("(k p) c -> p k c", p=128)),
                    writes=[(key, half)], q="pool")
        P.dma(lambda q: q.dma_start(out=fnw_b, in_=fnw_d.partition_broadcast(128)), writes=["fnw_b"])
        oat = [alloc([8, 512], BF16) for _ in range(2)]
        obt = [alloc([4, 512], BF16) for _ in range(2)]
        sgt = [alloc([16, 512], BF16) for _ in range(2)]
        mT = alloc([8, 512], BF16)
        m1 = [alloc([512], F32) for _ in range(2)]
        m2 = [alloc([512], F32) for _ in range(2)]
        xr = [alloc([D], F32) for _ in range(2)]
        xo = [alloc([D], F32) for _ in range(2)]
        yo = [alloc([D], F32) for _ in range(2)]
        ss = [alloc([1], F32) for _ in range(2)]
        rs = [alloc([1], F32) for _ in range(2)]
        junk2 = alloc([D], BF16)
        cnt = 0
        for tt8 in range(8):
            i = tt8 % 2
            sl = slice(tt8 * 512, (tt8 + 1) * 512)
            P.dma(lambda q, i=i, sl=sl: q.dma_start(out=oat[i], in_=oa_scr[:, :, sl].rearrange("h p t -> p h t")),
                  reads=[("oa_scr", hh, tt8) for hh in range(8)], writes=[("oat", i)])
            P.dma(lambda q, i=i, sl=sl: q.dma_start(out=obt[i], in_=ob_scr[:, :, sl].rearrange("h p t -> p h t")),
                  reads=[("ob_scr", hh, tt8) for hh in range(4)], writes=[("obt", i)])
            P.dma(lambda q, i=i, sl=sl: q.dma_start(out=sgt[i], in_=sg_scr[:, :, sl].rearrange("h p t -> p h t")),
                  reads=[("sg_scr", tt8)], writes=[("sgt", i)])
            for c in range(8):
                cs = slice(c * 128, (c + 1) * 128)
                ba = bank("all")
                for k in range(8):
                    P.op("pe", lambda e, b=ba, k=k, cs=cs, i=i: e.matmul(ps[b][:, :], lhsT=wodn[:, k, cs], rhs=oat[i][:, k, :],
                                                                         start=(k == 0), stop=(k == 7)),
                         reads=[("wodn", c // 4), ("oat", i)], writes=[("ps", ba)])
                bb = bank("all")
                for k in range(4):
                    P.op("pe", lambda e, b=bb, k=k, cs=cs, i=i: e.matmul(ps[b][:, :], lhsT=wodil[:, k, cs], rhs=obt[i][:, k, :],
                                                                         start=(k == 0), stop=(k == 3)),
                         reads=[("wodil", c // 4), ("obt", i)], writes=[("ps", bb)])
                mi = cnt % 2
                cnt += 1
                P.op("dve", lambda e, b=ba, mi=mi, i=i, c=c: e.tensor_tensor(out=m1[mi], in0=ps[b][:, :], in1=sgt[i][:, c, :], op=ALU.mult),
                     reads=[("sgt", i)], writes=[("m1", mi), ("ps", ba)])
                P.op("dve", lambda e, b=bb, mi=mi, i=i, c=c: e.tensor_tensor(out=m2[mi], in0=ps[b][:, :], in1=sgt[i][:, 8 + c, :], op=ALU.mult),
                     reads=[("sgt", i)], writes=[("m2", mi), ("ps", bb)])
                P.op("pool", lambda e, mi=mi, c=c: e.tensor_tensor(out=mT[:, c, :], in0=m1[mi], in1=m2[mi], op=ALU.add),
                     reads=[("m1", mi), ("m2", mi)], writes=[("mT", c)])
            for sub in range(4):
                tok0 = tt8 * 512 + sub * 128
                xi = sub % 2
                P.dma(lambda q, xi=xi, tok0=tok0: q.dma_start(out=xr[xi], in_=x_d[tok0:tok0 + 128, :]), writes=[("xr", xi)])
                for half in range(2):
                    b = bank("all")
                    hs = slice(half * 512, (half + 1) * 512)
                    for c in range(8):
                        P.op("pe", lambda e, b=b, c=c, sub=sub, hs=hs: e.matmul(
                            ps[b][:, :], lhsT=mT[:, c, sub * 128:(sub + 1) * 128], rhs=wout[:, c, hs], start=(c == 0), stop=(c == 7)),
                            reads=[("mT", c), ("wout", half)], writes=[("ps", b)])
                    P.op("dve", lambda e, b=b, xi=xi, hs=hs: e.tensor_tensor(out=xo[xi][:, hs], in0=ps[b][:, :], in1=xr[xi][:, hs], op=ALU.add),
                         reads=[("xr", xi)], writes=[("xo", xi, half), ("ps", b)])
                P.op("pool", lambda e, xi=xi: e.memset(ss[xi], 0.0), writes=[("ss", xi)])
                P.op("act", lambda e, xi=xi: e.activation(out=junk2, in_=xo[xi], func=ACT.Square, accum_out=ss[xi]),
                     reads=[("xo", xi, 0), ("xo", xi, 1), ("ss", xi)], writes=["junk2", ("ss", xi)])
                P.op("dve", lambda e, xi=xi: e.tensor_scalar(out=ss[xi], in0=ss[xi], scalar1=1.0 / D, scalar2=EPS, op0=ALU.mult, op1=ALU.add),
                     reads=[("ss", xi)], writes=[("ss", xi)])
                P.op("act", lambda e, xi=xi: e.activation(out=ss[xi], in_=ss[xi], func=ACT.Ln), reads=[("ss", xi)], writes=[("ss", xi)], tset="L", fd=1)
                P.op("act", lambda e, xi=xi: e.activation(out=rs[xi], in_=ss[xi], func=ACT.Exp, scale=-0.5), reads=[("ss", xi)], writes=[("rs", xi)], fd=1)
                P.op("dve", lambda e, xi=xi: e.scalar_tensor_tensor(out=yo[xi], in0=xo[xi], scalar=rs[xi], in1=fnw_b,
                                                                    op0=ALU.mult, op1=ALU.mult),
                     reads=[("xo", xi, 0), ("xo", xi, 1), ("rs", xi), "fnw_b"], writes=[("yo", xi)])
                P.dma(lambda q, xi=xi, tok0=tok0: q.dma_start(out=out_d[tok0:tok0 + 128, :], in_=yo[xi]),
                      reads=[("yo", xi)], writes=[("out", tok0)])
        cursor[0] = ph

    if stop_after is None:
        dump("sg_scr", sg_scr, [])
        cursor[0] = 0
        phase_c1()

    if RESCHEDULE:
        P.sim_time = P.reschedule()
    P.emit(final_wait_ops=[o for o in P.dma_last if o is not None])
    return nc


def _consts():
    c = np.zeros((128, 128 + 12 * 256 + 256), np.float32)
    c[:, 0:128] = np.eye(128, dtype=np.float32)
    slopes = (2.0 ** (-8.0 * np.arange(1, 13, dtype=np.float32) / 12)).reshape(3, 4)
    jk = np.arange(128)[:, None]
    iq = np.arange(128)[None, :]
    for gi in range(3):
        for h in range(4):
            s = slopes[gi, h] * DIL[gi]
            d0 = (iq - jk).astype(np.float32)
            b0 = np.where(iq >= jk, -s * d0, NEG)
            d1 = (128 + iq - jk).astype(np.float32)
            b1 = np.where(iq <= jk, -s * d1, NEG)
            g = gi * 4 + h
            c[:, 128 + g * 256:128 + g * 256 + 128] = b0
            c[:, 128 + g * 256 + 128:128 + g * 256 + 256] = b1
    MK = 128 + 3072
    j = np.arange(64)[:, None]
    i = np.arange(64)[None, :]
    c[0:64, MK:MK + 64] = np.where(i >= j, 0.0, NEG)
    c[0:64, MK + 64:MK + 128] = np.where(i > j, 0.0, NEG)
    c[0:64, MK + 128:MK + 192] = (j <= i).astype(np.float32)
    c[0:64, MK + 192:MK + 256] = 1.0
    return c


_NC_CACHE = {}


def _host_inputs(x, norm_w, w_in, conv_w, a_log, dt_bias, dn_norm_w, w_o_dn, w_o_dil, w_out, final_norm_w):
    f = lambda a: np.ascontiguousarray(np.asarray(a, dtype=np.float32))
    cw = f(conv_w)[0].reshape(4, 24, 128).transpose(2, 1, 0).reshape(128, 96)
    shared = {
        "w_in": f(w_in)[0], "w_o_dn": f(w_o_dn)[0], "w_o_dil": f(w_o_dil)[0], "w_out": f(w_out)[0],
        "norm_w": f(norm_w).reshape(1, D), "final_norm_w": f(final_norm_w).reshape(1, D),
        "conv_w_l": np.ascontiguousarray(cw), "a_log": f(a_log).reshape(1, 8), "dt_bias": f(dt_bias).reshape(1, 8),
        "dn_norm_w_l": f(dn_norm_w).reshape(128, 1), "consts": _consts(),
    }
    xs = f(x)
    return [dict(shared, x=xs[b]) for b in range(xs.shape[0])]


def kernel(x, norm_w, w_in, conv_w, a_log, dt_bias, dn_norm_w, w_o_dn, w_o_dil, w_out, final_norm_w):
    in_maps = _host_inputs(x, norm_w, w_in, conv_w, a_log, dt_bias, dn_norm_w, w_o_dn, w_o_dil, w_out, final_norm_w)
    if "nc" not in _NC_CACHE:
        _NC_CACHE["nc"] = build_nc()
    res = run_bass_kernel_spmd(_NC_CACHE["nc"], in_maps, core_ids=list(range(len(in_maps))))
    return np.stack([np.asarray(r["out"], dtype=np.float32).reshape(T, D) for r in res.results], axis=0)
```
